# Optimizing a Trainium2 kernel written in Bass

```python
import math
import jax, jax.numpy as jnp
from jax import lax
import numpy as np

D_MODEL = 1024
BATCH = 4
SEQ = 4096
DEPTH = 4

CHUNK = 64
Q_BLOCK = 128
MEM_LEN = 256
FOX_HEAD_DIM = 64
FOX_WIDTH = 3 * D_MODEL // 8
FOX_HEADS = FOX_WIDTH // FOX_HEAD_DIM
MLSTM_HEADS = 4
MLSTM_WIDTH = 3 * D_MODEL // 8
MLSTM_HEAD_DIM = MLSTM_WIDTH // MLSTM_HEADS
MEM_HEADS = 4
MEM_WIDTH = D_MODEL // 4
MEM_HEAD_DIM = MEM_WIDTH // MEM_HEADS
MIX_WIDTH = FOX_WIDTH + MLSTM_WIDTH + MEM_WIDTH
CONV_WIDTH = 4
LN_EPS = 1e-5
DEEPNORM_ALPHA = (2.0 * DEPTH) ** 0.25
DEEPNORM_BETA = (8.0 * DEPTH) ** -0.25
IN_SPLITS = (FOX_WIDTH, FOX_WIDTH, FOX_WIDTH, FOX_HEADS, FOX_WIDTH,
             MLSTM_WIDTH, MLSTM_WIDTH, MLSTM_WIDTH, MLSTM_HEADS, MLSTM_HEADS, MLSTM_WIDTH, MLSTM_WIDTH,
             MEM_WIDTH, MEM_WIDTH)
IN_COLS = sum(IN_SPLITS)

kernel_name = "fox_mlstm_memory_hybrid_deepnorm"


def layer_norm(x, g, b):
    xf = x.astype(jnp.float32)
    mu = jnp.mean(xf, axis=-1, keepdims=True)
    var = jnp.mean(jnp.square(xf - mu), axis=-1, keepdims=True)
    return ((xf - mu) * lax.rsqrt(var + LN_EPS) * g + b).astype(x.dtype)


def split_heads(t, n_heads):
    B, S, W = t.shape
    return t.reshape(B, S, n_heads, W // n_heads).transpose(0, 2, 1, 3)


def forgetting_attention(q, k, v, f_pre):
    B, S, _ = q.shape
    dt = q.dtype
    qh = split_heads(q.astype(jnp.float32), FOX_HEADS) * (FOX_HEAD_DIM ** -0.5)
    kh = split_heads(k.astype(jnp.float32), FOX_HEADS)
    vh = split_heads(v.astype(jnp.float32), FOX_HEADS)
    c = jnp.cumsum(jax.nn.log_sigmoid(f_pre.astype(jnp.float32)), axis=1).transpose(0, 2, 1)
    nb = S // Q_BLOCK
    qb = qh.reshape(B, FOX_HEADS, nb, Q_BLOCK, FOX_HEAD_DIM).transpose(2, 0, 1, 3, 4)
    cb = c.reshape(B, FOX_HEADS, nb, Q_BLOCK).transpose(2, 0, 1, 3)
    starts = jnp.arange(nb, dtype=jnp.int32) * Q_BLOCK
    key_pos = jnp.arange(S, dtype=jnp.int32)

    def block(args):
        q_blk, c_blk, start = args
        logits = (jnp.einsum('bhqd,bhkd->bhqk', q_blk, kh)
                  + c_blk[..., :, None] - c[..., None, :])
        q_pos = start + jnp.arange(Q_BLOCK, dtype=jnp.int32)
        causal = key_pos[None, :] <= q_pos[:, None]
        p = jax.nn.softmax(jnp.where(causal, logits, -jnp.inf), axis=-1)
        return jnp.einsum('bhqk,bhkd->bhqd', p, vh)

    out = lax.map(block, (qb, cb, starts))
    return out.transpose(1, 0, 3, 2, 4).reshape(B, S, FOX_WIDTH).astype(dt)


def causal_depthwise_conv(u, w, b):
    C = u.shape[-1]
    y = lax.conv_general_dilated(u, w[:, None, :].astype(u.dtype), window_strides=(1,),
                                 padding=((CONV_WIDTH - 1, 0),),
                                 dimension_numbers=('NWC', 'WIO', 'NWC'),
                                 feature_group_count=C)
    return y + b


def mlstm(q, k, v, i_pre, f_pre, o_pre, norm_g):
    B, S, _ = q.shape
    dt = q.dtype
    H, dh = MLSTM_HEADS, MLSTM_HEAD_DIM
    qh = split_heads(q.astype(jnp.float32), H)
    kh = split_heads(k.astype(jnp.float32), H) * (dh ** -0.5)
    vh = split_heads(v.astype(jnp.float32), H)
    log_i = i_pre.astype(jnp.float32).transpose(0, 2, 1)
    log_f = jax.nn.log_sigmoid(f_pre.astype(jnp.float32)).transpose(0, 2, 1)
    nc = S // CHUNK

    def chunks(t):
        return jnp.moveaxis(t.reshape(B, H, nc, CHUNK, *t.shape[3:]), 2, 0)

    tril = jnp.tril(jnp.ones((CHUNK, CHUNK), dtype=bool))

    def step(carry, xs):
        C, n, m = carry
        qc, kc, vc, ic, fc = xs
        b = jnp.cumsum(fc, axis=-1)
        log_d = jnp.where(tril, b[..., :, None] - b[..., None, :] + ic[..., None, :], -jnp.inf)
        inter = b + m[..., None]
        m_row = jnp.maximum(inter, jnp.max(log_d, axis=-1))
        s = jnp.einsum('bhtd,bhsd->bhts', qc, kc) * jnp.exp(log_d - m_row[..., None])
        dec = jnp.exp(inter - m_row)
        num = (jnp.einsum('bhts,bhse->bhte', s, vc)
               + dec[..., None] * jnp.einsum('bhtd,bhed->bhte', qc, C))
        den = jnp.sum(s, axis=-1) + dec * jnp.einsum('bhtd,bhd->bht', qc, n)
        h = num / jnp.maximum(jnp.abs(den), jnp.exp(-m_row))[..., None]
        b_last = b[..., -1]
        log_w = b_last[..., None] - b + ic
        m_new = jnp.maximum(b_last + m, jnp.max(log_w, axis=-1))
        w = jnp.exp(log_w - m_new[..., None])
        carry_dec = jnp.exp(b_last + m - m_new)
        C_new = carry_dec[..., None, None] * C + jnp.einsum('bhs,bhse,bhsd->bhed', w, vc, kc)
        n_new = carry_dec[..., None] * n + jnp.einsum('bhs,bhsd->bhd', w, kc)
        return (C_new, n_new, m_new), h

    init = (jnp.zeros((B, H, dh, dh), jnp.float32), jnp.zeros((B, H, dh), jnp.float32),
            jnp.zeros((B, H), jnp.float32))
    _, h = lax.scan(step, init, (chunks(qh), chunks(kh), chunks(vh), chunks(log_i), chunks(log_f)))
    h = jnp.moveaxis(h, 0, 2).reshape(B, H, S, dh).transpose(0, 2, 1, 3)
    h = jax.nn.sigmoid(o_pre.astype(jnp.float32)).reshape(B, S, H, dh) * h
    mu = jnp.mean(h, axis=-1, keepdims=True)
    var = jnp.mean(jnp.square(h - mu), axis=-1, keepdims=True)
    h = ((h - mu) * lax.rsqrt(var + LN_EPS)).reshape(B, S, MLSTM_WIDTH) * norm_g
    return h.astype(dt)


def memory_attention(q, mem_k, mem_v):
    B, S, _ = q.shape
    dt = q.dtype
    qh = split_heads(q.astype(jnp.float32), MEM_HEADS) * (MEM_HEAD_DIM ** -0.5)
    kh = split_heads(mem_k.astype(jnp.float32), MEM_HEADS)
    vh = split_heads(mem_v.astype(jnp.float32), MEM_HEADS)
    p = jax.nn.softmax(jnp.einsum('bhsd,bhmd->bhsm', qh, kh), axis=-1)
    out = jnp.einsum('bhsm,bhmd->bhsd', p, vh)
    return out.transpose(0, 2, 1, 3).reshape(B, S, MEM_WIDTH).astype(dt)


def hybrid_layer(x, mem, w_in, fox_f_bias, conv_w, conv_b, i_bias, f_bias, norm_g,
                 w_mem_kv, w_out, ln_g, ln_b):
    u = x @ w_in
    split_points = np.cumsum(IN_SPLITS)[:-1].tolist()
    (fq, fk, fv, ff, fz, mq, mk, mv, mi, mf, mo, mz, rq, rz) = jnp.split(u, split_points, axis=-1)
    y_fox = forgetting_attention(fq, fk, fv, ff + fox_f_bias) * jax.nn.silu(fz)
    qk = jax.nn.silu(causal_depthwise_conv(jnp.concatenate([mq, mk], axis=-1), conv_w, conv_b))
    mq_c, mk_c = jnp.split(qk, [MLSTM_WIDTH], axis=-1)
    y_ml = mlstm(mq_c, mk_c, mv, mi + i_bias, mf + f_bias, mo, norm_g) * jax.nn.silu(mz)
    mem_k, mem_v = jnp.split(mem @ w_mem_kv, [MEM_WIDTH], axis=-1)
    y_mem = memory_attention(rq, mem_k, mem_v) * jax.nn.silu(rz)
    y = jnp.concatenate([y_fox, y_ml, y_mem], axis=-1) @ w_out
    return layer_norm(DEEPNORM_ALPHA * x + y, ln_g, ln_b)


def setup_inputs(seed: int = 0) -> dict:
    key = jax.random.key(seed)
    ks = jax.random.split(key, 16)
    f32 = jnp.float32
    x = jax.random.normal(ks[0], (BATCH, SEQ, D_MODEL), f32)
    mem = jax.random.normal(ks[1], (BATCH, MEM_LEN, D_MODEL), f32)
    w_in = jax.random.normal(ks[2], (DEPTH, D_MODEL, IN_COLS), f32) * D_MODEL ** -0.5
    fox_f_bias = (jnp.linspace(1.0, 6.0, FOX_HEADS, dtype=f32)[None, :]
                  + 0.1 * jax.random.normal(ks[3], (DEPTH, FOX_HEADS), f32))
    mlstm_conv_w = jax.random.normal(ks[4], (DEPTH, CONV_WIDTH, 2 * MLSTM_WIDTH), f32) * CONV_WIDTH ** -0.5
    mlstm_conv_b = 0.01 * jax.random.normal(ks[5], (DEPTH, 2 * MLSTM_WIDTH), f32)
    mlstm_i_bias = 0.1 * jax.random.normal(ks[6], (DEPTH, MLSTM_HEADS), f32)
    mlstm_f_bias = (jnp.linspace(3.0, 6.0, MLSTM_HEADS, dtype=f32)[None, :]
                    + 0.1 * jax.random.normal(ks[7], (DEPTH, MLSTM_HEADS), f32))
    mlstm_norm_g = 1.0 + 0.02 * jax.random.normal(ks[8], (DEPTH, MLSTM_WIDTH), f32)
    w_mem_kv = jax.random.normal(ks[9], (DEPTH, D_MODEL, 2 * MEM_WIDTH), f32) * D_MODEL ** -0.5
    w_out = (jax.random.normal(ks[10], (DEPTH, MIX_WIDTH, D_MODEL), f32)
             * (MIX_WIDTH ** -0.5) * DEEPNORM_BETA)
    ln_g = 1.0 + 0.02 * jax.random.normal(ks[11], (DEPTH, D_MODEL), f32)
    ln_b = 0.02 * jax.random.normal(ks[12], (DEPTH, D_MODEL), f32)
    return {"x": x, "mem": mem, "w_in": w_in, "fox_f_bias": fox_f_bias,
            "mlstm_conv_w": mlstm_conv_w, "mlstm_conv_b": mlstm_conv_b,
            "mlstm_i_bias": mlstm_i_bias, "mlstm_f_bias": mlstm_f_bias,
            "mlstm_norm_g": mlstm_norm_g, "w_mem_kv": w_mem_kv, "w_out": w_out,
            "ln_g": ln_g, "ln_b": ln_b}


def reference(x, mem, w_in, fox_f_bias, mlstm_conv_w, mlstm_conv_b, mlstm_i_bias, mlstm_f_bias,
              mlstm_norm_g, w_mem_kv, w_out, ln_g, ln_b):
    for l in range(DEPTH):
        x = hybrid_layer(x, mem, w_in[l], fox_f_bias[l], mlstm_conv_w[l], mlstm_conv_b[l],
                         mlstm_i_bias[l], mlstm_f_bias[l], mlstm_norm_g[l], w_mem_kv[l],
                         w_out[l], ln_g[l], ln_b[l])
    return x
```

```python
import math
from contextlib import ExitStack
import numpy as np
import concourse.bass as bass
import concourse.mybir as mybir
from concourse.bass_utils import run_bass_kernel_spmd

F32 = mybir.dt.float32
BF16 = mybir.dt.bfloat16
AF = mybir.ActivationFunctionType
ALU = mybir.AluOpType

D = 1024
S_LEN = 4096
DEPTH = 4
NKC = 8
G = 512
NG = S_LEN // G
NT = S_LEN // 128
MEM_LEN = 256
LN_EPS = 1e-5
ALPHA = (2.0 * DEPTH) ** 0.25
IN_COLS = 3982
O_FQ, O_FK, O_FV, O_FF, O_FZ = 0, 384, 768, 1152, 1158
O_MQ, O_MK, O_MV, O_MI, O_MF, O_MO, O_MZ = 1542, 1926, 2310, 2694, 2698, 2702, 3086
O_RQ, O_RZ = 3470, 3726


class Stream:
    def __init__(self, name, sem, inc, q, dma):
        self.name, self.sem, self.inc, self.q, self.dma = name, sem, inc, q, dma
        self.n = 0


class Q:
    def __init__(self, name, h):
        self.name, self.h = name, h
        self.seen = {}
        self.stream = None


class Buf:
    __slots__ = ("name", "w", "r")

    def __init__(self, name):
        self.name = name
        self.w = None
        self.r = {}


class Sched:
    def __init__(self, nc):
        self.nc = nc
        self.PE = self._mkq("pe", nc.tensor)
        self.ACT = self._mkq("act", nc.scalar)
        self.DVE = self._mkq("dve", nc.vector)
        self.POOL = self._mkq("pool", nc.gpsimd)
        self.SP = Q("sp", nc.sync)
        self.queues = [self.PE, self.ACT, self.DVE, self.POOL, self.SP]
        self.streams = [q.stream for q in self.queues if q.stream is not None]
        self.nwaits = 0
        self.nops = 0

    def _mkq(self, name, h):
        q = Q(name, h)
        q.stream = Stream(name, self.nc.alloc_semaphore("s_" + name), 1, q, False)
        return q

    def dma_stream(self, name):
        s = Stream(name, self.nc.alloc_semaphore("d_" + name), 16, None, True)
        self.streams.append(s)
        return s

    def _wait(self, q, s, c):
        if q.seen.get(s, 0) >= c:
            return
        assert c <= s.n, f"wait on unissued signal {s.name} {c} > {s.n}"
        q.h.wait_ge(s.sem, c * s.inc)
        q.seen[s] = c
        self.nwaits += 1

    def op(self, q, emit, R=(), W=(), X=(), sig=True, stream=None):
        st = stream if stream is not None else q.stream
        deps = {}

        def add(d, same_ok):
            s, c = d
            if same_ok and (not s.dma) and s.q is q and stream is None:
                return
            if deps.get(s, 0) < c:
                deps[s] = c

        pe = q is self.PE
        for b in R:
            if b.w is not None:
                add(b.w, pe)
        for b in W:
            if b.w is not None:
                add(b.w, pe)
            for s, c in b.r.items():
                add((s, c), pe)
        for b in X:
            if b.w is not None:
                add(b.w, pe)
            for s, c in b.r.items():
                add((s, c), pe)
        for s, c in deps.items():
            if s.dma:
                c = s.n
            self._wait(q, s, c)
        ins = emit(q.h)
        cnt = st.n + 1
        if sig:
            ins.then_inc(st.sem, st.inc)
            st.n = cnt
        for b in R:
            if b.r.get(st, 0) < cnt:
                b.r[st] = cnt
        for b in W:
            b.w = (st, cnt)
            b.r = {}
        for b in X:
            b.w = (st, cnt)
            b.r = {}
        self.nops += 1
        return ins

    def barrier(self):
        for q in self.queues:
            for s in self.streams:
                if s.n > 0 and not (s.q is q):
                    self._wait(q, s, s.n)

    def finish(self, q):
        for s in self.streams:
            if s.n > 0 and not (s.q is q):
                self._wait(q, s, s.n)


class Tile:
    def __init__(self, handle, name, stream=None):
        self.t = handle
        self.b = Buf(name)
        self.st = stream

    def __getitem__(self, k):
        return self.t[k]


class Prog:
    def __init__(self, L, stop_after=None, dbg=()):
        import os
        dbg = tuple(dbg) + tuple(x for x in os.environ.get("KDBG", "").split(",") if x)
        self.L = L
        self.stop_after = stop_after
        self.dbg = set(dbg)
        nc = bass.Bass("TRN2", target_bir_lowering=False)
        self.nc = nc
        self.S = Sched(nc)
        self.uid = 0
        self.stack = None
        self.stream_pool = []
        self.all_streams = []
        self.phase_streams = []
        self._decl_dram()
        self._alloc()
        self._consts()
        self._build()

    def tile(self, name, shape, dt=F32, dma=False):
        self.uid += 1
        nm = f"{name}_{self.uid}"
        if self.stack is not None:
            hnd = self.stack.enter_context(self.nc.sbuf_tensor(nm, list(shape), dt))
        else:
            hnd = self.nc.alloc_sbuf_tensor(nm, list(shape), dt)
        st = None
        if dma:
            if self.stream_pool:
                st = self.stream_pool.pop()
            else:
                st = self.S.dma_stream(f"ds{len(self.all_streams)}")
                self.all_streams.append(st)
            if self.stack is not None:
                self.phase_streams.append(st)
        return Tile(hnd, nm, st)

    def sub_arena(self):
        prog = self

        class _Sub:
            def __enter__(self_):
                self_.outer = prog.stack
                self_.es = ExitStack()
                self_.es.__enter__()
                prog.stack = self_.es
                return self_

            def __exit__(self_, *exc):
                prog.S.barrier()
                prog.stack = self_.outer
                return self_.es.__exit__(*exc)
        return _Sub()

    def run_phase(self, fn, *a, **kw):
        assert self.stack is None
        with ExitStack() as es:
            self.stack = es
            self.phase_streams = []
            fn(*a, **kw)
            self.S.barrier()
            self.stream_pool.extend(self.phase_streams)
            self.phase_streams = []
            self.stack = None

    def dram_in(self, name, shape, dt=F32):
        return self.nc.dram_tensor(name, list(shape), dt, kind="ExternalInput").ap()

    def _decl_dram(self):
        L = self.L
        nc = self.nc
        self.x_in = self.dram_in("x", [S_LEN, D])
        self.mem_in = self.dram_in("mem", [MEM_LEN, D])
        self.wpk = self.dram_in("wpk", [L, 128, NKC * IN_COLS])
        self.woutpk = self.dram_in("woutpk", [L, 128, 9 * D])
        self.wmempk = self.dram_in("wmempk", [L, 128, NKC * 512])
        self.foxb = self.dram_in("foxb", [L, 6, 1])
        self.mib = self.dram_in("mib", [L, 4, 1])
        self.mfb = self.dram_in("mfb", [L, 4, 1])
        self.convw = self.dram_in("convw", [L, 96, 32])
        self.convb = self.dram_in("convb", [L, 96, 8])
        self.ngd = self.dram_in("ng", [L, 96, 4])
        self.lng = self.dram_in("lng", [L, 1, D])
        self.lnb = self.dram_in("lnb", [L, 1, D])
        self.out = nc.dram_tensor("out", [S_LEN, D], F32, kind="ExternalOutput").ap()
        self.xbuf = [nc.dram_tensor(f"xbuf{i}", [S_LEN, D], F32).ap() for i in range(2)] if L > 1 else []
        self.dbg_out = {}
        self.gdram = nc.dram_tensor("gdram", [2, 4 * NG, G], F32).ap()
        self.gdram_b = [Buf(f"gdram_g{g}") for g in range(NG)]

    def dbg_dram(self, name, shape, dt=F32):
        ap = self.nc.dram_tensor(name, list(shape), dt, kind="ExternalOutput").ap()
        self.dbg_out[name] = ap
        return ap

    def _alloc(self):
        nc, S = self.nc, self.S
        self.pb = []
        for i in range(8):
            t = nc.alloc_psum_tensor(f"pb{i}", [128, 512], F32)
            self.pb.append((t, Buf(f"pb{i}")))
        self.xT = nc.alloc_sbuf_tensor("xT", [128, NKC, S_LEN], BF16)
        self.xT_b = [Buf(f"xT_g{g}") for g in range(NG)]
        self.ycat = nc.alloc_sbuf_tensor("ycat", [128, 9, S_LEN], BF16)
        self.ycat_b = [[Buf(f"ycat_{c}_{g}") for g in range(NG)] for c in range(9)]
        self.stq = [S.dma_stream("stq0"), S.dma_stream("stq1")]
        self.memT = self.tile("memT", [128, NKC, MEM_LEN], BF16)
        self.gtok_all = self.tile("gtok_all", [128, NT, 4])
        self.gtokS_all = self.tile("gtokS_all", [128, NT, 4])

    def _consts(self):
        S = self.S
        self.onesf = self.tile("onesf", [128, 128])
        self.zerf = self.tile("zerf", [128, 128])
        self.identf = self.tile("identf", [128, 128])
        self.identb = self.tile("identb", [128, 128], BF16)
        self.mnegf = self.tile("mnegf", [128, 128])
        self.mnegb = self.tile("mnegb", [128, 128], BF16)
        self.avg96 = self.tile("avg96", [128, 96])
        self.fillsrc = self.tile("fillsrc", [128, 512], BF16)
        S.op(S.POOL, lambda h: h.memset(self.fillsrc[:], 0.5), W=[self.fillsrc.b])
        o, z = self.onesf, self.zerf
        S.op(S.POOL, lambda h: h.memset(o[:], 1.0), W=[o.b])
        S.op(S.POOL, lambda h: h.memset(z[:], 0.0), W=[z.b])
        S.op(S.POOL, lambda h: h.memset(self.avg96[:], 1.0 / 96.0), W=[self.avg96.b])
        S.op(S.POOL, lambda h: h.affine_select(out=self.identf[:], in_=o[:], pattern=[[1, 128]],
                                               compare_op=ALU.is_equal, fill=0.0, base=0, channel_multiplier=-1),
             R=[o.b], W=[self.identf.b])
        S.op(S.DVE, lambda h: h.tensor_copy(out=self.identb[:], in_=self.identf[:]), R=[self.identf.b], W=[self.identb.b])
        S.op(S.POOL, lambda h: h.affine_select(out=self.mnegf[:], in_=z[:], pattern=[[1, 128]],
                                               compare_op=ALU.is_ge, fill=-30000.0, base=0, channel_multiplier=-1),
             R=[z.b], W=[self.mnegf.b])
        S.op(S.DVE, lambda h: h.tensor_copy(out=self.mnegb[:], in_=self.mnegf[:]), R=[self.mnegf.b], W=[self.mnegb.b])

    def mm(self, bank, out_ap, pairs, R, start=True, stop=True, sgc=False):
        S = self.S
        n = len(pairs)
        for i, (lhsT, rhs) in enumerate(pairs):
            S.op(S.PE, lambda h, lhsT=lhsT, rhs=rhs, i=i: h.matmul(
                out_ap, lhsT=lhsT, rhs=rhs, start=(start and i == 0), stop=(stop and i == n - 1), skip_group_check=sgc),
                R=R, X=[self.pb[bank][1]], sig=(i == n - 1))

    def filler(self, n=256, bank=7):
        S = self.S
        pt, pbuf = self.pb[bank]
        S.op(S.PE, lambda h: h.matmul(pt[:, 0:n], lhsT=self.identb[:], rhs=self.fillsrc[:, 0:n], start=True, stop=True),
             R=[self.identb.b, self.fillsrc.b], X=[pbuf], sig=False)

    def bankrot(self, banks):
        st = {"i": 0}

        def nxt():
            b = banks[st["i"] % len(banks)]
            st["i"] += 1
            return b
        return nxt

    def build_xT_from_dram(self, x_src):
        S = self.S
        xin = [self.tile("xin", [128, D], dma=True) for _ in range(2)]
        rot = self.bankrot([0, 1, 2, 3])
        for t in range(NT):
            xt = xin[t % 2]
            S.op(S.SP, lambda h, xt=xt, t=t: h.dma_start(out=xt[:], in_=x_src[t * 128:(t + 1) * 128, :]),
                 W=[xt.b], stream=xt.st)
            self.transpose_into_xT(xt, t, rot)

    def transpose_into_xT(self, src, t, rot, eng=None):
        S = self.S
        for half in range(2):
            bk = rot()
            pt, pbuf = self.pb[bk]
            for c in range(4):
                kc = half * 4 + c
                S.op(S.PE, lambda h, kc=kc, c=c, pt=pt: h.transpose(
                    pt[:, c * 128:(c + 1) * 128], src[:, kc * 128:(kc + 1) * 128], self.identf[:]),
                    R=[src.b, self.identf.b], X=[pbuf], sig=(c == 3))
            e = eng if eng is not None else (S.ACT if half == 0 else S.DVE)
            dst = self.xT[:, half * 4:(half + 1) * 4, t * 128:(t + 1) * 128]
            srcp = pt[:, :].rearrange("p (c t) -> p c t", c=4)
            if e is S.ACT:
                S.op(e, lambda h, dst=dst, srcp=srcp: h.activation(out=dst, in_=srcp, func=AF.Copy),
                     X=[pbuf], W=[self.xT_b[t // 4]])
            else:
                S.op(e, lambda h, dst=dst, srcp=srcp: h.tensor_copy(out=dst, in_=srcp),
                     X=[pbuf], W=[self.xT_b[t // 4]])

    def load_w(self, dst_tile, src_ap):
        S = self.S
        self.n_sw = getattr(self, "n_sw", 0) + 1
        st = S.dma_stream(f"sw{self.n_sw}")
        S.op(S.POOL, lambda h: h.dma_start(out=dst_tile[:], in_=src_ap), W=[dst_tile.b], stream=st)


    def phase_fox(self, l):
        S = self.S
        sizes, offs = win_offsets()
        rot = self.bankrot([0, 1])
        rot_s = self.bankrot([2, 3, 4])
        rot_o = self.bankrot([5, 6])
        shiftrows = self.tile("shiftrows", [6, S_LEN], BF16)
        negc_tok = self.tile("negc_tok", [128, NT, 6])
        negcref_rep = self.tile("negcref_rep", [128, 6, NG])
        with self.sub_arena():
            rotG = self.bankrot([2, 3, 4])
            gens = [self.fox_prep(l, offs, rot, shiftrows, negc_tok, negcref_rep),
                    self.mlstm_gates(l, offs, rotG, self.gtok_all, self.gtokS_all, self.gdram, self.gdram_b, float(96 ** -0.5))]
            alive = True
            while alive:
                alive = False
                for gn in gens:
                    if next(gn, "end") != "end":
                        alive = True
        wh = [self.tile("wh", [128, NKC * 256], BF16, dma=True) for _ in range(2)]
        KaT = self.tile("KaT", [65, S_LEN], BF16)
        KaT_b = [Buf(f"KaT_g{g}") for g in range(NG)]
        S.op(S.POOL, lambda h: h.memset(KaT[64:65, :], 1.0), W=KaT_b)
        Vaug = self.tile("Vaug", [128, NT, 192], BF16)
        V_ones = Buf("V_ones")
        V_b = [[Buf(f"V_{par}_g{g}") for g in range(NG)] for par in range(2)]
        S.op(S.POOL, lambda h: h.memset(Vaug[:, :, 64:128], 1.0), W=[V_ones])
        QaT = [self.tile("QaT", [65, G], BF16, dma=True) for _ in range(2)]
        szf = [self.tile("szf", [128, G], BF16) for _ in range(2)]
        PT = [self.tile("PTf", [128, G], BF16) for _ in range(3)]
        rd = [self.tile("rdf", [128, G]) for _ in range(1)]
        tn = [self.tile("tnf", [128, G]) for _ in range(1)]
        thf = [self.tile("thf", [128, G]) for _ in range(2)]
        tabs = [self.tile("tab", [128, NT, NG]) for _ in range(2)]
        ipc = {"i": 0}

        def build_tab(hd):
            tab = tabs[hd % 2]
            for qg in range(NG):
                S.op(S.DVE, lambda h, qg=qg, tab=tab, hd=hd: h.tensor_scalar(
                    out=tab[:, :, qg], in0=negc_tok[:, :, hd], scalar1=negcref_rep[:, hd, qg:qg + 1], scalar2=None, op0=ALU.subtract),
                    R=[negc_tok.b, negcref_rep.b], W=[tab.b])

        def gen_proj(hd, g):
            odd = hd % 2
            W_ = wh[hd % 2]
            Wv = W_[:, :].rearrange("p (c n) -> p c n", c=NKC)
            gsl = slice(g * G, (g + 1) * G)
            qa, sz = QaT[g % 2], szf[g % 2]
            vc0 = 128 if odd else 0
            RR = [W_.b, self.xT_b[g]]
            bk = rot()
            pt, pbuf = self.pb[bk]
            for kc in range(NKC):
                S.op(S.PE, lambda h, kc=kc, pt=pt: h.matmul(pt[:, :], lhsT=Wv[:, kc, 0:128], rhs=self.xT[:, kc, gsl],
                                                            start=(kc == 0), stop=(kc == NKC - 1)), R=RR, X=[pbuf], sig=(kc == NKC - 1))
                if kc % 2 == 1 and kc < NKC - 1:
                    yield
            S.op(S.DVE, lambda h, pt=pt: h.tensor_scalar(out=qa[0:64, :], in0=pt[0:64, :], scalar1=0.125, scalar2=None, op0=ALU.mult), X=[pbuf], W=[qa.b])
            S.op(S.DVE, lambda h, pt=pt: h.tensor_copy(out=KaT[0:64, gsl], in_=pt[64:128, :]), X=[pbuf], W=[KaT_b[g]])
            S.op(S.SP, lambda h: h.dma_start(out=qa[64:65, :], in_=shiftrows[hd:hd + 1, gsl]), R=[shiftrows.b], W=[qa.b], stream=qa.st)
            yield
            bk = rot()
            pt, pbuf = self.pb[bk]
            for j in range(4):
                blk = slice(g * G + j * 128, g * G + (j + 1) * 128)
                for kc in range(NKC):
                    S.op(S.PE, lambda h, kc=kc, pt=pt, j=j, blk=blk: h.matmul(
                        pt[:, j * 64:(j + 1) * 64], lhsT=self.xT[:, kc, blk], rhs=Wv[:, kc, 128:192],
                        start=(kc == 0), stop=(kc == NKC - 1)), R=RR, X=[pbuf], sig=(kc == NKC - 1))
                    if kc == 3:
                        yield
                yield
            S.op(S.DVE, lambda h, pt=pt: h.tensor_copy(
                out=Vaug[:, 4 * g:4 * g + 4, vc0:vc0 + 64], in_=pt[:, 0:256].rearrange("p (j c) -> p j c", j=4)),
                X=[pbuf], W=[V_b[odd][g]])
            yield
            bk = rot()
            pt, pbuf = self.pb[bk]
            zr = slice(64, 128) if odd else slice(0, 64)
            for kc in range(NKC):
                if odd:
                    S.op(S.PE, lambda h, kc=kc, pt=pt: h.matmul(pt[:, :], lhsT=Wv[:, kc, 128:256], rhs=self.xT[:, kc, gsl],
                                                                start=(kc == 0), stop=(kc == NKC - 1)), R=RR, X=[pbuf], sig=(kc == NKC - 1))
                else:
                    S.op(S.PE, lambda h, kc=kc, pt=pt: h.matmul(pt[0:64, :], lhsT=Wv[:, kc, 192:256], rhs=self.xT[:, kc, gsl],
                                                                start=(kc == 0), stop=(kc == NKC - 1)), R=RR, X=[pbuf], sig=(kc == NKC - 1))
                if kc % 2 == 1 and kc < NKC - 1:
                    yield
            th = thf[g % 2]
            S.op(S.ACT, lambda h, pt=pt: h.activation(out=th[zr, :], in_=pt[zr, :], func=AF.Tanh, scale=0.5), X=[pbuf], W=[th.b])
            S.op(S.DVE, lambda h, pt=pt: h.scalar_tensor_tensor(out=sz[zr, :], in0=th[zr, :], scalar=1.0, in1=pt[zr, :], op0=ALU.add, op1=ALU.mult),
                 X=[pbuf], R=[th.b], W=[sz.b])
            yield

        N_CHUNKS = 20

        def attention(hd, g, chunks):
            odd = hd % 2
            lc0 = 64 if odd else 0
            tab = tabs[hd % 2]
            qa, sz = QaT[g % 2], szf[g % 2]
            nkb = 4 * g + 4
            emitted = {"n": 0}
            bo = rot_o()
            po, pobuf = self.pb[bo]

            def issue_st(kb):
                diag = kb >= 4 * g
                qoff = (kb - 4 * g) * 128 if diag else 0
                n = G - qoff
                bs = rot_s()
                ps_, psbuf = self.pb[bs]
                self.mm(bs, ps_[:, 0:n], [(KaT[0:65, kb * 128:(kb + 1) * 128], qa[0:65, qoff:G])], R=[KaT_b[kb // 4], qa.b],
                        start=True, stop=not diag)
                if diag:
                    self.mm(bs, ps_[:, 0:128], [(self.identb[:], self.mnegb[:])], R=[self.identb.b, self.mnegb.b], start=False, stop=True)
                return ps_, psbuf, qoff, n

            nxt = issue_st(0)
            for kb in range(nkb):
                cur = nxt
                if kb + 1 < nkb:
                    nxt = issue_st(kb + 1)
                ps_, psbuf, qoff, n = cur
                pt_ = PT[ipc["i"] % 3]
                ipc["i"] += 1
                S.op(S.ACT, lambda h, ps_=ps_, pt_=pt_, n=n, kb=kb: h.activation(
                    out=pt_[:, 0:n], in_=ps_[:, 0:n], func=AF.Exp, bias=tab[:, kb, g:g + 1]), X=[psbuf], R=[tab.b], W=[pt_.b])
                did = False
                if chunks is not None:
                    want = ((kb + 1) * N_CHUNKS + nkb - 1) // nkb
                    while emitted["n"] < want:
                        if next(chunks, "end") != "end":
                            did = True
                        emitted["n"] += 1
                if not did:
                    self.filler(256)
                self.mm(bo, po[:, qoff:G], [(Vaug[:, kb, lc0:lc0 + 128], pt_[:, 0:n])], R=[V_b[odd][kb // 4], V_ones, pt_.b],
                        start=(kb == 0), stop=(kb == nkb - 1))
            if chunks is not None:
                for _ in chunks:
                    pass
            self.attn_epilogue(po, pobuf, bool(odd), sz, rd[0], tn[0], hd // 2, g, half=True)

        self.load_w(wh[0], self.wpk[l][:, int(offs[1]):int(offs[2])])
        build_tab(0)
        for _ in gen_proj(0, 0):
            pass
        for hd in range(6):
            if hd + 1 < 6:
                self.load_w(wh[(hd + 1) % 2], self.wpk[l][:, int(offs[2 + hd]):int(offs[3 + hd])])
                build_tab(hd + 1)
            for g in range(NG):
                if g + 1 < NG:
                    chunks = gen_proj(hd, g + 1)
                elif hd + 1 < 6:
                    chunks = gen_proj(hd + 1, 0)
                else:
                    chunks = None
                attention(hd, g, chunks)


    def fox_prep(self, l, offs, rot, shiftrows, negc_tok, negcref_rep):
        S = self.S
        wff = self.tile("wff", [128, NKC * 6], BF16, dma=True)
        self.load_w(wff, self.wpk[l][:, int(offs[0]):int(offs[1])])
        wffv = wff[:, :].rearrange("p (c n) -> p c n", c=NKC)
        fb = self.tile("fb", [6, 1], dma=True)
        S.op(S.SP, lambda h: h.dma_start(out=fb[:], in_=self.foxb[l]), W=[fb.b], stream=fb.st)
        nfb = self.tile("nfb", [6, 1])
        S.op(S.DVE, lambda h: h.tensor_scalar(out=nfb[:], in0=fb[:], scalar1=-1.0, scalar2=None, op0=ALU.mult), R=[fb.b], W=[nfb.b])
        negcref6 = self.tile("negcref6", [6, NG])
        oh6 = self.tile("oh6", [6, 6, 128])
        for hh in range(6):
            S.op(S.POOL, lambda h, hh=hh: h.affine_select(
                out=oh6[:, hh, :], in_=self.onesf[0:6, :], pattern=[[0, 128]], compare_op=ALU.is_equal,
                fill=0.0, base=-hh, channel_multiplier=1), R=[self.onesf.b], W=[oh6.b])
        e_t = [self.tile("e_t", [6, G]) for _ in range(2)]
        negc = [self.tile("negc", [6, G]) for _ in range(2)]
        for g in range(NG):
            gsl = slice(g * G, (g + 1) * G)
            bk = rot()
            pt, pbuf = self.pb[bk]
            self.mm(bk, pt[0:6, :], [(wffv[:, kc, :], self.xT[:, kc, gsl]) for kc in range(NKC)], R=[wff.b, self.xT_b[g]])
            e, nc_, ncp = e_t[g % 2], negc[g % 2], negc[(g - 1) % 2]
            lt = e
            S.op(S.ACT, lambda h, pt=pt, e=e: h.activation(out=e[:], in_=pt[0:6, :], func=AF.Exp, scale=-1.0, bias=nfb[:, 0:1]),
                 X=[pbuf], R=[nfb.b], W=[e.b])
            S.op(S.ACT, lambda h, e=e, lt=lt: h.activation(out=lt[:], in_=e[:], func=AF.Ln, bias=1.0), R=[e.b], W=[lt.b])
            init = 0.0 if g == 0 else ncp[:, G - 1:G]
            S.op(S.DVE, lambda h, lt=lt, nc_=nc_, init=init: h.tensor_tensor_scan(
                out=nc_[:], data0=self.onesf[0:6, 0:1].to_broadcast([6, G]), data1=lt[:], initial=init, op0=ALU.mult, op1=ALU.add),
                R=[lt.b, self.onesf.b] + ([ncp.b] if g > 0 else []), W=[nc_.b])
            S.op(S.DVE, lambda h, nc_=nc_, g=g: h.tensor_copy(out=negcref6[:, g:g + 1], in_=nc_[:, 0:1]), R=[nc_.b], W=[negcref6.b])
            S.op(S.DVE, lambda h, nc_=nc_, gsl=gsl: h.tensor_scalar(out=shiftrows[:, gsl], in0=nc_[:], scalar1=nc_[:, 0:1], scalar2=-1.0,
                                                                 op0=ALU.subtract, op1=ALU.mult), R=[nc_.b], W=[shiftrows.b])
            bk = rot()
            pt, pbuf = self.pb[bk]
            for j in range(4):
                S.op(S.PE, lambda h, pt=pt, j=j, nc_=nc_: h.transpose(pt[:, j * 6:(j + 1) * 6], nc_[0:6, j * 128:(j + 1) * 128], self.identf[0:6, 0:6]),
                     R=[nc_.b, self.identf.b], X=[pbuf], sig=(j == 3))
            S.op(S.DVE, lambda h, pt=pt, g=g: h.tensor_copy(out=negc_tok[:, 4 * g:4 * g + 4, :], in_=pt[:, 0:24].rearrange("p (j c) -> p j c", j=4)),
                 X=[pbuf], W=[negc_tok.b])
            yield
        bk = rot()
        pt, pbuf = self.pb[bk]
        for hh in range(6):
            self.mm(bk, pt[:, hh * NG:(hh + 1) * NG], [(oh6[:, hh, :], negcref6[:, :])], R=[oh6.b, negcref6.b])
        S.op(S.DVE, lambda h: h.tensor_copy(out=negcref_rep[:, :, :], in_=pt[:, 0:6 * NG].rearrange("p (a b) -> p a b", a=6)),
             X=[pbuf], W=[negcref_rep.b])


    def mlstm_gates(self, l, offs, rot, gtok_all, gtokS_all, gdram, gdram_b, sK):
        S = self.S
        wmif = self.tile("wmif", [128, NKC * 8], BF16, dma=True)
        self.load_w(wmif, self.wpk[l][:, int(offs[7]):int(offs[8])])
        wmifv = wmif[:, :].rearrange("p (c n) -> p c n", c=NKC)
        ib = self.tile("ib", [4, 1], dma=True)
        fbm = self.tile("fbm", [4, 1], dma=True)
        for (t_, src) in ((ib, self.mib[l]), (fbm, self.mfb[l])):
            S.op(S.SP, lambda h, t_=t_, src=src: h.dma_start(out=t_[:], in_=src), W=[t_.b], stream=t_.st)
        nfbm = self.tile("nfbm", [4, 1])
        S.op(S.DVE, lambda h: h.tensor_scalar(out=nfbm[:], in0=fbm[:], scalar1=-1.0, scalar2=None, op0=ALU.mult), R=[fbm.b], W=[nfbm.b])
        e_t = [self.tile("me", [4, G]) for _ in range(2)]
        negF = [self.tile("negF", [4, G]) for _ in range(2)]
        gg = [self.tile("gg", [4, G]) for _ in range(2)]
        Gc = [self.tile("Gc", [4, G], dma=True) for _ in range(2)]
        nM = [self.tile("nM", [4, G], dma=True) for _ in range(2)]
        onesb = self.onesf[0:4, 0:1].to_broadcast([4, G])
        gview = [gdram[k].rearrange("(h g) t -> h g t", g=NG) for k in range(2)]
        for g in range(NG):
            gsl = slice(g * G, (g + 1) * G)
            xb = self.xT_b[g]
            e_, nF, g_, G_, nM_ = e_t[g % 2], negF[g % 2], gg[g % 2], Gc[g % 2], nM[g % 2]
            nFp, Gp = negF[(g - 1) % 2], Gc[(g - 1) % 2]
            bI = rot()
            pI, pIb = self.pb[bI]
            self.mm(bI, pI[0:4, :], [(wmifv[:, kc, 0:4], self.xT[:, kc, gsl]) for kc in range(NKC)], R=[wmif.b, xb])
            bF = rot()
            pF, pFb = self.pb[bF]
            self.mm(bF, pF[0:4, :], [(wmifv[:, kc, 4:8], self.xT[:, kc, gsl]) for kc in range(NKC)], R=[wmif.b, xb])
            S.op(S.ACT, lambda h, pF=pF, e_=e_: h.activation(out=e_[:], in_=pF[0:4, :], func=AF.Exp, scale=-1.0, bias=nfbm[:, 0:1]),
                 X=[pFb], R=[nfbm.b], W=[e_.b])
            S.op(S.ACT, lambda h, e_=e_: h.activation(out=e_[:], in_=e_[:], func=AF.Ln, bias=1.0), R=[e_.b], W=[e_.b])
            initF = 0.0 if g == 0 else nFp[:, G - 1:G]
            S.op(S.DVE, lambda h, initF=initF, nF=nF, e_=e_: h.tensor_tensor_scan(out=nF[:], data0=onesb, data1=e_[:], initial=initF,
                                                                                op0=ALU.mult, op1=ALU.add),
                 R=[e_.b, self.onesf.b] + ([nFp.b] if g > 0 else []), W=[nF.b])
            S.op(S.DVE, lambda h, pI=pI, g_=g_, nF=nF: h.scalar_tensor_tensor(out=g_[:], in0=pI[0:4, :], scalar=ib[:, 0:1], in1=nF[:],
                                                                             op0=ALU.add, op1=ALU.add), X=[pIb], R=[ib.b, nF.b], W=[g_.b])
            initG = 0.0 if g == 0 else Gp[:, G - 1:G]
            S.op(S.DVE, lambda h, initG=initG, G_=G_, g_=g_: h.tensor_tensor_scan(out=G_[:], data0=onesb, data1=g_[:], initial=initG,
                                                                                op0=ALU.mult, op1=ALU.max),
                 R=[g_.b, self.onesf.b] + ([Gp.b] if g > 0 else []), W=[G_.b])
            S.op(S.DVE, lambda h, nM_=nM_, nF=nF, G_=G_: h.tensor_tensor(out=nM_[:], in0=nF[:], in1=G_[:], op=ALU.subtract),
                 R=[nF.b, G_.b], W=[nM_.b])
            S.op(S.SP, lambda h, G_=G_, g=g: h.dma_start(out=gview[0][:, g, :], in_=G_[:]), R=[G_.b], W=[gdram_b[g]], stream=G_.st)
            S.op(S.SP, lambda h, nM_=nM_, g=g: h.dma_start(out=gview[1][:, g, :], in_=nM_[:]), R=[nM_.b], W=[gdram_b[g]], stream=nM_.st)
            bk = rot()
            pt, pbuf = self.pb[bk]
            for j in range(4):
                S.op(S.PE, lambda h, pt=pt, j=j, g_=g_: h.transpose(pt[:, j * 4:(j + 1) * 4], g_[0:4, j * 128:(j + 1) * 128], self.identf[0:4, 0:4]),
                     R=[g_.b, self.identf.b], X=[pbuf], sig=(j == 3))
            S.op(S.DVE, lambda h, pt=pt, g=g: h.tensor_copy(out=gtok_all[:, 4 * g:4 * g + 4, :], in_=pt[:, 0:16].rearrange("p (j c) -> p j c", j=4)),
                 X=[pbuf], W=[gtok_all.b])
            yield
        S.op(S.DVE, lambda h: h.tensor_scalar(out=gtokS_all[:, :, :], in0=gtok_all[:, :, :], scalar1=float(math.log(sK)), scalar2=None, op0=ALU.add),
             R=[gtok_all.b], W=[gtokS_all.b])

    def phase_mlstm(self, l):
        S = self.S
        sizes, offs = win_offsets()
        sK = float(96 ** -0.5)
        rot = self.bankrot([0, 1, 2])
        B_S, B_T, B_N, B_D = 4, 5, 6, 7
        gtok_all, gtokS_all = self.gtok_all, self.gtokS_all
        gdram, gdram_b = self.gdram, self.gdram_b
        cw = self.tile("cw", [96, 32], dma=True)
        cb = self.tile("cb", [96, 8], dma=True)
        ngt = self.tile("ngt", [96, 4], dma=True)
        for (t_, src) in ((cw, self.convw[l]), (cb, self.convb[l]), (ngt, self.ngd[l])):
            S.op(S.SP, lambda h, t_=t_, src=src: h.dma_start(out=t_[:], in_=src), W=[t_.b], stream=t_.st)
        lnhalf = self.tile("lnhalf", [128, 1])
        epsb = self.tile("epsb", [128, 1])
        S.op(S.POOL, lambda h: h.memset(lnhalf[:], float(math.log(0.5))), W=[lnhalf.b])
        S.op(S.POOL, lambda h: h.memset(epsb[:], float(LN_EPS)), W=[epsb.b])
        wml = [self.tile("wml", [128, NKC * 480], BF16, dma=True) for _ in range(2)]
        Grep = [self.tile("Grep", [128, G], dma=True) for _ in range(2)]
        clamp = [self.tile("clamp", [96, G], dma=True) for _ in range(2)]
        mu_chain = [self.tile("mu_chain", [128, 8]) for _ in range(2)]
        wexp = [self.tile("wexp", [128, 4]) for _ in range(2)]
        carry = [self.tile("carry", [128, 4]) for _ in range(2)]
        warg = self.tile("warg", [128, 4])
        carg = self.tile("carg", [128, 4])
        qpre = self.tile("qpre", [96, G + 3])
        kpre = self.tile("kpre", [96, G + 3])
        QT = [self.tile("QT", [96, G]) for _ in range(2)]
        KT = [self.tile("KT", [96, G]) for _ in range(2)]
        Vtok = [self.tile("Vtok", [128, 4, 192], BF16) for _ in range(2)]
        Vones = [Buf("Vtok_ones0"), Buf("Vtok_ones1")]
        for k in range(2):
            S.op(S.POOL, lambda h, k=k: h.memset(Vtok[k][:, :, 96:192], 1.0), W=[Vones[k]])
        sgo = [self.tile("sgo", [96, G]) for _ in range(2)]
        szm = [self.tile("szm", [96, G], BF16) for _ in range(2)]
        argDT = self.tile("argDT", [128, G])
        AT4 = self.tile("AT4", [128, G], BF16)
        decQ = self.tile("decQ", [96, G])
        decQb = self.tile("decQb", [96, G], BF16)
        Stb = self.tile("Stb", [96, 192], BF16)
        tBb = self.tile("tBb", [96, G], BF16)
        tAb = self.tile("tAb", [96, G], BF16)
        avg96b = self.tile("avg96b", [96, 96], BF16)
        S.op(S.DVE, lambda h: h.tensor_copy(out=avg96b[:], in_=self.avg96[0:96, :]), R=[self.avg96.b], W=[avg96b.b])
        Khat4 = self.tile("Khat4", [128, 4, 96], BF16)
        St = self.tile("St", [96, 192])
        tA = self.tile("tA", [96, G])
        tB = self.tile("tB", [96, G])
        tC = self.tile("tC", [96, G])
        cnt = {"b": 0}
        iters = [(hd, g) for hd in range(4) for g in range(NG)]

        def gen_A(i):
            hd, g = iters[i]
            k = i % 2
            gsl = slice(g * G, (g + 1) * G)
            xb = self.xT_b[g]
            W_ = wml[hd % 2]
            Wv = W_[:, :].rearrange("p (c n) -> p c n", c=NKC)
            RR = [W_.b, xb]
            Gr, cl, mu, we, ca = Grep[k], clamp[k], mu_chain[k], wexp[k], carry[k]
            mup = mu_chain[1 - k]
            row = hd * NG + g
            S.op(S.SP, lambda h: h.dma_start(out=Gr[:], in_=gdram[0][row:row + 1, :].partition_broadcast(128)),
                 R=[gdram_b[g]], W=[Gr.b], stream=Gr.st)
            S.op(S.SP, lambda h: h.dma_start(out=cl[:], in_=gdram[1][row:row + 1, :].partition_broadcast(96)),
                 R=[gdram_b[g]], W=[cl.b], stream=cl.st)
            S.op(S.ACT, lambda h: h.activation(out=cl[:], in_=cl[:], func=AF.Exp), R=[cl.b], W=[cl.b])
            if g == 0:
                S.op(S.POOL, lambda h: h.memset(mu[:, 0:1], 0.0), W=[mu.b])
                S.op(S.POOL, lambda h: h.memset(qpre[:, 0:3], 0.0), W=[qpre.b])
                S.op(S.POOL, lambda h: h.memset(kpre[:, 0:3], 0.0), W=[kpre.b])
            else:
                S.op(S.DVE, lambda h: h.tensor_copy(out=mu[:, 0:1], in_=mup[:, 4:5]), R=[mup.b], W=[mu.b])
            S.op(S.DVE, lambda h: h.tensor_copy(out=mu[:, 1:5], in_=Gr[:, :].rearrange("p (j t) -> p j t", j=4)[:, :, 127]),
                 R=[Gr.b], W=[mu.b])
            S.op(S.DVE, lambda h: h.tensor_tensor(out=warg[:], in0=gtok_all[:, 4 * g:4 * g + 4, hd], in1=mu[:, 1:5], op=ALU.subtract),
                 R=[gtok_all.b, mu.b], W=[warg.b])
            S.op(S.ACT, lambda h: h.activation(out=we[:], in_=warg[:], func=AF.Exp), R=[warg.b], W=[we.b])
            S.op(S.DVE, lambda h: h.tensor_tensor(out=carg[:], in0=mu[:, 0:4], in1=mu[:, 1:5], op=ALU.subtract), R=[mu.b], W=[carg.b])
            S.op(S.ACT, lambda h: h.activation(out=ca[:], in_=carg[:], func=AF.Exp), R=[carg.b], W=[ca.b])
            yield
            for (pre, acc, c0, qk) in ((qpre, QT[k], 0, 0), (kpre, KT[k], 96, 1)):
                if g > 0:
                    S.op(S.DVE, lambda h, pre=pre: h.tensor_copy(out=pre[:, 0:3], in_=pre[:, G:G + 3]), R=[pre.b], W=[pre.b])
                bk = rot()
                pt, pbuf = self.pb[bk]
                for kc in range(NKC):
                    S.op(S.PE, lambda h, kc=kc, pt=pt, c0=c0: h.matmul(pt[0:96, :], lhsT=Wv[:, kc, c0:c0 + 96], rhs=self.xT[:, kc, gsl],
                                                                      start=(kc == 0), stop=(kc == NKC - 1)), R=RR, X=[pbuf], sig=(kc == NKC - 1))
                    if kc % 2 == 1 and kc < NKC - 1:
                        yield
                S.op(S.ACT, lambda h, pt=pt, pre=pre: h.activation(out=pre[:, 3:G + 3], in_=pt[0:96, :], func=AF.Copy), X=[pbuf], W=[pre.b])
                yield
                wi = lambda tap, qk=qk: cw[:, (qk * 4 + hd) * 4 + tap:(qk * 4 + hd) * 4 + tap + 1]
                bi = cb[:, qk * 4 + hd:qk * 4 + hd + 1]
                S.op(S.DVE, lambda h, pre=pre, acc=acc, wi=wi, bi=bi: h.tensor_scalar(
                    out=acc[:], in0=pre[:, 3:G + 3], scalar1=wi(3), scalar2=bi, op0=ALU.mult, op1=ALU.add),
                    R=[pre.b, cw.b, cb.b], W=[acc.b])
                for kk in (1, 2, 3):
                    S.op(S.DVE, lambda h, pre=pre, acc=acc, wi=wi, kk=kk: h.scalar_tensor_tensor(
                        out=acc[:], in0=pre[:, 3 - kk:G + 3 - kk], scalar=wi(3 - kk), in1=acc[:], op0=ALU.mult, op1=ALU.add),
                        R=[pre.b, cw.b, acc.b], W=[acc.b])
                    if kk == 2:
                        yield
                yield
            bk = rot()
            pt, pbuf = self.pb[bk]
            Vt = Vtok[k]
            for j in range(4):
                blk = slice(g * G + j * 128, g * G + (j + 1) * 128)
                for kc in range(NKC):
                    S.op(S.PE, lambda h, kc=kc, pt=pt, j=j, blk=blk: h.matmul(
                        pt[:, j * 96:(j + 1) * 96], lhsT=self.xT[:, kc, blk], rhs=Wv[:, kc, 192:288],
                        start=(kc == 0), stop=(kc == NKC - 1)), R=RR, X=[pbuf], sig=(kc == NKC - 1))
                yield
            S.op(S.DVE, lambda h, pt=pt, Vt=Vt: h.tensor_copy(out=Vt[:, :, 0:96], in_=pt[:, 0:384].rearrange("p (j c) -> p j c", j=4)),
                 X=[pbuf], W=[Vt.b])
            yield
            held = []
            for c0 in (288, 384):
                bk = rot()
                pt, pbuf = self.pb[bk]
                for kc in range(NKC):
                    S.op(S.PE, lambda h, kc=kc, pt=pt, c0=c0: h.matmul(pt[0:96, :], lhsT=Wv[:, kc, c0:c0 + 96], rhs=self.xT[:, kc, gsl],
                                                                      start=(kc == 0), stop=(kc == NKC - 1)), R=RR, X=[pbuf], sig=(kc == NKC - 1))
                held.append((pt, pbuf))
            S.op(S.ACT, lambda h: h.activation(out=QT[k][:], in_=QT[k][:], func=AF.Silu), R=[QT[k].b], W=[QT[k].b])
            S.op(S.ACT, lambda h: h.activation(out=KT[k][:], in_=KT[k][:], func=AF.Silu), R=[KT[k].b], W=[KT[k].b])
            (pto, pbo), (ptz, pbz) = held
            S.op(S.ACT, lambda h: h.activation(out=sgo[k][:], in_=pto[0:96, :], func=AF.Tanh, scale=0.5), X=[pbo], W=[sgo[k].b])
            S.op(S.ACT, lambda h: h.activation(out=szm[k][:], in_=ptz[0:96, :], func=AF.Silu), X=[pbz], W=[szm[k].b])
            yield

        N_A = 30

        def run_BC(i, chunks):
            hd, g = iters[i]
            k = i % 2
            gsl = slice(g * G, (g + 1) * G)
            Gr, cl, mu, we, ca = Grep[k], clamp[k], mu_chain[k], wexp[k], carry[k]
            Q_, K_, Vt, Vo = QT[k], KT[k], Vtok[k], Vones[k]
            sg, sz = sgo[k], szm[k]
            pS, pSb = self.pb[B_S]
            pT, pTb = self.pb[B_T]
            pN, pNb = self.pb[B_N]
            pD, pDb = self.pb[B_D]

            fstate = {"pt": 0, "em": 0}
            N_PTS = 13

            def fill(n=1):
                did = False
                fstate["pt"] += 1
                if chunks is not None:
                    want = (fstate["pt"] * N_A + N_PTS - 1) // N_PTS
                    while fstate["em"] < want:
                        if next(chunks, "end") != "end":
                            did = True
                        fstate["em"] += 1
                if not did:
                    self.filler(512, bank=3)

            if g == 0:
                S.op(S.POOL, lambda h: h.memset(St[:], 0.0), W=[St.b])
                S.op(S.POOL, lambda h: h.memset(Stb[:], 0.0), W=[Stb.b])
            for j in range(4):
                bs = slice(j * 128, (j + 1) * 128)
                self.mm(B_S, pS[:, bs], [(K_[0:96, bs], Q_[0:96, bs])], R=[K_.b, Q_.b])
            S.op(S.DVE, lambda h: h.scalar_tensor_tensor(
                out=argDT[:, :].rearrange("p (j t) -> p j t", j=4), in0=Gr[:, :].rearrange("p (j t) -> p j t", j=4), scalar=-1.0,
                in1=self.mnegf[:, :].unsqueeze(1).to_broadcast([128, 4, 128]), op0=ALU.mult, op1=ALU.add),
                R=[Gr.b, self.mnegf.b], W=[argDT.b])
            for j in range(4):
                bs = slice(j * 128, (j + 1) * 128)
                S.op(S.ACT, lambda h, bs=bs, j=j: h.activation(out=argDT[:, bs], in_=argDT[:, bs], func=AF.Exp,
                                                              bias=gtokS_all[:, 4 * g + j, hd:hd + 1]), R=[argDT.b, gtokS_all.b], W=[argDT.b])
            for j in range(4):
                bs = slice(j * 128, (j + 1) * 128)
                S.op(S.ACT, lambda h, bs=bs, j=j: h.activation(out=decQ[:, bs], in_=Gr[0:96, bs], func=AF.Exp, scale=-1.0,
                                                              bias=mu[0:96, j:j + 1]), R=[Gr.b, mu.b], W=[decQ.b])
            fill()
            S.op(S.DVE, lambda h: h.tensor_tensor(out=AT4[:], in0=pS[:, :], in1=argDT[:], op=ALU.mult), X=[pSb], R=[argDT.b], W=[AT4.b])
            S.op(S.POOL, lambda h: h.tensor_tensor(out=decQb[:], in0=Q_[:], in1=decQ[:], op=ALU.mult), R=[Q_.b, decQ.b], W=[decQb.b])
            for j in range(4):
                bs = slice(j * 128, (j + 1) * 128)
                S.op(S.PE, lambda h, bs=bs, j=j: h.transpose(pT[:, j * 96:(j + 1) * 96], K_[0:96, bs], self.identf[0:96, 0:96]),
                     R=[K_.b, self.identf.b], X=[pTb], sig=(j == 3))
            S.op(S.DVE, lambda h: h.scalar_tensor_tensor(
                out=Khat4[:, :, :], in0=pT[:, 0:384].rearrange("p (j c) -> p j c", j=4), scalar=sK,
                in1=we[:, 0:4].unsqueeze(2).to_broadcast([128, 4, 96]), op0=ALU.mult, op1=ALU.mult),
                X=[pTb], R=[we.b], W=[Khat4.b])
            fill()
            for j in range(4):
                bs = slice(j * 128, (j + 1) * 128)
                self.mm(B_N, pN[0:96, bs], [(Vt[:, j, 0:96], AT4[:, bs])], R=[Vt.b, AT4.b], start=(j == 0), stop=False, sgc=True)
            for j in range(4):
                bs = slice(j * 128, (j + 1) * 128)
                self.mm(B_D, pD[0:96, bs], [(Vt[:, j, 96:192], AT4[:, bs])], R=[Vo, AT4.b], start=(j == 0), stop=False, sgc=True)
            for j in range(4):
                ub, ubuf, uo = (pT, pTb, B_T) if j < 2 else (pS, pSb, B_S)
                c0 = (j % 2) * 192
                self.mm(uo, ub[0:96, c0:c0 + 192], [(Khat4[:, j, :], Vt[:, j, :])], R=[Khat4.b, Vt.b, Vo])
            fill()
            for j in range(4):
                bs = slice(j * 128, (j + 1) * 128)
                ub, ubuf = (pT, pTb) if j < 2 else (pS, pSb)
                c0 = (j % 2) * 192
                self.mm(B_N, pN[0:96, bs], [(Stb[0:96, 0:96], decQb[:, bs])], R=[Stb.b, decQb.b], start=False, stop=True, sgc=True)
                self.mm(B_D, pD[0:96, bs], [(Stb[0:96, 96:192], decQb[:, bs])], R=[Stb.b, decQb.b], start=False, stop=True, sgc=True)
                S.op(S.DVE, lambda h, j=j, ub=ub, c0=c0: h.scalar_tensor_tensor(out=St[:], in0=St[:], scalar=ca[0:96, j:j + 1],
                                                                               in1=ub[0:96, c0:c0 + 192], op0=ALU.mult, op1=ALU.add),
                     X=[ubuf], R=[ca.b, St.b], W=[St.b])
                S.op(S.DVE, lambda h: h.tensor_copy(out=Stb[:], in_=St[:]), R=[St.b], W=[Stb.b])
                if j % 2 == 1:
                    fill()
            S.op(S.DVE, lambda h: h.tensor_tensor(out=tA[:], in0=pD[0:96, :], in1=cl[:], op=ALU.max), X=[pDb], R=[cl.b], W=[tA.b])
            S.op(S.DVE, lambda h: h.scalar_tensor_tensor(out=tA[:], in0=pD[0:96, :], scalar=-1.0, in1=tA[:], op0=ALU.mult, op1=ALU.max),
                 X=[pDb], R=[tA.b], W=[tA.b])
            fill()
            S.op(S.ACT, lambda h: h.activation(out=tA[:], in_=tA[:], func=AF.Ln), R=[tA.b], W=[tA.b])
            S.op(S.ACT, lambda h: h.activation(out=tA[:], in_=tA[:], func=AF.Exp, scale=-1.0, bias=lnhalf[0:96, 0:1]), R=[tA.b, lnhalf.b], W=[tA.b])
            fill()
            S.op(S.DVE, lambda h: h.tensor_tensor(out=tB[:], in0=pN[0:96, :], in1=tA[:], op=ALU.mult), X=[pNb], R=[tA.b], W=[tB.b])
            S.op(S.DVE, lambda h: h.scalar_tensor_tensor(out=tB[:], in0=sg[:], scalar=1.0, in1=tB[:], op0=ALU.add, op1=ALU.mult),
                 R=[tB.b, sg.b], W=[tB.b])
            fill()
            S.op(S.ACT, lambda h: h.activation(out=tBb[:], in_=tB[:], func=AF.Copy), R=[tB.b], W=[tBb.b])
            bk = rot()
            pt, pbuf = self.pb[bk]
            self.mm(bk, pt[0:96, :], [(avg96b[:, :], tBb[:, :])], R=[avg96b.b, tBb.b])
            fill()
            S.op(S.DVE, lambda h, pt=pt: h.tensor_tensor(out=tC[:], in0=tB[:], in1=pt[0:96, :], op=ALU.subtract), X=[pbuf], R=[tB.b], W=[tC.b])
            S.op(S.ACT, lambda h: h.activation(out=tAb[:], in_=tC[:], func=AF.Square), R=[tC.b], W=[tAb.b])
            fill()
            bk = rot()
            pt, pbuf = self.pb[bk]
            self.mm(bk, pt[0:96, :], [(avg96b[:, :], tAb[:, :])], R=[avg96b.b, tAb.b])
            fill()
            S.op(S.ACT, lambda h, pt=pt: h.activation(out=tB[:], in_=pt[0:96, :], func=AF.Ln, bias=epsb[0:96, 0:1]), X=[pbuf], R=[epsb.b], W=[tB.b])
            S.op(S.ACT, lambda h: h.activation(out=tB[:], in_=tB[:], func=AF.Exp, scale=-0.5), R=[tB.b], W=[tB.b])
            fill()
            S.op(S.DVE, lambda h: h.scalar_tensor_tensor(out=tC[:], in0=tC[:], scalar=ngt[:, hd:hd + 1], in1=tB[:], op0=ALU.mult, op1=ALU.mult),
                 R=[tC.b, ngt.b, tB.b], W=[tC.b])
            S.op(S.POOL, lambda h: h.tensor_tensor(out=self.ycat[0:96, 3 + hd, gsl], in0=tC[:], in1=sz[:], op=ALU.mult),
                 R=[tC.b, sz.b], W=[self.ycat_b[3 + hd][g]])
            if chunks is not None:
                for _ in chunks:
                    pass

        self.load_w(wml[0], self.wpk[l][:, int(offs[8]):int(offs[9])])
        for _ in gen_A(0):
            pass
        for i in range(len(iters)):
            hd, g = iters[i]
            if g == 0 and hd + 1 < 4:
                self.load_w(wml[(hd + 1) % 2], self.wpk[l][:, int(offs[9 + hd]):int(offs[10 + hd])])
            chunks = gen_A(i + 1) if i + 1 < len(iters) else None
            run_BC(i, chunks)

    def build_memT(self):
        S = self.S
        mt = [self.tile("memin", [128, D], dma=True) for _ in range(2)]
        rot = self.bankrot([4, 5, 6, 7])
        for t in range(2):
            S.op(S.SP, lambda h, t=t: h.dma_start(out=mt[t][:], in_=self.mem_in[t * 128:(t + 1) * 128, :]), W=[mt[t].b], stream=mt[t].st)
            for half in range(2):
                bk = rot()
                pt, pbuf = self.pb[bk]
                for c in range(4):
                    kc = half * 4 + c
                    S.op(S.PE, lambda h, kc=kc, c=c, pt=pt, t=t: h.transpose(
                        pt[:, c * 128:(c + 1) * 128], mt[t][:, kc * 128:(kc + 1) * 128], self.identf[:]),
                        R=[mt[t].b, self.identf.b], X=[pbuf], sig=(c == 3))
                dst = self.memT[:, half * 4:(half + 1) * 4, t * 128:(t + 1) * 128]
                srcp = pt[:, :].rearrange("p (c t) -> p c t", c=4)
                S.op(S.DVE, lambda h, dst=dst, srcp=srcp: h.tensor_copy(out=dst, in_=srcp), X=[pbuf], W=[self.memT.b])

    def phase_mem(self, l):
        S = self.S
        sizes, offs = win_offsets()
        wm = self.tile("wm", [128, NKC * 512], BF16, dma=True)
        wr = self.tile("wr", [128, NKC * 512], BF16, dma=True)
        self.load_w(wm, self.wmempk[l])
        self.load_w(wr, self.wpk[l][:, int(offs[12]):int(offs[13])])
        wmv = wm[:, :].rearrange("p (c n) -> p c n", c=NKC)
        wrv = wr[:, :].rearrange("p (c n) -> p c n", c=NKC)
        KmT = self.tile("KmT", [128, 2, MEM_LEN], BF16)
        Vm = self.tile("Vm", [128, 2, 4, 128], BF16)
        rot = self.bankrot([0, 1])
        rot_s = self.bankrot([2, 3, 4])
        rot_o = self.bankrot([5, 6])
        for p in range(2):
            bk = rot()
            pt, pbuf = self.pb[bk]
            self.mm(bk, pt[:, 0:MEM_LEN], [(wmv[:, kc, p * 128:(p + 1) * 128], self.memT[:, kc, :]) for kc in range(NKC)],
                    R=[wm.b, self.memT.b])
            S.op(S.DVE, lambda h, pt=pt, p=p: h.tensor_copy(out=KmT[:, p, :], in_=pt[:, 0:MEM_LEN]), X=[pbuf], W=[KmT.b])
        S.op(S.POOL, lambda h: h.memset(Vm[:], 1.0), W=[Vm.b])
        for mb in range(2):
            bk = rot()
            pt, pbuf = self.pb[bk]
            self.mm(bk, pt[:, 0:256], [(self.memT[:, kc, mb * 128:(mb + 1) * 128], wmv[:, kc, 256:512]) for kc in range(NKC)],
                    R=[wm.b, self.memT.b])
            for h4 in range(4):
                c0 = 0 if h4 % 2 == 0 else 64
                S.op(S.DVE, lambda h, pt=pt, mb=mb, h4=h4, c0=c0: h.tensor_copy(
                    out=Vm[:, mb, h4, c0:c0 + 64], in_=pt[:, h4 * 64:(h4 + 1) * 64]), X=[pbuf], W=[Vm.b])
        QTm = [self.tile("QTm", [128, G], BF16) for _ in range(2)]
        szp = [self.tile("szp", [128, G], BF16) for _ in range(2)]
        thm = self.tile("thm", [128, G])
        PT = [self.tile("PTm", [128, G], BF16) for _ in range(3)]
        rd = [self.tile("rdm", [128, G]) for _ in range(2)]
        tn = [self.tile("tnm", [128, G]) for _ in range(2)]
        it = 0
        ip = 0
        for g in range(NG):
            gsl = slice(g * G, (g + 1) * G)
            for p in range(2):
                q, z = QTm[it % 2], szp[it % 2]
                it += 1
                bk = rot()
                pt, pbuf = self.pb[bk]
                self.mm(bk, pt[:, :], [(wrv[:, kc, p * 128:(p + 1) * 128], self.xT[:, kc, gsl]) for kc in range(NKC)],
                        R=[wr.b, self.xT_b[g]])
                S.op(S.ACT, lambda h, pt=pt, q=q: h.activation(out=q[:], in_=pt[:, :], func=AF.Copy, scale=0.125), X=[pbuf], W=[q.b])
                bk = rot()
                pt, pbuf = self.pb[bk]
                self.mm(bk, pt[:, :], [(wrv[:, kc, 256 + p * 128:256 + (p + 1) * 128], self.xT[:, kc, gsl]) for kc in range(NKC)],
                        R=[wr.b, self.xT_b[g]])
                S.op(S.ACT, lambda h, pt=pt: h.activation(out=thm[:], in_=pt[:, :], func=AF.Exp, scale=-1.0), X=[pbuf], W=[thm.b])
                S.op(S.ACT, lambda h: h.activation(out=thm[:], in_=thm[:], func=AF.Ln, bias=1.0), R=[thm.b], W=[thm.b])
                S.op(S.ACT, lambda h: h.activation(out=thm[:], in_=thm[:], func=AF.Exp, scale=-1.0), R=[thm.b], W=[thm.b])
                S.op(S.DVE, lambda h, pt=pt, z=z: h.tensor_tensor(out=z[:], in0=pt[:, :], in1=thm[:], op=ALU.mult),
                     X=[pbuf], R=[thm.b], W=[z.b])
                for hh in range(2):
                    h4 = 2 * p + hh
                    r0 = 64 * hh
                    bo = rot_o()
                    po, pobuf = self.pb[bo]
                    sts = []
                    for mb in range(2):
                        bs = rot_s()
                        ps_, psbuf = self.pb[bs]
                        self.mm(bs, ps_[:, :], [(KmT[r0:r0 + 64, p, mb * 128:(mb + 1) * 128], q[r0:r0 + 64, :])], R=[KmT.b, q.b])
                        sts.append((ps_, psbuf))
                    for mb in range(2):
                        ps_, psbuf = sts[mb]
                        pt_ = PT[ip % 3]
                        ip += 1
                        S.op(S.ACT, lambda h, ps_=ps_, pt_=pt_: h.activation(out=pt_[:], in_=ps_[:, :], func=AF.Exp), X=[psbuf], W=[pt_.b])
                        self.mm(bo, po[:, :], [(Vm[:, mb, h4, :], pt_[:])], R=[Vm.b, pt_.b], start=(mb == 0), stop=(mb == 1))
                    self.attn_epilogue(po, pobuf, hh == 1, z, rd[h4 % 2], tn[h4 % 2], 7 + p, g, half=False, act_recip=True)

    def attn_epilogue(self, po, pobuf, odd, sz, rd, tn, chunk, g, half=False, act_recip=False):
        S = self.S
        gsl = slice(g * G, (g + 1) * G)
        nr = slice(64, 128) if odd else slice(0, 64)
        dr = slice(0, 64) if odd else slice(64, 128)
        if act_recip:
            S.op(S.ACT, lambda h: h.activation(out=rd[nr, :], in_=po[dr, :], func=AF.Ln), X=[pobuf], W=[rd.b])
            S.op(S.ACT, lambda h: h.activation(out=rd[nr, :], in_=rd[nr, :], func=AF.Exp, scale=-1.0), R=[rd.b], W=[rd.b])
        else:
            S.op(S.DVE, lambda h: h.reciprocal(out=rd[nr, :], in_=po[dr, :]), X=[pobuf], W=[rd.b])
        if half:
            S.op(S.DVE, lambda h: h.scalar_tensor_tensor(out=tn[nr, :], in0=po[nr, :], scalar=0.5, in1=rd[nr, :], op0=ALU.mult, op1=ALU.mult),
                 X=[pobuf], R=[rd.b], W=[tn.b])
        else:
            S.op(S.DVE, lambda h: h.tensor_tensor(out=tn[nr, :], in0=po[nr, :], in1=rd[nr, :], op=ALU.mult), X=[pobuf], R=[rd.b], W=[tn.b])
        S.op(S.POOL, lambda h: h.tensor_tensor(out=self.ycat[nr, chunk, gsl], in0=tn[nr, :], in1=sz[nr, :], op=ALU.mult),
             R=[tn.b, sz.b], W=[self.ycat_b[chunk][g]])

    def phase_final(self, l, x_src, x_dst, make_xT):
        S = self.S
        nc = self.nc
        wout = self.tile("wout", [128, 9 * D], BF16, dma=True)
        self.load_w(wout, self.woutpk[l])
        woutv = wout[:, :].rearrange("p (c n) -> p c n", c=9)
        gam = self.tile("gam", [128, D], dma=True)
        bet = self.tile("bet", [128, D], dma=True)
        S.op(S.SP, lambda h: h.dma_start(out=gam[:], in_=self.lng[l].partition_broadcast(128)), W=[gam.b], stream=gam.st)
        S.op(S.SP, lambda h: h.dma_start(out=bet[:], in_=self.lnb[l].partition_broadcast(128)), W=[bet.b], stream=bet.st)
        xin = [self.tile("xin", [128, D], dma=True) for _ in range(3)]
        epsf = self.tile("epsf", [128, 1])
        S.op(S.POOL, lambda h: h.memset(epsf[:], float(LN_EPS)), W=[epsf.b])
        tt = [self.tile("tt", [128, D]) for _ in range(2)]
        xo = [self.tile("xo", [128, D]) for _ in range(2)]
        stats = [self.tile("stats", [128, 16]) for _ in range(2)]
        krows = [128, 128, 128, 96, 96, 96, 96, 128, 128]
        rot_y = self.bankrot([0, 1, 2, 3])
        rot_t = self.bankrot([4, 5, 6, 7])

        def load_x(t):
            xt = xin[t % 3]
            S.op(S.SP, lambda h: h.dma_start(out=xt[:], in_=x_src[t * 128:(t + 1) * 128, :]), W=[xt.b], stream=xt.st)

        nmr = [self.tile("nmr", [128, 2]) for _ in range(2)]
        junk = self.tile("junk", [128, D], BF16)

        held = {}

        def stage_M_pe(t):
            g = t // 4
            hb = []
            for half in range(2):
                bk = rot_y()
                pt, pbuf = self.pb[bk]
                pairs = [(self.ycat[0:krows[c], c, t * 128:(t + 1) * 128], woutv[0:krows[c], c, half * 512:(half + 1) * 512])
                         for c in range(9)]
                self.mm(bk, pt[:, :], pairs, R=[wout.b] + [self.ycat_b[c][g] for c in range(9)])
                hb.append((pt, pbuf))
            held[t] = hb

        def stage_M_dve(t):
            xt, tq, sq = xin[t % 3], tt[t % 2], stats[t % 2]
            for half in range(2):
                pt, pbuf = held[t][half]
                S.op(S.DVE, lambda h, pt=pt, half=half: h.scalar_tensor_tensor(
                    out=tq[:, half * 512:(half + 1) * 512], in0=xt[:, half * 512:(half + 1) * 512], scalar=float(ALPHA),
                    in1=pt[:, :], op0=ALU.mult, op1=ALU.add), R=[xt.b], X=[pbuf], W=[tq.b])
            S.op(S.ACT, lambda h: h.activation(out=junk[:], in_=tq[:], func=AF.Identity, accum_out=sq[:, 0:1]), R=[tq.b], W=[junk.b, sq.b])
            S.op(S.ACT, lambda h: h.activation(out=junk[:], in_=tq[:], func=AF.Square, accum_out=sq[:, 1:2]), R=[tq.b], W=[junk.b, sq.b])

        def stage_N(t):
            tq, xq, sq, nm = tt[t % 2], xo[t % 2], stats[t % 2], nmr[t % 2]
            S.op(S.DVE, lambda h: h.tensor_scalar(out=sq[:, 12:13], in0=sq[:, 0:1], scalar1=1.0 / D, scalar2=None, op0=ALU.mult), R=[sq.b], W=[sq.b])
            S.op(S.DVE, lambda h: h.tensor_tensor(out=sq[:, 2:3], in0=sq[:, 12:13], in1=sq[:, 12:13], op=ALU.mult), R=[sq.b], W=[sq.b])
            S.op(S.DVE, lambda h: h.scalar_tensor_tensor(out=sq[:, 13:14], in0=sq[:, 1:2], scalar=1.0 / D, in1=sq[:, 2:3],
                                                         op0=ALU.mult, op1=ALU.subtract), R=[sq.b], W=[sq.b])
            S.op(S.ACT, lambda h: h.activation(out=sq[:, 14:15], in_=sq[:, 13:14], func=AF.Ln, bias=epsf[:, 0:1]), R=[sq.b, epsf.b], W=[sq.b])
            S.op(S.ACT, lambda h: h.activation(out=nm[:, 0:1], in_=sq[:, 14:15], func=AF.Exp, scale=-0.5), R=[sq.b], W=[nm.b])
            S.op(S.DVE, lambda h: h.tensor_scalar(out=nm[:, 1:2], in0=sq[:, 12:13], scalar1=nm[:, 0:1], scalar2=-1.0, op0=ALU.mult, op1=ALU.mult),
                 R=[sq.b, nm.b], W=[nm.b])
            S.op(S.ACT, lambda h: h.activation(out=tq[:], in_=tq[:], func=AF.Identity, scale=nm[:, 0:1], bias=nm[:, 1:2]), R=[tq.b, nm.b], W=[tq.b])

        def stage_N_b(t):
            tq, xq = tt[t % 2], xo[t % 2]
            S.op(S.POOL, lambda h: h.tensor_tensor(out=xq[:], in0=tq[:], in1=gam[:], op=ALU.mult), R=[tq.b, gam.b], W=[xq.b])
            S.op(S.POOL, lambda h: h.tensor_tensor(out=xq[:], in0=xq[:], in1=bet[:], op=ALU.add), R=[xq.b, bet.b], W=[xq.b])
            st = self.stq[t % 2]
            S.op(S.SP, lambda h: h.dma_start(out=x_dst[t * 128:(t + 1) * 128, :], in_=xq[:]), R=[xq.b], stream=st)

        load_x(0)
        load_x(1)
        stage_M_pe(0)
        stage_M_dve(0)
        for t in range(NT):
            if t + 2 < NT:
                load_x(t + 2)
            if t + 1 < NT:
                stage_M_pe(t + 1)
            stage_N(t)
            stage_N_b(t)
            if t + 1 < NT:
                stage_M_dve(t + 1)
            if make_xT and t >= 1:
                self.transpose_into_xT(xo[(t - 1) % 2], t - 1, rot_t)
        if make_xT:
            self.transpose_into_xT(xo[(NT - 1) % 2], NT - 1, rot_t)

    def _build(self):
        S = self.S
        L = self.L
        for l in range(L):
            x_src = self.x_in if l == 0 else self.xbuf[(l - 1) % 2]
            x_dst = self.out if l == L - 1 else self.xbuf[l % 2]
            if l == 0:
                self.run_phase(self.build_xT_from_dram, x_src)
            if l == 0:
                self.run_phase(self.build_memT)
            self.run_phase(self.phase_fox, l)
            if "noml" not in self.dbg:
                self.run_phase(self.phase_mlstm, l)
            self.run_phase(self.phase_mem, l)
            if "ycat" in self.dbg:
                S.barrier()
                d = self.dbg_dram("dbg_ycat", [128, 9 * S_LEN], BF16)
                S.op(S.SP, lambda h: h.dma_start(out=d, in_=self.ycat[:, :, :].rearrange("p c s -> p (c s)")),
                     R=[b for cb in self.ycat_b for b in cb], stream=self.stq[0])
            self.run_phase(self.phase_final, l, x_src, x_dst, make_xT=(l < L - 1))
        S.finish(S.SP)


def _pack_win(w):
    w3 = w.reshape(NKC, 128, IN_COLS).transpose(1, 0, 2)
    groups = []
    groups.append(np.arange(O_FF, O_FF + 6))
    for h in range(6):
        groups.append(np.concatenate([np.arange(o + 64 * h, o + 64 * (h + 1)) for o in (O_FQ, O_FK, O_FV, O_FZ)]))
    groups.append(np.concatenate([np.arange(O_MI, O_MI + 4), np.arange(O_MF, O_MF + 4)]))
    for h in range(4):
        groups.append(np.concatenate([np.arange(o + 96 * h, o + 96 * (h + 1)) for o in (O_MQ, O_MK, O_MV, O_MO, O_MZ)]))
    groups.append(np.concatenate([np.arange(O_RQ, O_RQ + 256), np.arange(O_RZ, O_RZ + 256)]))
    parts = [np.ascontiguousarray(w3[:, :, gidx]).reshape(128, -1) for gidx in groups]
    return np.concatenate(parts, axis=1)


def win_offsets():
    sizes = [6] + [256] * 6 + [8] + [480] * 4 + [512]
    offs = np.concatenate([[0], np.cumsum([NKC * s for s in sizes])])
    return sizes, offs


def _pack_wout(w):
    out = np.zeros((128, 9, D), np.float32)
    r = 0
    for c, k in enumerate([128, 128, 128, 96, 96, 96, 96, 128, 128]):
        out[0:k, c, :] = w[r:r + k, :]
        r += k
    return out.reshape(128, 9 * D)


def _pack_wmem(w):
    return np.ascontiguousarray(w.reshape(NKC, 128, 512).transpose(1, 0, 2)).reshape(128, NKC * 512)


def pack_layers(inp, layers):
    f = np.float32
    d = {}
    d["wpk"] = np.stack([_pack_win(np.asarray(inp["w_in"][l], f)) for l in layers])
    d["woutpk"] = np.stack([_pack_wout(np.asarray(inp["w_out"][l], f)) for l in layers])
    d["wmempk"] = np.stack([_pack_wmem(np.asarray(inp["w_mem_kv"][l], f)) for l in layers])
    d["foxb"] = np.stack([np.asarray(inp["fox_f_bias"][l], f).reshape(6, 1) for l in layers])
    d["mib"] = np.stack([np.asarray(inp["mlstm_i_bias"][l], f).reshape(4, 1) for l in layers])
    d["mfb"] = np.stack([np.asarray(inp["mlstm_f_bias"][l], f).reshape(4, 1) for l in layers])
    d["convw"] = np.stack([np.ascontiguousarray(np.asarray(inp["mlstm_conv_w"][l], f).reshape(4, 2, 4, 96).transpose(3, 1, 2, 0)).reshape(96, 32)
                           for l in layers])
    d["convb"] = np.stack([np.ascontiguousarray(np.asarray(inp["mlstm_conv_b"][l], f).reshape(2, 4, 96).transpose(2, 0, 1)).reshape(96, 8)
                           for l in layers])
    d["ng"] = np.stack([np.ascontiguousarray(np.asarray(inp["mlstm_norm_g"][l], f).reshape(4, 96).T) for l in layers])
    d["lng"] = np.stack([np.asarray(inp["ln_g"][l], f).reshape(1, D) for l in layers])
    d["lnb"] = np.stack([np.asarray(inp["ln_b"][l], f).reshape(1, D) for l in layers])
    return d


_PROG_CACHE = {}


def get_prog(L, **kw):
    key = (L, tuple(sorted(kw.items())))
    if key not in _PROG_CACHE:
        _PROG_CACHE[key] = Prog(L, **kw)
    return _PROG_CACHE[key]


def run_layers(x, mem, packed, n_cores=8):
    L = packed["wpk"].shape[0]
    prog = get_prog(L)
    B = x.shape[0]
    in_maps = []
    for c in range(n_cores):
        b = c % B
        m = {"x": np.ascontiguousarray(x[b]), "mem": np.ascontiguousarray(mem[b])}
        m.update(packed)
        in_maps.append(m)
    res = run_bass_kernel_spmd(prog.nc, in_maps, core_ids=list(range(n_cores)))
    return np.stack([np.asarray(res.results[b]["out"]) for b in range(B)])


FUSED = True


def kernel(x, mem, w_in, fox_f_bias, mlstm_conv_w, mlstm_conv_b, mlstm_i_bias, mlstm_f_bias,
           mlstm_norm_g, w_mem_kv, w_out, ln_g, ln_b):
    inp = dict(w_in=w_in, fox_f_bias=fox_f_bias, mlstm_conv_w=mlstm_conv_w, mlstm_conv_b=mlstm_conv_b,
               mlstm_i_bias=mlstm_i_bias, mlstm_f_bias=mlstm_f_bias, mlstm_norm_g=mlstm_norm_g,
               w_mem_kv=w_mem_kv, w_out=w_out, ln_g=ln_g, ln_b=ln_b)
    x = np.asarray(x, np.float32)
    mem = np.asarray(mem, np.float32)
    if FUSED:
        return run_layers(x, mem, pack_layers(inp, list(range(DEPTH))))
    for l in range(DEPTH):
        x = run_layers(x, mem, pack_layers(inp, [l]))
    return x
```

```python
import math
from contextlib import ExitStack
import numpy as np
import concourse.bass as bass
import concourse.mybir as mybir
from concourse.bass_utils import run_bass_kernel_spmd

F32 = mybir.dt.float32
BF16 = mybir.dt.bfloat16
AF = mybir.ActivationFunctionType
ALU = mybir.AluOpType

D = 1024
S_LEN = 4096
DEPTH = 4
NKC = 8
G = 512
NG = S_LEN // G
NT = S_LEN // 128
MEM_LEN = 256
LN_EPS = 1e-5
ALPHA = (2.0 * DEPTH) ** 0.25
IN_COLS = 3982
O_FQ, O_FK, O_FV, O_FF, O_FZ = 0, 384, 768, 1152, 1158
O_MQ, O_MK, O_MV, O_MI, O_MF, O_MO, O_MZ = 1542, 1926, 2310, 2694, 2698, 2702, 3086
O_RQ, O_RZ = 3470, 3726


class Stream:
    def __init__(self, name, sem, inc, q, dma):
        self.name, self.sem, self.inc, self.q, self.dma = name, sem, inc, q, dma
        self.n = 0


class Q:
    def __init__(self, name, h):
        self.name, self.h = name, h
        self.seen = {}
        self.stream = None


class Buf:
    __slots__ = ("name", "w", "r")

    def __init__(self, name):
        self.name = name
        self.w = None
        self.r = {}


class Sched:
    def __init__(self, nc):
        self.nc = nc
        self.PE = self._mkq("pe", nc.tensor)
        self.ACT = self._mkq("act", nc.scalar)
        self.DVE = self._mkq("dve", nc.vector)
        self.POOL = self._mkq("pool", nc.gpsimd)
        self.SP = Q("sp", nc.sync)
        self.queues = [self.PE, self.ACT, self.DVE, self.POOL, self.SP]
        self.streams = [q.stream for q in self.queues if q.stream is not None]
        self.nwaits = 0
        self.nops = 0

    def _mkq(self, name, h):
        q = Q(name, h)
        q.stream = Stream(name, self.nc.alloc_semaphore("s_" + name), 1, q, False)
        return q

    def dma_stream(self, name):
        s = Stream(name, self.nc.alloc_semaphore("d_" + name), 16, None, True)
        self.streams.append(s)
        return s

    def _wait(self, q, s, c):
        if q.seen.get(s, 0) >= c:
            return
        assert c <= s.n, f"wait on unissued signal {s.name} {c} > {s.n}"
        q.h.wait_ge(s.sem, c * s.inc)
        q.seen[s] = c
        self.nwaits += 1

    def op(self, q, emit, R=(), W=(), X=(), sig=True, stream=None):
        st = stream if stream is not None else q.stream
        deps = {}

        def add(d, same_ok):
            s, c = d
            if same_ok and (not s.dma) and s.q is q and stream is None:
                return
            if deps.get(s, 0) < c:
                deps[s] = c

        pe = q is self.PE
        for b in R:
            if b.w is not None:
                add(b.w, pe)
        for b in W:
            if b.w is not None:
                add(b.w, pe)
            for s, c in b.r.items():
                add((s, c), pe)
        for b in X:
            if b.w is not None:
                add(b.w, pe)
            for s, c in b.r.items():
                add((s, c), pe)
        for s, c in deps.items():
            if s.dma:
                c = s.n
            self._wait(q, s, c)
        ins = emit(q.h)
        cnt = st.n + 1
        if sig:
            ins.then_inc(st.sem, st.inc)
            st.n = cnt
        for b in R:
            if b.r.get(st, 0) < cnt:
                b.r[st] = cnt
        for b in W:
            b.w = (st, cnt)
            b.r = {}
        for b in X:
            b.w = (st, cnt)
            b.r = {}
        self.nops += 1
        return ins

    def barrier(self):
        for q in self.queues:
            for s in self.streams:
                if s.n > 0 and not (s.q is q):
                    self._wait(q, s, s.n)

    def finish(self, q):
        for s in self.streams:
            if s.n > 0 and not (s.q is q):
                self._wait(q, s, s.n)


class Tile:
    def __init__(self, handle, name, stream=None):
        self.t = handle
        self.b = Buf(name)
        self.st = stream

    def __getitem__(self, k):
        return self.t[k]


class Prog:
    def __init__(self, L, stop_after=None, dbg=()):
        import os
        dbg = tuple(dbg) + tuple(x for x in os.environ.get("KDBG", "").split(",") if x)
        self.L = L
        self.stop_after = stop_after
        self.dbg = set(dbg)
        nc = bass.Bass("TRN2", target_bir_lowering=False)
        self.nc = nc
        self.S = Sched(nc)
        self.uid = 0
        self.stack = None
        self.stream_pool = []
        self.all_streams = []
        self.phase_streams = []
        self._decl_dram()
        self._alloc()
        self._consts()
        self._build()

    def tile(self, name, shape, dt=F32, dma=False):
        self.uid += 1
        nm = f"{name}_{self.uid}"
        if self.stack is not None:
            hnd = self.stack.enter_context(self.nc.sbuf_tensor(nm, list(shape), dt))
        else:
            hnd = self.nc.alloc_sbuf_tensor(nm, list(shape), dt)
        st = None
        if dma:
            if self.stream_pool:
                st = self.stream_pool.pop()
            else:
                st = self.S.dma_stream(f"ds{len(self.all_streams)}")
                self.all_streams.append(st)
            if self.stack is not None:
                self.phase_streams.append(st)
        return Tile(hnd, nm, st)

    def sub_arena(self):
        prog = self

        class _Sub:
            def __enter__(self_):
                self_.outer = prog.stack
                self_.es = ExitStack()
                self_.es.__enter__()
                prog.stack = self_.es
                return self_

            def __exit__(self_, *exc):
                prog.S.barrier()
                prog.stack = self_.outer
                return self_.es.__exit__(*exc)
        return _Sub()

    def run_phase(self, fn, *a, **kw):
        assert self.stack is None
        with ExitStack() as es:
            self.stack = es
            self.phase_streams = []
            fn(*a, **kw)
            self.S.barrier()
            self.stream_pool.extend(self.phase_streams)
            self.phase_streams = []
            self.stack = None

    def dram_in(self, name, shape, dt=F32):
        return self.nc.dram_tensor(name, list(shape), dt, kind="ExternalInput").ap()

    def _decl_dram(self):
        L = self.L
        nc = self.nc
        self.x_in = self.dram_in("x", [S_LEN, D])
        self.mem_in = self.dram_in("mem", [MEM_LEN, D])
        self.wpk = self.dram_in("wpk", [L, 128, NKC * IN_COLS])
        self.woutpk = self.dram_in("woutpk", [L, 128, 9 * D])
        self.wmempk = self.dram_in("wmempk", [L, 128, NKC * 512])
        self.foxb = self.dram_in("foxb", [L, 6, 1])
        self.mib = self.dram_in("mib", [L, 4, 1])
        self.mfb = self.dram_in("mfb", [L, 4, 1])
        self.convw = self.dram_in("convw", [L, 96, 32])
        self.convb = self.dram_in("convb", [L, 96, 8])
        self.ngd = self.dram_in("ng", [L, 96, 4])
        self.lng = self.dram_in("lng", [L, 1, D])
        self.lnb = self.dram_in("lnb", [L, 1, D])
        self.out = nc.dram_tensor("out", [S_LEN, D], F32, kind="ExternalOutput").ap()
        self.xbuf = [nc.dram_tensor(f"xbuf{i}", [S_LEN, D], F32).ap() for i in range(2)] if L > 1 else []
        self.dbg_out = {}
        self.gdram = nc.dram_tensor("gdram", [2, 4 * NG, G], F32).ap()
        self.gdram_b = [Buf(f"gdram_g{g}") for g in range(NG)]

    def dbg_dram(self, name, shape, dt=F32):
        ap = self.nc.dram_tensor(name, list(shape), dt, kind="ExternalOutput").ap()
        self.dbg_out[name] = ap
        return ap

    def _alloc(self):
        nc, S = self.nc, self.S
        self.pb = []
        for i in range(8):
            t = nc.alloc_psum_tensor(f"pb{i}", [128, 512], F32)
            self.pb.append((t, Buf(f"pb{i}")))
        self.xT = nc.alloc_sbuf_tensor("xT", [128, NKC, S_LEN], BF16)
        self.xT_b = [Buf(f"xT_g{g}") for g in range(NG)]
        self.ycat = nc.alloc_sbuf_tensor("ycat", [128, 9, S_LEN], BF16)
        self.ycat_b = [[Buf(f"ycat_{c}_{g}") for g in range(NG)] for c in range(9)]
        self.stq = [S.dma_stream("stq0"), S.dma_stream("stq1")]
        self.memT = self.tile("memT", [128, NKC, MEM_LEN], BF16)
        self.gtok_all = self.tile("gtok_all", [128, NT, 4])
        self.gtokS_all = self.tile("gtokS_all", [128, NT, 4])

    def _consts(self):
        S = self.S
        self.onesf = self.tile("onesf", [128, 128])
        self.zerf = self.tile("zerf", [128, 128])
        self.identf = self.tile("identf", [128, 128])
        self.identb = self.tile("identb", [128, 128], BF16)
        self.mnegf = self.tile("mnegf", [128, 128])
        self.mnegb = self.tile("mnegb", [128, 128], BF16)
        self.avg96 = self.tile("avg96", [128, 96])
        self.fillsrc = self.tile("fillsrc", [128, 512], BF16)
        S.op(S.POOL, lambda h: h.memset(self.fillsrc[:], 0.5), W=[self.fillsrc.b])
        o, z = self.onesf, self.zerf
        S.op(S.POOL, lambda h: h.memset(o[:], 1.0), W=[o.b])
        S.op(S.POOL, lambda h: h.memset(z[:], 0.0), W=[z.b])
        S.op(S.POOL, lambda h: h.memset(self.avg96[:], 1.0 / 96.0), W=[self.avg96.b])
        S.op(S.POOL, lambda h: h.affine_select(out=self.identf[:], in_=o[:], pattern=[[1, 128]],
                                               compare_op=ALU.is_equal, fill=0.0, base=0, channel_multiplier=-1),
             R=[o.b], W=[self.identf.b])
        S.op(S.DVE, lambda h: h.tensor_copy(out=self.identb[:], in_=self.identf[:]), R=[self.identf.b], W=[self.identb.b])
        S.op(S.POOL, lambda h: h.affine_select(out=self.mnegf[:], in_=z[:], pattern=[[1, 128]],
                                               compare_op=ALU.is_ge, fill=-30000.0, base=0, channel_multiplier=-1),
             R=[z.b], W=[self.mnegf.b])
        S.op(S.DVE, lambda h: h.tensor_copy(out=self.mnegb[:], in_=self.mnegf[:]), R=[self.mnegf.b], W=[self.mnegb.b])

    def mm(self, bank, out_ap, pairs, R, start=True, stop=True, sgc=False):
        S = self.S
        n = len(pairs)
        for i, (lhsT, rhs) in enumerate(pairs):
            S.op(S.PE, lambda h, lhsT=lhsT, rhs=rhs, i=i: h.matmul(
                out_ap, lhsT=lhsT, rhs=rhs, start=(start and i == 0), stop=(stop and i == n - 1), skip_group_check=sgc),
                R=R, X=[self.pb[bank][1]], sig=(i == n - 1))

    def filler(self, n=256, bank=7):
        S = self.S
        pt, pbuf = self.pb[bank]
        S.op(S.PE, lambda h: h.matmul(pt[:, 0:n], lhsT=self.identb[:], rhs=self.fillsrc[:, 0:n], start=True, stop=True),
             R=[self.identb.b, self.fillsrc.b], X=[pbuf], sig=False)

    def bankrot(self, banks):
        st = {"i": 0}

        def nxt():
            b = banks[st["i"] % len(banks)]
            st["i"] += 1
            return b
        return nxt

    def build_xT_from_dram(self, x_src):
        S = self.S
        xin = [self.tile("xin", [128, D], dma=True) for _ in range(2)]
        rot = self.bankrot([0, 1, 2, 3])
        for t in range(NT):
            xt = xin[t % 2]
            S.op(S.SP, lambda h, xt=xt, t=t: h.dma_start(out=xt[:], in_=x_src[t * 128:(t + 1) * 128, :]),
                 W=[xt.b], stream=xt.st)
            self.transpose_into_xT(xt, t, rot)

    def transpose_into_xT(self, src, t, rot, eng=None):
        S = self.S
        for half in range(2):
            bk = rot()
            pt, pbuf = self.pb[bk]
            for c in range(4):
                kc = half * 4 + c
                S.op(S.PE, lambda h, kc=kc, c=c, pt=pt: h.transpose(
                    pt[:, c * 128:(c + 1) * 128], src[:, kc * 128:(kc + 1) * 128], self.identf[:]),
                    R=[src.b, self.identf.b], X=[pbuf], sig=(c == 3))
            e = eng if eng is not None else (S.ACT if half == 0 else S.DVE)
            dst = self.xT[:, half * 4:(half + 1) * 4, t * 128:(t + 1) * 128]
            srcp = pt[:, :].rearrange("p (c t) -> p c t", c=4)
            if e is S.ACT:
                S.op(e, lambda h, dst=dst, srcp=srcp: h.activation(out=dst, in_=srcp, func=AF.Copy),
                     X=[pbuf], W=[self.xT_b[t // 4]])
            else:
                S.op(e, lambda h, dst=dst, srcp=srcp: h.tensor_copy(out=dst, in_=srcp),
                     X=[pbuf], W=[self.xT_b[t // 4]])

    def load_w(self, dst_tile, src_ap):
        S = self.S
        self.n_sw = getattr(self, "n_sw", 0) + 1
        st = S.dma_stream(f"sw{self.n_sw}")
        S.op(S.POOL, lambda h: h.dma_start(out=dst_tile[:], in_=src_ap), W=[dst_tile.b], stream=st)


    def phase_fox(self, l):
        S = self.S
        sizes, offs = win_offsets()
        rot = self.bankrot([0, 1])
        rot_s = self.bankrot([2, 3, 4])
        rot_o = self.bankrot([5, 6])
        shiftrows = self.tile("shiftrows", [6, S_LEN], BF16)
        negc_tok = self.tile("negc_tok", [128, NT, 6])
        negcref_rep = self.tile("negcref_rep", [128, 6, NG])
        with self.sub_arena():
            rotG = self.bankrot([2, 3, 4])
            gens = [self.fox_prep(l, offs, rot, shiftrows, negc_tok, negcref_rep),
                    self.mlstm_gates(l, offs, rotG, self.gtok_all, self.gtokS_all, self.gdram, self.gdram_b, float(96 ** -0.5))]
            alive = True
            while alive:
                alive = False
                for gn in gens:
                    if next(gn, "end") != "end":
                        alive = True
        wh = [self.tile("wh", [128, NKC * 256], BF16, dma=True) for _ in range(2)]
        KaT = self.tile("KaT", [65, S_LEN], BF16)
        KaT_b = [Buf(f"KaT_g{g}") for g in range(NG)]
        S.op(S.POOL, lambda h: h.memset(KaT[64:65, :], 1.0), W=KaT_b)
        Vaug = self.tile("Vaug", [128, NT, 192], BF16)
        V_ones = Buf("V_ones")
        V_b = [[Buf(f"V_{par}_g{g}") for g in range(NG)] for par in range(2)]
        S.op(S.POOL, lambda h: h.memset(Vaug[:, :, 64:128], 1.0), W=[V_ones])
        QaT = [self.tile("QaT", [65, G], BF16, dma=True) for _ in range(2)]
        szf = [self.tile("szf", [128, G], BF16) for _ in range(2)]
        PT = [self.tile("PTf", [128, G], BF16) for _ in range(3)]
        rd = [self.tile("rdf", [128, G]) for _ in range(1)]
        tn = [self.tile("tnf", [128, G]) for _ in range(1)]
        thf = [self.tile("thf", [128, G]) for _ in range(2)]
        tabs = [self.tile("tab", [128, NT, NG]) for _ in range(2)]
        ipc = {"i": 0}

        def build_tab(hd):
            tab = tabs[hd % 2]
            for qg in range(NG):
                S.op(S.DVE, lambda h, qg=qg, tab=tab, hd=hd: h.tensor_scalar(
                    out=tab[:, :, qg], in0=negc_tok[:, :, hd], scalar1=negcref_rep[:, hd, qg:qg + 1], scalar2=None, op0=ALU.subtract),
                    R=[negc_tok.b, negcref_rep.b], W=[tab.b])

        def gen_proj(hd, g):
            odd = hd % 2
            W_ = wh[hd % 2]
            Wv = W_[:, :].rearrange("p (c n) -> p c n", c=NKC)
            gsl = slice(g * G, (g + 1) * G)
            qa, sz = QaT[g % 2], szf[g % 2]
            vc0 = 128 if odd else 0
            RR = [W_.b, self.xT_b[g]]
            bk = rot()
            pt, pbuf = self.pb[bk]
            for kc in range(NKC):
                S.op(S.PE, lambda h, kc=kc, pt=pt: h.matmul(pt[:, :], lhsT=Wv[:, kc, 0:128], rhs=self.xT[:, kc, gsl],
                                                            start=(kc == 0), stop=(kc == NKC - 1)), R=RR, X=[pbuf], sig=(kc == NKC - 1))
                if kc % 2 == 1 and kc < NKC - 1:
                    yield
            S.op(S.DVE, lambda h, pt=pt: h.tensor_scalar(out=qa[0:64, :], in0=pt[0:64, :], scalar1=0.125, scalar2=None, op0=ALU.mult), X=[pbuf], W=[qa.b])
            S.op(S.DVE, lambda h, pt=pt: h.tensor_copy(out=KaT[0:64, gsl], in_=pt[64:128, :]), X=[pbuf], W=[KaT_b[g]])
            S.op(S.SP, lambda h: h.dma_start(out=qa[64:65, :], in_=shiftrows[hd:hd + 1, gsl]), R=[shiftrows.b], W=[qa.b], stream=qa.st)
            yield
            bk = rot()
            pt, pbuf = self.pb[bk]
            for j in range(4):
                blk = slice(g * G + j * 128, g * G + (j + 1) * 128)
                for kc in range(NKC):
                    S.op(S.PE, lambda h, kc=kc, pt=pt, j=j, blk=blk: h.matmul(
                        pt[:, j * 64:(j + 1) * 64], lhsT=self.xT[:, kc, blk], rhs=Wv[:, kc, 128:192],
                        start=(kc == 0), stop=(kc == NKC - 1)), R=RR, X=[pbuf], sig=(kc == NKC - 1))
                    if kc == 3:
                        yield
                yield
            S.op(S.DVE, lambda h, pt=pt: h.tensor_copy(
                out=Vaug[:, 4 * g:4 * g + 4, vc0:vc0 + 64], in_=pt[:, 0:256].rearrange("p (j c) -> p j c", j=4)),
                X=[pbuf], W=[V_b[odd][g]])
            yield
            bk = rot()
            pt, pbuf = self.pb[bk]
            zr = slice(64, 128) if odd else slice(0, 64)
            for kc in range(NKC):
                if odd:
                    S.op(S.PE, lambda h, kc=kc, pt=pt: h.matmul(pt[:, :], lhsT=Wv[:, kc, 128:256], rhs=self.xT[:, kc, gsl],
                                                                start=(kc == 0), stop=(kc == NKC - 1)), R=RR, X=[pbuf], sig=(kc == NKC - 1))
                else:
                    S.op(S.PE, lambda h, kc=kc, pt=pt: h.matmul(pt[0:64, :], lhsT=Wv[:, kc, 192:256], rhs=self.xT[:, kc, gsl],
                                                                start=(kc == 0), stop=(kc == NKC - 1)), R=RR, X=[pbuf], sig=(kc == NKC - 1))
                if kc % 2 == 1 and kc < NKC - 1:
                    yield
            th = thf[g % 2]
            S.op(S.ACT, lambda h, pt=pt: h.activation(out=th[zr, :], in_=pt[zr, :], func=AF.Tanh, scale=0.5), X=[pbuf], W=[th.b])
            S.op(S.DVE, lambda h, pt=pt: h.scalar_tensor_tensor(out=sz[zr, :], in0=th[zr, :], scalar=1.0, in1=pt[zr, :], op0=ALU.add, op1=ALU.mult),
                 X=[pbuf], R=[th.b], W=[sz.b])
            yield

        N_CHUNKS = 20

        def attention(hd, g, chunks):
            odd = hd % 2
            lc0 = 64 if odd else 0
            tab = tabs[hd % 2]
            qa, sz = QaT[g % 2], szf[g % 2]
            nkb = 4 * g + 4
            emitted = {"n": 0}
            bo = rot_o()
            po, pobuf = self.pb[bo]

            def issue_st(kb):
                diag = kb >= 4 * g
                qoff = (kb - 4 * g) * 128 if diag else 0
                n = G - qoff
                bs = rot_s()
                ps_, psbuf = self.pb[bs]
                self.mm(bs, ps_[:, 0:n], [(KaT[0:65, kb * 128:(kb + 1) * 128], qa[0:65, qoff:G])], R=[KaT_b[kb // 4], qa.b],
                        start=True, stop=not diag)
                if diag:
                    self.mm(bs, ps_[:, 0:128], [(self.identb[:], self.mnegb[:])], R=[self.identb.b, self.mnegb.b], start=False, stop=True)
                return ps_, psbuf, qoff, n

            nxt = issue_st(0)
            for kb in range(nkb):
                cur = nxt
                if kb + 1 < nkb:
                    nxt = issue_st(kb + 1)
                ps_, psbuf, qoff, n = cur
                pt_ = PT[ipc["i"] % 3]
                ipc["i"] += 1
                S.op(S.ACT, lambda h, ps_=ps_, pt_=pt_, n=n, kb=kb: h.activation(
                    out=pt_[:, 0:n], in_=ps_[:, 0:n], func=AF.Exp, bias=tab[:, kb, g:g + 1]), X=[psbuf], R=[tab.b], W=[pt_.b])
                did = False
                if chunks is not None:
                    want = ((kb + 1) * N_CHUNKS + nkb - 1) // nkb
                    while emitted["n"] < want:
                        if next(chunks, "end") != "end":
                            did = True
                        emitted["n"] += 1
                if not did:
                    self.filler(256)
                self.mm(bo, po[:, qoff:G], [(Vaug[:, kb, lc0:lc0 + 128], pt_[:, 0:n])], R=[V_b[odd][kb // 4], V_ones, pt_.b],
                        start=(kb == 0), stop=(kb == nkb - 1))
            if chunks is not None:
                for _ in chunks:
                    pass
            self.attn_epilogue(po, pobuf, bool(odd), sz, rd[0], tn[0], hd // 2, g, half=True)

        self.load_w(wh[0], self.wpk[l][:, int(offs[1]):int(offs[2])])
        build_tab(0)
        for _ in gen_proj(0, 0):
            pass
        for hd in range(6):
            if hd + 1 < 6:
                self.load_w(wh[(hd + 1) % 2], self.wpk[l][:, int(offs[2 + hd]):int(offs[3 + hd])])
                build_tab(hd + 1)
            for g in range(NG):
                if g + 1 < NG:
                    chunks = gen_proj(hd, g + 1)
                elif hd + 1 < 6:
                    chunks = gen_proj(hd + 1, 0)
                else:
                    chunks = None
                attention(hd, g, chunks)


    def fox_prep(self, l, offs, rot, shiftrows, negc_tok, negcref_rep):
        S = self.S
        wff = self.tile("wff", [128, NKC * 6], BF16, dma=True)
        self.load_w(wff, self.wpk[l][:, int(offs[0]):int(offs[1])])
        wffv = wff[:, :].rearrange("p (c n) -> p c n", c=NKC)
        fb = self.tile("fb", [6, 1], dma=True)
        S.op(S.SP, lambda h: h.dma_start(out=fb[:], in_=self.foxb[l]), W=[fb.b], stream=fb.st)
        nfb = self.tile("nfb", [6, 1])
        S.op(S.DVE, lambda h: h.tensor_scalar(out=nfb[:], in0=fb[:], scalar1=-1.0, scalar2=None, op0=ALU.mult), R=[fb.b], W=[nfb.b])
        negcref6 = self.tile("negcref6", [6, NG])
        oh6 = self.tile("oh6", [6, 6, 128])
        for hh in range(6):
            S.op(S.POOL, lambda h, hh=hh: h.affine_select(
                out=oh6[:, hh, :], in_=self.onesf[0:6, :], pattern=[[0, 128]], compare_op=ALU.is_equal,
                fill=0.0, base=-hh, channel_multiplier=1), R=[self.onesf.b], W=[oh6.b])
        e_t = [self.tile("e_t", [6, G]) for _ in range(2)]
        negc = [self.tile("negc", [6, G]) for _ in range(2)]
        for g in range(NG):
            gsl = slice(g * G, (g + 1) * G)
            bk = rot()
            pt, pbuf = self.pb[bk]
            self.mm(bk, pt[0:6, :], [(wffv[:, kc, :], self.xT[:, kc, gsl]) for kc in range(NKC)], R=[wff.b, self.xT_b[g]])
            e, nc_, ncp = e_t[g % 2], negc[g % 2], negc[(g - 1) % 2]
            lt = e
            S.op(S.ACT, lambda h, pt=pt, e=e: h.activation(out=e[:], in_=pt[0:6, :], func=AF.Exp, scale=-1.0, bias=nfb[:, 0:1]),
                 X=[pbuf], R=[nfb.b], W=[e.b])
            S.op(S.ACT, lambda h, e=e, lt=lt: h.activation(out=lt[:], in_=e[:], func=AF.Ln, bias=1.0), R=[e.b], W=[lt.b])
            init = 0.0 if g == 0 else ncp[:, G - 1:G]
            S.op(S.DVE, lambda h, lt=lt, nc_=nc_, init=init: h.tensor_tensor_scan(
                out=nc_[:], data0=self.onesf[0:6, 0:1].to_broadcast([6, G]), data1=lt[:], initial=init, op0=ALU.mult, op1=ALU.add),
                R=[lt.b, self.onesf.b] + ([ncp.b] if g > 0 else []), W=[nc_.b])
            S.op(S.DVE, lambda h, nc_=nc_, g=g: h.tensor_copy(out=negcref6[:, g:g + 1], in_=nc_[:, 0:1]), R=[nc_.b], W=[negcref6.b])
            S.op(S.DVE, lambda h, nc_=nc_, gsl=gsl: h.tensor_scalar(out=shiftrows[:, gsl], in0=nc_[:], scalar1=nc_[:, 0:1], scalar2=-1.0,
                                                                 op0=ALU.subtract, op1=ALU.mult), R=[nc_.b], W=[shiftrows.b])
            bk = rot()
            pt, pbuf = self.pb[bk]
            for j in range(4):
                S.op(S.PE, lambda h, pt=pt, j=j, nc_=nc_: h.transpose(pt[:, j * 6:(j + 1) * 6], nc_[0:6, j * 128:(j + 1) * 128], self.identf[0:6, 0:6]),
                     R=[nc_.b, self.identf.b], X=[pbuf], sig=(j == 3))
            S.op(S.DVE, lambda h, pt=pt, g=g: h.tensor_copy(out=negc_tok[:, 4 * g:4 * g + 4, :], in_=pt[:, 0:24].rearrange("p (j c) -> p j c", j=4)),
                 X=[pbuf], W=[negc_tok.b])
            yield
        bk = rot()
        pt, pbuf = self.pb[bk]
        for hh in range(6):
            self.mm(bk, pt[:, hh * NG:(hh + 1) * NG], [(oh6[:, hh, :], negcref6[:, :])], R=[oh6.b, negcref6.b])
        S.op(S.DVE, lambda h: h.tensor_copy(out=negcref_rep[:, :, :], in_=pt[:, 0:6 * NG].rearrange("p (a b) -> p a b", a=6)),
             X=[pbuf], W=[negcref_rep.b])


    def mlstm_gates(self, l, offs, rot, gtok_all, gtokS_all, gdram, gdram_b, sK):
        S = self.S
        wmif = self.tile("wmif", [128, NKC * 8], BF16, dma=True)
        self.load_w(wmif, self.wpk[l][:, int(offs[7]):int(offs[8])])
        wmifv = wmif[:, :].rearrange("p (c n) -> p c n", c=NKC)
        ib = self.tile("ib", [4, 1], dma=True)
        fbm = self.tile("fbm", [4, 1], dma=True)
        for (t_, src) in ((ib, self.mib[l]), (fbm, self.mfb[l])):
            S.op(S.SP, lambda h, t_=t_, src=src: h.dma_start(out=t_[:], in_=src), W=[t_.b], stream=t_.st)
        nfbm = self.tile("nfbm", [4, 1])
        S.op(S.DVE, lambda h: h.tensor_scalar(out=nfbm[:], in0=fbm[:], scalar1=-1.0, scalar2=None, op0=ALU.mult), R=[fbm.b], W=[nfbm.b])
        e_t = [self.tile("me", [4, G]) for _ in range(2)]
        negF = [self.tile("negF", [4, G]) for _ in range(2)]
        gg = [self.tile("gg", [4, G]) for _ in range(2)]
        Gc = [self.tile("Gc", [4, G], dma=True) for _ in range(2)]
        nM = [self.tile("nM", [4, G], dma=True) for _ in range(2)]
        onesb = self.onesf[0:4, 0:1].to_broadcast([4, G])
        gview = [gdram[k].rearrange("(h g) t -> h g t", g=NG) for k in range(2)]
        for g in range(NG):
            gsl = slice(g * G, (g + 1) * G)
            xb = self.xT_b[g]
            e_, nF, g_, G_, nM_ = e_t[g % 2], negF[g % 2], gg[g % 2], Gc[g % 2], nM[g % 2]
            nFp, Gp = negF[(g - 1) % 2], Gc[(g - 1) % 2]
            bI = rot()
            pI, pIb = self.pb[bI]
            self.mm(bI, pI[0:4, :], [(wmifv[:, kc, 0:4], self.xT[:, kc, gsl]) for kc in range(NKC)], R=[wmif.b, xb])
            bF = rot()
            pF, pFb = self.pb[bF]
            self.mm(bF, pF[0:4, :], [(wmifv[:, kc, 4:8], self.xT[:, kc, gsl]) for kc in range(NKC)], R=[wmif.b, xb])
            S.op(S.ACT, lambda h, pF=pF, e_=e_: h.activation(out=e_[:], in_=pF[0:4, :], func=AF.Exp, scale=-1.0, bias=nfbm[:, 0:1]),
                 X=[pFb], R=[nfbm.b], W=[e_.b])
            S.op(S.ACT, lambda h, e_=e_: h.activation(out=e_[:], in_=e_[:], func=AF.Ln, bias=1.0), R=[e_.b], W=[e_.b])
            initF = 0.0 if g == 0 else nFp[:, G - 1:G]
            S.op(S.DVE, lambda h, initF=initF, nF=nF, e_=e_: h.tensor_tensor_scan(out=nF[:], data0=onesb, data1=e_[:], initial=initF,
                                                                                op0=ALU.mult, op1=ALU.add),
                 R=[e_.b, self.onesf.b] + ([nFp.b] if g > 0 else []), W=[nF.b])
            S.op(S.DVE, lambda h, pI=pI, g_=g_, nF=nF: h.scalar_tensor_tensor(out=g_[:], in0=pI[0:4, :], scalar=ib[:, 0:1], in1=nF[:],
                                                                             op0=ALU.add, op1=ALU.add), X=[pIb], R=[ib.b, nF.b], W=[g_.b])
            initG = 0.0 if g == 0 else Gp[:, G - 1:G]
            S.op(S.DVE, lambda h, initG=initG, G_=G_, g_=g_: h.tensor_tensor_scan(out=G_[:], data0=onesb, data1=g_[:], initial=initG,
                                                                                op0=ALU.mult, op1=ALU.max),
                 R=[g_.b, self.onesf.b] + ([Gp.b] if g > 0 else []), W=[G_.b])
            S.op(S.DVE, lambda h, nM_=nM_, nF=nF, G_=G_: h.tensor_tensor(out=nM_[:], in0=nF[:], in1=G_[:], op=ALU.subtract),
                 R=[nF.b, G_.b], W=[nM_.b])
            S.op(S.SP, lambda h, G_=G_, g=g: h.dma_start(out=gview[0][:, g, :], in_=G_[:]), R=[G_.b], W=[gdram_b[g]], stream=G_.st)
            S.op(S.SP, lambda h, nM_=nM_, g=g: h.dma_start(out=gview[1][:, g, :], in_=nM_[:]), R=[nM_.b], W=[gdram_b[g]], stream=nM_.st)
            bk = rot()
            pt, pbuf = self.pb[bk]
            for j in range(4):
                S.op(S.PE, lambda h, pt=pt, j=j, g_=g_: h.transpose(pt[:, j * 4:(j + 1) * 4], g_[0:4, j * 128:(j + 1) * 128], self.identf[0:4, 0:4]),
                     R=[g_.b, self.identf.b], X=[pbuf], sig=(j == 3))
            S.op(S.DVE, lambda h, pt=pt, g=g: h.tensor_copy(out=gtok_all[:, 4 * g:4 * g + 4, :], in_=pt[:, 0:16].rearrange("p (j c) -> p j c", j=4)),
                 X=[pbuf], W=[gtok_all.b])
            yield
        S.op(S.DVE, lambda h: h.tensor_scalar(out=gtokS_all[:, :, :], in0=gtok_all[:, :, :], scalar1=float(math.log(sK)), scalar2=None, op0=ALU.add),
             R=[gtok_all.b], W=[gtokS_all.b])

    def phase_mlstm(self, l):
        S = self.S
        sizes, offs = win_offsets()
        sK = float(96 ** -0.5)
        rot = self.bankrot([0, 1, 2])
        B_S, B_T, B_N, B_D = 4, 5, 6, 7
        gtok_all, gtokS_all = self.gtok_all, self.gtokS_all
        gdram, gdram_b = self.gdram, self.gdram_b
        cw = self.tile("cw", [96, 32], dma=True)
        cb = self.tile("cb", [96, 8], dma=True)
        ngt = self.tile("ngt", [96, 4], dma=True)
        for (t_, src) in ((cw, self.convw[l]), (cb, self.convb[l]), (ngt, self.ngd[l])):
            S.op(S.SP, lambda h, t_=t_, src=src: h.dma_start(out=t_[:], in_=src), W=[t_.b], stream=t_.st)
        lnhalf = self.tile("lnhalf", [128, 1])
        epsb = self.tile("epsb", [128, 1])
        S.op(S.POOL, lambda h: h.memset(lnhalf[:], float(math.log(0.5))), W=[lnhalf.b])
        S.op(S.POOL, lambda h: h.memset(epsb[:], float(LN_EPS)), W=[epsb.b])
        wml = [self.tile("wml", [128, NKC * 480], BF16, dma=True) for _ in range(2)]
        Grep = [self.tile("Grep", [128, G], dma=True) for _ in range(2)]
        clamp = [self.tile("clamp", [96, G], dma=True) for _ in range(2)]
        mu_chain = [self.tile("mu_chain", [128, 8]) for _ in range(2)]
        wexp = [self.tile("wexp", [128, 4]) for _ in range(2)]
        carry = [self.tile("carry", [128, 4]) for _ in range(2)]
        warg = self.tile("warg", [128, 4])
        carg = self.tile("carg", [128, 4])
        qpre = self.tile("qpre", [96, G + 3])
        kpre = self.tile("kpre", [96, G + 3])
        QT = [self.tile("QT", [96, G]) for _ in range(2)]
        KT = [self.tile("KT", [96, G]) for _ in range(2)]
        Vtok = [self.tile("Vtok", [128, 4, 192], BF16) for _ in range(2)]
        Vones = [Buf("Vtok_ones0"), Buf("Vtok_ones1")]
        for k in range(2):
            S.op(S.POOL, lambda h, k=k: h.memset(Vtok[k][:, :, 96:192], 1.0), W=[Vones[k]])
        sgo = [self.tile("sgo", [96, G]) for _ in range(2)]
        szm = [self.tile("szm", [96, G], BF16) for _ in range(2)]
        argDT = self.tile("argDT", [128, G])
        DTb = [Buf(f"DTb{j}") for j in range(4)]
        decb = [Buf(f"decb{j}") for j in range(4)]
        AT4 = self.tile("AT4", [128, G], BF16)
        decQ = self.tile("decQ", [96, G])
        decQb = self.tile("decQb", [96, G], BF16)
        Stb = self.tile("Stb", [96, 192], BF16)
        tBb = self.tile("tBb", [96, G], BF16)
        tAb = self.tile("tAb", [96, G], BF16)
        avg96b = self.tile("avg96b", [96, 96], BF16)
        S.op(S.DVE, lambda h: h.tensor_copy(out=avg96b[:], in_=self.avg96[0:96, :]), R=[self.avg96.b], W=[avg96b.b])
        Khat4 = self.tile("Khat4", [128, 4, 96], BF16)
        St = self.tile("St", [96, 192])
        tA = self.tile("tA", [96, G])
        tB = self.tile("tB", [96, G])
        tC = self.tile("tC", [96, G])
        cnt = {"b": 0}
        iters = [(hd, g) for hd in range(4) for g in range(NG)]

        def gen_A(i):
            hd, g = iters[i]
            k = i % 2
            gsl = slice(g * G, (g + 1) * G)
            xb = self.xT_b[g]
            W_ = wml[hd % 2]
            Wv = W_[:, :].rearrange("p (c n) -> p c n", c=NKC)
            RR = [W_.b, xb]
            Gr, cl, mu, we, ca = Grep[k], clamp[k], mu_chain[k], wexp[k], carry[k]
            mup = mu_chain[1 - k]
            row = hd * NG + g
            S.op(S.SP, lambda h: h.dma_start(out=Gr[:], in_=gdram[0][row:row + 1, :].partition_broadcast(128)),
                 R=[gdram_b[g]], W=[Gr.b], stream=Gr.st)
            S.op(S.SP, lambda h: h.dma_start(out=cl[:], in_=gdram[1][row:row + 1, :].partition_broadcast(96)),
                 R=[gdram_b[g]], W=[cl.b], stream=cl.st)
            S.op(S.ACT, lambda h: h.activation(out=cl[:], in_=cl[:], func=AF.Exp), R=[cl.b], W=[cl.b])
            if g == 0:
                S.op(S.POOL, lambda h: h.memset(mu[:, 0:1], 0.0), W=[mu.b])
                S.op(S.POOL, lambda h: h.memset(qpre[:, 0:3], 0.0), W=[qpre.b])
                S.op(S.POOL, lambda h: h.memset(kpre[:, 0:3], 0.0), W=[kpre.b])
            else:
                S.op(S.DVE, lambda h: h.tensor_copy(out=mu[:, 0:1], in_=mup[:, 4:5]), R=[mup.b], W=[mu.b])
            S.op(S.DVE, lambda h: h.tensor_copy(out=mu[:, 1:5], in_=Gr[:, :].rearrange("p (j t) -> p j t", j=4)[:, :, 127]),
                 R=[Gr.b], W=[mu.b])
            S.op(S.DVE, lambda h: h.tensor_tensor(out=warg[:], in0=gtok_all[:, 4 * g:4 * g + 4, hd], in1=mu[:, 1:5], op=ALU.subtract),
                 R=[gtok_all.b, mu.b], W=[warg.b])
            S.op(S.ACT, lambda h: h.activation(out=we[:], in_=warg[:], func=AF.Exp), R=[warg.b], W=[we.b])
            S.op(S.DVE, lambda h: h.tensor_tensor(out=carg[:], in0=mu[:, 0:4], in1=mu[:, 1:5], op=ALU.subtract), R=[mu.b], W=[carg.b])
            S.op(S.ACT, lambda h: h.activation(out=ca[:], in_=carg[:], func=AF.Exp), R=[carg.b], W=[ca.b])
            yield
            for (pre, acc, c0, qk) in ((qpre, QT[k], 0, 0), (kpre, KT[k], 96, 1)):
                if g > 0:
                    S.op(S.DVE, lambda h, pre=pre: h.tensor_copy(out=pre[:, 0:3], in_=pre[:, G:G + 3]), R=[pre.b], W=[pre.b])
                bk = rot()
                pt, pbuf = self.pb[bk]
                for kc in range(NKC):
                    S.op(S.PE, lambda h, kc=kc, pt=pt, c0=c0: h.matmul(pt[0:96, :], lhsT=Wv[:, kc, c0:c0 + 96], rhs=self.xT[:, kc, gsl],
                                                                      start=(kc == 0), stop=(kc == NKC - 1)), R=RR, X=[pbuf], sig=(kc == NKC - 1))
                    if kc % 2 == 1 and kc < NKC - 1:
                        yield
                S.op(S.ACT, lambda h, pt=pt, pre=pre: h.activation(out=pre[:, 3:G + 3], in_=pt[0:96, :], func=AF.Copy), X=[pbuf], W=[pre.b])
                yield
                wi = lambda tap, qk=qk: cw[:, (qk * 4 + hd) * 4 + tap:(qk * 4 + hd) * 4 + tap + 1]
                bi = cb[:, qk * 4 + hd:qk * 4 + hd + 1]
                S.op(S.DVE, lambda h, pre=pre, acc=acc, wi=wi, bi=bi: h.tensor_scalar(
                    out=acc[:], in0=pre[:, 3:G + 3], scalar1=wi(3), scalar2=bi, op0=ALU.mult, op1=ALU.add),
                    R=[pre.b, cw.b, cb.b], W=[acc.b])
                for kk in (1, 2, 3):
                    S.op(S.DVE, lambda h, pre=pre, acc=acc, wi=wi, kk=kk: h.scalar_tensor_tensor(
                        out=acc[:], in0=pre[:, 3 - kk:G + 3 - kk], scalar=wi(3 - kk), in1=acc[:], op0=ALU.mult, op1=ALU.add),
                        R=[pre.b, cw.b, acc.b], W=[acc.b])
                    if kk == 2:
                        yield
                yield
            bk = rot()
            pt, pbuf = self.pb[bk]
            Vt = Vtok[k]
            for j in range(4):
                blk = slice(g * G + j * 128, g * G + (j + 1) * 128)
                for kc in range(NKC):
                    S.op(S.PE, lambda h, kc=kc, pt=pt, j=j, blk=blk: h.matmul(
                        pt[:, j * 96:(j + 1) * 96], lhsT=self.xT[:, kc, blk], rhs=Wv[:, kc, 192:288],
                        start=(kc == 0), stop=(kc == NKC - 1)), R=RR, X=[pbuf], sig=(kc == NKC - 1))
                yield
            S.op(S.DVE, lambda h, pt=pt, Vt=Vt: h.tensor_copy(out=Vt[:, :, 0:96], in_=pt[:, 0:384].rearrange("p (j c) -> p j c", j=4)),
                 X=[pbuf], W=[Vt.b])
            yield
            held = []
            for c0 in (288, 384):
                bk = rot()
                pt, pbuf = self.pb[bk]
                for kc in range(NKC):
                    S.op(S.PE, lambda h, kc=kc, pt=pt, c0=c0: h.matmul(pt[0:96, :], lhsT=Wv[:, kc, c0:c0 + 96], rhs=self.xT[:, kc, gsl],
                                                                      start=(kc == 0), stop=(kc == NKC - 1)), R=RR, X=[pbuf], sig=(kc == NKC - 1))
                held.append((pt, pbuf))
            S.op(S.ACT, lambda h: h.activation(out=QT[k][:], in_=QT[k][:], func=AF.Silu), R=[QT[k].b], W=[QT[k].b])
            S.op(S.ACT, lambda h: h.activation(out=KT[k][:], in_=KT[k][:], func=AF.Silu), R=[KT[k].b], W=[KT[k].b])
            (pto, pbo), (ptz, pbz) = held
            S.op(S.ACT, lambda h: h.activation(out=sgo[k][:], in_=pto[0:96, :], func=AF.Tanh, scale=0.5), X=[pbo], W=[sgo[k].b])
            S.op(S.ACT, lambda h: h.activation(out=szm[k][:], in_=ptz[0:96, :], func=AF.Silu), X=[pbz], W=[szm[k].b])
            yield

        N_A = 30

        def run_BC(i, chunks):
            hd, g = iters[i]
            k = i % 2
            gsl = slice(g * G, (g + 1) * G)
            Gr, cl, mu, we, ca = Grep[k], clamp[k], mu_chain[k], wexp[k], carry[k]
            Q_, K_, Vt, Vo = QT[k], KT[k], Vtok[k], Vones[k]
            sg, sz = sgo[k], szm[k]
            pS, pSb = self.pb[B_S]
            pT, pTb = self.pb[B_T]
            pN, pNb = self.pb[B_N]
            pD, pDb = self.pb[B_D]

            fstate = {"pt": 0, "em": 0}
            N_PTS = 13

            def fill(n=1):
                did = False
                fstate["pt"] += 1
                if chunks is not None:
                    want = (fstate["pt"] * N_A + N_PTS - 1) // N_PTS
                    while fstate["em"] < want:
                        if next(chunks, "end") != "end":
                            did = True
                        fstate["em"] += 1
                if not did:
                    self.filler(512, bank=3)

            if g == 0:
                S.op(S.POOL, lambda h: h.memset(St[:], 0.0), W=[St.b])
                S.op(S.POOL, lambda h: h.memset(Stb[:], 0.0), W=[Stb.b])
            for j in range(4):
                bs = slice(j * 128, (j + 1) * 128)
                self.mm(B_S, pS[:, bs], [(K_[0:96, bs], Q_[0:96, bs])], R=[K_.b, Q_.b])
            S.op(S.DVE, lambda h: h.scalar_tensor_tensor(
                out=argDT[:, :].rearrange("p (j t) -> p j t", j=4), in0=Gr[:, :].rearrange("p (j t) -> p j t", j=4), scalar=-1.0,
                in1=self.mnegf[:, :].unsqueeze(1).to_broadcast([128, 4, 128]), op0=ALU.mult, op1=ALU.add),
                R=[Gr.b, self.mnegf.b], W=[argDT.b] + DTb)
            for j in range(4):
                bs = slice(j * 128, (j + 1) * 128)
                S.op(S.ACT, lambda h, bs=bs, j=j: h.activation(out=argDT[:, bs], in_=argDT[:, bs], func=AF.Exp,
                                                              bias=gtokS_all[:, 4 * g + j, hd:hd + 1]), R=[argDT.b, gtokS_all.b], W=[DTb[j]])
            for j in range(4):
                bs = slice(j * 128, (j + 1) * 128)
                S.op(S.ACT, lambda h, bs=bs, j=j: h.activation(out=decQ[:, bs], in_=Gr[0:96, bs], func=AF.Exp, scale=-1.0,
                                                              bias=mu[0:96, j:j + 1]), R=[Gr.b, mu.b], W=[decb[j]])
            fill()
            S.op(S.DVE, lambda h: h.tensor_tensor(out=AT4[:], in0=pS[:, :], in1=argDT[:], op=ALU.mult), X=[pSb], R=[argDT.b] + DTb, W=[AT4.b])
            S.op(S.POOL, lambda h: h.tensor_tensor(out=decQb[:], in0=Q_[:], in1=decQ[:], op=ALU.mult), R=[Q_.b, decQ.b] + decb, W=[decQb.b])
            for j in range(4):
                bs = slice(j * 128, (j + 1) * 128)
                S.op(S.PE, lambda h, bs=bs, j=j: h.transpose(pT[:, j * 96:(j + 1) * 96], K_[0:96, bs], self.identf[0:96, 0:96]),
                     R=[K_.b, self.identf.b], X=[pTb], sig=(j == 3))
            S.op(S.DVE, lambda h: h.scalar_tensor_tensor(
                out=Khat4[:, :, :], in0=pT[:, 0:384].rearrange("p (j c) -> p j c", j=4), scalar=sK,
                in1=we[:, 0:4].unsqueeze(2).to_broadcast([128, 4, 96]), op0=ALU.mult, op1=ALU.mult),
                X=[pTb], R=[we.b], W=[Khat4.b])
            fill()
            for j in range(4):
                bs = slice(j * 128, (j + 1) * 128)
                self.mm(B_N, pN[0:96, bs], [(Vt[:, j, 0:96], AT4[:, bs])], R=[Vt.b, AT4.b], start=(j == 0), stop=False, sgc=True)
            for j in range(4):
                bs = slice(j * 128, (j + 1) * 128)
                self.mm(B_D, pD[0:96, bs], [(Vt[:, j, 96:192], AT4[:, bs])], R=[Vo, AT4.b], start=(j == 0), stop=False, sgc=True)
            for j in range(4):
                ub, ubuf, uo = (pT, pTb, B_T) if j < 2 else (pS, pSb, B_S)
                c0 = (j % 2) * 192
                self.mm(uo, ub[0:96, c0:c0 + 192], [(Khat4[:, j, :], Vt[:, j, :])], R=[Khat4.b, Vt.b, Vo])
            fill()
            for j in range(4):
                bs = slice(j * 128, (j + 1) * 128)
                ub, ubuf = (pT, pTb) if j < 2 else (pS, pSb)
                c0 = (j % 2) * 192
                self.mm(B_N, pN[0:96, bs], [(Stb[0:96, 0:96], decQb[:, bs])], R=[Stb.b, decQb.b], start=False, stop=True, sgc=True)
                self.mm(B_D, pD[0:96, bs], [(Stb[0:96, 96:192], decQb[:, bs])], R=[Stb.b, decQb.b], start=False, stop=True, sgc=True)
                S.op(S.DVE, lambda h, j=j, ub=ub, c0=c0: h.scalar_tensor_tensor(out=St[:], in0=St[:], scalar=ca[0:96, j:j + 1],
                                                                               in1=ub[0:96, c0:c0 + 192], op0=ALU.mult, op1=ALU.add),
                     X=[ubuf], R=[ca.b, St.b], W=[St.b])
                S.op(S.DVE, lambda h: h.tensor_copy(out=Stb[:], in_=St[:]), R=[St.b], W=[Stb.b])
                if j % 2 == 1:
                    fill()
            S.op(S.DVE, lambda h: h.tensor_tensor(out=tA[:], in0=pD[0:96, :], in1=cl[:], op=ALU.max), X=[pDb], R=[cl.b], W=[tA.b])
            S.op(S.DVE, lambda h: h.scalar_tensor_tensor(out=tA[:], in0=pD[0:96, :], scalar=-1.0, in1=tA[:], op0=ALU.mult, op1=ALU.max),
                 X=[pDb], R=[tA.b], W=[tA.b])
            fill()
            S.op(S.ACT, lambda h: h.activation(out=tA[:], in_=tA[:], func=AF.Ln), R=[tA.b], W=[tA.b])
            S.op(S.ACT, lambda h: h.activation(out=tA[:], in_=tA[:], func=AF.Exp, scale=-1.0, bias=lnhalf[0:96, 0:1]), R=[tA.b, lnhalf.b], W=[tA.b])
            fill()
            S.op(S.DVE, lambda h: h.tensor_tensor(out=tB[:], in0=pN[0:96, :], in1=tA[:], op=ALU.mult), X=[pNb], R=[tA.b], W=[tB.b])
            S.op(S.DVE, lambda h: h.scalar_tensor_tensor(out=tB[:], in0=sg[:], scalar=1.0, in1=tB[:], op0=ALU.add, op1=ALU.mult),
                 R=[tB.b, sg.b], W=[tB.b])
            fill()
            S.op(S.ACT, lambda h: h.activation(out=tBb[:], in_=tB[:], func=AF.Copy), R=[tB.b], W=[tBb.b])
            bk = rot()
            pt, pbuf = self.pb[bk]
            self.mm(bk, pt[0:96, :], [(avg96b[:, :], tBb[:, :])], R=[avg96b.b, tBb.b])
            fill()
            S.op(S.DVE, lambda h, pt=pt: h.tensor_tensor(out=tC[:], in0=tB[:], in1=pt[0:96, :], op=ALU.subtract), X=[pbuf], R=[tB.b], W=[tC.b])
            S.op(S.ACT, lambda h: h.activation(out=tAb[:], in_=tC[:], func=AF.Square), R=[tC.b], W=[tAb.b])
            fill()
            bk = rot()
            pt, pbuf = self.pb[bk]
            self.mm(bk, pt[0:96, :], [(avg96b[:, :], tAb[:, :])], R=[avg96b.b, tAb.b])
            fill()
            S.op(S.ACT, lambda h, pt=pt: h.activation(out=tB[:], in_=pt[0:96, :], func=AF.Ln, bias=epsb[0:96, 0:1]), X=[pbuf], R=[epsb.b], W=[tB.b])
            S.op(S.ACT, lambda h: h.activation(out=tB[:], in_=tB[:], func=AF.Exp, scale=-0.5), R=[tB.b], W=[tB.b])
            fill()
            S.op(S.DVE, lambda h: h.scalar_tensor_tensor(out=tC[:], in0=tC[:], scalar=ngt[:, hd:hd + 1], in1=tB[:], op0=ALU.mult, op1=ALU.mult),
                 R=[tC.b, ngt.b, tB.b], W=[tC.b])
            S.op(S.POOL, lambda h: h.tensor_tensor(out=self.ycat[0:96, 3 + hd, gsl], in0=tC[:], in1=sz[:], op=ALU.mult),
                 R=[tC.b, sz.b], W=[self.ycat_b[3 + hd][g]])
            if chunks is not None:
                for _ in chunks:
                    pass

        self.load_w(wml[0], self.wpk[l][:, int(offs[8]):int(offs[9])])
        for _ in gen_A(0):
            pass
        for i in range(len(iters)):
            hd, g = iters[i]
            if g == 0 and hd + 1 < 4:
                self.load_w(wml[(hd + 1) % 2], self.wpk[l][:, int(offs[9 + hd]):int(offs[10 + hd])])
            chunks = gen_A(i + 1) if i + 1 < len(iters) else None
            run_BC(i, chunks)

    def build_memT(self):
        S = self.S
        mt = [self.tile("memin", [128, D], dma=True) for _ in range(2)]
        rot = self.bankrot([4, 5, 6, 7])
        for t in range(2):
            S.op(S.SP, lambda h, t=t: h.dma_start(out=mt[t][:], in_=self.mem_in[t * 128:(t + 1) * 128, :]), W=[mt[t].b], stream=mt[t].st)
            for half in range(2):
                bk = rot()
                pt, pbuf = self.pb[bk]
                for c in range(4):
                    kc = half * 4 + c
                    S.op(S.PE, lambda h, kc=kc, c=c, pt=pt, t=t: h.transpose(
                        pt[:, c * 128:(c + 1) * 128], mt[t][:, kc * 128:(kc + 1) * 128], self.identf[:]),
                        R=[mt[t].b, self.identf.b], X=[pbuf], sig=(c == 3))
                dst = self.memT[:, half * 4:(half + 1) * 4, t * 128:(t + 1) * 128]
                srcp = pt[:, :].rearrange("p (c t) -> p c t", c=4)
                S.op(S.DVE, lambda h, dst=dst, srcp=srcp: h.tensor_copy(out=dst, in_=srcp), X=[pbuf], W=[self.memT.b])

    def phase_mem(self, l):
        S = self.S
        sizes, offs = win_offsets()
        wm = self.tile("wm", [128, NKC * 512], BF16, dma=True)
        wr = self.tile("wr", [128, NKC * 512], BF16, dma=True)
        self.load_w(wm, self.wmempk[l])
        self.load_w(wr, self.wpk[l][:, int(offs[12]):int(offs[13])])
        wmv = wm[:, :].rearrange("p (c n) -> p c n", c=NKC)
        wrv = wr[:, :].rearrange("p (c n) -> p c n", c=NKC)
        KmT = self.tile("KmT", [128, 2, MEM_LEN], BF16)
        Vm = self.tile("Vm", [128, 2, 4, 128], BF16)
        rot = self.bankrot([0, 1])
        rot_s = self.bankrot([2, 3, 4])
        rot_o = self.bankrot([5, 6])
        for p in range(2):
            bk = rot()
            pt, pbuf = self.pb[bk]
            self.mm(bk, pt[:, 0:MEM_LEN], [(wmv[:, kc, p * 128:(p + 1) * 128], self.memT[:, kc, :]) for kc in range(NKC)],
                    R=[wm.b, self.memT.b])
            S.op(S.DVE, lambda h, pt=pt, p=p: h.tensor_copy(out=KmT[:, p, :], in_=pt[:, 0:MEM_LEN]), X=[pbuf], W=[KmT.b])
        S.op(S.POOL, lambda h: h.memset(Vm[:], 1.0), W=[Vm.b])
        for mb in range(2):
            bk = rot()
            pt, pbuf = self.pb[bk]
            self.mm(bk, pt[:, 0:256], [(self.memT[:, kc, mb * 128:(mb + 1) * 128], wmv[:, kc, 256:512]) for kc in range(NKC)],
                    R=[wm.b, self.memT.b])
            for h4 in range(4):
                c0 = 0 if h4 % 2 == 0 else 64
                S.op(S.DVE, lambda h, pt=pt, mb=mb, h4=h4, c0=c0: h.tensor_copy(
                    out=Vm[:, mb, h4, c0:c0 + 64], in_=pt[:, h4 * 64:(h4 + 1) * 64]), X=[pbuf], W=[Vm.b])
        QTm = [self.tile("QTm", [128, G], BF16) for _ in range(2)]
        szp = [self.tile("szp", [128, G], BF16) for _ in range(2)]
        thm = self.tile("thm", [128, G])
        PT = [self.tile("PTm", [128, G], BF16) for _ in range(3)]
        rd = [self.tile("rdm", [128, G]) for _ in range(2)]
        tn = [self.tile("tnm", [128, G]) for _ in range(2)]
        it = 0
        ip = 0
        for g in range(NG):
            gsl = slice(g * G, (g + 1) * G)
            for p in range(2):
                q, z = QTm[it % 2], szp[it % 2]
                it += 1
                bk = rot()
                pt, pbuf = self.pb[bk]
                self.mm(bk, pt[:, :], [(wrv[:, kc, p * 128:(p + 1) * 128], self.xT[:, kc, gsl]) for kc in range(NKC)],
                        R=[wr.b, self.xT_b[g]])
                S.op(S.ACT, lambda h, pt=pt, q=q: h.activation(out=q[:], in_=pt[:, :], func=AF.Copy, scale=0.125), X=[pbuf], W=[q.b])
                bk = rot()
                pt, pbuf = self.pb[bk]
                self.mm(bk, pt[:, :], [(wrv[:, kc, 256 + p * 128:256 + (p + 1) * 128], self.xT[:, kc, gsl]) for kc in range(NKC)],
                        R=[wr.b, self.xT_b[g]])
                S.op(S.ACT, lambda h, pt=pt: h.activation(out=thm[:], in_=pt[:, :], func=AF.Tanh, scale=0.5), X=[pbuf], W=[thm.b])
                S.op(S.DVE, lambda h, pt=pt, z=z: h.scalar_tensor_tensor(out=z[:], in0=thm[:], scalar=1.0, in1=pt[:, :], op0=ALU.add, op1=ALU.mult),
                     X=[pbuf], R=[thm.b], W=[z.b])
                for hh in range(2):
                    h4 = 2 * p + hh
                    r0 = 64 * hh
                    bo = rot_o()
                    po, pobuf = self.pb[bo]
                    sts = []
                    for mb in range(2):
                        bs = rot_s()
                        ps_, psbuf = self.pb[bs]
                        self.mm(bs, ps_[:, :], [(KmT[r0:r0 + 64, p, mb * 128:(mb + 1) * 128], q[r0:r0 + 64, :])], R=[KmT.b, q.b])
                        sts.append((ps_, psbuf))
                    for mb in range(2):
                        ps_, psbuf = sts[mb]
                        pt_ = PT[ip % 3]
                        ip += 1
                        S.op(S.ACT, lambda h, ps_=ps_, pt_=pt_: h.activation(out=pt_[:], in_=ps_[:, :], func=AF.Exp), X=[psbuf], W=[pt_.b])
                        self.mm(bo, po[:, :], [(Vm[:, mb, h4, :], pt_[:])], R=[Vm.b, pt_.b], start=(mb == 0), stop=(mb == 1))
                    self.attn_epilogue(po, pobuf, hh == 1, z, rd[h4 % 2], tn[h4 % 2], 7 + p, g, half=True)

    def attn_epilogue(self, po, pobuf, odd, sz, rd, tn, chunk, g, half=False, act_recip=False):
        S = self.S
        gsl = slice(g * G, (g + 1) * G)
        nr = slice(64, 128) if odd else slice(0, 64)
        dr = slice(0, 64) if odd else slice(64, 128)
        if act_recip:
            S.op(S.ACT, lambda h: h.activation(out=rd[nr, :], in_=po[dr, :], func=AF.Ln), X=[pobuf], W=[rd.b])
            S.op(S.ACT, lambda h: h.activation(out=rd[nr, :], in_=rd[nr, :], func=AF.Exp, scale=-1.0), R=[rd.b], W=[rd.b])
        else:
            S.op(S.DVE, lambda h: h.reciprocal(out=rd[nr, :], in_=po[dr, :]), X=[pobuf], W=[rd.b])
        if half:
            S.op(S.DVE, lambda h: h.scalar_tensor_tensor(out=tn[nr, :], in0=po[nr, :], scalar=0.5, in1=rd[nr, :], op0=ALU.mult, op1=ALU.mult),
                 X=[pobuf], R=[rd.b], W=[tn.b])
        else:
            S.op(S.DVE, lambda h: h.tensor_tensor(out=tn[nr, :], in0=po[nr, :], in1=rd[nr, :], op=ALU.mult), X=[pobuf], R=[rd.b], W=[tn.b])
        S.op(S.POOL, lambda h: h.tensor_tensor(out=self.ycat[nr, chunk, gsl], in0=tn[nr, :], in1=sz[nr, :], op=ALU.mult),
             R=[tn.b, sz.b], W=[self.ycat_b[chunk][g]])

    def phase_final(self, l, x_src, x_dst, make_xT):
        S = self.S
        nc = self.nc
        wout = self.tile("wout", [128, 9 * D], BF16, dma=True)
        self.load_w(wout, self.woutpk[l])
        woutv = wout[:, :].rearrange("p (c n) -> p c n", c=9)
        gam = self.tile("gam", [128, D], dma=True)
        bet = self.tile("bet", [128, D], dma=True)
        S.op(S.SP, lambda h: h.dma_start(out=gam[:], in_=self.lng[l].partition_broadcast(128)), W=[gam.b], stream=gam.st)
        S.op(S.SP, lambda h: h.dma_start(out=bet[:], in_=self.lnb[l].partition_broadcast(128)), W=[bet.b], stream=bet.st)
        xin = [self.tile("xin", [128, D], dma=True) for _ in range(3)]
        epsf = self.tile("epsf", [128, 1])
        S.op(S.POOL, lambda h: h.memset(epsf[:], float(LN_EPS)), W=[epsf.b])
        tt = [self.tile("tt", [128, D]) for _ in range(2)]
        xo = [self.tile("xo", [128, D]) for _ in range(2)]
        stats = [self.tile("stats", [128, 16]) for _ in range(2)]
        krows = [128, 128, 128, 96, 96, 96, 96, 128, 128]
        rot_y = self.bankrot([0, 1, 2, 3])
        rot_t = self.bankrot([4, 5, 6, 7])

        def load_x(t):
            xt = xin[t % 3]
            S.op(S.SP, lambda h: h.dma_start(out=xt[:], in_=x_src[t * 128:(t + 1) * 128, :]), W=[xt.b], stream=xt.st)

        nmr = [self.tile("nmr", [128, 2]) for _ in range(2)]
        junk = self.tile("junk", [128, D], BF16)

        held = {}

        def stage_M_pe(t):
            g = t // 4
            hb = []
            for half in range(2):
                bk = rot_y()
                pt, pbuf = self.pb[bk]
                pairs = [(self.ycat[0:krows[c], c, t * 128:(t + 1) * 128], woutv[0:krows[c], c, half * 512:(half + 1) * 512])
                         for c in range(9)]
                self.mm(bk, pt[:, :], pairs, R=[wout.b] + [self.ycat_b[c][g] for c in range(9)])
                hb.append((pt, pbuf))
            held[t] = hb

        def stage_M_dve(t):
            xt, tq, sq = xin[t % 3], tt[t % 2], stats[t % 2]
            for half in range(2):
                pt, pbuf = held[t][half]
                S.op(S.DVE, lambda h, pt=pt, half=half: h.scalar_tensor_tensor(
                    out=tq[:, half * 512:(half + 1) * 512], in0=xt[:, half * 512:(half + 1) * 512], scalar=float(ALPHA),
                    in1=pt[:, :], op0=ALU.mult, op1=ALU.add), R=[xt.b], X=[pbuf], W=[tq.b])
            S.op(S.ACT, lambda h: h.activation(out=junk[:], in_=tq[:], func=AF.Identity, accum_out=sq[:, 0:1]), R=[tq.b], W=[junk.b, sq.b])
            S.op(S.ACT, lambda h: h.activation(out=junk[:], in_=tq[:], func=AF.Square, accum_out=sq[:, 1:2]), R=[tq.b], W=[junk.b, sq.b])

        def stage_N(t):
            tq, xq, sq, nm = tt[t % 2], xo[t % 2], stats[t % 2], nmr[t % 2]
            S.op(S.DVE, lambda h: h.tensor_scalar(out=sq[:, 12:13], in0=sq[:, 0:1], scalar1=1.0 / D, scalar2=None, op0=ALU.mult), R=[sq.b], W=[sq.b])
            S.op(S.DVE, lambda h: h.tensor_tensor(out=sq[:, 2:3], in0=sq[:, 12:13], in1=sq[:, 12:13], op=ALU.mult), R=[sq.b], W=[sq.b])
            S.op(S.DVE, lambda h: h.scalar_tensor_tensor(out=sq[:, 13:14], in0=sq[:, 1:2], scalar=1.0 / D, in1=sq[:, 2:3],
                                                         op0=ALU.mult, op1=ALU.subtract), R=[sq.b], W=[sq.b])
            S.op(S.ACT, lambda h: h.activation(out=sq[:, 14:15], in_=sq[:, 13:14], func=AF.Ln, bias=epsf[:, 0:1]), R=[sq.b, epsf.b], W=[sq.b])
            S.op(S.ACT, lambda h: h.activation(out=nm[:, 0:1], in_=sq[:, 14:15], func=AF.Exp, scale=-0.5), R=[sq.b], W=[nm.b])
            S.op(S.DVE, lambda h: h.tensor_scalar(out=nm[:, 1:2], in0=sq[:, 12:13], scalar1=nm[:, 0:1], scalar2=-1.0, op0=ALU.mult, op1=ALU.mult),
                 R=[sq.b, nm.b], W=[nm.b])
            S.op(S.ACT, lambda h: h.activation(out=tq[:], in_=tq[:], func=AF.Identity, scale=nm[:, 0:1], bias=nm[:, 1:2]), R=[tq.b, nm.b], W=[tq.b])

        def stage_N_b(t):
            tq, xq = tt[t % 2], xo[t % 2]
            S.op(S.POOL, lambda h: h.tensor_tensor(out=xq[:], in0=tq[:], in1=gam[:], op=ALU.mult), R=[tq.b, gam.b], W=[xq.b])
            S.op(S.POOL, lambda h: h.tensor_tensor(out=xq[:], in0=xq[:], in1=bet[:], op=ALU.add), R=[xq.b, bet.b], W=[xq.b])
            st = self.stq[t % 2]
            S.op(S.SP, lambda h: h.dma_start(out=x_dst[t * 128:(t + 1) * 128, :], in_=xq[:]), R=[xq.b], stream=st)

        load_x(0)
        load_x(1)
        stage_M_pe(0)
        stage_M_dve(0)
        for t in range(NT):
            if t + 2 < NT:
                load_x(t + 2)
            if t + 1 < NT:
                stage_M_pe(t + 1)
            stage_N(t)
            stage_N_b(t)
            if t + 1 < NT:
                stage_M_dve(t + 1)
            if make_xT and t >= 1:
                self.transpose_into_xT(xo[(t - 1) % 2], t - 1, rot_t)
        if make_xT:
            self.transpose_into_xT(xo[(NT - 1) % 2], NT - 1, rot_t)

    def _build(self):
        S = self.S
        L = self.L
        for l in range(L):
            x_src = self.x_in if l == 0 else self.xbuf[(l - 1) % 2]
            x_dst = self.out if l == L - 1 else self.xbuf[l % 2]
            if l == 0:
                self.run_phase(self.build_xT_from_dram, x_src)
            if l == 0:
                self.run_phase(self.build_memT)
            self.run_phase(self.phase_fox, l)
            if "noml" not in self.dbg:
                self.run_phase(self.phase_mlstm, l)
            self.run_phase(self.phase_mem, l)
            if "ycat" in self.dbg:
                S.barrier()
                d = self.dbg_dram("dbg_ycat", [128, 9 * S_LEN], BF16)
                S.op(S.SP, lambda h: h.dma_start(out=d, in_=self.ycat[:, :, :].rearrange("p c s -> p (c s)")),
                     R=[b for cb in self.ycat_b for b in cb], stream=self.stq[0])
            self.run_phase(self.phase_final, l, x_src, x_dst, make_xT=(l < L - 1))
        S.finish(S.SP)


def _pack_win(w):
    w3 = w.reshape(NKC, 128, IN_COLS).transpose(1, 0, 2)
    groups = []
    groups.append(np.arange(O_FF, O_FF + 6))
    for h in range(6):
        groups.append(np.concatenate([np.arange(o + 64 * h, o + 64 * (h + 1)) for o in (O_FQ, O_FK, O_FV, O_FZ)]))
    groups.append(np.concatenate([np.arange(O_MI, O_MI + 4), np.arange(O_MF, O_MF + 4)]))
    for h in range(4):
        groups.append(np.concatenate([np.arange(o + 96 * h, o + 96 * (h + 1)) for o in (O_MQ, O_MK, O_MV, O_MO, O_MZ)]))
    groups.append(np.concatenate([np.arange(O_RQ, O_RQ + 256), np.arange(O_RZ, O_RZ + 256)]))
    parts = [np.ascontiguousarray(w3[:, :, gidx]).reshape(128, -1) for gidx in groups]
    return np.concatenate(parts, axis=1)


def win_offsets():
    sizes = [6] + [256] * 6 + [8] + [480] * 4 + [512]
    offs = np.concatenate([[0], np.cumsum([NKC * s for s in sizes])])
    return sizes, offs


def _pack_wout(w):
    out = np.zeros((128, 9, D), np.float32)
    r = 0
    for c, k in enumerate([128, 128, 128, 96, 96, 96, 96, 128, 128]):
        out[0:k, c, :] = w[r:r + k, :]
        r += k
    return out.reshape(128, 9 * D)


def _pack_wmem(w):
    return np.ascontiguousarray(w.reshape(NKC, 128, 512).transpose(1, 0, 2)).reshape(128, NKC * 512)


def pack_layers(inp, layers):
    f = np.float32
    d = {}
    d["wpk"] = np.stack([_pack_win(np.asarray(inp["w_in"][l], f)) for l in layers])
    d["woutpk"] = np.stack([_pack_wout(np.asarray(inp["w_out"][l], f)) for l in layers])
    d["wmempk"] = np.stack([_pack_wmem(np.asarray(inp["w_mem_kv"][l], f)) for l in layers])
    d["foxb"] = np.stack([np.asarray(inp["fox_f_bias"][l], f).reshape(6, 1) for l in layers])
    d["mib"] = np.stack([np.asarray(inp["mlstm_i_bias"][l], f).reshape(4, 1) for l in layers])
    d["mfb"] = np.stack([np.asarray(inp["mlstm_f_bias"][l], f).reshape(4, 1) for l in layers])
    d["convw"] = np.stack([np.ascontiguousarray(np.asarray(inp["mlstm_conv_w"][l], f).reshape(4, 2, 4, 96).transpose(3, 1, 2, 0)).reshape(96, 32)
                           for l in layers])
    d["convb"] = np.stack([np.ascontiguousarray(np.asarray(inp["mlstm_conv_b"][l], f).reshape(2, 4, 96).transpose(2, 0, 1)).reshape(96, 8)
                           for l in layers])
    d["ng"] = np.stack([np.ascontiguousarray(np.asarray(inp["mlstm_norm_g"][l], f).reshape(4, 96).T) for l in layers])
    d["lng"] = np.stack([np.asarray(inp["ln_g"][l], f).reshape(1, D) for l in layers])
    d["lnb"] = np.stack([np.asarray(inp["ln_b"][l], f).reshape(1, D) for l in layers])
    return d


_PROG_CACHE = {}


def get_prog(L, **kw):
    key = (L, tuple(sorted(kw.items())))
    if key not in _PROG_CACHE:
        _PROG_CACHE[key] = Prog(L, **kw)
    return _PROG_CACHE[key]


def run_layers(x, mem, packed, n_cores=8):
    L = packed["wpk"].shape[0]
    prog = get_prog(L)
    B = x.shape[0]
    in_maps = []
    for c in range(n_cores):
        b = c % B
        m = {"x": np.ascontiguousarray(x[b]), "mem": np.ascontiguousarray(mem[b])}
        m.update(packed)
        in_maps.append(m)
    res = run_bass_kernel_spmd(prog.nc, in_maps, core_ids=list(range(n_cores)))
    return np.stack([np.asarray(res.results[b]["out"]) for b in range(B)])


FUSED = True


def kernel(x, mem, w_in, fox_f_bias, mlstm_conv_w, mlstm_conv_b, mlstm_i_bias, mlstm_f_bias,
           mlstm_norm_g, w_mem_kv, w_out, ln_g, ln_b):
    inp = dict(w_in=w_in, fox_f_bias=fox_f_bias, mlstm_conv_w=mlstm_conv_w, mlstm_conv_b=mlstm_conv_b,
               mlstm_i_bias=mlstm_i_bias, mlstm_f_bias=mlstm_f_bias, mlstm_norm_g=mlstm_norm_g,
               w_mem_kv=w_mem_kv, w_out=w_out, ln_g=ln_g, ln_b=ln_b)
    x = np.asarray(x, np.float32)
    mem = np.asarray(mem, np.float32)
    if FUSED:
        return run_layers(x, mem, pack_layers(inp, list(range(DEPTH))))
    for l in range(DEPTH):
        x = run_layers(x, mem, pack_layers(inp, [l]))
    return x
```

```python
import math
from contextlib import ExitStack
import numpy as np
import concourse.bass as bass
import concourse.mybir as mybir
from concourse.bass_utils import run_bass_kernel_spmd

F32 = mybir.dt.float32
BF16 = mybir.dt.bfloat16
AF = mybir.ActivationFunctionType
ALU = mybir.AluOpType

D = 1024
S_LEN = 4096
DEPTH = 4
NKC = 8
G = 512
NG = S_LEN // G
NT = S_LEN // 128
MEM_LEN = 256
LN_EPS = 1e-5
ALPHA = (2.0 * DEPTH) ** 0.25
IN_COLS = 3982
O_FQ, O_FK, O_FV, O_FF, O_FZ = 0, 384, 768, 1152, 1158
O_MQ, O_MK, O_MV, O_MI, O_MF, O_MO, O_MZ = 1542, 1926, 2310, 2694, 2698, 2702, 3086
O_RQ, O_RZ = 3470, 3726


class Stream:
    def __init__(self, name, sem, inc, q, dma):
        self.name, self.sem, self.inc, self.q, self.dma = name, sem, inc, q, dma
        self.n = 0


class Q:
    def __init__(self, name, h):
        self.name, self.h = name, h
        self.seen = {}
        self.stream = None


class Buf:
    __slots__ = ("name", "w", "r")

    def __init__(self, name):
        self.name = name
        self.w = None
        self.r = {}


class Sched:
    def __init__(self, nc):
        self.nc = nc
        self.PE = self._mkq("pe", nc.tensor)
        self.ACT = self._mkq("act", nc.scalar)
        self.DVE = self._mkq("dve", nc.vector)
        self.POOL = self._mkq("pool", nc.gpsimd)
        self.SP = Q("sp", nc.sync)
        self.queues = [self.PE, self.ACT, self.DVE, self.POOL, self.SP]
        self.streams = [q.stream for q in self.queues if q.stream is not None]
        self.nwaits = 0
        self.nops = 0

    def _mkq(self, name, h):
        q = Q(name, h)
        q.stream = Stream(name, self.nc.alloc_semaphore("s_" + name), 1, q, False)
        return q

    def dma_stream(self, name):
        s = Stream(name, self.nc.alloc_semaphore("d_" + name), 16, None, True)
        self.streams.append(s)
        return s

    def _wait(self, q, s, c):
        if q.seen.get(s, 0) >= c:
            return
        assert c <= s.n, f"wait on unissued signal {s.name} {c} > {s.n}"
        q.h.wait_ge(s.sem, c * s.inc)
        q.seen[s] = c
        self.nwaits += 1

    def op(self, q, emit, R=(), W=(), X=(), sig=True, stream=None):
        st = stream if stream is not None else q.stream
        deps = {}

        def add(d, same_ok):
            s, c = d
            if same_ok and (not s.dma) and s.q is q and stream is None:
                return
            if deps.get(s, 0) < c:
                deps[s] = c

        pe = q is self.PE
        for b in R:
            if b.w is not None:
                add(b.w, pe)
        for b in W:
            if b.w is not None:
                add(b.w, pe)
            for s, c in b.r.items():
                add((s, c), pe)
        for b in X:
            if b.w is not None:
                add(b.w, pe)
            for s, c in b.r.items():
                add((s, c), pe)
        for s, c in deps.items():
            if s.dma:
                c = s.n
            self._wait(q, s, c)
        ins = emit(q.h)
        cnt = st.n + 1
        if sig:
            ins.then_inc(st.sem, st.inc)
            st.n = cnt
        for b in R:
            if b.r.get(st, 0) < cnt:
                b.r[st] = cnt
        for b in W:
            b.w = (st, cnt)
            b.r = {}
        for b in X:
            b.w = (st, cnt)
            b.r = {}
        self.nops += 1
        return ins

    def barrier(self):
        for q in self.queues:
            for s in self.streams:
                if s.n > 0 and not (s.q is q):
                    self._wait(q, s, s.n)

    def finish(self, q):
        for s in self.streams:
            if s.n > 0 and not (s.q is q):
                self._wait(q, s, s.n)


class Tile:
    def __init__(self, handle, name, stream=None):
        self.t = handle
        self.b = Buf(name)
        self.st = stream

    def __getitem__(self, k):
        return self.t[k]


class Prog:
    def __init__(self, L, stop_after=None, dbg=()):
        import os
        dbg = tuple(dbg) + tuple(x for x in os.environ.get("KDBG", "").split(",") if x)
        self.L = L
        self.stop_after = stop_after
        self.dbg = set(dbg)
        nc = bass.Bass("TRN2", target_bir_lowering=False)
        self.nc = nc
        self.S = Sched(nc)
        self.uid = 0
        self.stack = None
        self.stream_pool = []
        self.all_streams = []
        self.phase_streams = []
        self._decl_dram()
        self._alloc()
        self._consts()
        self._build()

    def tile(self, name, shape, dt=F32, dma=False):
        self.uid += 1
        nm = f"{name}_{self.uid}"
        if self.stack is not None:
            hnd = self.stack.enter_context(self.nc.sbuf_tensor(nm, list(shape), dt))
        else:
            hnd = self.nc.alloc_sbuf_tensor(nm, list(shape), dt)
        st = None
        if dma:
            if self.stream_pool:
                st = self.stream_pool.pop()
            else:
                st = self.S.dma_stream(f"ds{len(self.all_streams)}")
                self.all_streams.append(st)
            if self.stack is not None:
                self.phase_streams.append(st)
        return Tile(hnd, nm, st)

    def sub_arena(self):
        prog = self

        class _Sub:
            def __enter__(self_):
                self_.outer = prog.stack
                self_.es = ExitStack()
                self_.es.__enter__()
                prog.stack = self_.es
                return self_

            def __exit__(self_, *exc):
                prog.S.barrier()
                prog.stack = self_.outer
                return self_.es.__exit__(*exc)
        return _Sub()

    def run_phase(self, fn, *a, **kw):
        assert self.stack is None
        with ExitStack() as es:
            self.stack = es
            self.phase_streams = []
            fn(*a, **kw)
            self.S.barrier()
            self.stream_pool.extend(self.phase_streams)
            self.phase_streams = []
            self.stack = None

    def dram_in(self, name, shape, dt=F32):
        return self.nc.dram_tensor(name, list(shape), dt, kind="ExternalInput").ap()

    def _decl_dram(self):
        L = self.L
        nc = self.nc
        self.x_in = self.dram_in("x", [S_LEN, D])
        self.mem_in = self.dram_in("mem", [MEM_LEN, D])
        self.wpk = self.dram_in("wpk", [L, 128, NKC * IN_COLS])
        self.woutpk = self.dram_in("woutpk", [L, 128, 9 * D])
        self.wmempk = self.dram_in("wmempk", [L, 128, NKC * 512])
        self.foxb = self.dram_in("foxb", [L, 6, 1])
        self.mib = self.dram_in("mib", [L, 4, 1])
        self.mfb = self.dram_in("mfb", [L, 4, 1])
        self.convw = self.dram_in("convw", [L, 96, 32])
        self.convb = self.dram_in("convb", [L, 96, 8])
        self.ngd = self.dram_in("ng", [L, 96, 4])
        self.lng = self.dram_in("lng", [L, 1, D])
        self.lnb = self.dram_in("lnb", [L, 1, D])
        self.out = nc.dram_tensor("out", [S_LEN, D], F32, kind="ExternalOutput").ap()
        self.xbuf = [nc.dram_tensor(f"xbuf{i}", [S_LEN, D], F32).ap() for i in range(2)] if L > 1 else []
        self.dbg_out = {}
        self.gdram = nc.dram_tensor("gdram", [2, 4 * NG, G], F32).ap()
        self.gdram_b = [Buf(f"gdram_g{g}") for g in range(NG)]

    def dbg_dram(self, name, shape, dt=F32):
        ap = self.nc.dram_tensor(name, list(shape), dt, kind="ExternalOutput").ap()
        self.dbg_out[name] = ap
        return ap

    def _alloc(self):
        nc, S = self.nc, self.S
        self.pb = []
        for i in range(8):
            t = nc.alloc_psum_tensor(f"pb{i}", [128, 512], F32)
            self.pb.append((t, Buf(f"pb{i}")))
        self.xT = nc.alloc_sbuf_tensor("xT", [128, NKC, S_LEN], BF16)
        self.xT_b = [Buf(f"xT_g{g}") for g in range(NG)]
        self.ycat = nc.alloc_sbuf_tensor("ycat", [128, 9, S_LEN], BF16)
        self.ycat_b = [[Buf(f"ycat_{c}_{g}") for g in range(NG)] for c in range(9)]
        self.stq = [S.dma_stream("stq0"), S.dma_stream("stq1")]
        self.memT = self.tile("memT", [128, NKC, MEM_LEN], BF16)
        self.gtok_all = self.tile("gtok_all", [128, NT, 4])
        self.gtokS_all = self.tile("gtokS_all", [128, NT, 4])

    def _consts(self):
        S = self.S
        self.onesf = self.tile("onesf", [128, 128])
        self.zerf = self.tile("zerf", [128, 128])
        self.identf = self.tile("identf", [128, 128])
        self.identb = self.tile("identb", [128, 128], BF16)
        self.mnegf = self.tile("mnegf", [128, 128])
        self.mnegb = self.tile("mnegb", [128, 128], BF16)
        self.avg96 = self.tile("avg96", [128, 96])
        self.fillsrc = self.tile("fillsrc", [128, 512], BF16)
        S.op(S.POOL, lambda h: h.memset(self.fillsrc[:], 0.5), W=[self.fillsrc.b])
        o, z = self.onesf, self.zerf
        S.op(S.POOL, lambda h: h.memset(o[:], 1.0), W=[o.b])
        S.op(S.POOL, lambda h: h.memset(z[:], 0.0), W=[z.b])
        S.op(S.POOL, lambda h: h.memset(self.avg96[:], 1.0 / 96.0), W=[self.avg96.b])
        S.op(S.POOL, lambda h: h.affine_select(out=self.identf[:], in_=o[:], pattern=[[1, 128]],
                                               compare_op=ALU.is_equal, fill=0.0, base=0, channel_multiplier=-1),
             R=[o.b], W=[self.identf.b])
        S.op(S.DVE, lambda h: h.tensor_copy(out=self.identb[:], in_=self.identf[:]), R=[self.identf.b], W=[self.identb.b])
        S.op(S.POOL, lambda h: h.affine_select(out=self.mnegf[:], in_=z[:], pattern=[[1, 128]],
                                               compare_op=ALU.is_ge, fill=-30000.0, base=0, channel_multiplier=-1),
             R=[z.b], W=[self.mnegf.b])
        S.op(S.DVE, lambda h: h.tensor_copy(out=self.mnegb[:], in_=self.mnegf[:]), R=[self.mnegf.b], W=[self.mnegb.b])

    def mm(self, bank, out_ap, pairs, R, start=True, stop=True, sgc=False):
        S = self.S
        n = len(pairs)
        for i, (lhsT, rhs) in enumerate(pairs):
            S.op(S.PE, lambda h, lhsT=lhsT, rhs=rhs, i=i: h.matmul(
                out_ap, lhsT=lhsT, rhs=rhs, start=(start and i == 0), stop=(stop and i == n - 1), skip_group_check=sgc),
                R=R, X=[self.pb[bank][1]], sig=(i == n - 1))

    def filler(self, n=256, bank=7):
        S = self.S
        pt, pbuf = self.pb[bank]
        S.op(S.PE, lambda h: h.matmul(pt[:, 0:n], lhsT=self.identb[:], rhs=self.fillsrc[:, 0:n], start=True, stop=True),
             R=[self.identb.b, self.fillsrc.b], X=[pbuf], sig=False)

    def bankrot(self, banks):
        st = {"i": 0}

        def nxt():
            b = banks[st["i"] % len(banks)]
            st["i"] += 1
            return b
        return nxt

    def build_xT_from_dram(self, x_src):
        S = self.S
        xin = [self.tile("xin", [128, D], dma=True) for _ in range(2)]
        rot = self.bankrot([0, 1, 2, 3])
        for t in range(NT):
            xt = xin[t % 2]
            S.op(S.SP, lambda h, xt=xt, t=t: h.dma_start(out=xt[:], in_=x_src[t * 128:(t + 1) * 128, :]),
                 W=[xt.b], stream=xt.st)
            self.transpose_into_xT(xt, t, rot)

    def transpose_into_xT(self, src, t, rot, eng=None):
        S = self.S
        for half in range(2):
            bk = rot()
            pt, pbuf = self.pb[bk]
            for c in range(4):
                kc = half * 4 + c
                S.op(S.PE, lambda h, kc=kc, c=c, pt=pt: h.transpose(
                    pt[:, c * 128:(c + 1) * 128], src[:, kc * 128:(kc + 1) * 128], self.identf[:]),
                    R=[src.b, self.identf.b], X=[pbuf], sig=(c == 3))
            e = eng if eng is not None else (S.ACT if half == 0 else S.DVE)
            dst = self.xT[:, half * 4:(half + 1) * 4, t * 128:(t + 1) * 128]
            srcp = pt[:, :].rearrange("p (c t) -> p c t", c=4)
            if e is S.ACT:
                S.op(e, lambda h, dst=dst, srcp=srcp: h.activation(out=dst, in_=srcp, func=AF.Copy),
                     X=[pbuf], W=[self.xT_b[t // 4]])
            else:
                S.op(e, lambda h, dst=dst, srcp=srcp: h.tensor_copy(out=dst, in_=srcp),
                     X=[pbuf], W=[self.xT_b[t // 4]])

    def load_w(self, dst_tile, src_ap):
        S = self.S
        self.n_sw = getattr(self, "n_sw", 0) + 1
        st = S.dma_stream(f"sw{self.n_sw}")
        S.op(S.POOL, lambda h: h.dma_start(out=dst_tile[:], in_=src_ap), W=[dst_tile.b], stream=st)


    def phase_fox(self, l):
        S = self.S
        sizes, offs = win_offsets()
        rot = self.bankrot([0, 1])
        rot_s = self.bankrot([2, 3, 4])
        rot_o = self.bankrot([5, 6])
        shiftrows = self.tile("shiftrows", [6, S_LEN], BF16)
        negc_tok = self.tile("negc_tok", [128, NT, 6])
        negcref_rep = self.tile("negcref_rep", [128, 6, NG])
        with self.sub_arena():
            rotG = self.bankrot([2, 3, 4])
            gens = [self.fox_prep(l, offs, rot, shiftrows, negc_tok, negcref_rep),
                    self.mlstm_gates(l, offs, rotG, self.gtok_all, self.gtokS_all, self.gdram, self.gdram_b, float(96 ** -0.5))]
            alive = True
            while alive:
                alive = False
                for gn in gens:
                    if next(gn, "end") != "end":
                        alive = True
        wh = [self.tile("wh", [128, NKC * 256], BF16, dma=True) for _ in range(2)]
        KaT = self.tile("KaT", [65, S_LEN], BF16)
        KaT_b = [Buf(f"KaT_g{g}") for g in range(NG)]
        S.op(S.POOL, lambda h: h.memset(KaT[64:65, :], 1.0), W=KaT_b)
        Vaug = self.tile("Vaug", [128, NT, 192], BF16)
        V_ones = Buf("V_ones")
        V_b = [[Buf(f"V_{par}_g{g}") for g in range(NG)] for par in range(2)]
        S.op(S.POOL, lambda h: h.memset(Vaug[:, :, 64:128], 1.0), W=[V_ones])
        QaT = [self.tile("QaT", [65, G], BF16, dma=True) for _ in range(2)]
        szf = [self.tile("szf", [128, G], BF16) for _ in range(2)]
        PT = [self.tile("PTf", [128, G], BF16) for _ in range(3)]
        rd = [self.tile("rdf", [128, G]) for _ in range(1)]
        tn = [self.tile("tnf", [128, G]) for _ in range(1)]
        thf = [self.tile("thf", [128, G]) for _ in range(2)]
        tabs = [self.tile("tab", [128, NT, NG]) for _ in range(2)]
        ipc = {"i": 0}

        def build_tab(hd):
            tab = tabs[hd % 2]
            for qg in range(NG):
                S.op(S.DVE, lambda h, qg=qg, tab=tab, hd=hd: h.tensor_scalar(
                    out=tab[:, :, qg], in0=negc_tok[:, :, hd], scalar1=negcref_rep[:, hd, qg:qg + 1], scalar2=None, op0=ALU.subtract),
                    R=[negc_tok.b, negcref_rep.b], W=[tab.b])

        def gen_proj(hd, g):
            odd = hd % 2
            W_ = wh[hd % 2]
            Wv = W_[:, :].rearrange("p (c n) -> p c n", c=NKC)
            gsl = slice(g * G, (g + 1) * G)
            qa, sz = QaT[g % 2], szf[g % 2]
            vc0 = 128 if odd else 0
            RR = [W_.b, self.xT_b[g]]
            bk = rot()
            pt, pbuf = self.pb[bk]
            for kc in range(NKC):
                S.op(S.PE, lambda h, kc=kc, pt=pt: h.matmul(pt[:, :], lhsT=Wv[:, kc, 0:128], rhs=self.xT[:, kc, gsl],
                                                            start=(kc == 0), stop=(kc == NKC - 1)), R=RR, X=[pbuf], sig=(kc == NKC - 1))
                if kc % 2 == 1 and kc < NKC - 1:
                    yield
            S.op(S.DVE, lambda h, pt=pt: h.tensor_scalar(out=qa[0:64, :], in0=pt[0:64, :], scalar1=0.125, scalar2=None, op0=ALU.mult), X=[pbuf], W=[qa.b])
            S.op(S.DVE, lambda h, pt=pt: h.tensor_copy(out=KaT[0:64, gsl], in_=pt[64:128, :]), X=[pbuf], W=[KaT_b[g]])
            S.op(S.SP, lambda h: h.dma_start(out=qa[64:65, :], in_=shiftrows[hd:hd + 1, gsl]), R=[shiftrows.b], W=[qa.b], stream=qa.st)
            yield
            bk = rot()
            pt, pbuf = self.pb[bk]
            for j in range(4):
                blk = slice(g * G + j * 128, g * G + (j + 1) * 128)
                for kc in range(NKC):
                    S.op(S.PE, lambda h, kc=kc, pt=pt, j=j, blk=blk: h.matmul(
                        pt[:, j * 64:(j + 1) * 64], lhsT=self.xT[:, kc, blk], rhs=Wv[:, kc, 128:192],
                        start=(kc == 0), stop=(kc == NKC - 1)), R=RR, X=[pbuf], sig=(kc == NKC - 1))
                    if kc == 3:
                        yield
                yield
            S.op(S.DVE, lambda h, pt=pt: h.tensor_copy(
                out=Vaug[:, 4 * g:4 * g + 4, vc0:vc0 + 64], in_=pt[:, 0:256].rearrange("p (j c) -> p j c", j=4)),
                X=[pbuf], W=[V_b[odd][g]])
            yield
            bk = rot()
            pt, pbuf = self.pb[bk]
            zr = slice(64, 128) if odd else slice(0, 64)
            for kc in range(NKC):
                if odd:
                    S.op(S.PE, lambda h, kc=kc, pt=pt: h.matmul(pt[:, :], lhsT=Wv[:, kc, 128:256], rhs=self.xT[:, kc, gsl],
                                                                start=(kc == 0), stop=(kc == NKC - 1)), R=RR, X=[pbuf], sig=(kc == NKC - 1))
                else:
                    S.op(S.PE, lambda h, kc=kc, pt=pt: h.matmul(pt[0:64, :], lhsT=Wv[:, kc, 192:256], rhs=self.xT[:, kc, gsl],
                                                                start=(kc == 0), stop=(kc == NKC - 1)), R=RR, X=[pbuf], sig=(kc == NKC - 1))
                if kc % 2 == 1 and kc < NKC - 1:
                    yield
            th = thf[g % 2]
            S.op(S.ACT, lambda h, pt=pt: h.activation(out=th[zr, :], in_=pt[zr, :], func=AF.Tanh, scale=0.5), X=[pbuf], W=[th.b])
            S.op(S.DVE, lambda h, pt=pt: h.scalar_tensor_tensor(out=sz[zr, :], in0=th[zr, :], scalar=1.0, in1=pt[zr, :], op0=ALU.add, op1=ALU.mult),
                 X=[pbuf], R=[th.b], W=[sz.b])
            yield

        N_CHUNKS = 20

        def attention(hd, g, chunks):
            odd = hd % 2
            lc0 = 64 if odd else 0
            tab = tabs[hd % 2]
            qa, sz = QaT[g % 2], szf[g % 2]
            nkb = 4 * g + 4
            emitted = {"n": 0}
            bo = rot_o()
            po, pobuf = self.pb[bo]

            def issue_st(kb):
                diag = kb >= 4 * g
                qoff = (kb - 4 * g) * 128 if diag else 0
                n = G - qoff
                bs = rot_s()
                ps_, psbuf = self.pb[bs]
                self.mm(bs, ps_[:, 0:n], [(KaT[0:65, kb * 128:(kb + 1) * 128], qa[0:65, qoff:G])], R=[KaT_b[kb // 4], qa.b],
                        start=True, stop=not diag)
                if diag:
                    self.mm(bs, ps_[:, 0:128], [(self.identb[:], self.mnegb[:])], R=[self.identb.b, self.mnegb.b], start=False, stop=True)
                return ps_, psbuf, qoff, n

            nxt = issue_st(0)
            for kb in range(nkb):
                cur = nxt
                if kb + 1 < nkb:
                    nxt = issue_st(kb + 1)
                ps_, psbuf, qoff, n = cur
                pt_ = PT[ipc["i"] % 3]
                ipc["i"] += 1
                S.op(S.ACT, lambda h, ps_=ps_, pt_=pt_, n=n, kb=kb: h.activation(
                    out=pt_[:, 0:n], in_=ps_[:, 0:n], func=AF.Exp, bias=tab[:, kb, g:g + 1]), X=[psbuf], R=[tab.b], W=[pt_.b])
                did = False
                if chunks is not None:
                    want = ((kb + 1) * N_CHUNKS + nkb - 1) // nkb
                    while emitted["n"] < want:
                        if next(chunks, "end") != "end":
                            did = True
                        emitted["n"] += 1
                if not did:
                    self.filler(256)
                self.mm(bo, po[:, qoff:G], [(Vaug[:, kb, lc0:lc0 + 128], pt_[:, 0:n])], R=[V_b[odd][kb // 4], V_ones, pt_.b],
                        start=(kb == 0), stop=(kb == nkb - 1))
            if chunks is not None:
                for _ in chunks:
                    pass
            self.attn_epilogue(po, pobuf, bool(odd), sz, rd[0], tn[0], hd // 2, g, half=True)

        self.load_w(wh[0], self.wpk[l][:, int(offs[1]):int(offs[2])])
        build_tab(0)
        for _ in gen_proj(0, 0):
            pass
        for hd in range(6):
            if hd + 1 < 6:
                self.load_w(wh[(hd + 1) % 2], self.wpk[l][:, int(offs[2 + hd]):int(offs[3 + hd])])
                build_tab(hd + 1)
            for g in range(NG):
                if g + 1 < NG:
                    chunks = gen_proj(hd, g + 1)
                elif hd + 1 < 6:
                    chunks = gen_proj(hd + 1, 0)
                else:
                    chunks = None
                attention(hd, g, chunks)


    def fox_prep(self, l, offs, rot, shiftrows, negc_tok, negcref_rep):
        S = self.S
        wff = self.tile("wff", [128, NKC * 6], BF16, dma=True)
        self.load_w(wff, self.wpk[l][:, int(offs[0]):int(offs[1])])
        wffv = wff[:, :].rearrange("p (c n) -> p c n", c=NKC)
        fb = self.tile("fb", [6, 1], dma=True)
        S.op(S.SP, lambda h: h.dma_start(out=fb[:], in_=self.foxb[l]), W=[fb.b], stream=fb.st)
        nfb = self.tile("nfb", [6, 1])
        S.op(S.DVE, lambda h: h.tensor_scalar(out=nfb[:], in0=fb[:], scalar1=-1.0, scalar2=None, op0=ALU.mult), R=[fb.b], W=[nfb.b])
        negcref6 = self.tile("negcref6", [6, NG])
        oh6 = self.tile("oh6", [6, 6, 128])
        for hh in range(6):
            S.op(S.POOL, lambda h, hh=hh: h.affine_select(
                out=oh6[:, hh, :], in_=self.onesf[0:6, :], pattern=[[0, 128]], compare_op=ALU.is_equal,
                fill=0.0, base=-hh, channel_multiplier=1), R=[self.onesf.b], W=[oh6.b])
        e_t = [self.tile("e_t", [6, G]) for _ in range(2)]
        negc = [self.tile("negc", [6, G]) for _ in range(2)]
        for g in range(NG):
            gsl = slice(g * G, (g + 1) * G)
            bk = rot()
            pt, pbuf = self.pb[bk]
            self.mm(bk, pt[0:6, :], [(wffv[:, kc, :], self.xT[:, kc, gsl]) for kc in range(NKC)], R=[wff.b, self.xT_b[g]])
            e, nc_, ncp = e_t[g % 2], negc[g % 2], negc[(g - 1) % 2]
            lt = e
            S.op(S.ACT, lambda h, pt=pt, e=e: h.activation(out=e[:], in_=pt[0:6, :], func=AF.Exp, scale=-1.0, bias=nfb[:, 0:1]),
                 X=[pbuf], R=[nfb.b], W=[e.b])
            S.op(S.ACT, lambda h, e=e, lt=lt: h.activation(out=lt[:], in_=e[:], func=AF.Ln, bias=1.0), R=[e.b], W=[lt.b])
            init = 0.0 if g == 0 else ncp[:, G - 1:G]
            S.op(S.DVE, lambda h, lt=lt, nc_=nc_, init=init: h.tensor_tensor_scan(
                out=nc_[:], data0=self.onesf[0:6, 0:1].to_broadcast([6, G]), data1=lt[:], initial=init, op0=ALU.mult, op1=ALU.add),
                R=[lt.b, self.onesf.b] + ([ncp.b] if g > 0 else []), W=[nc_.b])
            S.op(S.DVE, lambda h, nc_=nc_, g=g: h.tensor_copy(out=negcref6[:, g:g + 1], in_=nc_[:, 0:1]), R=[nc_.b], W=[negcref6.b])
            S.op(S.DVE, lambda h, nc_=nc_, gsl=gsl: h.tensor_scalar(out=shiftrows[:, gsl], in0=nc_[:], scalar1=nc_[:, 0:1], scalar2=-1.0,
                                                                 op0=ALU.subtract, op1=ALU.mult), R=[nc_.b], W=[shiftrows.b])
            bk = rot()
            pt, pbuf = self.pb[bk]
            for j in range(4):
                S.op(S.PE, lambda h, pt=pt, j=j, nc_=nc_: h.transpose(pt[:, j * 6:(j + 1) * 6], nc_[0:6, j * 128:(j + 1) * 128], self.identf[0:6, 0:6]),
                     R=[nc_.b, self.identf.b], X=[pbuf], sig=(j == 3))
            S.op(S.DVE, lambda h, pt=pt, g=g: h.tensor_copy(out=negc_tok[:, 4 * g:4 * g + 4, :], in_=pt[:, 0:24].rearrange("p (j c) -> p j c", j=4)),
                 X=[pbuf], W=[negc_tok.b])
            yield
        bk = rot()
        pt, pbuf = self.pb[bk]
        for hh in range(6):
            self.mm(bk, pt[:, hh * NG:(hh + 1) * NG], [(oh6[:, hh, :], negcref6[:, :])], R=[oh6.b, negcref6.b])
        S.op(S.DVE, lambda h: h.tensor_copy(out=negcref_rep[:, :, :], in_=pt[:, 0:6 * NG].rearrange("p (a b) -> p a b", a=6)),
             X=[pbuf], W=[negcref_rep.b])


    def mlstm_gates(self, l, offs, rot, gtok_all, gtokS_all, gdram, gdram_b, sK):
        S = self.S
        wmif = self.tile("wmif", [128, NKC * 8], BF16, dma=True)
        self.load_w(wmif, self.wpk[l][:, int(offs[7]):int(offs[8])])
        wmifv = wmif[:, :].rearrange("p (c n) -> p c n", c=NKC)
        ib = self.tile("ib", [4, 1], dma=True)
        fbm = self.tile("fbm", [4, 1], dma=True)
        for (t_, src) in ((ib, self.mib[l]), (fbm, self.mfb[l])):
            S.op(S.SP, lambda h, t_=t_, src=src: h.dma_start(out=t_[:], in_=src), W=[t_.b], stream=t_.st)
        nfbm = self.tile("nfbm", [4, 1])
        S.op(S.DVE, lambda h: h.tensor_scalar(out=nfbm[:], in0=fbm[:], scalar1=-1.0, scalar2=None, op0=ALU.mult), R=[fbm.b], W=[nfbm.b])
        e_t = [self.tile("me", [4, G]) for _ in range(2)]
        negF = [self.tile("negF", [4, G]) for _ in range(2)]
        gg = [self.tile("gg", [4, G]) for _ in range(2)]
        Gc = [self.tile("Gc", [4, G], dma=True) for _ in range(2)]
        nM = [self.tile("nM", [4, G], dma=True) for _ in range(2)]
        onesb = self.onesf[0:4, 0:1].to_broadcast([4, G])
        gview = [gdram[k].rearrange("(h g) t -> h g t", g=NG) for k in range(2)]
        for g in range(NG):
            gsl = slice(g * G, (g + 1) * G)
            xb = self.xT_b[g]
            e_, nF, g_, G_, nM_ = e_t[g % 2], negF[g % 2], gg[g % 2], Gc[g % 2], nM[g % 2]
            nFp, Gp = negF[(g - 1) % 2], Gc[(g - 1) % 2]
            bI = rot()
            pI, pIb = self.pb[bI]
            self.mm(bI, pI[0:4, :], [(wmifv[:, kc, 0:4], self.xT[:, kc, gsl]) for kc in range(NKC)], R=[wmif.b, xb])
            bF = rot()
            pF, pFb = self.pb[bF]
            self.mm(bF, pF[0:4, :], [(wmifv[:, kc, 4:8], self.xT[:, kc, gsl]) for kc in range(NKC)], R=[wmif.b, xb])
            S.op(S.ACT, lambda h, pF=pF, e_=e_: h.activation(out=e_[:], in_=pF[0:4, :], func=AF.Exp, scale=-1.0, bias=nfbm[:, 0:1]),
                 X=[pFb], R=[nfbm.b], W=[e_.b])
            S.op(S.ACT, lambda h, e_=e_: h.activation(out=e_[:], in_=e_[:], func=AF.Ln, bias=1.0), R=[e_.b], W=[e_.b])
            initF = 0.0 if g == 0 else nFp[:, G - 1:G]
            S.op(S.DVE, lambda h, initF=initF, nF=nF, e_=e_: h.tensor_tensor_scan(out=nF[:], data0=onesb, data1=e_[:], initial=initF,
                                                                                op0=ALU.mult, op1=ALU.add),
                 R=[e_.b, self.onesf.b] + ([nFp.b] if g > 0 else []), W=[nF.b])
            S.op(S.DVE, lambda h, pI=pI, g_=g_, nF=nF: h.scalar_tensor_tensor(out=g_[:], in0=pI[0:4, :], scalar=ib[:, 0:1], in1=nF[:],
                                                                             op0=ALU.add, op1=ALU.add), X=[pIb], R=[ib.b, nF.b], W=[g_.b])
            initG = 0.0 if g == 0 else Gp[:, G - 1:G]
            S.op(S.DVE, lambda h, initG=initG, G_=G_, g_=g_: h.tensor_tensor_scan(out=G_[:], data0=onesb, data1=g_[:], initial=initG,
                                                                                op0=ALU.mult, op1=ALU.max),
                 R=[g_.b, self.onesf.b] + ([Gp.b] if g > 0 else []), W=[G_.b])
            S.op(S.DVE, lambda h, nM_=nM_, nF=nF, G_=G_: h.tensor_tensor(out=nM_[:], in0=nF[:], in1=G_[:], op=ALU.subtract),
                 R=[nF.b, G_.b], W=[nM_.b])
            S.op(S.SP, lambda h, G_=G_, g=g: h.dma_start(out=gview[0][:, g, :], in_=G_[:]), R=[G_.b], W=[gdram_b[g]], stream=G_.st)
            S.op(S.SP, lambda h, nM_=nM_, g=g: h.dma_start(out=gview[1][:, g, :], in_=nM_[:]), R=[nM_.b], W=[gdram_b[g]], stream=nM_.st)
            bk = rot()
            pt, pbuf = self.pb[bk]
            for j in range(4):
                S.op(S.PE, lambda h, pt=pt, j=j, g_=g_: h.transpose(pt[:, j * 4:(j + 1) * 4], g_[0:4, j * 128:(j + 1) * 128], self.identf[0:4, 0:4]),
                     R=[g_.b, self.identf.b], X=[pbuf], sig=(j == 3))
            S.op(S.DVE, lambda h, pt=pt, g=g: h.tensor_copy(out=gtok_all[:, 4 * g:4 * g + 4, :], in_=pt[:, 0:16].rearrange("p (j c) -> p j c", j=4)),
                 X=[pbuf], W=[gtok_all.b])
            yield
        S.op(S.DVE, lambda h: h.tensor_scalar(out=gtokS_all[:, :, :], in0=gtok_all[:, :, :], scalar1=float(math.log(sK)), scalar2=None, op0=ALU.add),
             R=[gtok_all.b], W=[gtokS_all.b])

    def phase_mlstm(self, l):
        S = self.S
        sizes, offs = win_offsets()
        sK = float(96 ** -0.5)
        rot = self.bankrot([0, 1, 2])
        B_S, B_T, B_N, B_D = 4, 5, 6, 7
        gtok_all, gtokS_all = self.gtok_all, self.gtokS_all
        gdram, gdram_b = self.gdram, self.gdram_b
        cw = self.tile("cw", [96, 32], dma=True)
        cb = self.tile("cb", [96, 8], dma=True)
        ngt = self.tile("ngt", [96, 4], dma=True)
        for (t_, src) in ((cw, self.convw[l]), (cb, self.convb[l]), (ngt, self.ngd[l])):
            S.op(S.SP, lambda h, t_=t_, src=src: h.dma_start(out=t_[:], in_=src), W=[t_.b], stream=t_.st)
        lnhalf = self.tile("lnhalf", [128, 1])
        epsb = self.tile("epsb", [128, 1])
        S.op(S.POOL, lambda h: h.memset(lnhalf[:], float(math.log(0.5))), W=[lnhalf.b])
        S.op(S.POOL, lambda h: h.memset(epsb[:], float(LN_EPS)), W=[epsb.b])
        wml = [self.tile("wml", [128, NKC * 480], BF16, dma=True) for _ in range(2)]
        Grep = [self.tile("Grep", [128, G], dma=True) for _ in range(2)]
        clamp = [self.tile("clamp", [96, G], dma=True) for _ in range(2)]
        mu_chain = [self.tile("mu_chain", [128, 8]) for _ in range(2)]
        wexp = [self.tile("wexp", [128, 4]) for _ in range(2)]
        carry = [self.tile("carry", [128, 4]) for _ in range(2)]
        warg = self.tile("warg", [128, 4])
        carg = self.tile("carg", [128, 4])
        qpre = self.tile("qpre", [96, G + 3])
        kpre = self.tile("kpre", [96, G + 3])
        QT = [self.tile("QT", [96, G]) for _ in range(2)]
        KT = [self.tile("KT", [96, G]) for _ in range(2)]
        Vtok = [self.tile("Vtok", [128, 4, 192], BF16) for _ in range(2)]
        Vones = [Buf("Vtok_ones0"), Buf("Vtok_ones1")]
        for k in range(2):
            S.op(S.POOL, lambda h, k=k: h.memset(Vtok[k][:, :, 96:192], 1.0), W=[Vones[k]])
        sgo = [self.tile("sgo", [96, G]) for _ in range(2)]
        szm = [self.tile("szm", [96, G], BF16) for _ in range(2)]
        argDT = self.tile("argDT", [128, G])
        DTb = [Buf(f"DTb{j}") for j in range(4)]
        decb = [Buf(f"decb{j}") for j in range(4)]
        AT4 = self.tile("AT4", [128, G], BF16)
        decQ = self.tile("decQ", [96, G])
        decQb = self.tile("decQb", [96, G], BF16)
        Stb = self.tile("Stb", [96, 192], BF16)
        tBb = self.tile("tBb", [96, G], BF16)
        tAb = self.tile("tAb", [96, G], BF16)
        avg96b = self.tile("avg96b", [96, 96], BF16)
        S.op(S.DVE, lambda h: h.tensor_copy(out=avg96b[:], in_=self.avg96[0:96, :]), R=[self.avg96.b], W=[avg96b.b])
        Khat4 = self.tile("Khat4", [128, 4, 96], BF16)
        St = self.tile("St", [96, 192])
        tA = self.tile("tA", [96, G])
        tB = self.tile("tB", [96, G])
        tC = self.tile("tC", [96, G])
        cnt = {"b": 0}
        iters = [(hd, g) for hd in range(4) for g in range(NG)]

        def gen_A(i):
            hd, g = iters[i]
            k = i % 2
            gsl = slice(g * G, (g + 1) * G)
            xb = self.xT_b[g]
            W_ = wml[hd % 2]
            Wv = W_[:, :].rearrange("p (c n) -> p c n", c=NKC)
            RR = [W_.b, xb]
            Gr, cl, mu, we, ca = Grep[k], clamp[k], mu_chain[k], wexp[k], carry[k]
            mup = mu_chain[1 - k]
            row = hd * NG + g
            S.op(S.SP, lambda h: h.dma_start(out=Gr[:], in_=gdram[0][row:row + 1, :].partition_broadcast(128)),
                 R=[gdram_b[g]], W=[Gr.b], stream=Gr.st)
            S.op(S.SP, lambda h: h.dma_start(out=cl[:], in_=gdram[1][row:row + 1, :].partition_broadcast(96)),
                 R=[gdram_b[g]], W=[cl.b], stream=cl.st)
            S.op(S.ACT, lambda h: h.activation(out=cl[:], in_=cl[:], func=AF.Exp), R=[cl.b], W=[cl.b])
            if g == 0:
                S.op(S.POOL, lambda h: h.memset(mu[:, 0:1], 0.0), W=[mu.b])
                S.op(S.POOL, lambda h: h.memset(qpre[:, 0:3], 0.0), W=[qpre.b])
                S.op(S.POOL, lambda h: h.memset(kpre[:, 0:3], 0.0), W=[kpre.b])
            else:
                S.op(S.DVE, lambda h: h.tensor_copy(out=mu[:, 0:1], in_=mup[:, 4:5]), R=[mup.b], W=[mu.b])
            S.op(S.DVE, lambda h: h.tensor_copy(out=mu[:, 1:5], in_=Gr[:, :].rearrange("p (j t) -> p j t", j=4)[:, :, 127]),
                 R=[Gr.b], W=[mu.b])
            S.op(S.DVE, lambda h: h.tensor_tensor(out=warg[:], in0=gtok_all[:, 4 * g:4 * g + 4, hd], in1=mu[:, 1:5], op=ALU.subtract),
                 R=[gtok_all.b, mu.b], W=[warg.b])
            S.op(S.ACT, lambda h: h.activation(out=we[:], in_=warg[:], func=AF.Exp), R=[warg.b], W=[we.b])
            S.op(S.DVE, lambda h: h.tensor_tensor(out=carg[:], in0=mu[:, 0:4], in1=mu[:, 1:5], op=ALU.subtract), R=[mu.b], W=[carg.b])
            S.op(S.ACT, lambda h: h.activation(out=ca[:], in_=carg[:], func=AF.Exp), R=[carg.b], W=[ca.b])
            yield
            for (pre, acc, c0, qk) in ((qpre, QT[k], 0, 0), (kpre, KT[k], 96, 1)):
                if g > 0:
                    S.op(S.DVE, lambda h, pre=pre: h.tensor_copy(out=pre[:, 0:3], in_=pre[:, G:G + 3]), R=[pre.b], W=[pre.b])
                bk = rot()
                pt, pbuf = self.pb[bk]
                for kc in range(NKC):
                    S.op(S.PE, lambda h, kc=kc, pt=pt, c0=c0: h.matmul(pt[0:96, :], lhsT=Wv[:, kc, c0:c0 + 96], rhs=self.xT[:, kc, gsl],
                                                                      start=(kc == 0), stop=(kc == NKC - 1)), R=RR, X=[pbuf], sig=(kc == NKC - 1))
                    if kc % 2 == 1 and kc < NKC - 1:
                        yield
                S.op(S.ACT, lambda h, pt=pt, pre=pre: h.activation(out=pre[:, 3:G + 3], in_=pt[0:96, :], func=AF.Copy), X=[pbuf], W=[pre.b])
                yield
                wi = lambda tap, qk=qk: cw[:, (qk * 4 + hd) * 4 + tap:(qk * 4 + hd) * 4 + tap + 1]
                bi = cb[:, qk * 4 + hd:qk * 4 + hd + 1]
                S.op(S.DVE, lambda h, pre=pre, acc=acc, wi=wi, bi=bi: h.tensor_scalar(
                    out=acc[:], in0=pre[:, 3:G + 3], scalar1=wi(3), scalar2=bi, op0=ALU.mult, op1=ALU.add),
                    R=[pre.b, cw.b, cb.b], W=[acc.b])
                for kk in (1, 2, 3):
                    S.op(S.DVE, lambda h, pre=pre, acc=acc, wi=wi, kk=kk: h.scalar_tensor_tensor(
                        out=acc[:], in0=pre[:, 3 - kk:G + 3 - kk], scalar=wi(3 - kk), in1=acc[:], op0=ALU.mult, op1=ALU.add),
                        R=[pre.b, cw.b, acc.b], W=[acc.b])
                    if kk == 2:
                        yield
                yield
            bk = rot()
            pt, pbuf = self.pb[bk]
            Vt = Vtok[k]
            for j in range(4):
                blk = slice(g * G + j * 128, g * G + (j + 1) * 128)
                for kc in range(NKC):
                    S.op(S.PE, lambda h, kc=kc, pt=pt, j=j, blk=blk: h.matmul(
                        pt[:, j * 96:(j + 1) * 96], lhsT=self.xT[:, kc, blk], rhs=Wv[:, kc, 192:288],
                        start=(kc == 0), stop=(kc == NKC - 1)), R=RR, X=[pbuf], sig=(kc == NKC - 1))
                yield
            S.op(S.DVE, lambda h, pt=pt, Vt=Vt: h.tensor_copy(out=Vt[:, :, 0:96], in_=pt[:, 0:384].rearrange("p (j c) -> p j c", j=4)),
                 X=[pbuf], W=[Vt.b])
            yield
            held = []
            for c0 in (288, 384):
                bk = rot()
                pt, pbuf = self.pb[bk]
                for kc in range(NKC):
                    S.op(S.PE, lambda h, kc=kc, pt=pt, c0=c0: h.matmul(pt[0:96, :], lhsT=Wv[:, kc, c0:c0 + 96], rhs=self.xT[:, kc, gsl],
                                                                      start=(kc == 0), stop=(kc == NKC - 1)), R=RR, X=[pbuf], sig=(kc == NKC - 1))
                held.append((pt, pbuf))
            S.op(S.ACT, lambda h: h.activation(out=QT[k][:], in_=QT[k][:], func=AF.Silu), R=[QT[k].b], W=[QT[k].b])
            S.op(S.ACT, lambda h: h.activation(out=KT[k][:], in_=KT[k][:], func=AF.Silu), R=[KT[k].b], W=[KT[k].b])
            (pto, pbo), (ptz, pbz) = held
            S.op(S.ACT, lambda h: h.activation(out=sgo[k][:], in_=pto[0:96, :], func=AF.Tanh, scale=0.5), X=[pbo], W=[sgo[k].b])
            S.op(S.ACT, lambda h: h.activation(out=szm[k][:], in_=ptz[0:96, :], func=AF.Silu), X=[pbz], W=[szm[k].b])
            yield

        N_A = 30

        def run_BC(i, chunks):
            hd, g = iters[i]
            k = i % 2
            gsl = slice(g * G, (g + 1) * G)
            Gr, cl, mu, we, ca = Grep[k], clamp[k], mu_chain[k], wexp[k], carry[k]
            Q_, K_, Vt, Vo = QT[k], KT[k], Vtok[k], Vones[k]
            sg, sz = sgo[k], szm[k]
            pS, pSb = self.pb[B_S]
            pT, pTb = self.pb[B_T]
            pN, pNb = self.pb[B_N]
            pD, pDb = self.pb[B_D]

            fstate = {"pt": 0, "em": 0}
            N_PTS = 13

            def fill(n=1):
                did = False
                fstate["pt"] += 1
                if chunks is not None:
                    want = (fstate["pt"] * N_A + N_PTS - 1) // N_PTS
                    while fstate["em"] < want:
                        if next(chunks, "end") != "end":
                            did = True
                        fstate["em"] += 1
                if not did:
                    self.filler(512, bank=3)

            if g == 0:
                S.op(S.POOL, lambda h: h.memset(St[:], 0.0), W=[St.b])
                S.op(S.POOL, lambda h: h.memset(Stb[:], 0.0), W=[Stb.b])
            for j in range(4):
                bs = slice(j * 128, (j + 1) * 128)
                self.mm(B_S, pS[:, bs], [(K_[0:96, bs], Q_[0:96, bs])], R=[K_.b, Q_.b])
            S.op(S.DVE, lambda h: h.scalar_tensor_tensor(
                out=argDT[:, :].rearrange("p (j t) -> p j t", j=4), in0=Gr[:, :].rearrange("p (j t) -> p j t", j=4), scalar=-1.0,
                in1=self.mnegf[:, :].unsqueeze(1).to_broadcast([128, 4, 128]), op0=ALU.mult, op1=ALU.add),
                R=[Gr.b, self.mnegf.b], W=[argDT.b] + DTb)
            for j in range(4):
                bs = slice(j * 128, (j + 1) * 128)
                S.op(S.ACT, lambda h, bs=bs, j=j: h.activation(out=argDT[:, bs], in_=argDT[:, bs], func=AF.Exp,
                                                              bias=gtokS_all[:, 4 * g + j, hd:hd + 1]), R=[argDT.b, gtokS_all.b], W=[DTb[j]])
            for j in range(4):
                bs = slice(j * 128, (j + 1) * 128)
                S.op(S.ACT, lambda h, bs=bs, j=j: h.activation(out=decQ[:, bs], in_=Gr[0:96, bs], func=AF.Exp, scale=-1.0,
                                                              bias=mu[0:96, j:j + 1]), R=[Gr.b, mu.b], W=[decb[j]])
            fill()
            S.op(S.DVE, lambda h: h.tensor_tensor(out=AT4[:], in0=pS[:, :], in1=argDT[:], op=ALU.mult), X=[pSb], R=[argDT.b] + DTb, W=[AT4.b])
            S.op(S.POOL, lambda h: h.tensor_tensor(out=decQb[:], in0=Q_[:], in1=decQ[:], op=ALU.mult), R=[Q_.b, decQ.b] + decb, W=[decQb.b])
            for j in range(4):
                bs = slice(j * 128, (j + 1) * 128)
                S.op(S.PE, lambda h, bs=bs, j=j: h.transpose(pT[:, j * 96:(j + 1) * 96], K_[0:96, bs], self.identf[0:96, 0:96]),
                     R=[K_.b, self.identf.b], X=[pTb], sig=(j == 3))
            S.op(S.DVE, lambda h: h.scalar_tensor_tensor(
                out=Khat4[:, :, :], in0=pT[:, 0:384].rearrange("p (j c) -> p j c", j=4), scalar=sK,
                in1=we[:, 0:4].unsqueeze(2).to_broadcast([128, 4, 96]), op0=ALU.mult, op1=ALU.mult),
                X=[pTb], R=[we.b], W=[Khat4.b])
            fill()
            for j in range(4):
                bs = slice(j * 128, (j + 1) * 128)
                self.mm(B_N, pN[0:96, bs], [(Vt[:, j, 0:96], AT4[:, bs])], R=[Vt.b, AT4.b], start=(j == 0), stop=False, sgc=True)
            for j in range(4):
                bs = slice(j * 128, (j + 1) * 128)
                self.mm(B_D, pD[0:96, bs], [(Vt[:, j, 96:192], AT4[:, bs])], R=[Vo, AT4.b], start=(j == 0), stop=False, sgc=True)
            for j in range(4):
                ub, ubuf, uo = (pT, pTb, B_T) if j < 2 else (pS, pSb, B_S)
                c0 = (j % 2) * 192
                self.mm(uo, ub[0:96, c0:c0 + 192], [(Khat4[:, j, :], Vt[:, j, :])], R=[Khat4.b, Vt.b, Vo])
            fill()
            for j in range(4):
                bs = slice(j * 128, (j + 1) * 128)
                ub, ubuf = (pT, pTb) if j < 2 else (pS, pSb)
                c0 = (j % 2) * 192
                self.mm(B_N, pN[0:96, bs], [(Stb[0:96, 0:96], decQb[:, bs])], R=[Stb.b, decQb.b], start=False, stop=True, sgc=True)
                self.mm(B_D, pD[0:96, bs], [(Stb[0:96, 96:192], decQb[:, bs])], R=[Stb.b, decQb.b], start=False, stop=True, sgc=True)
                S.op(S.DVE, lambda h, j=j, ub=ub, c0=c0: h.scalar_tensor_tensor(out=St[:], in0=St[:], scalar=ca[0:96, j:j + 1],
                                                                               in1=ub[0:96, c0:c0 + 192], op0=ALU.mult, op1=ALU.add),
                     X=[ubuf], R=[ca.b, St.b], W=[St.b])
                S.op(S.DVE, lambda h: h.tensor_copy(out=Stb[:], in_=St[:]), R=[St.b], W=[Stb.b])
                if j % 2 == 1:
                    fill()
            S.op(S.DVE, lambda h: h.tensor_tensor(out=tA[:], in0=pD[0:96, :], in1=cl[:], op=ALU.max), X=[pDb], R=[cl.b], W=[tA.b])
            S.op(S.DVE, lambda h: h.scalar_tensor_tensor(out=tA[:], in0=pD[0:96, :], scalar=-1.0, in1=tA[:], op0=ALU.mult, op1=ALU.max),
                 X=[pDb], R=[tA.b], W=[tA.b])
            fill()
            S.op(S.ACT, lambda h: h.activation(out=tA[:], in_=tA[:], func=AF.Ln), R=[tA.b], W=[tA.b])
            S.op(S.ACT, lambda h: h.activation(out=tA[:], in_=tA[:], func=AF.Exp, scale=-1.0, bias=lnhalf[0:96, 0:1]), R=[tA.b, lnhalf.b], W=[tA.b])
            fill()
            S.op(S.DVE, lambda h: h.tensor_tensor(out=tB[:], in0=pN[0:96, :], in1=tA[:], op=ALU.mult), X=[pNb], R=[tA.b], W=[tB.b])
            S.op(S.DVE, lambda h: h.scalar_tensor_tensor(out=tB[:], in0=sg[:], scalar=1.0, in1=tB[:], op0=ALU.add, op1=ALU.mult),
                 R=[tB.b, sg.b], W=[tB.b])
            fill()
            S.op(S.ACT, lambda h: h.activation(out=tBb[:], in_=tB[:], func=AF.Copy), R=[tB.b], W=[tBb.b])
            bk = rot()
            pt, pbuf = self.pb[bk]
            self.mm(bk, pt[0:96, :], [(avg96b[:, :], tBb[:, :])], R=[avg96b.b, tBb.b])
            fill()
            S.op(S.DVE, lambda h, pt=pt: h.tensor_tensor(out=tC[:], in0=tB[:], in1=pt[0:96, :], op=ALU.subtract), X=[pbuf], R=[tB.b], W=[tC.b])
            S.op(S.ACT, lambda h: h.activation(out=tAb[:], in_=tC[:], func=AF.Square), R=[tC.b], W=[tAb.b])
            fill()
            bk = rot()
            pt, pbuf = self.pb[bk]
            self.mm(bk, pt[0:96, :], [(avg96b[:, :], tAb[:, :])], R=[avg96b.b, tAb.b])
            fill()
            S.op(S.ACT, lambda h, pt=pt: h.activation(out=tB[:], in_=pt[0:96, :], func=AF.Ln, bias=epsb[0:96, 0:1]), X=[pbuf], R=[epsb.b], W=[tB.b])
            S.op(S.ACT, lambda h: h.activation(out=tB[:], in_=tB[:], func=AF.Exp, scale=-0.5), R=[tB.b], W=[tB.b])
            fill()
            S.op(S.DVE, lambda h: h.scalar_tensor_tensor(out=tC[:], in0=tC[:], scalar=ngt[:, hd:hd + 1], in1=tB[:], op0=ALU.mult, op1=ALU.mult),
                 R=[tC.b, ngt.b, tB.b], W=[tC.b])
            S.op(S.POOL, lambda h: h.tensor_tensor(out=self.ycat[0:96, 3 + hd, gsl], in0=tC[:], in1=sz[:], op=ALU.mult),
                 R=[tC.b, sz.b], W=[self.ycat_b[3 + hd][g]])
            if chunks is not None:
                for _ in chunks:
                    pass

        self.load_w(wml[0], self.wpk[l][:, int(offs[8]):int(offs[9])])
        for _ in gen_A(0):
            pass
        for i in range(len(iters)):
            hd, g = iters[i]
            if g == 0 and hd + 1 < 4:
                self.load_w(wml[(hd + 1) % 2], self.wpk[l][:, int(offs[9 + hd]):int(offs[10 + hd])])
            chunks = gen_A(i + 1) if i + 1 < len(iters) else None
            run_BC(i, chunks)

    def build_memT(self):
        S = self.S
        mt = [self.tile("memin", [128, D], dma=True) for _ in range(2)]
        rot = self.bankrot([4, 5, 6, 7])
        for t in range(2):
            S.op(S.SP, lambda h, t=t: h.dma_start(out=mt[t][:], in_=self.mem_in[t * 128:(t + 1) * 128, :]), W=[mt[t].b], stream=mt[t].st)
            for half in range(2):
                bk = rot()
                pt, pbuf = self.pb[bk]
                for c in range(4):
                    kc = half * 4 + c
                    S.op(S.PE, lambda h, kc=kc, c=c, pt=pt, t=t: h.transpose(
                        pt[:, c * 128:(c + 1) * 128], mt[t][:, kc * 128:(kc + 1) * 128], self.identf[:]),
                        R=[mt[t].b, self.identf.b], X=[pbuf], sig=(c == 3))
                dst = self.memT[:, half * 4:(half + 1) * 4, t * 128:(t + 1) * 128]
                srcp = pt[:, :].rearrange("p (c t) -> p c t", c=4)
                S.op(S.DVE, lambda h, dst=dst, srcp=srcp: h.tensor_copy(out=dst, in_=srcp), X=[pbuf], W=[self.memT.b])

    def phase_mem(self, l):
        S = self.S
        sizes, offs = win_offsets()
        wm = self.tile("wm", [128, NKC * 512], BF16, dma=True)
        wr = self.tile("wr", [128, NKC * 512], BF16, dma=True)
        self.load_w(wm, self.wmempk[l])
        self.load_w(wr, self.wpk[l][:, int(offs[12]):int(offs[13])])
        wmv = wm[:, :].rearrange("p (c n) -> p c n", c=NKC)
        wrv = wr[:, :].rearrange("p (c n) -> p c n", c=NKC)
        KmT = self.tile("KmT", [128, 2, MEM_LEN], BF16)
        Vm = self.tile("Vm", [128, 2, 4, 128], BF16)
        rot = self.bankrot([0, 1])
        rot_s = self.bankrot([2, 3, 4])
        rot_o = self.bankrot([5, 6])
        for p in range(2):
            bk = rot()
            pt, pbuf = self.pb[bk]
            self.mm(bk, pt[:, 0:MEM_LEN], [(wmv[:, kc, p * 128:(p + 1) * 128], self.memT[:, kc, :]) for kc in range(NKC)],
                    R=[wm.b, self.memT.b])
            S.op(S.DVE, lambda h, pt=pt, p=p: h.tensor_copy(out=KmT[:, p, :], in_=pt[:, 0:MEM_LEN]), X=[pbuf], W=[KmT.b])
        S.op(S.POOL, lambda h: h.memset(Vm[:], 1.0), W=[Vm.b])
        for mb in range(2):
            bk = rot()
            pt, pbuf = self.pb[bk]
            self.mm(bk, pt[:, 0:256], [(self.memT[:, kc, mb * 128:(mb + 1) * 128], wmv[:, kc, 256:512]) for kc in range(NKC)],
                    R=[wm.b, self.memT.b])
            for h4 in range(4):
                c0 = 0 if h4 % 2 == 0 else 64
                S.op(S.DVE, lambda h, pt=pt, mb=mb, h4=h4, c0=c0: h.tensor_copy(
                    out=Vm[:, mb, h4, c0:c0 + 64], in_=pt[:, h4 * 64:(h4 + 1) * 64]), X=[pbuf], W=[Vm.b])
        QTm = [self.tile("QTm", [128, G], BF16) for _ in range(2)]
        szp = [self.tile("szp", [128, G], BF16) for _ in range(2)]
        thm = self.tile("thm", [128, G])
        PT = [self.tile("PTm", [128, G], BF16) for _ in range(3)]
        rd = [self.tile("rdm", [128, G]) for _ in range(2)]
        tn = [self.tile("tnm", [128, G]) for _ in range(2)]
        it = 0
        ip = 0
        for g in range(NG):
            gsl = slice(g * G, (g + 1) * G)
            for p in range(2):
                q, z = QTm[it % 2], szp[it % 2]
                it += 1
                bk = rot()
                pt, pbuf = self.pb[bk]
                self.mm(bk, pt[:, :], [(wrv[:, kc, p * 128:(p + 1) * 128], self.xT[:, kc, gsl]) for kc in range(NKC)],
                        R=[wr.b, self.xT_b[g]])
                S.op(S.ACT, lambda h, pt=pt, q=q: h.activation(out=q[:], in_=pt[:, :], func=AF.Copy, scale=0.125), X=[pbuf], W=[q.b])
                bk = rot()
                pt, pbuf = self.pb[bk]
                self.mm(bk, pt[:, :], [(wrv[:, kc, 256 + p * 128:256 + (p + 1) * 128], self.xT[:, kc, gsl]) for kc in range(NKC)],
                        R=[wr.b, self.xT_b[g]])
                S.op(S.ACT, lambda h, pt=pt: h.activation(out=thm[:], in_=pt[:, :], func=AF.Tanh, scale=0.5), X=[pbuf], W=[thm.b])
                S.op(S.DVE, lambda h, pt=pt, z=z: h.scalar_tensor_tensor(out=z[:], in0=thm[:], scalar=1.0, in1=pt[:, :], op0=ALU.add, op1=ALU.mult),
                     X=[pbuf], R=[thm.b], W=[z.b])
                for hh in range(2):
                    h4 = 2 * p + hh
                    r0 = 64 * hh
                    bo = rot_o()
                    po, pobuf = self.pb[bo]
                    sts = []
                    for mb in range(2):
                        bs = rot_s()
                        ps_, psbuf = self.pb[bs]
                        self.mm(bs, ps_[:, :], [(KmT[r0:r0 + 64, p, mb * 128:(mb + 1) * 128], q[r0:r0 + 64, :])], R=[KmT.b, q.b])
                        sts.append((ps_, psbuf))
                    for mb in range(2):
                        ps_, psbuf = sts[mb]
                        pt_ = PT[ip % 3]
                        ip += 1
                        S.op(S.ACT, lambda h, ps_=ps_, pt_=pt_: h.activation(out=pt_[:], in_=ps_[:, :], func=AF.Exp), X=[psbuf], W=[pt_.b])
                        self.mm(bo, po[:, :], [(Vm[:, mb, h4, :], pt_[:])], R=[Vm.b, pt_.b], start=(mb == 0), stop=(mb == 1))
                    self.attn_epilogue(po, pobuf, hh == 1, z, rd[h4 % 2], tn[h4 % 2], 7 + p, g, half=True)

    def attn_epilogue(self, po, pobuf, odd, sz, rd, tn, chunk, g, half=False, act_recip=False):
        S = self.S
        gsl = slice(g * G, (g + 1) * G)
        nr = slice(64, 128) if odd else slice(0, 64)
        dr = slice(0, 64) if odd else slice(64, 128)
        if act_recip:
            S.op(S.ACT, lambda h: h.activation(out=rd[nr, :], in_=po[dr, :], func=AF.Ln), X=[pobuf], W=[rd.b])
            S.op(S.ACT, lambda h: h.activation(out=rd[nr, :], in_=rd[nr, :], func=AF.Exp, scale=-1.0), R=[rd.b], W=[rd.b])
        else:
            S.op(S.DVE, lambda h: h.reciprocal(out=rd[nr, :], in_=po[dr, :]), X=[pobuf], W=[rd.b])
        if half:
            S.op(S.DVE, lambda h: h.scalar_tensor_tensor(out=tn[nr, :], in0=po[nr, :], scalar=0.5, in1=rd[nr, :], op0=ALU.mult, op1=ALU.mult),
                 X=[pobuf], R=[rd.b], W=[tn.b])
        else:
            S.op(S.DVE, lambda h: h.tensor_tensor(out=tn[nr, :], in0=po[nr, :], in1=rd[nr, :], op=ALU.mult), X=[pobuf], R=[rd.b], W=[tn.b])
        S.op(S.POOL, lambda h: h.tensor_tensor(out=self.ycat[nr, chunk, gsl], in0=tn[nr, :], in1=sz[nr, :], op=ALU.mult),
             R=[tn.b, sz.b], W=[self.ycat_b[chunk][g]])

    def phase_final(self, l, x_src, x_dst, make_xT):
        S = self.S
        nc = self.nc
        wout = self.tile("wout", [128, 9 * D], BF16, dma=True)
        self.load_w(wout, self.woutpk[l])
        woutv = wout[:, :].rearrange("p (c n) -> p c n", c=9)
        gam = self.tile("gam", [128, D], dma=True)
        bet = self.tile("bet", [128, D], dma=True)
        S.op(S.SP, lambda h: h.dma_start(out=gam[:], in_=self.lng[l].partition_broadcast(128)), W=[gam.b], stream=gam.st)
        S.op(S.SP, lambda h: h.dma_start(out=bet[:], in_=self.lnb[l].partition_broadcast(128)), W=[bet.b], stream=bet.st)
        xin = [self.tile("xin", [128, D], dma=True) for _ in range(3)]
        epsf = self.tile("epsf", [128, 1])
        S.op(S.POOL, lambda h: h.memset(epsf[:], float(LN_EPS)), W=[epsf.b])
        tt = [self.tile("tt", [128, D]) for _ in range(2)]
        xo = [self.tile("xo", [128, D]) for _ in range(2)]
        stats = [self.tile("stats", [128, 16]) for _ in range(2)]
        krows = [128, 128, 128, 96, 96, 96, 96, 128, 128]
        rot_y = self.bankrot([0, 1, 2, 3])
        rot_t = self.bankrot([4, 5, 6, 7])

        def load_x(t):
            xt = xin[t % 3]
            S.op(S.SP, lambda h: h.dma_start(out=xt[:], in_=x_src[t * 128:(t + 1) * 128, :]), W=[xt.b], stream=xt.st)

        nmr = [self.tile("nmr", [128, 2]) for _ in range(2)]
        junk = self.tile("junk", [128, D], BF16)

        held = {}

        def stage_M_pe(t):
            g = t // 4
            hb = []
            for half in range(2):
                bk = rot_y()
                pt, pbuf = self.pb[bk]
                pairs = [(self.ycat[0:krows[c], c, t * 128:(t + 1) * 128], woutv[0:krows[c], c, half * 512:(half + 1) * 512])
                         for c in range(9)]
                self.mm(bk, pt[:, :], pairs, R=[wout.b] + [self.ycat_b[c][g] for c in range(9)])
                hb.append((pt, pbuf))
            held[t] = hb

        def stage_M_dve(t):
            xt, tq, sq = xin[t % 3], tt[t % 2], stats[t % 2]
            for half in range(2):
                pt, pbuf = held[t][half]
                S.op(S.DVE, lambda h, pt=pt, half=half: h.scalar_tensor_tensor(
                    out=tq[:, half * 512:(half + 1) * 512], in0=xt[:, half * 512:(half + 1) * 512], scalar=float(ALPHA),
                    in1=pt[:, :], op0=ALU.mult, op1=ALU.add), R=[xt.b], X=[pbuf], W=[tq.b])
            for half in range(2):
                S.op(S.DVE, lambda h, half=half: h.bn_stats(out=sq[:, half * 6:(half + 1) * 6],
                                                            in_=tq[:, half * 512:(half + 1) * 512]), R=[tq.b], W=[sq.b])

        def stage_N(t):
            tq, xq, sq, nm = tt[t % 2], xo[t % 2], stats[t % 2], nmr[t % 2]
            S.op(S.DVE, lambda h: h.bn_aggr(out=sq[:, 12:14], in_=sq[:, 0:12]), R=[sq.b], W=[sq.b])
            S.op(S.ACT, lambda h: h.activation(out=sq[:, 14:15], in_=sq[:, 13:14], func=AF.Ln, bias=epsf[:, 0:1]), R=[sq.b, epsf.b], W=[sq.b])
            S.op(S.ACT, lambda h: h.activation(out=nm[:, 0:1], in_=sq[:, 14:15], func=AF.Exp, scale=-0.5), R=[sq.b], W=[nm.b])
            S.op(S.DVE, lambda h: h.tensor_scalar(out=nm[:, 1:2], in0=sq[:, 12:13], scalar1=nm[:, 0:1], scalar2=-1.0, op0=ALU.mult, op1=ALU.mult),
                 R=[sq.b, nm.b], W=[nm.b])
            S.op(S.ACT, lambda h: h.activation(out=tq[:], in_=tq[:], func=AF.Identity, scale=nm[:, 0:1], bias=nm[:, 1:2]), R=[tq.b, nm.b], W=[tq.b])

        def stage_N_b(t):
            tq, xq = tt[t % 2], xo[t % 2]
            S.op(S.POOL, lambda h: h.tensor_tensor(out=xq[:], in0=tq[:], in1=gam[:], op=ALU.mult), R=[tq.b, gam.b], W=[xq.b])
            S.op(S.POOL, lambda h: h.tensor_tensor(out=xq[:], in0=xq[:], in1=bet[:], op=ALU.add), R=[xq.b, bet.b], W=[xq.b])
            st = self.stq[t % 2]
            S.op(S.SP, lambda h: h.dma_start(out=x_dst[t * 128:(t + 1) * 128, :], in_=xq[:]), R=[xq.b], stream=st)

        load_x(0)
        load_x(1)
        stage_M_pe(0)
        stage_M_dve(0)
        for t in range(NT):
            if t + 2 < NT:
                load_x(t + 2)
            if t + 1 < NT:
                stage_M_pe(t + 1)
            stage_N(t)
            stage_N_b(t)
            if t + 1 < NT:
                stage_M_dve(t + 1)
            if make_xT and t >= 1:
                self.transpose_into_xT(xo[(t - 1) % 2], t - 1, rot_t)
        if make_xT:
            self.transpose_into_xT(xo[(NT - 1) % 2], NT - 1, rot_t)

    def _build(self):
        S = self.S
        L = self.L
        for l in range(L):
            x_src = self.x_in if l == 0 else self.xbuf[(l - 1) % 2]
            x_dst = self.out if l == L - 1 else self.xbuf[l % 2]
            if l == 0:
                self.run_phase(self.build_xT_from_dram, x_src)
            if l == 0:
                self.run_phase(self.build_memT)
            self.run_phase(self.phase_fox, l)
            if "noml" not in self.dbg:
                self.run_phase(self.phase_mlstm, l)
            self.run_phase(self.phase_mem, l)
            if "ycat" in self.dbg:
                S.barrier()
                d = self.dbg_dram("dbg_ycat", [128, 9 * S_LEN], BF16)
                S.op(S.SP, lambda h: h.dma_start(out=d, in_=self.ycat[:, :, :].rearrange("p c s -> p (c s)")),
                     R=[b for cb in self.ycat_b for b in cb], stream=self.stq[0])
            self.run_phase(self.phase_final, l, x_src, x_dst, make_xT=(l < L - 1))
        S.finish(S.SP)


def _pack_win(w):
    w3 = w.reshape(NKC, 128, IN_COLS).transpose(1, 0, 2)
    groups = []
    groups.append(np.arange(O_FF, O_FF + 6))
    for h in range(6):
        groups.append(np.concatenate([np.arange(o + 64 * h, o + 64 * (h + 1)) for o in (O_FQ, O_FK, O_FV, O_FZ)]))
    groups.append(np.concatenate([np.arange(O_MI, O_MI + 4), np.arange(O_MF, O_MF + 4)]))
    for h in range(4):
        groups.append(np.concatenate([np.arange(o + 96 * h, o + 96 * (h + 1)) for o in (O_MQ, O_MK, O_MV, O_MO, O_MZ)]))
    groups.append(np.concatenate([np.arange(O_RQ, O_RQ + 256), np.arange(O_RZ, O_RZ + 256)]))
    parts = [np.ascontiguousarray(w3[:, :, gidx]).reshape(128, -1) for gidx in groups]
    return np.concatenate(parts, axis=1)


def win_offsets():
    sizes = [6] + [256] * 6 + [8] + [480] * 4 + [512]
    offs = np.concatenate([[0], np.cumsum([NKC * s for s in sizes])])
    return sizes, offs


def _pack_wout(w):
    out = np.zeros((128, 9, D), np.float32)
    r = 0
    for c, k in enumerate([128, 128, 128, 96, 96, 96, 96, 128, 128]):
        out[0:k, c, :] = w[r:r + k, :]
        r += k
    return out.reshape(128, 9 * D)


def _pack_wmem(w):
    return np.ascontiguousarray(w.reshape(NKC, 128, 512).transpose(1, 0, 2)).reshape(128, NKC * 512)


def pack_layers(inp, layers):
    f = np.float32
    d = {}
    d["wpk"] = np.stack([_pack_win(np.asarray(inp["w_in"][l], f)) for l in layers])
    d["woutpk"] = np.stack([_pack_wout(np.asarray(inp["w_out"][l], f)) for l in layers])
    d["wmempk"] = np.stack([_pack_wmem(np.asarray(inp["w_mem_kv"][l], f)) for l in layers])
    d["foxb"] = np.stack([np.asarray(inp["fox_f_bias"][l], f).reshape(6, 1) for l in layers])
    d["mib"] = np.stack([np.asarray(inp["mlstm_i_bias"][l], f).reshape(4, 1) for l in layers])
    d["mfb"] = np.stack([np.asarray(inp["mlstm_f_bias"][l], f).reshape(4, 1) for l in layers])
    d["convw"] = np.stack([np.ascontiguousarray(np.asarray(inp["mlstm_conv_w"][l], f).reshape(4, 2, 4, 96).transpose(3, 1, 2, 0)).reshape(96, 32)
                           for l in layers])
    d["convb"] = np.stack([np.ascontiguousarray(np.asarray(inp["mlstm_conv_b"][l], f).reshape(2, 4, 96).transpose(2, 0, 1)).reshape(96, 8)
                           for l in layers])
    d["ng"] = np.stack([np.ascontiguousarray(np.asarray(inp["mlstm_norm_g"][l], f).reshape(4, 96).T) for l in layers])
    d["lng"] = np.stack([np.asarray(inp["ln_g"][l], f).reshape(1, D) for l in layers])
    d["lnb"] = np.stack([np.asarray(inp["ln_b"][l], f).reshape(1, D) for l in layers])
    return d


_PROG_CACHE = {}


def get_prog(L, **kw):
    key = (L, tuple(sorted(kw.items())))
    if key not in _PROG_CACHE:
        _PROG_CACHE[key] = Prog(L, **kw)
    return _PROG_CACHE[key]


def run_layers(x, mem, packed, n_cores=8):
    L = packed["wpk"].shape[0]
    prog = get_prog(L)
    B = x.shape[0]
    in_maps = []
    for c in range(n_cores):
        b = c % B
        m = {"x": np.ascontiguousarray(x[b]), "mem": np.ascontiguousarray(mem[b])}
        m.update(packed)
        in_maps.append(m)
    res = run_bass_kernel_spmd(prog.nc, in_maps, core_ids=list(range(n_cores)))
    return np.stack([np.asarray(res.results[b]["out"]) for b in range(B)])


FUSED = True


def kernel(x, mem, w_in, fox_f_bias, mlstm_conv_w, mlstm_conv_b, mlstm_i_bias, mlstm_f_bias,
           mlstm_norm_g, w_mem_kv, w_out, ln_g, ln_b):
    inp = dict(w_in=w_in, fox_f_bias=fox_f_bias, mlstm_conv_w=mlstm_conv_w, mlstm_conv_b=mlstm_conv_b,
               mlstm_i_bias=mlstm_i_bias, mlstm_f_bias=mlstm_f_bias, mlstm_norm_g=mlstm_norm_g,
               w_mem_kv=w_mem_kv, w_out=w_out, ln_g=ln_g, ln_b=ln_b)
    x = np.asarray(x, np.float32)
    mem = np.asarray(mem, np.float32)
    if FUSED:
        return run_layers(x, mem, pack_layers(inp, list(range(DEPTH))))
    for l in range(DEPTH):
        x = run_layers(x, mem, pack_layers(inp, [l]))
    return x
```

```python
import math
from contextlib import ExitStack
import numpy as np
import concourse.bass as bass
import concourse.mybir as mybir
from concourse.bass_utils import run_bass_kernel_spmd

F32 = mybir.dt.float32
BF16 = mybir.dt.bfloat16
AF = mybir.ActivationFunctionType
ALU = mybir.AluOpType

D = 1024
S_LEN = 4096
DEPTH = 4
NKC = 8
G = 512
NG = S_LEN // G
NT = S_LEN // 128
MEM_LEN = 256
LN_EPS = 1e-5
ALPHA = (2.0 * DEPTH) ** 0.25
IN_COLS = 3982
O_FQ, O_FK, O_FV, O_FF, O_FZ = 0, 384, 768, 1152, 1158
O_MQ, O_MK, O_MV, O_MI, O_MF, O_MO, O_MZ = 1542, 1926, 2310, 2694, 2698, 2702, 3086
O_RQ, O_RZ = 3470, 3726


class Stream:
    def __init__(self, name, sem, inc, q, dma):
        self.name, self.sem, self.inc, self.q, self.dma = name, sem, inc, q, dma
        self.n = 0


class Q:
    def __init__(self, name, h):
        self.name, self.h = name, h
        self.seen = {}
        self.stream = None


class Buf:
    __slots__ = ("name", "w", "r")

    def __init__(self, name):
        self.name = name
        self.w = None
        self.r = {}


class Sched:
    def __init__(self, nc):
        self.nc = nc
        self.PE = self._mkq("pe", nc.tensor)
        self.ACT = self._mkq("act", nc.scalar)
        self.DVE = self._mkq("dve", nc.vector)
        self.POOL = self._mkq("pool", nc.gpsimd)
        self.SP = Q("sp", nc.sync)
        self.queues = [self.PE, self.ACT, self.DVE, self.POOL, self.SP]
        self.streams = [q.stream for q in self.queues if q.stream is not None]
        self.nwaits = 0
        self.nops = 0

    def _mkq(self, name, h):
        q = Q(name, h)
        q.stream = Stream(name, self.nc.alloc_semaphore("s_" + name), 1, q, False)
        return q

    def dma_stream(self, name):
        s = Stream(name, self.nc.alloc_semaphore("d_" + name), 16, None, True)
        self.streams.append(s)
        return s

    def _wait(self, q, s, c):
        if q.seen.get(s, 0) >= c:
            return
        assert c <= s.n, f"wait on unissued signal {s.name} {c} > {s.n}"
        q.h.wait_ge(s.sem, c * s.inc)
        q.seen[s] = c
        self.nwaits += 1

    def op(self, q, emit, R=(), W=(), X=(), sig=True, stream=None):
        st = stream if stream is not None else q.stream
        deps = {}

        def add(d, same_ok):
            s, c = d
            if same_ok and (not s.dma) and s.q is q and stream is None:
                return
            if deps.get(s, 0) < c:
                deps[s] = c

        pe = q is self.PE
        for b in R:
            if b.w is not None:
                add(b.w, pe)
        for b in W:
            if b.w is not None:
                add(b.w, pe)
            for s, c in b.r.items():
                add((s, c), pe)
        for b in X:
            if b.w is not None:
                add(b.w, pe)
            for s, c in b.r.items():
                add((s, c), pe)
        for s, c in deps.items():
            if s.dma:
                c = s.n
            self._wait(q, s, c)
        ins = emit(q.h)
        cnt = st.n + 1
        if sig:
            ins.then_inc(st.sem, st.inc)
            st.n = cnt
        for b in R:
            if b.r.get(st, 0) < cnt:
                b.r[st] = cnt
        for b in W:
            b.w = (st, cnt)
            b.r = {}
        for b in X:
            b.w = (st, cnt)
            b.r = {}
        self.nops += 1
        return ins

    def barrier(self):
        for q in self.queues:
            for s in self.streams:
                if s.n > 0 and not (s.q is q):
                    self._wait(q, s, s.n)

    def finish(self, q):
        for s in self.streams:
            if s.n > 0 and not (s.q is q):
                self._wait(q, s, s.n)


class Tile:
    def __init__(self, handle, name, stream=None):
        self.t = handle
        self.b = Buf(name)
        self.st = stream

    def __getitem__(self, k):
        return self.t[k]


class Prog:
    def __init__(self, L, stop_after=None, dbg=()):
        import os
        dbg = tuple(dbg) + tuple(x for x in os.environ.get("KDBG", "").split(",") if x)
        self.L = L
        self.stop_after = stop_after
        self.dbg = set(dbg)
        nc = bass.Bass("TRN2", target_bir_lowering=False)
        self.nc = nc
        self.S = Sched(nc)
        self.uid = 0
        self.stack = None
        self.stream_pool = []
        self.all_streams = []
        self.phase_streams = []
        self._decl_dram()
        self._alloc()
        self._consts()
        self._build()

    def tile(self, name, shape, dt=F32, dma=False):
        self.uid += 1
        nm = f"{name}_{self.uid}"
        if self.stack is not None:
            hnd = self.stack.enter_context(self.nc.sbuf_tensor(nm, list(shape), dt))
        else:
            hnd = self.nc.alloc_sbuf_tensor(nm, list(shape), dt)
        st = None
        if dma:
            if self.stream_pool:
                st = self.stream_pool.pop()
            else:
                st = self.S.dma_stream(f"ds{len(self.all_streams)}")
                self.all_streams.append(st)
            if self.stack is not None:
                self.phase_streams.append(st)
        return Tile(hnd, nm, st)

    def sub_arena(self):
        prog = self

        class _Sub:
            def __enter__(self_):
                self_.outer = prog.stack
                self_.es = ExitStack()
                self_.es.__enter__()
                prog.stack = self_.es
                return self_

            def __exit__(self_, *exc):
                prog.S.barrier()
                prog.stack = self_.outer
                return self_.es.__exit__(*exc)
        return _Sub()

    def run_phase(self, fn, *a, **kw):
        assert self.stack is None
        with ExitStack() as es:
            self.stack = es
            self.phase_streams = []
            fn(*a, **kw)
            self.S.barrier()
            self.stream_pool.extend(self.phase_streams)
            self.phase_streams = []
            self.stack = None

    def dram_in(self, name, shape, dt=F32):
        return self.nc.dram_tensor(name, list(shape), dt, kind="ExternalInput").ap()

    def _decl_dram(self):
        L = self.L
        nc = self.nc
        self.x_in = self.dram_in("x", [S_LEN, D])
        self.mem_in = self.dram_in("mem", [MEM_LEN, D])
        self.wpk = self.dram_in("wpk", [L, 128, NKC * IN_COLS])
        self.woutpk = self.dram_in("woutpk", [L, 128, 9 * D])
        self.wmempk = self.dram_in("wmempk", [L, 128, NKC * 512])
        self.foxb = self.dram_in("foxb", [L, 6, 1])
        self.mib = self.dram_in("mib", [L, 4, 1])
        self.mfb = self.dram_in("mfb", [L, 4, 1])
        self.convw = self.dram_in("convw", [L, 96, 32])
        self.convb = self.dram_in("convb", [L, 96, 8])
        self.ngd = self.dram_in("ng", [L, 96, 4])
        self.lng = self.dram_in("lng", [L, 1, D])
        self.lnb = self.dram_in("lnb", [L, 1, D])
        self.out = nc.dram_tensor("out", [S_LEN, D], F32, kind="ExternalOutput").ap()
        self.xbuf = [nc.dram_tensor(f"xbuf{i}", [S_LEN, D], F32).ap() for i in range(2)] if L > 1 else []
        self.dbg_out = {}
        self.gdram = nc.dram_tensor("gdram", [2, 4 * NG, G], F32).ap()
        self.gdram_b = [Buf(f"gdram_g{g}") for g in range(NG)]

    def dbg_dram(self, name, shape, dt=F32):
        ap = self.nc.dram_tensor(name, list(shape), dt, kind="ExternalOutput").ap()
        self.dbg_out[name] = ap
        return ap

    def _alloc(self):
        nc, S = self.nc, self.S
        self.pb = []
        for i in range(8):
            t = nc.alloc_psum_tensor(f"pb{i}", [128, 512], F32)
            self.pb.append((t, Buf(f"pb{i}")))
        self.xT = nc.alloc_sbuf_tensor("xT", [128, NKC, S_LEN], BF16)
        self.xT_b = [Buf(f"xT_g{g}") for g in range(NG)]
        self.ycat = nc.alloc_sbuf_tensor("ycat", [128, 9, S_LEN], BF16)
        self.ycat_b = [[Buf(f"ycat_{c}_{g}") for g in range(NG)] for c in range(9)]
        self.stq = [S.dma_stream("stq0"), S.dma_stream("stq1")]
        self.memT = self.tile("memT", [128, NKC, MEM_LEN], BF16)
        self.gtok_all = self.tile("gtok_all", [128, NT, 4])
        self.gtokS_all = self.tile("gtokS_all", [128, NT, 4])

    def _consts(self):
        S = self.S
        self.onesf = self.tile("onesf", [128, 128])
        self.zerf = self.tile("zerf", [128, 128])
        self.identf = self.tile("identf", [128, 128])
        self.identb = self.tile("identb", [128, 128], BF16)
        self.mnegf = self.tile("mnegf", [128, 128])
        self.mnegb = self.tile("mnegb", [128, 128], BF16)
        self.avg96 = self.tile("avg96", [128, 96])
        self.fillsrc = self.tile("fillsrc", [128, 512], BF16)
        S.op(S.POOL, lambda h: h.memset(self.fillsrc[:], 0.5), W=[self.fillsrc.b])
        o, z = self.onesf, self.zerf
        S.op(S.POOL, lambda h: h.memset(o[:], 1.0), W=[o.b])
        S.op(S.POOL, lambda h: h.memset(z[:], 0.0), W=[z.b])
        S.op(S.POOL, lambda h: h.memset(self.avg96[:], 1.0 / 96.0), W=[self.avg96.b])
        S.op(S.POOL, lambda h: h.affine_select(out=self.identf[:], in_=o[:], pattern=[[1, 128]],
                                               compare_op=ALU.is_equal, fill=0.0, base=0, channel_multiplier=-1),
             R=[o.b], W=[self.identf.b])
        S.op(S.DVE, lambda h: h.tensor_copy(out=self.identb[:], in_=self.identf[:]), R=[self.identf.b], W=[self.identb.b])
        S.op(S.POOL, lambda h: h.affine_select(out=self.mnegf[:], in_=z[:], pattern=[[1, 128]],
                                               compare_op=ALU.is_ge, fill=-30000.0, base=0, channel_multiplier=-1),
             R=[z.b], W=[self.mnegf.b])
        S.op(S.DVE, lambda h: h.tensor_copy(out=self.mnegb[:], in_=self.mnegf[:]), R=[self.mnegf.b], W=[self.mnegb.b])

    def mm(self, bank, out_ap, pairs, R, start=True, stop=True, sgc=False):
        S = self.S
        n = len(pairs)
        for i, (lhsT, rhs) in enumerate(pairs):
            S.op(S.PE, lambda h, lhsT=lhsT, rhs=rhs, i=i: h.matmul(
                out_ap, lhsT=lhsT, rhs=rhs, start=(start and i == 0), stop=(stop and i == n - 1), skip_group_check=sgc),
                R=R, X=[self.pb[bank][1]], sig=(i == n - 1))

    def filler(self, n=256, bank=7):
        S = self.S
        pt, pbuf = self.pb[bank]
        S.op(S.PE, lambda h: h.matmul(pt[:, 0:n], lhsT=self.identb[:], rhs=self.fillsrc[:, 0:n], start=True, stop=True),
             R=[self.identb.b, self.fillsrc.b], X=[pbuf], sig=False)

    def bankrot(self, banks):
        st = {"i": 0}

        def nxt():
            b = banks[st["i"] % len(banks)]
            st["i"] += 1
            return b
        return nxt

    def build_xT_from_dram(self, x_src):
        S = self.S
        xin = [self.tile("xin", [128, D], dma=True) for _ in range(2)]
        rot = self.bankrot([0, 1, 2, 3])
        for t in range(NT):
            xt = xin[t % 2]
            S.op(S.SP, lambda h, xt=xt, t=t: h.dma_start(out=xt[:], in_=x_src[t * 128:(t + 1) * 128, :]),
                 W=[xt.b], stream=xt.st)
            self.transpose_into_xT(xt, t, rot)

    def transpose_into_xT(self, src, t, rot, eng=None):
        S = self.S
        for half in range(2):
            bk = rot()
            pt, pbuf = self.pb[bk]
            for c in range(4):
                kc = half * 4 + c
                S.op(S.PE, lambda h, kc=kc, c=c, pt=pt: h.transpose(
                    pt[:, c * 128:(c + 1) * 128], src[:, kc * 128:(kc + 1) * 128], self.identf[:]),
                    R=[src.b, self.identf.b], X=[pbuf], sig=(c == 3))
            e = eng if eng is not None else (S.ACT if half == 0 else S.DVE)
            dst = self.xT[:, half * 4:(half + 1) * 4, t * 128:(t + 1) * 128]
            srcp = pt[:, :].rearrange("p (c t) -> p c t", c=4)
            if e is S.ACT:
                S.op(e, lambda h, dst=dst, srcp=srcp: h.activation(out=dst, in_=srcp, func=AF.Copy),
                     X=[pbuf], W=[self.xT_b[t // 4]])
            else:
                S.op(e, lambda h, dst=dst, srcp=srcp: h.tensor_copy(out=dst, in_=srcp),
                     X=[pbuf], W=[self.xT_b[t // 4]])

    def load_w(self, dst_tile, src_ap):
        S = self.S
        self.n_sw = getattr(self, "n_sw", 0) + 1
        st = S.dma_stream(f"sw{self.n_sw}")
        S.op(S.POOL, lambda h: h.dma_start(out=dst_tile[:], in_=src_ap), W=[dst_tile.b], stream=st)


    def phase_fox(self, l):
        S = self.S
        sizes, offs = win_offsets()
        rot = self.bankrot([0, 1])
        rot_s = self.bankrot([2, 3, 4])
        rot_o = self.bankrot([5, 6])
        shiftrows = self.tile("shiftrows", [6, S_LEN], BF16)
        negc_tok = self.tile("negc_tok", [128, NT, 6])
        negcref_rep = self.tile("negcref_rep", [128, 6, NG])
        with self.sub_arena():
            rotG = self.bankrot([2, 3, 4])
            gens = [self.fox_prep(l, offs, rot, shiftrows, negc_tok, negcref_rep),
                    self.mlstm_gates(l, offs, rotG, self.gtok_all, self.gtokS_all, self.gdram, self.gdram_b, float(96 ** -0.5))]
            alive = True
            while alive:
                alive = False
                for gn in gens:
                    if next(gn, "end") != "end":
                        alive = True
        wh = [self.tile("wh", [128, NKC * 320], BF16, dma=True), self.tile("wh", [128, NKC * 192], BF16, dma=True)]
        szp = self.tile("szp", [128, NG, G], BF16)
        szp_b = [Buf(f"szp_g{g}") for g in range(NG)]
        KaT = self.tile("KaT", [65, S_LEN], BF16)
        KaT_b = [Buf(f"KaT_g{g}") for g in range(NG)]
        S.op(S.POOL, lambda h: h.memset(KaT[64:65, :], 1.0), W=KaT_b)
        Vaug = self.tile("Vaug", [128, NT, 192], BF16)
        V_ones = Buf("V_ones")
        V_b = [[Buf(f"V_{par}_g{g}") for g in range(NG)] for par in range(2)]
        S.op(S.POOL, lambda h: h.memset(Vaug[:, :, 64:128], 1.0), W=[V_ones])
        QaT = [self.tile("QaT", [65, G], BF16, dma=True) for _ in range(2)]
        szf = [self.tile("szf", [128, G], BF16) for _ in range(2)]
        PT = [self.tile("PTf", [128, G], BF16) for _ in range(3)]
        rd = [self.tile("rdf", [128, G]) for _ in range(1)]
        tn = [self.tile("tnf", [128, G]) for _ in range(1)]
        thf = [self.tile("thf", [128, G]) for _ in range(2)]
        tabs = [self.tile("tab", [128, NT, NG]) for _ in range(2)]
        ipc = {"i": 0}

        def build_tab(hd):
            tab = tabs[hd % 2]
            for qg in range(NG):
                S.op(S.DVE, lambda h, qg=qg, tab=tab, hd=hd: h.tensor_scalar(
                    out=tab[:, :, qg], in0=negc_tok[:, :, hd], scalar1=negcref_rep[:, hd, qg:qg + 1], scalar2=None, op0=ALU.subtract),
                    R=[negc_tok.b, negcref_rep.b], W=[tab.b])

        def gen_proj(hd, g):
            odd = hd % 2
            W_ = wh[hd % 2]
            Wv = W_[:, :].rearrange("p (c n) -> p c n", c=NKC)
            gsl = slice(g * G, (g + 1) * G)
            qa = QaT[g % 2]
            vc0 = 128 if odd else 0
            RR = [W_.b, self.xT_b[g]]
            bk = rot()
            pt, pbuf = self.pb[bk]
            for kc in range(NKC):
                S.op(S.PE, lambda h, kc=kc, pt=pt: h.matmul(pt[:, :], lhsT=Wv[:, kc, 0:128], rhs=self.xT[:, kc, gsl],
                                                            start=(kc == 0), stop=(kc == NKC - 1)), R=RR, X=[pbuf], sig=(kc == NKC - 1))
                if kc % 2 == 1 and kc < NKC - 1:
                    yield
            S.op(S.DVE, lambda h, pt=pt: h.tensor_scalar(out=qa[0:64, :], in0=pt[0:64, :], scalar1=0.125, scalar2=None, op0=ALU.mult), X=[pbuf], W=[qa.b])
            S.op(S.DVE, lambda h, pt=pt: h.tensor_copy(out=KaT[0:64, gsl], in_=pt[64:128, :]), X=[pbuf], W=[KaT_b[g]])
            S.op(S.SP, lambda h: h.dma_start(out=qa[64:65, :], in_=shiftrows[hd:hd + 1, gsl]), R=[shiftrows.b], W=[qa.b], stream=qa.st)
            yield
            bk = rot()
            pt, pbuf = self.pb[bk]
            for j in range(4):
                blk = slice(g * G + j * 128, g * G + (j + 1) * 128)
                for kc in range(NKC):
                    S.op(S.PE, lambda h, kc=kc, pt=pt, j=j, blk=blk: h.matmul(
                        pt[:, j * 64:(j + 1) * 64], lhsT=self.xT[:, kc, blk], rhs=Wv[:, kc, 128:192],
                        start=(kc == 0), stop=(kc == NKC - 1)), R=RR, X=[pbuf], sig=(kc == NKC - 1))
                    if kc == 3:
                        yield
                yield
            S.op(S.DVE, lambda h, pt=pt: h.tensor_copy(
                out=Vaug[:, 4 * g:4 * g + 4, vc0:vc0 + 64], in_=pt[:, 0:256].rearrange("p (j c) -> p j c", j=4)),
                X=[pbuf], W=[V_b[odd][g]])
            yield
            if not odd:
                bk = rot()
                pt, pbuf = self.pb[bk]
                for kc in range(NKC):
                    S.op(S.PE, lambda h, kc=kc, pt=pt: h.matmul(pt[:, :], lhsT=Wv[:, kc, 192:320], rhs=self.xT[:, kc, gsl],
                                                                start=(kc == 0), stop=(kc == NKC - 1)), R=RR, X=[pbuf], sig=(kc == NKC - 1))
                    if kc % 2 == 1 and kc < NKC - 1:
                        yield
                th = thf[g % 2]
                S.op(S.ACT, lambda h, pt=pt: h.activation(out=th[:, :], in_=pt[:, :], func=AF.Tanh, scale=0.5), X=[pbuf], W=[th.b])
                S.op(S.DVE, lambda h, pt=pt: h.scalar_tensor_tensor(out=szp[:, g, :], in0=th[:, :], scalar=1.0, in1=pt[:, :], op0=ALU.add, op1=ALU.mult),
                     X=[pbuf], R=[th.b], W=[szp_b[g]])
            yield

        N_CHUNKS = 20

        def attention(hd, g, chunks):
            odd = hd % 2
            lc0 = 64 if odd else 0
            tab = tabs[hd % 2]
            qa = QaT[g % 2]
            nkb = 4 * g + 4
            emitted = {"n": 0}
            bo = rot_o()
            po, pobuf = self.pb[bo]

            def issue_st(kb):
                diag = kb >= 4 * g
                qoff = (kb - 4 * g) * 128 if diag else 0
                n = G - qoff
                bs = rot_s()
                ps_, psbuf = self.pb[bs]
                self.mm(bs, ps_[:, 0:n], [(KaT[0:65, kb * 128:(kb + 1) * 128], qa[0:65, qoff:G])], R=[KaT_b[kb // 4], qa.b],
                        start=True, stop=not diag)
                if diag:
                    self.mm(bs, ps_[:, 0:128], [(self.identb[:], self.mnegb[:])], R=[self.identb.b, self.mnegb.b], start=False, stop=True)
                return ps_, psbuf, qoff, n

            nxt = issue_st(0)
            for kb in range(nkb):
                cur = nxt
                if kb + 1 < nkb:
                    nxt = issue_st(kb + 1)
                ps_, psbuf, qoff, n = cur
                pt_ = PT[ipc["i"] % 3]
                ipc["i"] += 1
                S.op(S.ACT, lambda h, ps_=ps_, pt_=pt_, n=n, kb=kb: h.activation(
                    out=pt_[:, 0:n], in_=ps_[:, 0:n], func=AF.Exp, bias=tab[:, kb, g:g + 1]), X=[psbuf], R=[tab.b], W=[pt_.b])
                did = False
                if chunks is not None:
                    want = ((kb + 1) * N_CHUNKS + nkb - 1) // nkb
                    while emitted["n"] < want:
                        if next(chunks, "end") != "end":
                            did = True
                        emitted["n"] += 1
                if not did:
                    self.filler(256)
                self.mm(bo, po[:, qoff:G], [(Vaug[:, kb, lc0:lc0 + 128], pt_[:, 0:n])], R=[V_b[odd][kb // 4], V_ones, pt_.b],
                        start=(kb == 0), stop=(kb == nkb - 1))
            if chunks is not None:
                for _ in chunks:
                    pass
            self.attn_epilogue(po, pobuf, bool(odd), (szp, g, szp_b[g]), rd[0], tn[0], hd // 2, g, half=True)

        self.load_w(wh[0], self.wpk[l][:, int(offs[1]):int(offs[2])])
        build_tab(0)
        for _ in gen_proj(0, 0):
            pass
        for hd in range(6):
            if hd + 1 < 6:
                self.load_w(wh[(hd + 1) % 2], self.wpk[l][:, int(offs[2 + hd]):int(offs[3 + hd])])
                build_tab(hd + 1)
            for g in range(NG):
                if g + 1 < NG:
                    chunks = gen_proj(hd, g + 1)
                elif hd + 1 < 6:
                    chunks = gen_proj(hd + 1, 0)
                else:
                    chunks = None
                attention(hd, g, chunks)


    def fox_prep(self, l, offs, rot, shiftrows, negc_tok, negcref_rep):
        S = self.S
        wff = self.tile("wff", [128, NKC * 6], BF16, dma=True)
        self.load_w(wff, self.wpk[l][:, int(offs[0]):int(offs[1])])
        wffv = wff[:, :].rearrange("p (c n) -> p c n", c=NKC)
        fb = self.tile("fb", [6, 1], dma=True)
        S.op(S.SP, lambda h: h.dma_start(out=fb[:], in_=self.foxb[l]), W=[fb.b], stream=fb.st)
        nfb = self.tile("nfb", [6, 1])
        S.op(S.DVE, lambda h: h.tensor_scalar(out=nfb[:], in0=fb[:], scalar1=-1.0, scalar2=None, op0=ALU.mult), R=[fb.b], W=[nfb.b])
        negcref6 = self.tile("negcref6", [6, NG])
        oh6 = self.tile("oh6", [6, 6, 128])
        for hh in range(6):
            S.op(S.POOL, lambda h, hh=hh: h.affine_select(
                out=oh6[:, hh, :], in_=self.onesf[0:6, :], pattern=[[0, 128]], compare_op=ALU.is_equal,
                fill=0.0, base=-hh, channel_multiplier=1), R=[self.onesf.b], W=[oh6.b])
        e_t = [self.tile("e_t", [6, G]) for _ in range(2)]
        negc = [self.tile("negc", [6, G]) for _ in range(2)]
        for g in range(NG):
            gsl = slice(g * G, (g + 1) * G)
            bk = rot()
            pt, pbuf = self.pb[bk]
            self.mm(bk, pt[0:6, :], [(wffv[:, kc, :], self.xT[:, kc, gsl]) for kc in range(NKC)], R=[wff.b, self.xT_b[g]])
            e, nc_, ncp = e_t[g % 2], negc[g % 2], negc[(g - 1) % 2]
            lt = e
            S.op(S.ACT, lambda h, pt=pt, e=e: h.activation(out=e[:], in_=pt[0:6, :], func=AF.Exp, scale=-1.0, bias=nfb[:, 0:1]),
                 X=[pbuf], R=[nfb.b], W=[e.b])
            S.op(S.ACT, lambda h, e=e, lt=lt: h.activation(out=lt[:], in_=e[:], func=AF.Ln, bias=1.0), R=[e.b], W=[lt.b])
            init = 0.0 if g == 0 else ncp[:, G - 1:G]
            S.op(S.DVE, lambda h, lt=lt, nc_=nc_, init=init: h.tensor_tensor_scan(
                out=nc_[:], data0=self.onesf[0:6, 0:1].to_broadcast([6, G]), data1=lt[:], initial=init, op0=ALU.mult, op1=ALU.add),
                R=[lt.b, self.onesf.b] + ([ncp.b] if g > 0 else []), W=[nc_.b])
            S.op(S.DVE, lambda h, nc_=nc_, g=g: h.tensor_copy(out=negcref6[:, g:g + 1], in_=nc_[:, 0:1]), R=[nc_.b], W=[negcref6.b])
            S.op(S.DVE, lambda h, nc_=nc_, gsl=gsl: h.tensor_scalar(out=shiftrows[:, gsl], in0=nc_[:], scalar1=nc_[:, 0:1], scalar2=-1.0,
                                                                 op0=ALU.subtract, op1=ALU.mult), R=[nc_.b], W=[shiftrows.b])
            bk = rot()
            pt, pbuf = self.pb[bk]
            for j in range(4):
                S.op(S.PE, lambda h, pt=pt, j=j, nc_=nc_: h.transpose(pt[:, j * 6:(j + 1) * 6], nc_[0:6, j * 128:(j + 1) * 128], self.identf[0:6, 0:6]),
                     R=[nc_.b, self.identf.b], X=[pbuf], sig=(j == 3))
            S.op(S.DVE, lambda h, pt=pt, g=g: h.tensor_copy(out=negc_tok[:, 4 * g:4 * g + 4, :], in_=pt[:, 0:24].rearrange("p (j c) -> p j c", j=4)),
                 X=[pbuf], W=[negc_tok.b])
            yield
        bk = rot()
        pt, pbuf = self.pb[bk]
        for hh in range(6):
            self.mm(bk, pt[:, hh * NG:(hh + 1) * NG], [(oh6[:, hh, :], negcref6[:, :])], R=[oh6.b, negcref6.b])
        S.op(S.DVE, lambda h: h.tensor_copy(out=negcref_rep[:, :, :], in_=pt[:, 0:6 * NG].rearrange("p (a b) -> p a b", a=6)),
             X=[pbuf], W=[negcref_rep.b])


    def mlstm_gates(self, l, offs, rot, gtok_all, gtokS_all, gdram, gdram_b, sK):
        S = self.S
        wmif = self.tile("wmif", [128, NKC * 8], BF16, dma=True)
        self.load_w(wmif, self.wpk[l][:, int(offs[7]):int(offs[8])])
        wmifv = wmif[:, :].rearrange("p (c n) -> p c n", c=NKC)
        ib = self.tile("ib", [4, 1], dma=True)
        fbm = self.tile("fbm", [4, 1], dma=True)
        for (t_, src) in ((ib, self.mib[l]), (fbm, self.mfb[l])):
            S.op(S.SP, lambda h, t_=t_, src=src: h.dma_start(out=t_[:], in_=src), W=[t_.b], stream=t_.st)
        nfbm = self.tile("nfbm", [4, 1])
        S.op(S.DVE, lambda h: h.tensor_scalar(out=nfbm[:], in0=fbm[:], scalar1=-1.0, scalar2=None, op0=ALU.mult), R=[fbm.b], W=[nfbm.b])
        e_t = [self.tile("me", [4, G]) for _ in range(2)]
        negF = [self.tile("negF", [4, G]) for _ in range(2)]
        gg = [self.tile("gg", [4, G]) for _ in range(2)]
        Gc = [self.tile("Gc", [4, G], dma=True) for _ in range(2)]
        nM = [self.tile("nM", [4, G], dma=True) for _ in range(2)]
        onesb = self.onesf[0:4, 0:1].to_broadcast([4, G])
        gview = [gdram[k].rearrange("(h g) t -> h g t", g=NG) for k in range(2)]
        for g in range(NG):
            gsl = slice(g * G, (g + 1) * G)
            xb = self.xT_b[g]
            e_, nF, g_, G_, nM_ = e_t[g % 2], negF[g % 2], gg[g % 2], Gc[g % 2], nM[g % 2]
            nFp, Gp = negF[(g - 1) % 2], Gc[(g - 1) % 2]
            bI = rot()
            pI, pIb = self.pb[bI]
            self.mm(bI, pI[0:4, :], [(wmifv[:, kc, 0:4], self.xT[:, kc, gsl]) for kc in range(NKC)], R=[wmif.b, xb])
            bF = rot()
            pF, pFb = self.pb[bF]
            self.mm(bF, pF[0:4, :], [(wmifv[:, kc, 4:8], self.xT[:, kc, gsl]) for kc in range(NKC)], R=[wmif.b, xb])
            S.op(S.ACT, lambda h, pF=pF, e_=e_: h.activation(out=e_[:], in_=pF[0:4, :], func=AF.Exp, scale=-1.0, bias=nfbm[:, 0:1]),
                 X=[pFb], R=[nfbm.b], W=[e_.b])
            S.op(S.ACT, lambda h, e_=e_: h.activation(out=e_[:], in_=e_[:], func=AF.Ln, bias=1.0), R=[e_.b], W=[e_.b])
            initF = 0.0 if g == 0 else nFp[:, G - 1:G]
            S.op(S.DVE, lambda h, initF=initF, nF=nF, e_=e_: h.tensor_tensor_scan(out=nF[:], data0=onesb, data1=e_[:], initial=initF,
                                                                                op0=ALU.mult, op1=ALU.add),
                 R=[e_.b, self.onesf.b] + ([nFp.b] if g > 0 else []), W=[nF.b])
            S.op(S.DVE, lambda h, pI=pI, g_=g_, nF=nF: h.scalar_tensor_tensor(out=g_[:], in0=pI[0:4, :], scalar=ib[:, 0:1], in1=nF[:],
                                                                             op0=ALU.add, op1=ALU.add), X=[pIb], R=[ib.b, nF.b], W=[g_.b])
            initG = 0.0 if g == 0 else Gp[:, G - 1:G]
            S.op(S.DVE, lambda h, initG=initG, G_=G_, g_=g_: h.tensor_tensor_scan(out=G_[:], data0=onesb, data1=g_[:], initial=initG,
                                                                                op0=ALU.mult, op1=ALU.max),
                 R=[g_.b, self.onesf.b] + ([Gp.b] if g > 0 else []), W=[G_.b])
            S.op(S.DVE, lambda h, nM_=nM_, nF=nF, G_=G_: h.tensor_tensor(out=nM_[:], in0=nF[:], in1=G_[:], op=ALU.subtract),
                 R=[nF.b, G_.b], W=[nM_.b])
            S.op(S.SP, lambda h, G_=G_, g=g: h.dma_start(out=gview[0][:, g, :], in_=G_[:]), R=[G_.b], W=[gdram_b[g]], stream=G_.st)
            S.op(S.SP, lambda h, nM_=nM_, g=g: h.dma_start(out=gview[1][:, g, :], in_=nM_[:]), R=[nM_.b], W=[gdram_b[g]], stream=nM_.st)
            bk = rot()
            pt, pbuf = self.pb[bk]
            for j in range(4):
                S.op(S.PE, lambda h, pt=pt, j=j, g_=g_: h.transpose(pt[:, j * 4:(j + 1) * 4], g_[0:4, j * 128:(j + 1) * 128], self.identf[0:4, 0:4]),
                     R=[g_.b, self.identf.b], X=[pbuf], sig=(j == 3))
            S.op(S.DVE, lambda h, pt=pt, g=g: h.tensor_copy(out=gtok_all[:, 4 * g:4 * g + 4, :], in_=pt[:, 0:16].rearrange("p (j c) -> p j c", j=4)),
                 X=[pbuf], W=[gtok_all.b])
            yield
        S.op(S.DVE, lambda h: h.tensor_scalar(out=gtokS_all[:, :, :], in0=gtok_all[:, :, :], scalar1=float(math.log(sK)), scalar2=None, op0=ALU.add),
             R=[gtok_all.b], W=[gtokS_all.b])

    def phase_mlstm(self, l):
        S = self.S
        sizes, offs = win_offsets()
        sK = float(96 ** -0.5)
        rot = self.bankrot([0, 1, 2])
        B_S, B_T, B_N, B_D = 4, 5, 6, 7
        gtok_all, gtokS_all = self.gtok_all, self.gtokS_all
        gdram, gdram_b = self.gdram, self.gdram_b
        cw = self.tile("cw", [96, 32], dma=True)
        cb = self.tile("cb", [96, 8], dma=True)
        ngt = self.tile("ngt", [96, 4], dma=True)
        for (t_, src) in ((cw, self.convw[l]), (cb, self.convb[l]), (ngt, self.ngd[l])):
            S.op(S.SP, lambda h, t_=t_, src=src: h.dma_start(out=t_[:], in_=src), W=[t_.b], stream=t_.st)
        lnhalf = self.tile("lnhalf", [128, 1])
        epsb = self.tile("epsb", [128, 1])
        S.op(S.POOL, lambda h: h.memset(lnhalf[:], float(math.log(0.5))), W=[lnhalf.b])
        S.op(S.POOL, lambda h: h.memset(epsb[:], float(LN_EPS)), W=[epsb.b])
        wml = [self.tile("wml", [128, NKC * 480], BF16, dma=True) for _ in range(2)]
        Grep = [self.tile("Grep", [128, G], dma=True) for _ in range(2)]
        clamp = [self.tile("clamp", [96, G], dma=True) for _ in range(2)]
        mu_chain = [self.tile("mu_chain", [128, 8]) for _ in range(2)]
        wexp = [self.tile("wexp", [128, 4]) for _ in range(2)]
        carry = [self.tile("carry", [128, 4]) for _ in range(2)]
        warg = self.tile("warg", [128, 4])
        carg = self.tile("carg", [128, 4])
        qpre = self.tile("qpre", [96, G + 3])
        kpre = self.tile("kpre", [96, G + 3])
        QT = [self.tile("QT", [96, G]) for _ in range(2)]
        KT = [self.tile("KT", [96, G]) for _ in range(2)]
        Vtok = [self.tile("Vtok", [128, 4, 192], BF16) for _ in range(2)]
        Vones = [Buf("Vtok_ones0"), Buf("Vtok_ones1")]
        for k in range(2):
            S.op(S.POOL, lambda h, k=k: h.memset(Vtok[k][:, :, 96:192], 1.0), W=[Vones[k]])
        sgo = [self.tile("sgo", [96, G]) for _ in range(2)]
        szm = [self.tile("szm", [96, G], BF16) for _ in range(2)]
        argDT = self.tile("argDT", [128, G])
        DTb = [Buf(f"DTb{j}") for j in range(4)]
        decb = [Buf(f"decb{j}") for j in range(4)]
        AT4 = self.tile("AT4", [128, G], BF16)
        decQ = self.tile("decQ", [96, G])
        decQb = self.tile("decQb", [96, G], BF16)
        Stb = self.tile("Stb", [96, 192], BF16)
        tBb = self.tile("tBb", [96, G], BF16)
        tAb = self.tile("tAb", [96, G], BF16)
        avg96b = self.tile("avg96b", [96, 96], BF16)
        S.op(S.DVE, lambda h: h.tensor_copy(out=avg96b[:], in_=self.avg96[0:96, :]), R=[self.avg96.b], W=[avg96b.b])
        Khat4 = self.tile("Khat4", [128, 4, 96], BF16)
        St = self.tile("St", [96, 192])
        tA = self.tile("tA", [96, G])
        tB = self.tile("tB", [96, G])
        tC = self.tile("tC", [96, G])
        cnt = {"b": 0}
        iters = [(hd, g) for hd in range(4) for g in range(NG)]

        def gen_A(i):
            hd, g = iters[i]
            k = i % 2
            gsl = slice(g * G, (g + 1) * G)
            xb = self.xT_b[g]
            W_ = wml[hd % 2]
            Wv = W_[:, :].rearrange("p (c n) -> p c n", c=NKC)
            RR = [W_.b, xb]
            Gr, cl, mu, we, ca = Grep[k], clamp[k], mu_chain[k], wexp[k], carry[k]
            mup = mu_chain[1 - k]
            row = hd * NG + g
            S.op(S.SP, lambda h: h.dma_start(out=Gr[:], in_=gdram[0][row:row + 1, :].partition_broadcast(128)),
                 R=[gdram_b[g]], W=[Gr.b], stream=Gr.st)
            S.op(S.SP, lambda h: h.dma_start(out=cl[:], in_=gdram[1][row:row + 1, :].partition_broadcast(96)),
                 R=[gdram_b[g]], W=[cl.b], stream=cl.st)
            S.op(S.ACT, lambda h: h.activation(out=cl[:], in_=cl[:], func=AF.Exp), R=[cl.b], W=[cl.b])
            if g == 0:
                S.op(S.POOL, lambda h: h.memset(mu[:, 0:1], 0.0), W=[mu.b])
                S.op(S.POOL, lambda h: h.memset(qpre[:, 0:3], 0.0), W=[qpre.b])
                S.op(S.POOL, lambda h: h.memset(kpre[:, 0:3], 0.0), W=[kpre.b])
            else:
                S.op(S.DVE, lambda h: h.tensor_copy(out=mu[:, 0:1], in_=mup[:, 4:5]), R=[mup.b], W=[mu.b])
            S.op(S.DVE, lambda h: h.tensor_copy(out=mu[:, 1:5], in_=Gr[:, :].rearrange("p (j t) -> p j t", j=4)[:, :, 127]),
                 R=[Gr.b], W=[mu.b])
            S.op(S.DVE, lambda h: h.tensor_tensor(out=warg[:], in0=gtok_all[:, 4 * g:4 * g + 4, hd], in1=mu[:, 1:5], op=ALU.subtract),
                 R=[gtok_all.b, mu.b], W=[warg.b])
            S.op(S.ACT, lambda h: h.activation(out=we[:], in_=warg[:], func=AF.Exp), R=[warg.b], W=[we.b])
            S.op(S.DVE, lambda h: h.tensor_tensor(out=carg[:], in0=mu[:, 0:4], in1=mu[:, 1:5], op=ALU.subtract), R=[mu.b], W=[carg.b])
            S.op(S.ACT, lambda h: h.activation(out=ca[:], in_=carg[:], func=AF.Exp), R=[carg.b], W=[ca.b])
            yield
            for (pre, acc, c0, qk) in ((qpre, QT[k], 0, 0), (kpre, KT[k], 96, 1)):
                if g > 0:
                    S.op(S.DVE, lambda h, pre=pre: h.tensor_copy(out=pre[:, 0:3], in_=pre[:, G:G + 3]), R=[pre.b], W=[pre.b])
                bk = rot()
                pt, pbuf = self.pb[bk]
                for kc in range(NKC):
                    S.op(S.PE, lambda h, kc=kc, pt=pt, c0=c0: h.matmul(pt[0:96, :], lhsT=Wv[:, kc, c0:c0 + 96], rhs=self.xT[:, kc, gsl],
                                                                      start=(kc == 0), stop=(kc == NKC - 1)), R=RR, X=[pbuf], sig=(kc == NKC - 1))
                    if kc % 2 == 1 and kc < NKC - 1:
                        yield
                S.op(S.ACT, lambda h, pt=pt, pre=pre: h.activation(out=pre[:, 3:G + 3], in_=pt[0:96, :], func=AF.Copy), X=[pbuf], W=[pre.b])
                yield
                wi = lambda tap, qk=qk: cw[:, (qk * 4 + hd) * 4 + tap:(qk * 4 + hd) * 4 + tap + 1]
                bi = cb[:, qk * 4 + hd:qk * 4 + hd + 1]
                S.op(S.DVE, lambda h, pre=pre, acc=acc, wi=wi, bi=bi: h.tensor_scalar(
                    out=acc[:], in0=pre[:, 3:G + 3], scalar1=wi(3), scalar2=bi, op0=ALU.mult, op1=ALU.add),
                    R=[pre.b, cw.b, cb.b], W=[acc.b])
                for kk in (1, 2, 3):
                    S.op(S.DVE, lambda h, pre=pre, acc=acc, wi=wi, kk=kk: h.scalar_tensor_tensor(
                        out=acc[:], in0=pre[:, 3 - kk:G + 3 - kk], scalar=wi(3 - kk), in1=acc[:], op0=ALU.mult, op1=ALU.add),
                        R=[pre.b, cw.b, acc.b], W=[acc.b])
                    if kk == 2:
                        yield
                yield
            bk = rot()
            pt, pbuf = self.pb[bk]
            Vt = Vtok[k]
            for j in range(4):
                blk = slice(g * G + j * 128, g * G + (j + 1) * 128)
                for kc in range(NKC):
                    S.op(S.PE, lambda h, kc=kc, pt=pt, j=j, blk=blk: h.matmul(
                        pt[:, j * 96:(j + 1) * 96], lhsT=self.xT[:, kc, blk], rhs=Wv[:, kc, 192:288],
                        start=(kc == 0), stop=(kc == NKC - 1)), R=RR, X=[pbuf], sig=(kc == NKC - 1))
                yield
            S.op(S.DVE, lambda h, pt=pt, Vt=Vt: h.tensor_copy(out=Vt[:, :, 0:96], in_=pt[:, 0:384].rearrange("p (j c) -> p j c", j=4)),
                 X=[pbuf], W=[Vt.b])
            yield
            held = []
            for c0 in (288, 384):
                bk = rot()
                pt, pbuf = self.pb[bk]
                for kc in range(NKC):
                    S.op(S.PE, lambda h, kc=kc, pt=pt, c0=c0: h.matmul(pt[0:96, :], lhsT=Wv[:, kc, c0:c0 + 96], rhs=self.xT[:, kc, gsl],
                                                                      start=(kc == 0), stop=(kc == NKC - 1)), R=RR, X=[pbuf], sig=(kc == NKC - 1))
                held.append((pt, pbuf))
            S.op(S.ACT, lambda h: h.activation(out=QT[k][:], in_=QT[k][:], func=AF.Silu), R=[QT[k].b], W=[QT[k].b])
            S.op(S.ACT, lambda h: h.activation(out=KT[k][:], in_=KT[k][:], func=AF.Silu), R=[KT[k].b], W=[KT[k].b])
            (pto, pbo), (ptz, pbz) = held
            S.op(S.ACT, lambda h: h.activation(out=sgo[k][:], in_=pto[0:96, :], func=AF.Tanh, scale=0.5), X=[pbo], W=[sgo[k].b])
            S.op(S.ACT, lambda h: h.activation(out=szm[k][:], in_=ptz[0:96, :], func=AF.Silu), X=[pbz], W=[szm[k].b])
            yield

        N_A = 30

        def run_BC(i, chunks):
            hd, g = iters[i]
            k = i % 2
            gsl = slice(g * G, (g + 1) * G)
            Gr, cl, mu, we, ca = Grep[k], clamp[k], mu_chain[k], wexp[k], carry[k]
            Q_, K_, Vt, Vo = QT[k], KT[k], Vtok[k], Vones[k]
            sg, sz = sgo[k], szm[k]
            pS, pSb = self.pb[B_S]
            pT, pTb = self.pb[B_T]
            pN, pNb = self.pb[B_N]
            pD, pDb = self.pb[B_D]

            fstate = {"pt": 0, "em": 0}
            N_PTS = 13

            def fill(n=1):
                did = False
                fstate["pt"] += 1
                if chunks is not None:
                    want = (fstate["pt"] * N_A + N_PTS - 1) // N_PTS
                    while fstate["em"] < want:
                        if next(chunks, "end") != "end":
                            did = True
                        fstate["em"] += 1
                if not did:
                    self.filler(512, bank=3)

            if g == 0:
                S.op(S.POOL, lambda h: h.memset(St[:], 0.0), W=[St.b])
                S.op(S.POOL, lambda h: h.memset(Stb[:], 0.0), W=[Stb.b])
            for j in range(4):
                bs = slice(j * 128, (j + 1) * 128)
                self.mm(B_S, pS[:, bs], [(K_[0:96, bs], Q_[0:96, bs])], R=[K_.b, Q_.b])
            S.op(S.DVE, lambda h: h.scalar_tensor_tensor(
                out=argDT[:, :].rearrange("p (j t) -> p j t", j=4), in0=Gr[:, :].rearrange("p (j t) -> p j t", j=4), scalar=-1.0,
                in1=self.mnegf[:, :].unsqueeze(1).to_broadcast([128, 4, 128]), op0=ALU.mult, op1=ALU.add),
                R=[Gr.b, self.mnegf.b], W=[argDT.b] + DTb)
            for j in range(4):
                bs = slice(j * 128, (j + 1) * 128)
                S.op(S.ACT, lambda h, bs=bs, j=j: h.activation(out=argDT[:, bs], in_=argDT[:, bs], func=AF.Exp,
                                                              bias=gtokS_all[:, 4 * g + j, hd:hd + 1]), R=[argDT.b, gtokS_all.b], W=[DTb[j]])
            for j in range(4):
                bs = slice(j * 128, (j + 1) * 128)
                S.op(S.ACT, lambda h, bs=bs, j=j: h.activation(out=decQ[:, bs], in_=Gr[0:96, bs], func=AF.Exp, scale=-1.0,
                                                              bias=mu[0:96, j:j + 1]), R=[Gr.b, mu.b], W=[decb[j]])
            fill()
            S.op(S.DVE, lambda h: h.tensor_tensor(out=AT4[:], in0=pS[:, :], in1=argDT[:], op=ALU.mult), X=[pSb], R=[argDT.b] + DTb, W=[AT4.b])
            S.op(S.POOL, lambda h: h.tensor_tensor(out=decQb[:], in0=Q_[:], in1=decQ[:], op=ALU.mult), R=[Q_.b, decQ.b] + decb, W=[decQb.b])
            for j in range(4):
                bs = slice(j * 128, (j + 1) * 128)
                S.op(S.PE, lambda h, bs=bs, j=j: h.transpose(pT[:, j * 96:(j + 1) * 96], K_[0:96, bs], self.identf[0:96, 0:96]),
                     R=[K_.b, self.identf.b], X=[pTb], sig=(j == 3))
            S.op(S.DVE, lambda h: h.scalar_tensor_tensor(
                out=Khat4[:, :, :], in0=pT[:, 0:384].rearrange("p (j c) -> p j c", j=4), scalar=sK,
                in1=we[:, 0:4].unsqueeze(2).to_broadcast([128, 4, 96]), op0=ALU.mult, op1=ALU.mult),
                X=[pTb], R=[we.b], W=[Khat4.b])
            fill()
            for j in range(4):
                bs = slice(j * 128, (j + 1) * 128)
                self.mm(B_N, pN[0:96, bs], [(Vt[:, j, 0:96], AT4[:, bs])], R=[Vt.b, AT4.b], start=(j == 0), stop=False, sgc=True)
            for j in range(4):
                bs = slice(j * 128, (j + 1) * 128)
                self.mm(B_D, pD[0:96, bs], [(Vt[:, j, 96:192], AT4[:, bs])], R=[Vo, AT4.b], start=(j == 0), stop=False, sgc=True)
            for j in range(4):
                ub, ubuf, uo = (pT, pTb, B_T) if j < 2 else (pS, pSb, B_S)
                c0 = (j % 2) * 192
                self.mm(uo, ub[0:96, c0:c0 + 192], [(Khat4[:, j, :], Vt[:, j, :])], R=[Khat4.b, Vt.b, Vo])
            fill()
            for j in range(4):
                bs = slice(j * 128, (j + 1) * 128)
                ub, ubuf = (pT, pTb) if j < 2 else (pS, pSb)
                c0 = (j % 2) * 192
                self.mm(B_N, pN[0:96, bs], [(Stb[0:96, 0:96], decQb[:, bs])], R=[Stb.b, decQb.b], start=False, stop=True, sgc=True)
                self.mm(B_D, pD[0:96, bs], [(Stb[0:96, 96:192], decQb[:, bs])], R=[Stb.b, decQb.b], start=False, stop=True, sgc=True)
                S.op(S.DVE, lambda h, j=j, ub=ub, c0=c0: h.scalar_tensor_tensor(out=St[:], in0=St[:], scalar=ca[0:96, j:j + 1],
                                                                               in1=ub[0:96, c0:c0 + 192], op0=ALU.mult, op1=ALU.add),
                     X=[ubuf], R=[ca.b, St.b], W=[St.b])
                S.op(S.DVE, lambda h: h.tensor_copy(out=Stb[:], in_=St[:]), R=[St.b], W=[Stb.b])
                if j % 2 == 1:
                    fill()
            S.op(S.DVE, lambda h: h.tensor_tensor(out=tA[:], in0=pD[0:96, :], in1=cl[:], op=ALU.max), X=[pDb], R=[cl.b], W=[tA.b])
            S.op(S.DVE, lambda h: h.scalar_tensor_tensor(out=tA[:], in0=pD[0:96, :], scalar=-1.0, in1=tA[:], op0=ALU.mult, op1=ALU.max),
                 X=[pDb], R=[tA.b], W=[tA.b])
            fill()
            S.op(S.ACT, lambda h: h.activation(out=tA[:], in_=tA[:], func=AF.Ln), R=[tA.b], W=[tA.b])
            S.op(S.ACT, lambda h: h.activation(out=tA[:], in_=tA[:], func=AF.Exp, scale=-1.0, bias=lnhalf[0:96, 0:1]), R=[tA.b, lnhalf.b], W=[tA.b])
            fill()
            S.op(S.DVE, lambda h: h.tensor_tensor(out=tB[:], in0=pN[0:96, :], in1=tA[:], op=ALU.mult), X=[pNb], R=[tA.b], W=[tB.b])
            S.op(S.DVE, lambda h: h.scalar_tensor_tensor(out=tB[:], in0=sg[:], scalar=1.0, in1=tB[:], op0=ALU.add, op1=ALU.mult),
                 R=[tB.b, sg.b], W=[tB.b])
            fill()
            S.op(S.ACT, lambda h: h.activation(out=tBb[:], in_=tB[:], func=AF.Copy), R=[tB.b], W=[tBb.b])
            bk = rot()
            pt, pbuf = self.pb[bk]
            self.mm(bk, pt[0:96, :], [(avg96b[:, :], tBb[:, :])], R=[avg96b.b, tBb.b])
            fill()
            S.op(S.DVE, lambda h, pt=pt: h.tensor_tensor(out=tC[:], in0=tB[:], in1=pt[0:96, :], op=ALU.subtract), X=[pbuf], R=[tB.b], W=[tC.b])
            S.op(S.ACT, lambda h: h.activation(out=tAb[:], in_=tC[:], func=AF.Square), R=[tC.b], W=[tAb.b])
            fill()
            bk = rot()
            pt, pbuf = self.pb[bk]
            self.mm(bk, pt[0:96, :], [(avg96b[:, :], tAb[:, :])], R=[avg96b.b, tAb.b])
            fill()
            S.op(S.ACT, lambda h, pt=pt: h.activation(out=tB[:], in_=pt[0:96, :], func=AF.Ln, bias=epsb[0:96, 0:1]), X=[pbuf], R=[epsb.b], W=[tB.b])
            S.op(S.ACT, lambda h: h.activation(out=tB[:], in_=tB[:], func=AF.Exp, scale=-0.5), R=[tB.b], W=[tB.b])
            fill()
            S.op(S.DVE, lambda h: h.scalar_tensor_tensor(out=tC[:], in0=tC[:], scalar=ngt[:, hd:hd + 1], in1=tB[:], op0=ALU.mult, op1=ALU.mult),
                 R=[tC.b, ngt.b, tB.b], W=[tC.b])
            S.op(S.POOL, lambda h: h.tensor_tensor(out=self.ycat[0:96, 3 + hd, gsl], in0=tC[:], in1=sz[:], op=ALU.mult),
                 R=[tC.b, sz.b], W=[self.ycat_b[3 + hd][g]])
            if chunks is not None:
                for _ in chunks:
                    pass

        self.load_w(wml[0], self.wpk[l][:, int(offs[8]):int(offs[9])])
        for _ in gen_A(0):
            pass
        for i in range(len(iters)):
            hd, g = iters[i]
            if g == 0 and hd + 1 < 4:
                self.load_w(wml[(hd + 1) % 2], self.wpk[l][:, int(offs[9 + hd]):int(offs[10 + hd])])
            chunks = gen_A(i + 1) if i + 1 < len(iters) else None
            run_BC(i, chunks)

    def build_memT(self):
        S = self.S
        mt = [self.tile("memin", [128, D], dma=True) for _ in range(2)]
        rot = self.bankrot([4, 5, 6, 7])
        for t in range(2):
            S.op(S.SP, lambda h, t=t: h.dma_start(out=mt[t][:], in_=self.mem_in[t * 128:(t + 1) * 128, :]), W=[mt[t].b], stream=mt[t].st)
            for half in range(2):
                bk = rot()
                pt, pbuf = self.pb[bk]
                for c in range(4):
                    kc = half * 4 + c
                    S.op(S.PE, lambda h, kc=kc, c=c, pt=pt, t=t: h.transpose(
                        pt[:, c * 128:(c + 1) * 128], mt[t][:, kc * 128:(kc + 1) * 128], self.identf[:]),
                        R=[mt[t].b, self.identf.b], X=[pbuf], sig=(c == 3))
                dst = self.memT[:, half * 4:(half + 1) * 4, t * 128:(t + 1) * 128]
                srcp = pt[:, :].rearrange("p (c t) -> p c t", c=4)
                S.op(S.DVE, lambda h, dst=dst, srcp=srcp: h.tensor_copy(out=dst, in_=srcp), X=[pbuf], W=[self.memT.b])

    def phase_mem(self, l):
        S = self.S
        sizes, offs = win_offsets()
        wm = self.tile("wm", [128, NKC * 512], BF16, dma=True)
        wr = self.tile("wr", [128, NKC * 512], BF16, dma=True)
        self.load_w(wm, self.wmempk[l])
        self.load_w(wr, self.wpk[l][:, int(offs[12]):int(offs[13])])
        wmv = wm[:, :].rearrange("p (c n) -> p c n", c=NKC)
        wrv = wr[:, :].rearrange("p (c n) -> p c n", c=NKC)
        KmT = self.tile("KmT", [128, 2, MEM_LEN], BF16)
        Vm = self.tile("Vm", [128, 2, 4, 128], BF16)
        rot = self.bankrot([0, 1])
        rot_s = self.bankrot([2, 3, 4])
        rot_o = self.bankrot([5, 6])
        for p in range(2):
            bk = rot()
            pt, pbuf = self.pb[bk]
            self.mm(bk, pt[:, 0:MEM_LEN], [(wmv[:, kc, p * 128:(p + 1) * 128], self.memT[:, kc, :]) for kc in range(NKC)],
                    R=[wm.b, self.memT.b])
            S.op(S.DVE, lambda h, pt=pt, p=p: h.tensor_copy(out=KmT[:, p, :], in_=pt[:, 0:MEM_LEN]), X=[pbuf], W=[KmT.b])
        S.op(S.POOL, lambda h: h.memset(Vm[:], 1.0), W=[Vm.b])
        for mb in range(2):
            bk = rot()
            pt, pbuf = self.pb[bk]
            self.mm(bk, pt[:, 0:256], [(self.memT[:, kc, mb * 128:(mb + 1) * 128], wmv[:, kc, 256:512]) for kc in range(NKC)],
                    R=[wm.b, self.memT.b])
            for h4 in range(4):
                c0 = 0 if h4 % 2 == 0 else 64
                S.op(S.DVE, lambda h, pt=pt, mb=mb, h4=h4, c0=c0: h.tensor_copy(
                    out=Vm[:, mb, h4, c0:c0 + 64], in_=pt[:, h4 * 64:(h4 + 1) * 64]), X=[pbuf], W=[Vm.b])
        QTm = [self.tile("QTm", [128, G], BF16) for _ in range(2)]
        szp = [self.tile("szp", [128, G], BF16) for _ in range(2)]
        thm = self.tile("thm", [128, G])
        PT = [self.tile("PTm", [128, G], BF16) for _ in range(3)]
        rd = [self.tile("rdm", [128, G]) for _ in range(2)]
        tn = [self.tile("tnm", [128, G]) for _ in range(2)]
        it = 0
        ip = 0
        for g in range(NG):
            gsl = slice(g * G, (g + 1) * G)
            for p in range(2):
                q, z = QTm[it % 2], szp[it % 2]
                it += 1
                bk = rot()
                pt, pbuf = self.pb[bk]
                self.mm(bk, pt[:, :], [(wrv[:, kc, p * 128:(p + 1) * 128], self.xT[:, kc, gsl]) for kc in range(NKC)],
                        R=[wr.b, self.xT_b[g]])
                S.op(S.ACT, lambda h, pt=pt, q=q: h.activation(out=q[:], in_=pt[:, :], func=AF.Copy, scale=0.125), X=[pbuf], W=[q.b])
                bk = rot()
                pt, pbuf = self.pb[bk]
                self.mm(bk, pt[:, :], [(wrv[:, kc, 256 + p * 128:256 + (p + 1) * 128], self.xT[:, kc, gsl]) for kc in range(NKC)],
                        R=[wr.b, self.xT_b[g]])
                S.op(S.ACT, lambda h, pt=pt: h.activation(out=thm[:], in_=pt[:, :], func=AF.Tanh, scale=0.5), X=[pbuf], W=[thm.b])
                S.op(S.DVE, lambda h, pt=pt, z=z: h.scalar_tensor_tensor(out=z[:], in0=thm[:], scalar=1.0, in1=pt[:, :], op0=ALU.add, op1=ALU.mult),
                     X=[pbuf], R=[thm.b], W=[z.b])
                for hh in range(2):
                    h4 = 2 * p + hh
                    r0 = 64 * hh
                    bo = rot_o()
                    po, pobuf = self.pb[bo]
                    sts = []
                    for mb in range(2):
                        bs = rot_s()
                        ps_, psbuf = self.pb[bs]
                        self.mm(bs, ps_[:, :], [(KmT[r0:r0 + 64, p, mb * 128:(mb + 1) * 128], q[r0:r0 + 64, :])], R=[KmT.b, q.b])
                        sts.append((ps_, psbuf))
                    for mb in range(2):
                        ps_, psbuf = sts[mb]
                        pt_ = PT[ip % 3]
                        ip += 1
                        S.op(S.ACT, lambda h, ps_=ps_, pt_=pt_: h.activation(out=pt_[:], in_=ps_[:, :], func=AF.Exp), X=[psbuf], W=[pt_.b])
                        self.mm(bo, po[:, :], [(Vm[:, mb, h4, :], pt_[:])], R=[Vm.b, pt_.b], start=(mb == 0), stop=(mb == 1))
                    self.attn_epilogue(po, pobuf, hh == 1, z, rd[h4 % 2], tn[h4 % 2], 7 + p, g, half=True)

    def attn_epilogue(self, po, pobuf, odd, sz, rd, tn, chunk, g, half=False, act_recip=False):
        S = self.S
        gsl = slice(g * G, (g + 1) * G)
        nr = slice(64, 128) if odd else slice(0, 64)
        dr = slice(0, 64) if odd else slice(64, 128)
        if act_recip:
            S.op(S.ACT, lambda h: h.activation(out=rd[nr, :], in_=po[dr, :], func=AF.Ln), X=[pobuf], W=[rd.b])
            S.op(S.ACT, lambda h: h.activation(out=rd[nr, :], in_=rd[nr, :], func=AF.Exp, scale=-1.0), R=[rd.b], W=[rd.b])
        else:
            S.op(S.DVE, lambda h: h.reciprocal(out=rd[nr, :], in_=po[dr, :]), X=[pobuf], W=[rd.b])
        if half:
            S.op(S.DVE, lambda h: h.scalar_tensor_tensor(out=tn[nr, :], in0=po[nr, :], scalar=0.5, in1=rd[nr, :], op0=ALU.mult, op1=ALU.mult),
                 X=[pobuf], R=[rd.b], W=[tn.b])
        else:
            S.op(S.DVE, lambda h: h.tensor_tensor(out=tn[nr, :], in0=po[nr, :], in1=rd[nr, :], op=ALU.mult), X=[pobuf], R=[rd.b], W=[tn.b])
        if isinstance(sz, tuple):
            szt, sg_, szb = sz
            S.op(S.POOL, lambda h: h.tensor_tensor(out=self.ycat[nr, chunk, gsl], in0=tn[nr, :], in1=szt[nr, sg_, :], op=ALU.mult),
                 R=[tn.b, szb], W=[self.ycat_b[chunk][g]])
        else:
            S.op(S.POOL, lambda h: h.tensor_tensor(out=self.ycat[nr, chunk, gsl], in0=tn[nr, :], in1=sz[nr, :], op=ALU.mult),
                 R=[tn.b, sz.b], W=[self.ycat_b[chunk][g]])

    def phase_final(self, l, x_src, x_dst, make_xT):
        S = self.S
        nc = self.nc
        wout = self.tile("wout", [128, 9 * D], BF16, dma=True)
        self.load_w(wout, self.woutpk[l])
        woutv = wout[:, :].rearrange("p (c n) -> p c n", c=9)
        gam = self.tile("gam", [128, D], dma=True)
        bet = self.tile("bet", [128, D], dma=True)
        S.op(S.SP, lambda h: h.dma_start(out=gam[:], in_=self.lng[l].partition_broadcast(128)), W=[gam.b], stream=gam.st)
        S.op(S.SP, lambda h: h.dma_start(out=bet[:], in_=self.lnb[l].partition_broadcast(128)), W=[bet.b], stream=bet.st)
        xin = [self.tile("xin", [128, D], dma=True) for _ in range(3)]
        epsf = self.tile("epsf", [128, 1])
        S.op(S.POOL, lambda h: h.memset(epsf[:], float(LN_EPS)), W=[epsf.b])
        tt = [self.tile("tt", [128, D]) for _ in range(2)]
        xo = [self.tile("xo", [128, D]) for _ in range(2)]
        stats = [self.tile("stats", [128, 16]) for _ in range(2)]
        krows = [128, 128, 128, 96, 96, 96, 96, 128, 128]
        rot_y = self.bankrot([0, 1, 2, 3])
        rot_t = self.bankrot([4, 5, 6, 7])

        def load_x(t):
            xt = xin[t % 3]
            S.op(S.SP, lambda h: h.dma_start(out=xt[:], in_=x_src[t * 128:(t + 1) * 128, :]), W=[xt.b], stream=xt.st)

        nmr = [self.tile("nmr", [128, 2]) for _ in range(2)]
        junk = self.tile("junk", [128, D], BF16)

        held = {}

        def stage_M_pe(t):
            g = t // 4
            hb = []
            for half in range(2):
                bk = rot_y()
                pt, pbuf = self.pb[bk]
                pairs = [(self.ycat[0:krows[c], c, t * 128:(t + 1) * 128], woutv[0:krows[c], c, half * 512:(half + 1) * 512])
                         for c in range(9)]
                self.mm(bk, pt[:, :], pairs, R=[wout.b] + [self.ycat_b[c][g] for c in range(9)])
                hb.append((pt, pbuf))
            held[t] = hb

        def stage_M_dve(t):
            xt, tq, sq = xin[t % 3], tt[t % 2], stats[t % 2]
            for half in range(2):
                pt, pbuf = held[t][half]
                S.op(S.DVE, lambda h, pt=pt, half=half: h.scalar_tensor_tensor(
                    out=tq[:, half * 512:(half + 1) * 512], in0=xt[:, half * 512:(half + 1) * 512], scalar=float(ALPHA),
                    in1=pt[:, :], op0=ALU.mult, op1=ALU.add), R=[xt.b], X=[pbuf], W=[tq.b])
            for half in range(2):
                S.op(S.DVE, lambda h, half=half: h.bn_stats(out=sq[:, half * 6:(half + 1) * 6],
                                                            in_=tq[:, half * 512:(half + 1) * 512]), R=[tq.b], W=[sq.b])

        def stage_N(t):
            tq, xq, sq, nm = tt[t % 2], xo[t % 2], stats[t % 2], nmr[t % 2]
            S.op(S.DVE, lambda h: h.bn_aggr(out=sq[:, 12:14], in_=sq[:, 0:12]), R=[sq.b], W=[sq.b])
            S.op(S.ACT, lambda h: h.activation(out=sq[:, 14:15], in_=sq[:, 13:14], func=AF.Ln, bias=epsf[:, 0:1]), R=[sq.b, epsf.b], W=[sq.b])
            S.op(S.ACT, lambda h: h.activation(out=nm[:, 0:1], in_=sq[:, 14:15], func=AF.Exp, scale=-0.5), R=[sq.b], W=[nm.b])
            S.op(S.DVE, lambda h: h.tensor_scalar(out=nm[:, 1:2], in0=sq[:, 12:13], scalar1=nm[:, 0:1], scalar2=-1.0, op0=ALU.mult, op1=ALU.mult),
                 R=[sq.b, nm.b], W=[nm.b])
            S.op(S.ACT, lambda h: h.activation(out=tq[:], in_=tq[:], func=AF.Identity, scale=nm[:, 0:1], bias=nm[:, 1:2]), R=[tq.b, nm.b], W=[tq.b])

        def stage_N_b(t):
            tq, xq = tt[t % 2], xo[t % 2]
            S.op(S.POOL, lambda h: h.tensor_tensor(out=xq[:], in0=tq[:], in1=gam[:], op=ALU.mult), R=[tq.b, gam.b], W=[xq.b])
            S.op(S.POOL, lambda h: h.tensor_tensor(out=xq[:], in0=xq[:], in1=bet[:], op=ALU.add), R=[xq.b, bet.b], W=[xq.b])
            st = self.stq[t % 2]
            S.op(S.SP, lambda h: h.dma_start(out=x_dst[t * 128:(t + 1) * 128, :], in_=xq[:]), R=[xq.b], stream=st)

        load_x(0)
        load_x(1)
        stage_M_pe(0)
        stage_M_dve(0)
        for t in range(NT):
            if t + 2 < NT:
                load_x(t + 2)
            if t + 1 < NT:
                stage_M_pe(t + 1)
            stage_N(t)
            stage_N_b(t)
            if t + 1 < NT:
                stage_M_dve(t + 1)
            if make_xT and t >= 1:
                self.transpose_into_xT(xo[(t - 1) % 2], t - 1, rot_t)
        if make_xT:
            self.transpose_into_xT(xo[(NT - 1) % 2], NT - 1, rot_t)

    def _build(self):
        S = self.S
        L = self.L
        for l in range(L):
            x_src = self.x_in if l == 0 else self.xbuf[(l - 1) % 2]
            x_dst = self.out if l == L - 1 else self.xbuf[l % 2]
            if l == 0:
                self.run_phase(self.build_xT_from_dram, x_src)
            if l == 0:
                self.run_phase(self.build_memT)
            self.run_phase(self.phase_fox, l)
            if "noml" not in self.dbg:
                self.run_phase(self.phase_mlstm, l)
            self.run_phase(self.phase_mem, l)
            if "ycat" in self.dbg:
                S.barrier()
                d = self.dbg_dram("dbg_ycat", [128, 9 * S_LEN], BF16)
                S.op(S.SP, lambda h: h.dma_start(out=d, in_=self.ycat[:, :, :].rearrange("p c s -> p (c s)")),
                     R=[b for cb in self.ycat_b for b in cb], stream=self.stq[0])
            self.run_phase(self.phase_final, l, x_src, x_dst, make_xT=(l < L - 1))
        S.finish(S.SP)


def _pack_win(w):
    w3 = w.reshape(NKC, 128, IN_COLS).transpose(1, 0, 2)
    groups = []
    groups.append(np.arange(O_FF, O_FF + 6))
    for h in range(6):
        cols = [np.arange(o + 64 * h, o + 64 * (h + 1)) for o in (O_FQ, O_FK, O_FV)]
        if h % 2 == 0:
            cols.append(np.arange(O_FZ + 64 * h, O_FZ + 64 * (h + 2)))
        groups.append(np.concatenate(cols))
    groups.append(np.concatenate([np.arange(O_MI, O_MI + 4), np.arange(O_MF, O_MF + 4)]))
    for h in range(4):
        groups.append(np.concatenate([np.arange(o + 96 * h, o + 96 * (h + 1)) for o in (O_MQ, O_MK, O_MV, O_MO, O_MZ)]))
    groups.append(np.concatenate([np.arange(O_RQ, O_RQ + 256), np.arange(O_RZ, O_RZ + 256)]))
    parts = [np.ascontiguousarray(w3[:, :, gidx]).reshape(128, -1) for gidx in groups]
    return np.concatenate(parts, axis=1)


def win_offsets():
    sizes = [6] + [320, 192] * 3 + [8] + [480] * 4 + [512]
    offs = np.concatenate([[0], np.cumsum([NKC * s for s in sizes])])
    return sizes, offs


def _pack_wout(w):
    out = np.zeros((128, 9, D), np.float32)
    r = 0
    for c, k in enumerate([128, 128, 128, 96, 96, 96, 96, 128, 128]):
        out[0:k, c, :] = w[r:r + k, :]
        r += k
    return out.reshape(128, 9 * D)


def _pack_wmem(w):
    return np.ascontiguousarray(w.reshape(NKC, 128, 512).transpose(1, 0, 2)).reshape(128, NKC * 512)


def pack_layers(inp, layers):
    f = np.float32
    d = {}
    d["wpk"] = np.stack([_pack_win(np.asarray(inp["w_in"][l], f)) for l in layers])
    d["woutpk"] = np.stack([_pack_wout(np.asarray(inp["w_out"][l], f)) for l in layers])
    d["wmempk"] = np.stack([_pack_wmem(np.asarray(inp["w_mem_kv"][l], f)) for l in layers])
    d["foxb"] = np.stack([np.asarray(inp["fox_f_bias"][l], f).reshape(6, 1) for l in layers])
    d["mib"] = np.stack([np.asarray(inp["mlstm_i_bias"][l], f).reshape(4, 1) for l in layers])
    d["mfb"] = np.stack([np.asarray(inp["mlstm_f_bias"][l], f).reshape(4, 1) for l in layers])
    d["convw"] = np.stack([np.ascontiguousarray(np.asarray(inp["mlstm_conv_w"][l], f).reshape(4, 2, 4, 96).transpose(3, 1, 2, 0)).reshape(96, 32)
                           for l in layers])
    d["convb"] = np.stack([np.ascontiguousarray(np.asarray(inp["mlstm_conv_b"][l], f).reshape(2, 4, 96).transpose(2, 0, 1)).reshape(96, 8)
                           for l in layers])
    d["ng"] = np.stack([np.ascontiguousarray(np.asarray(inp["mlstm_norm_g"][l], f).reshape(4, 96).T) for l in layers])
    d["lng"] = np.stack([np.asarray(inp["ln_g"][l], f).reshape(1, D) for l in layers])
    d["lnb"] = np.stack([np.asarray(inp["ln_b"][l], f).reshape(1, D) for l in layers])
    return d


_PROG_CACHE = {}


def get_prog(L, **kw):
    key = (L, tuple(sorted(kw.items())))
    if key not in _PROG_CACHE:
        _PROG_CACHE[key] = Prog(L, **kw)
    return _PROG_CACHE[key]


def run_layers(x, mem, packed, n_cores=8):
    L = packed["wpk"].shape[0]
    prog = get_prog(L)
    B = x.shape[0]
    in_maps = []
    for c in range(n_cores):
        b = c % B
        m = {"x": np.ascontiguousarray(x[b]), "mem": np.ascontiguousarray(mem[b])}
        m.update(packed)
        in_maps.append(m)
    res = run_bass_kernel_spmd(prog.nc, in_maps, core_ids=list(range(n_cores)))
    return np.stack([np.asarray(res.results[b]["out"]) for b in range(B)])


FUSED = True


def kernel(x, mem, w_in, fox_f_bias, mlstm_conv_w, mlstm_conv_b, mlstm_i_bias, mlstm_f_bias,
           mlstm_norm_g, w_mem_kv, w_out, ln_g, ln_b):
    inp = dict(w_in=w_in, fox_f_bias=fox_f_bias, mlstm_conv_w=mlstm_conv_w, mlstm_conv_b=mlstm_conv_b,
               mlstm_i_bias=mlstm_i_bias, mlstm_f_bias=mlstm_f_bias, mlstm_norm_g=mlstm_norm_g,
               w_mem_kv=w_mem_kv, w_out=w_out, ln_g=ln_g, ln_b=ln_b)
    x = np.asarray(x, np.float32)
    mem = np.asarray(mem, np.float32)
    if FUSED:
        return run_layers(x, mem, pack_layers(inp, list(range(DEPTH))))
    for l in range(DEPTH):
        x = run_layers(x, mem, pack_layers(inp, [l]))
    return x
```

```python
import math
from contextlib import ExitStack
import numpy as np
import concourse.bass as bass
import concourse.mybir as mybir
from concourse.bass_utils import run_bass_kernel_spmd

F32 = mybir.dt.float32
BF16 = mybir.dt.bfloat16
AF = mybir.ActivationFunctionType
ALU = mybir.AluOpType

D = 1024
S_LEN = 4096
DEPTH = 4
NKC = 8
G = 512
NG = S_LEN // G
NT = S_LEN // 128
MEM_LEN = 256
LN_EPS = 1e-5
ALPHA = (2.0 * DEPTH) ** 0.25
IN_COLS = 3982
O_FQ, O_FK, O_FV, O_FF, O_FZ = 0, 384, 768, 1152, 1158
O_MQ, O_MK, O_MV, O_MI, O_MF, O_MO, O_MZ = 1542, 1926, 2310, 2694, 2698, 2702, 3086
O_RQ, O_RZ = 3470, 3726


class Stream:
    def __init__(self, name, sem, inc, q, dma):
        self.name, self.sem, self.inc, self.q, self.dma = name, sem, inc, q, dma
        self.n = 0


class Q:
    def __init__(self, name, h):
        self.name, self.h = name, h
        self.seen = {}
        self.stream = None


class Buf:
    __slots__ = ("name", "w", "r")

    def __init__(self, name):
        self.name = name
        self.w = None
        self.r = {}


class Sched:
    def __init__(self, nc):
        self.nc = nc
        self.PE = self._mkq("pe", nc.tensor)
        self.ACT = self._mkq("act", nc.scalar)
        self.DVE = self._mkq("dve", nc.vector)
        self.POOL = self._mkq("pool", nc.gpsimd)
        self.SP = Q("sp", nc.sync)
        self.queues = [self.PE, self.ACT, self.DVE, self.POOL, self.SP]
        self.streams = [q.stream for q in self.queues if q.stream is not None]
        self.nwaits = 0
        self.nops = 0

    def _mkq(self, name, h):
        q = Q(name, h)
        q.stream = Stream(name, self.nc.alloc_semaphore("s_" + name), 1, q, False)
        return q

    def dma_stream(self, name):
        s = Stream(name, self.nc.alloc_semaphore("d_" + name), 16, None, True)
        self.streams.append(s)
        return s

    def _wait(self, q, s, c):
        if q.seen.get(s, 0) >= c:
            return
        assert c <= s.n, f"wait on unissued signal {s.name} {c} > {s.n}"
        q.h.wait_ge(s.sem, c * s.inc)
        q.seen[s] = c
        self.nwaits += 1

    def op(self, q, emit, R=(), W=(), X=(), sig=True, stream=None):
        st = stream if stream is not None else q.stream
        deps = {}

        def add(d, same_ok):
            s, c = d
            if same_ok and (not s.dma) and s.q is q and stream is None:
                return
            if deps.get(s, 0) < c:
                deps[s] = c

        pe = q is self.PE
        for b in R:
            if b.w is not None:
                add(b.w, pe)
        for b in W:
            if b.w is not None:
                add(b.w, pe)
            for s, c in b.r.items():
                add((s, c), pe)
        for b in X:
            if b.w is not None:
                add(b.w, pe)
            for s, c in b.r.items():
                add((s, c), pe)
        for s, c in deps.items():
            if s.dma:
                c = s.n
            self._wait(q, s, c)
        ins = emit(q.h)
        cnt = st.n + 1
        if sig:
            ins.then_inc(st.sem, st.inc)
            st.n = cnt
        for b in R:
            if b.r.get(st, 0) < cnt:
                b.r[st] = cnt
        for b in W:
            b.w = (st, cnt)
            b.r = {}
        for b in X:
            b.w = (st, cnt)
            b.r = {}
        self.nops += 1
        return ins

    def barrier(self):
        for q in self.queues:
            for s in self.streams:
                if s.n > 0:
                    self._wait(q, s, s.n)

    def finish(self, q):
        for s in self.streams:
            if s.n > 0 and not (s.q is q):
                self._wait(q, s, s.n)


class Tile:
    def __init__(self, handle, name, stream=None):
        self.t = handle
        self.b = Buf(name)
        self.st = stream

    def __getitem__(self, k):
        return self.t[k]


class Prog:
    def __init__(self, L, stop_after=None, dbg=()):
        import os
        dbg = tuple(dbg) + tuple(x for x in os.environ.get("KDBG", "").split(",") if x)
        self.L = L
        self.stop_after = stop_after
        self.dbg = set(dbg)
        nc = bass.Bass("TRN2", target_bir_lowering=False)
        self.nc = nc
        self.S = Sched(nc)
        self.uid = 0
        self.stack = None
        self.stream_pool = []
        self.all_streams = []
        self.phase_streams = []
        self._decl_dram()
        self._alloc()
        self._consts()
        self._build()

    def tile(self, name, shape, dt=F32, dma=False):
        self.uid += 1
        nm = f"{name}_{self.uid}"
        if self.stack is not None:
            hnd = self.stack.enter_context(self.nc.sbuf_tensor(nm, list(shape), dt))
        else:
            hnd = self.nc.alloc_sbuf_tensor(nm, list(shape), dt)
        st = None
        if dma:
            if self.stream_pool:
                st = self.stream_pool.pop()
            else:
                st = self.S.dma_stream(f"ds{len(self.all_streams)}")
                self.all_streams.append(st)
            if self.stack is not None:
                self.phase_streams.append(st)
        return Tile(hnd, nm, st)

    def sub_arena(self):
        prog = self

        class _Sub:
            def __enter__(self_):
                self_.outer = prog.stack
                self_.es = ExitStack()
                self_.es.__enter__()
                prog.stack = self_.es
                return self_

            def __exit__(self_, *exc):
                prog.S.barrier()
                prog.stack = self_.outer
                return self_.es.__exit__(*exc)
        return _Sub()

    def run_phase(self, fn, *a, **kw):
        assert self.stack is None
        with ExitStack() as es:
            self.stack = es
            self.phase_streams = []
            fn(*a, **kw)
            self.S.barrier()
            self.stream_pool.extend(self.phase_streams)
            self.phase_streams = []
            self.stack = None

    def dram_in(self, name, shape, dt=F32):
        return self.nc.dram_tensor(name, list(shape), dt, kind="ExternalInput").ap()

    def _decl_dram(self):
        L = self.L
        nc = self.nc
        self.x_in = self.dram_in("x", [S_LEN, D])
        self.mem_in = self.dram_in("mem", [MEM_LEN, D])
        self.wpk = self.dram_in("wpk", [L, 128, NKC * IN_COLS])
        self.woutpk = self.dram_in("woutpk", [L, 128, 9 * D])
        self.wmempk = self.dram_in("wmempk", [L, 128, NKC * 512])
        self.foxb = self.dram_in("foxb", [L, 6, 1])
        self.mib = self.dram_in("mib", [L, 4, 1])
        self.mfb = self.dram_in("mfb", [L, 4, 1])
        self.convw = self.dram_in("convw", [L, 96, 32])
        self.convb = self.dram_in("convb", [L, 96, 8])
        self.ngd = self.dram_in("ng", [L, 96, 4])
        self.lng = self.dram_in("lng", [L, 1, D])
        self.lnb = self.dram_in("lnb", [L, 1, D])
        self.out = nc.dram_tensor("out", [S_LEN, D], F32, kind="ExternalOutput").ap()
        self.xbuf = [nc.dram_tensor(f"xbuf{i}", [S_LEN, D], F32).ap() for i in range(2)] if L > 1 else []
        self.dbg_out = {}
        self.gdram = nc.dram_tensor("gdram", [2, 4 * NG, G], F32).ap()
        self.gdram_b = [Buf(f"gdram_g{g}") for g in range(NG)]

    def dbg_dram(self, name, shape, dt=F32):
        ap = self.nc.dram_tensor(name, list(shape), dt, kind="ExternalOutput").ap()
        self.dbg_out[name] = ap
        return ap

    def _alloc(self):
        nc, S = self.nc, self.S
        self.pb = []
        for i in range(8):
            t = nc.alloc_psum_tensor(f"pb{i}", [128, 512], F32)
            self.pb.append((t, Buf(f"pb{i}")))
        self.xT = nc.alloc_sbuf_tensor("xT", [128, NKC, S_LEN], BF16)
        self.xT_b = [Buf(f"xT_g{g}") for g in range(NG)]
        self.ycat = nc.alloc_sbuf_tensor("ycat", [128, 9, S_LEN], BF16)
        self.ycat_b = [[Buf(f"ycat_{c}_{g}") for g in range(NG)] for c in range(9)]
        self.stq = [S.dma_stream("stq0"), S.dma_stream("stq1")]
        self.memT = self.tile("memT", [128, NKC, MEM_LEN], BF16)
        self.gtok_all = self.tile("gtok_all", [128, NT, 4])
        self.gtokS_all = self.tile("gtokS_all", [128, NT, 4])

    def _consts(self):
        S = self.S
        self.onesf = self.tile("onesf", [128, 128])
        self.zerf = self.tile("zerf", [128, 128])
        self.identf = self.tile("identf", [128, 128])
        self.identb = self.tile("identb", [128, 128], BF16)
        self.mnegf = self.tile("mnegf", [128, 128])
        self.mnegb = self.tile("mnegb", [128, 128], BF16)
        self.avg96 = self.tile("avg96", [128, 96])
        self.fillsrc = self.tile("fillsrc", [128, 512], BF16)
        S.op(S.POOL, lambda h: h.memset(self.fillsrc[:], 0.5), W=[self.fillsrc.b])
        o, z = self.onesf, self.zerf
        S.op(S.POOL, lambda h: h.memset(o[:], 1.0), W=[o.b])
        S.op(S.POOL, lambda h: h.memset(z[:], 0.0), W=[z.b])
        S.op(S.POOL, lambda h: h.memset(self.avg96[:], 1.0 / 96.0), W=[self.avg96.b])
        S.op(S.POOL, lambda h: h.affine_select(out=self.identf[:], in_=o[:], pattern=[[1, 128]],
                                               compare_op=ALU.is_equal, fill=0.0, base=0, channel_multiplier=-1),
             R=[o.b], W=[self.identf.b])
        S.op(S.DVE, lambda h: h.tensor_copy(out=self.identb[:], in_=self.identf[:]), R=[self.identf.b], W=[self.identb.b])
        S.op(S.POOL, lambda h: h.affine_select(out=self.mnegf[:], in_=z[:], pattern=[[1, 128]],
                                               compare_op=ALU.is_ge, fill=-30000.0, base=0, channel_multiplier=-1),
             R=[z.b], W=[self.mnegf.b])
        S.op(S.DVE, lambda h: h.tensor_copy(out=self.mnegb[:], in_=self.mnegf[:]), R=[self.mnegf.b], W=[self.mnegb.b])

    def mm(self, bank, out_ap, pairs, R, start=True, stop=True, sgc=False):
        S = self.S
        n = len(pairs)
        for i, (lhsT, rhs) in enumerate(pairs):
            S.op(S.PE, lambda h, lhsT=lhsT, rhs=rhs, i=i: h.matmul(
                out_ap, lhsT=lhsT, rhs=rhs, start=(start and i == 0), stop=(stop and i == n - 1), skip_group_check=sgc),
                R=R, X=[self.pb[bank][1]], sig=(i == n - 1))

    def filler(self, n=256, bank=7):
        S = self.S
        pt, pbuf = self.pb[bank]
        S.op(S.PE, lambda h: h.matmul(pt[:, 0:n], lhsT=self.identb[:], rhs=self.fillsrc[:, 0:n], start=True, stop=True),
             R=[self.identb.b, self.fillsrc.b], X=[pbuf], sig=False)

    def bankrot(self, banks):
        st = {"i": 0}

        def nxt():
            b = banks[st["i"] % len(banks)]
            st["i"] += 1
            return b
        return nxt

    def build_xT_from_dram(self, x_src):
        S = self.S
        xin = [self.tile("xin", [128, D], dma=True) for _ in range(2)]
        rot = self.bankrot([0, 1, 2, 3])
        for t in range(NT):
            xt = xin[t % 2]
            S.op(S.SP, lambda h, xt=xt, t=t: h.dma_start(out=xt[:], in_=x_src[t * 128:(t + 1) * 128, :]),
                 W=[xt.b], stream=xt.st)
            self.transpose_into_xT(xt, t, rot)

    def transpose_into_xT(self, src, t, rot, eng=None):
        S = self.S
        for half in range(2):
            bk = rot()
            pt, pbuf = self.pb[bk]
            for c in range(4):
                kc = half * 4 + c
                S.op(S.PE, lambda h, kc=kc, c=c, pt=pt: h.transpose(
                    pt[:, c * 128:(c + 1) * 128], src[:, kc * 128:(kc + 1) * 128], self.identf[:]),
                    R=[src.b, self.identf.b], X=[pbuf], sig=(c == 3))
            e = eng if eng is not None else (S.ACT if half == 0 else S.DVE)
            dst = self.xT[:, half * 4:(half + 1) * 4, t * 128:(t + 1) * 128]
            srcp = pt[:, :].rearrange("p (c t) -> p c t", c=4)
            if e is S.ACT:
                S.op(e, lambda h, dst=dst, srcp=srcp: h.activation(out=dst, in_=srcp, func=AF.Copy),
                     X=[pbuf], W=[self.xT_b[t // 4]])
            else:
                S.op(e, lambda h, dst=dst, srcp=srcp: h.tensor_copy(out=dst, in_=srcp),
                     X=[pbuf], W=[self.xT_b[t // 4]])

    def load_w(self, dst_tile, src_ap):
        S = self.S
        self.n_sw = getattr(self, "n_sw", 0) + 1
        st = S.dma_stream(f"sw{self.n_sw}")
        S.op(S.POOL, lambda h: h.dma_start(out=dst_tile[:], in_=src_ap), W=[dst_tile.b], stream=st)


    def phase_fox(self, l):
        S = self.S
        sizes, offs = win_offsets()
        rot = self.bankrot([0, 1])
        rot_s = self.bankrot([2, 3, 4])
        rot_o = self.bankrot([5, 6])
        shiftrows = self.tile("shiftrows", [6, S_LEN], BF16)
        negc_tok = self.tile("negc_tok", [128, NT, 6])
        negcref_rep = self.tile("negcref_rep", [128, 6, NG])
        with self.sub_arena():
            rotG = self.bankrot([2, 3, 4])
            gens = [self.fox_prep(l, offs, rot, shiftrows, negc_tok, negcref_rep),
                    self.mlstm_gates(l, offs, rotG, self.gtok_all, self.gtokS_all, self.gdram, self.gdram_b, float(96 ** -0.5))]
            alive = True
            while alive:
                alive = False
                for gn in gens:
                    if next(gn, "end") != "end":
                        alive = True
        wh = [self.tile("wh", [128, NKC * 320], BF16, dma=True), self.tile("wh", [128, NKC * 192], BF16, dma=True)]
        szp = self.tile("szp", [128, NG, G], BF16)
        szp_b = [Buf(f"szp_g{g}") for g in range(NG)]
        KaT = self.tile("KaT", [65, S_LEN], BF16)
        KaT_b = [Buf(f"KaT_g{g}") for g in range(NG)]
        S.op(S.POOL, lambda h: h.memset(KaT[64:65, :], 1.0), W=KaT_b)
        Vaug = self.tile("Vaug", [128, NT, 192], BF16)
        V_ones = Buf("V_ones")
        V_b = [[Buf(f"V_{par}_g{g}") for g in range(NG)] for par in range(2)]
        S.op(S.POOL, lambda h: h.memset(Vaug[:, :, 64:128], 1.0), W=[V_ones])
        QaT = [self.tile("QaT", [65, G], BF16, dma=True) for _ in range(2)]
        szf = [self.tile("szf", [128, G], BF16) for _ in range(2)]
        PT = [self.tile("PTf", [128, G], BF16) for _ in range(3)]
        rd = [self.tile("rdf", [128, G]) for _ in range(1)]
        tn = [self.tile("tnf", [128, G]) for _ in range(1)]
        thf = [self.tile("thf", [128, G]) for _ in range(2)]
        tabs = [self.tile("tab", [128, NT, NG]) for _ in range(2)]
        ipc = {"i": 0}

        def build_tab(hd):
            tab = tabs[hd % 2]
            for qg in range(NG):
                S.op(S.DVE, lambda h, qg=qg, tab=tab, hd=hd: h.tensor_scalar(
                    out=tab[:, :, qg], in0=negc_tok[:, :, hd], scalar1=negcref_rep[:, hd, qg:qg + 1], scalar2=None, op0=ALU.subtract),
                    R=[negc_tok.b, negcref_rep.b], W=[tab.b])

        def gen_proj(hd, g):
            odd = hd % 2
            W_ = wh[hd % 2]
            Wv = W_[:, :].rearrange("p (c n) -> p c n", c=NKC)
            gsl = slice(g * G, (g + 1) * G)
            qa = QaT[g % 2]
            vc0 = 128 if odd else 0
            RR = [W_.b, self.xT_b[g]]
            bk = rot()
            pt, pbuf = self.pb[bk]
            for kc in range(NKC):
                S.op(S.PE, lambda h, kc=kc, pt=pt: h.matmul(pt[:, :], lhsT=Wv[:, kc, 0:128], rhs=self.xT[:, kc, gsl],
                                                            start=(kc == 0), stop=(kc == NKC - 1)), R=RR, X=[pbuf], sig=(kc == NKC - 1))
                if kc % 2 == 1 and kc < NKC - 1:
                    yield
            S.op(S.DVE, lambda h, pt=pt: h.tensor_scalar(out=qa[0:64, :], in0=pt[0:64, :], scalar1=0.125, scalar2=None, op0=ALU.mult), X=[pbuf], W=[qa.b])
            S.op(S.DVE, lambda h, pt=pt: h.tensor_copy(out=KaT[0:64, gsl], in_=pt[64:128, :]), X=[pbuf], W=[KaT_b[g]])
            S.op(S.SP, lambda h: h.dma_start(out=qa[64:65, :], in_=shiftrows[hd:hd + 1, gsl]), R=[shiftrows.b], W=[qa.b], stream=qa.st)
            yield
            bk = rot()
            pt, pbuf = self.pb[bk]
            for j in range(4):
                blk = slice(g * G + j * 128, g * G + (j + 1) * 128)
                for kc in range(NKC):
                    S.op(S.PE, lambda h, kc=kc, pt=pt, j=j, blk=blk: h.matmul(
                        pt[:, j * 64:(j + 1) * 64], lhsT=self.xT[:, kc, blk], rhs=Wv[:, kc, 128:192],
                        start=(kc == 0), stop=(kc == NKC - 1)), R=RR, X=[pbuf], sig=(kc == NKC - 1))
                    if kc == 3:
                        yield
                yield
            S.op(S.DVE, lambda h, pt=pt: h.tensor_copy(
                out=Vaug[:, 4 * g:4 * g + 4, vc0:vc0 + 64], in_=pt[:, 0:256].rearrange("p (j c) -> p j c", j=4)),
                X=[pbuf], W=[V_b[odd][g]])
            yield
            if not odd:
                bk = rot()
                pt, pbuf = self.pb[bk]
                for kc in range(NKC):
                    S.op(S.PE, lambda h, kc=kc, pt=pt: h.matmul(pt[:, :], lhsT=Wv[:, kc, 192:320], rhs=self.xT[:, kc, gsl],
                                                                start=(kc == 0), stop=(kc == NKC - 1)), R=RR, X=[pbuf], sig=(kc == NKC - 1))
                    if kc % 2 == 1 and kc < NKC - 1:
                        yield
                th = thf[g % 2]
                S.op(S.ACT, lambda h, pt=pt: h.activation(out=th[:, :], in_=pt[:, :], func=AF.Tanh, scale=0.5), X=[pbuf], W=[th.b])
                S.op(S.DVE, lambda h, pt=pt: h.scalar_tensor_tensor(out=szp[:, g, :], in0=th[:, :], scalar=1.0, in1=pt[:, :], op0=ALU.add, op1=ALU.mult),
                     X=[pbuf], R=[th.b], W=[szp_b[g]])
            yield

        N_CHUNKS = 20

        def attention(hd, g, chunks):
            odd = hd % 2
            lc0 = 64 if odd else 0
            tab = tabs[hd % 2]
            qa = QaT[g % 2]
            nkb = 4 * g + 4
            emitted = {"n": 0}
            bo = rot_o()
            po, pobuf = self.pb[bo]

            def issue_st(kb):
                diag = kb >= 4 * g
                qoff = (kb - 4 * g) * 128 if diag else 0
                n = G - qoff
                bs = rot_s()
                ps_, psbuf = self.pb[bs]
                self.mm(bs, ps_[:, 0:n], [(KaT[0:65, kb * 128:(kb + 1) * 128], qa[0:65, qoff:G])], R=[KaT_b[kb // 4], qa.b],
                        start=True, stop=not diag)
                if diag:
                    self.mm(bs, ps_[:, 0:128], [(self.identb[:], self.mnegb[:])], R=[self.identb.b, self.mnegb.b], start=False, stop=True)
                return ps_, psbuf, qoff, n

            nxt = issue_st(0)
            for kb in range(nkb):
                cur = nxt
                if kb + 1 < nkb:
                    nxt = issue_st(kb + 1)
                ps_, psbuf, qoff, n = cur
                pt_ = PT[ipc["i"] % 3]
                ipc["i"] += 1
                S.op(S.ACT, lambda h, ps_=ps_, pt_=pt_, n=n, kb=kb: h.activation(
                    out=pt_[:, 0:n], in_=ps_[:, 0:n], func=AF.Exp, bias=tab[:, kb, g:g + 1]), X=[psbuf], R=[tab.b], W=[pt_.b])
                did = False
                if chunks is not None:
                    want = ((kb + 1) * N_CHUNKS + nkb - 1) // nkb
                    while emitted["n"] < want:
                        if next(chunks, "end") != "end":
                            did = True
                        emitted["n"] += 1
                if not did:
                    self.filler(256)
                self.mm(bo, po[:, qoff:G], [(Vaug[:, kb, lc0:lc0 + 128], pt_[:, 0:n])], R=[V_b[odd][kb // 4], V_ones, pt_.b],
                        start=(kb == 0), stop=(kb == nkb - 1))
            if chunks is not None:
                for _ in chunks:
                    pass
            self.attn_epilogue(po, pobuf, bool(odd), (szp, g, szp_b[g]), rd[0], tn[0], hd // 2, g, half=True)

        self.load_w(wh[0], self.wpk[l][:, int(offs[1]):int(offs[2])])
        build_tab(0)
        for _ in gen_proj(0, 0):
            pass
        for hd in range(6):
            if hd + 1 < 6:
                self.load_w(wh[(hd + 1) % 2], self.wpk[l][:, int(offs[2 + hd]):int(offs[3 + hd])])
                build_tab(hd + 1)
            for g in range(NG):
                if g + 1 < NG:
                    chunks = gen_proj(hd, g + 1)
                elif hd + 1 < 6:
                    chunks = gen_proj(hd + 1, 0)
                else:
                    chunks = None
                attention(hd, g, chunks)


    def fox_prep(self, l, offs, rot, shiftrows, negc_tok, negcref_rep):
        S = self.S
        wff = self.tile("wff", [128, NKC * 6], BF16, dma=True)
        self.load_w(wff, self.wpk[l][:, int(offs[0]):int(offs[1])])
        wffv = wff[:, :].rearrange("p (c n) -> p c n", c=NKC)
        fb = self.tile("fb", [6, 1], dma=True)
        S.op(S.SP, lambda h: h.dma_start(out=fb[:], in_=self.foxb[l]), W=[fb.b], stream=fb.st)
        nfb = self.tile("nfb", [6, 1])
        S.op(S.DVE, lambda h: h.tensor_scalar(out=nfb[:], in0=fb[:], scalar1=-1.0, scalar2=None, op0=ALU.mult), R=[fb.b], W=[nfb.b])
        negcref6 = self.tile("negcref6", [6, NG])
        oh6 = self.tile("oh6", [6, 6, 128])
        for hh in range(6):
            S.op(S.POOL, lambda h, hh=hh: h.affine_select(
                out=oh6[:, hh, :], in_=self.onesf[0:6, :], pattern=[[0, 128]], compare_op=ALU.is_equal,
                fill=0.0, base=-hh, channel_multiplier=1), R=[self.onesf.b], W=[oh6.b])
        e_t = [self.tile("e_t", [6, G]) for _ in range(2)]
        negc = [self.tile("negc", [6, G]) for _ in range(2)]
        for g in range(NG):
            gsl = slice(g * G, (g + 1) * G)
            bk = rot()
            pt, pbuf = self.pb[bk]
            self.mm(bk, pt[0:6, :], [(wffv[:, kc, :], self.xT[:, kc, gsl]) for kc in range(NKC)], R=[wff.b, self.xT_b[g]])
            e, nc_, ncp = e_t[g % 2], negc[g % 2], negc[(g - 1) % 2]
            lt = e
            S.op(S.ACT, lambda h, pt=pt, e=e: h.activation(out=e[:], in_=pt[0:6, :], func=AF.Exp, scale=-1.0, bias=nfb[:, 0:1]),
                 X=[pbuf], R=[nfb.b], W=[e.b])
            S.op(S.ACT, lambda h, e=e, lt=lt: h.activation(out=lt[:], in_=e[:], func=AF.Ln, bias=1.0), R=[e.b], W=[lt.b])
            init = 0.0 if g == 0 else ncp[:, G - 1:G]
            S.op(S.DVE, lambda h, lt=lt, nc_=nc_, init=init: h.tensor_tensor_scan(
                out=nc_[:], data0=self.onesf[0:6, 0:1].to_broadcast([6, G]), data1=lt[:], initial=init, op0=ALU.mult, op1=ALU.add),
                R=[lt.b, self.onesf.b] + ([ncp.b] if g > 0 else []), W=[nc_.b])
            S.op(S.DVE, lambda h, nc_=nc_, g=g: h.tensor_copy(out=negcref6[:, g:g + 1], in_=nc_[:, 0:1]), R=[nc_.b], W=[negcref6.b])
            S.op(S.DVE, lambda h, nc_=nc_, gsl=gsl: h.tensor_scalar(out=shiftrows[:, gsl], in0=nc_[:], scalar1=nc_[:, 0:1], scalar2=-1.0,
                                                                 op0=ALU.subtract, op1=ALU.mult), R=[nc_.b], W=[shiftrows.b])
            bk = rot()
            pt, pbuf = self.pb[bk]
            for j in range(4):
                S.op(S.PE, lambda h, pt=pt, j=j, nc_=nc_: h.transpose(pt[:, j * 6:(j + 1) * 6], nc_[0:6, j * 128:(j + 1) * 128], self.identf[0:6, 0:6]),
                     R=[nc_.b, self.identf.b], X=[pbuf], sig=(j == 3))
            S.op(S.DVE, lambda h, pt=pt, g=g: h.tensor_copy(out=negc_tok[:, 4 * g:4 * g + 4, :], in_=pt[:, 0:24].rearrange("p (j c) -> p j c", j=4)),
                 X=[pbuf], W=[negc_tok.b])
            yield
        bk = rot()
        pt, pbuf = self.pb[bk]
        for hh in range(6):
            self.mm(bk, pt[:, hh * NG:(hh + 1) * NG], [(oh6[:, hh, :], negcref6[:, :])], R=[oh6.b, negcref6.b])
        S.op(S.DVE, lambda h: h.tensor_copy(out=negcref_rep[:, :, :], in_=pt[:, 0:6 * NG].rearrange("p (a b) -> p a b", a=6)),
             X=[pbuf], W=[negcref_rep.b])


    def mlstm_gates(self, l, offs, rot, gtok_all, gtokS_all, gdram, gdram_b, sK):
        S = self.S
        wmif = self.tile("wmif", [128, NKC * 8], BF16, dma=True)
        self.load_w(wmif, self.wpk[l][:, int(offs[7]):int(offs[8])])
        wmifv = wmif[:, :].rearrange("p (c n) -> p c n", c=NKC)
        ib = self.tile("ib", [4, 1], dma=True)
        fbm = self.tile("fbm", [4, 1], dma=True)
        for (t_, src) in ((ib, self.mib[l]), (fbm, self.mfb[l])):
            S.op(S.SP, lambda h, t_=t_, src=src: h.dma_start(out=t_[:], in_=src), W=[t_.b], stream=t_.st)
        nfbm = self.tile("nfbm", [4, 1])
        S.op(S.DVE, lambda h: h.tensor_scalar(out=nfbm[:], in0=fbm[:], scalar1=-1.0, scalar2=None, op0=ALU.mult), R=[fbm.b], W=[nfbm.b])
        e_t = [self.tile("me", [4, G]) for _ in range(2)]
        negF = [self.tile("negF", [4, G]) for _ in range(2)]
        gg = [self.tile("gg", [4, G]) for _ in range(2)]
        Gc = [self.tile("Gc", [4, G], dma=True) for _ in range(2)]
        nM = [self.tile("nM", [4, G], dma=True) for _ in range(2)]
        onesb = self.onesf[0:4, 0:1].to_broadcast([4, G])
        gview = [gdram[k].rearrange("(h g) t -> h g t", g=NG) for k in range(2)]
        for g in range(NG):
            gsl = slice(g * G, (g + 1) * G)
            xb = self.xT_b[g]
            e_, nF, g_, G_, nM_ = e_t[g % 2], negF[g % 2], gg[g % 2], Gc[g % 2], nM[g % 2]
            nFp, Gp = negF[(g - 1) % 2], Gc[(g - 1) % 2]
            bI = rot()
            pI, pIb = self.pb[bI]
            self.mm(bI, pI[0:4, :], [(wmifv[:, kc, 0:4], self.xT[:, kc, gsl]) for kc in range(NKC)], R=[wmif.b, xb])
            bF = rot()
            pF, pFb = self.pb[bF]
            self.mm(bF, pF[0:4, :], [(wmifv[:, kc, 4:8], self.xT[:, kc, gsl]) for kc in range(NKC)], R=[wmif.b, xb])
            S.op(S.ACT, lambda h, pF=pF, e_=e_: h.activation(out=e_[:], in_=pF[0:4, :], func=AF.Exp, scale=-1.0, bias=nfbm[:, 0:1]),
                 X=[pFb], R=[nfbm.b], W=[e_.b])
            S.op(S.ACT, lambda h, e_=e_: h.activation(out=e_[:], in_=e_[:], func=AF.Ln, bias=1.0), R=[e_.b], W=[e_.b])
            initF = 0.0 if g == 0 else nFp[:, G - 1:G]
            S.op(S.DVE, lambda h, initF=initF, nF=nF, e_=e_: h.tensor_tensor_scan(out=nF[:], data0=onesb, data1=e_[:], initial=initF,
                                                                                op0=ALU.mult, op1=ALU.add),
                 R=[e_.b, self.onesf.b] + ([nFp.b] if g > 0 else []), W=[nF.b])
            S.op(S.DVE, lambda h, pI=pI, g_=g_, nF=nF: h.scalar_tensor_tensor(out=g_[:], in0=pI[0:4, :], scalar=ib[:, 0:1], in1=nF[:],
                                                                             op0=ALU.add, op1=ALU.add), X=[pIb], R=[ib.b, nF.b], W=[g_.b])
            initG = 0.0 if g == 0 else Gp[:, G - 1:G]
            S.op(S.DVE, lambda h, initG=initG, G_=G_, g_=g_: h.tensor_tensor_scan(out=G_[:], data0=onesb, data1=g_[:], initial=initG,
                                                                                op0=ALU.mult, op1=ALU.max),
                 R=[g_.b, self.onesf.b] + ([Gp.b] if g > 0 else []), W=[G_.b])
            S.op(S.DVE, lambda h, nM_=nM_, nF=nF, G_=G_: h.tensor_tensor(out=nM_[:], in0=nF[:], in1=G_[:], op=ALU.subtract),
                 R=[nF.b, G_.b], W=[nM_.b])
            S.op(S.SP, lambda h, G_=G_, g=g: h.dma_start(out=gview[0][:, g, :], in_=G_[:]), R=[G_.b], W=[gdram_b[g]], stream=G_.st)
            S.op(S.SP, lambda h, nM_=nM_, g=g: h.dma_start(out=gview[1][:, g, :], in_=nM_[:]), R=[nM_.b], W=[gdram_b[g]], stream=nM_.st)
            bk = rot()
            pt, pbuf = self.pb[bk]
            for j in range(4):
                S.op(S.PE, lambda h, pt=pt, j=j, g_=g_: h.transpose(pt[:, j * 4:(j + 1) * 4], g_[0:4, j * 128:(j + 1) * 128], self.identf[0:4, 0:4]),
                     R=[g_.b, self.identf.b], X=[pbuf], sig=(j == 3))
            S.op(S.DVE, lambda h, pt=pt, g=g: h.tensor_copy(out=gtok_all[:, 4 * g:4 * g + 4, :], in_=pt[:, 0:16].rearrange("p (j c) -> p j c", j=4)),
                 X=[pbuf], W=[gtok_all.b])
            yield
        S.op(S.DVE, lambda h: h.tensor_scalar(out=gtokS_all[:, :, :], in0=gtok_all[:, :, :], scalar1=float(math.log(sK)), scalar2=None, op0=ALU.add),
             R=[gtok_all.b], W=[gtokS_all.b])

    def phase_mlstm(self, l):
        S = self.S
        sizes, offs = win_offsets()
        sK = float(96 ** -0.5)
        rot = self.bankrot([0, 1, 2])
        B_S, B_T, B_N, B_D = 4, 5, 6, 7
        gtok_all, gtokS_all = self.gtok_all, self.gtokS_all
        gdram, gdram_b = self.gdram, self.gdram_b
        cw = self.tile("cw", [96, 32], dma=True)
        cb = self.tile("cb", [96, 8], dma=True)
        ngt = self.tile("ngt", [96, 4], dma=True)
        for (t_, src) in ((cw, self.convw[l]), (cb, self.convb[l]), (ngt, self.ngd[l])):
            S.op(S.SP, lambda h, t_=t_, src=src: h.dma_start(out=t_[:], in_=src), W=[t_.b], stream=t_.st)
        lnhalf = self.tile("lnhalf", [128, 1])
        epsb = self.tile("epsb", [128, 1])
        S.op(S.POOL, lambda h: h.memset(lnhalf[:], float(math.log(0.5))), W=[lnhalf.b])
        S.op(S.POOL, lambda h: h.memset(epsb[:], float(LN_EPS)), W=[epsb.b])
        wml = [self.tile("wml", [128, NKC * 480], BF16, dma=True) for _ in range(2)]
        Grep = [self.tile("Grep", [128, G], dma=True) for _ in range(2)]
        clamp = [self.tile("clamp", [96, G], dma=True) for _ in range(2)]
        mu_chain = [self.tile("mu_chain", [128, 8]) for _ in range(2)]
        wexp = [self.tile("wexp", [128, 4]) for _ in range(2)]
        carry = [self.tile("carry", [128, 4]) for _ in range(2)]
        warg = self.tile("warg", [128, 4])
        carg = self.tile("carg", [128, 4])
        qpre = self.tile("qpre", [96, G + 3])
        kpre = self.tile("kpre", [96, G + 3])
        QT = [self.tile("QT", [96, G]) for _ in range(2)]
        KT = [self.tile("KT", [96, G]) for _ in range(2)]
        Vtok = [self.tile("Vtok", [128, 4, 192], BF16) for _ in range(2)]
        Vones = [Buf("Vtok_ones0"), Buf("Vtok_ones1")]
        for k in range(2):
            S.op(S.POOL, lambda h, k=k: h.memset(Vtok[k][:, :, 96:192], 1.0), W=[Vones[k]])
        sgo = [self.tile("sgo", [96, G]) for _ in range(2)]
        szm = [self.tile("szm", [96, G], BF16) for _ in range(2)]
        argDT = self.tile("argDT", [128, G])
        DTb = [Buf(f"DTb{j}") for j in range(4)]
        decb = [Buf(f"decb{j}") for j in range(4)]
        AT4 = self.tile("AT4", [128, G], BF16)
        decQ = self.tile("decQ", [96, G])
        decQb = self.tile("decQb", [96, G], BF16)
        Stb = self.tile("Stb", [96, 192], BF16)
        tBb = self.tile("tBb", [96, G], BF16)
        tAb = self.tile("tAb", [96, G], BF16)
        avg96b = self.tile("avg96b", [96, 96], BF16)
        S.op(S.DVE, lambda h: h.tensor_copy(out=avg96b[:], in_=self.avg96[0:96, :]), R=[self.avg96.b], W=[avg96b.b])
        Khat4 = self.tile("Khat4", [128, 4, 96], BF16)
        St = self.tile("St", [96, 192])
        tA = self.tile("tA", [96, G])
        tB = self.tile("tB", [96, G])
        tC = self.tile("tC", [96, G])
        cnt = {"b": 0}
        iters = [(hd, g) for hd in range(4) for g in range(NG)]

        def gen_A(i):
            hd, g = iters[i]
            k = i % 2
            gsl = slice(g * G, (g + 1) * G)
            xb = self.xT_b[g]
            W_ = wml[hd % 2]
            Wv = W_[:, :].rearrange("p (c n) -> p c n", c=NKC)
            RR = [W_.b, xb]
            Gr, cl, mu, we, ca = Grep[k], clamp[k], mu_chain[k], wexp[k], carry[k]
            mup = mu_chain[1 - k]
            row = hd * NG + g
            S.op(S.SP, lambda h: h.dma_start(out=Gr[:], in_=gdram[0][row:row + 1, :].partition_broadcast(128)),
                 R=[gdram_b[g]], W=[Gr.b], stream=Gr.st)
            S.op(S.SP, lambda h: h.dma_start(out=cl[:], in_=gdram[1][row:row + 1, :].partition_broadcast(96)),
                 R=[gdram_b[g]], W=[cl.b], stream=cl.st)
            S.op(S.ACT, lambda h: h.activation(out=cl[:], in_=cl[:], func=AF.Exp), R=[cl.b], W=[cl.b])
            if g == 0:
                S.op(S.POOL, lambda h: h.memset(mu[:, 0:1], 0.0), W=[mu.b])
                S.op(S.POOL, lambda h: h.memset(qpre[:, 0:3], 0.0), W=[qpre.b])
                S.op(S.POOL, lambda h: h.memset(kpre[:, 0:3], 0.0), W=[kpre.b])
            else:
                S.op(S.DVE, lambda h: h.tensor_copy(out=mu[:, 0:1], in_=mup[:, 4:5]), R=[mup.b], W=[mu.b])
            S.op(S.DVE, lambda h: h.tensor_copy(out=mu[:, 1:5], in_=Gr[:, :].rearrange("p (j t) -> p j t", j=4)[:, :, 127]),
                 R=[Gr.b], W=[mu.b])
            S.op(S.DVE, lambda h: h.tensor_tensor(out=warg[:], in0=gtok_all[:, 4 * g:4 * g + 4, hd], in1=mu[:, 1:5], op=ALU.subtract),
                 R=[gtok_all.b, mu.b], W=[warg.b])
            S.op(S.ACT, lambda h: h.activation(out=we[:], in_=warg[:], func=AF.Exp), R=[warg.b], W=[we.b])
            S.op(S.DVE, lambda h: h.tensor_tensor(out=carg[:], in0=mu[:, 0:4], in1=mu[:, 1:5], op=ALU.subtract), R=[mu.b], W=[carg.b])
            S.op(S.ACT, lambda h: h.activation(out=ca[:], in_=carg[:], func=AF.Exp), R=[carg.b], W=[ca.b])
            yield
            for (pre, acc, c0, qk) in ((qpre, QT[k], 0, 0), (kpre, KT[k], 96, 1)):
                if g > 0:
                    S.op(S.DVE, lambda h, pre=pre: h.tensor_copy(out=pre[:, 0:3], in_=pre[:, G:G + 3]), R=[pre.b], W=[pre.b])
                bk = rot()
                pt, pbuf = self.pb[bk]
                for kc in range(NKC):
                    S.op(S.PE, lambda h, kc=kc, pt=pt, c0=c0: h.matmul(pt[0:96, :], lhsT=Wv[:, kc, c0:c0 + 96], rhs=self.xT[:, kc, gsl],
                                                                      start=(kc == 0), stop=(kc == NKC - 1)), R=RR, X=[pbuf], sig=(kc == NKC - 1))
                    if kc % 2 == 1 and kc < NKC - 1:
                        yield
                S.op(S.ACT, lambda h, pt=pt, pre=pre: h.activation(out=pre[:, 3:G + 3], in_=pt[0:96, :], func=AF.Copy), X=[pbuf], W=[pre.b])
                yield
                wi = lambda tap, qk=qk: cw[:, (qk * 4 + hd) * 4 + tap:(qk * 4 + hd) * 4 + tap + 1]
                bi = cb[:, qk * 4 + hd:qk * 4 + hd + 1]
                S.op(S.DVE, lambda h, pre=pre, acc=acc, wi=wi, bi=bi: h.tensor_scalar(
                    out=acc[:], in0=pre[:, 3:G + 3], scalar1=wi(3), scalar2=bi, op0=ALU.mult, op1=ALU.add),
                    R=[pre.b, cw.b, cb.b], W=[acc.b])
                for kk in (1, 2, 3):
                    S.op(S.DVE, lambda h, pre=pre, acc=acc, wi=wi, kk=kk: h.scalar_tensor_tensor(
                        out=acc[:], in0=pre[:, 3 - kk:G + 3 - kk], scalar=wi(3 - kk), in1=acc[:], op0=ALU.mult, op1=ALU.add),
                        R=[pre.b, cw.b, acc.b], W=[acc.b])
                    if kk == 2:
                        yield
                yield
            bk = rot()
            pt, pbuf = self.pb[bk]
            Vt = Vtok[k]
            for j in range(4):
                blk = slice(g * G + j * 128, g * G + (j + 1) * 128)
                for kc in range(NKC):
                    S.op(S.PE, lambda h, kc=kc, pt=pt, j=j, blk=blk: h.matmul(
                        pt[:, j * 96:(j + 1) * 96], lhsT=self.xT[:, kc, blk], rhs=Wv[:, kc, 192:288],
                        start=(kc == 0), stop=(kc == NKC - 1)), R=RR, X=[pbuf], sig=(kc == NKC - 1))
                yield
            S.op(S.DVE, lambda h, pt=pt, Vt=Vt: h.tensor_copy(out=Vt[:, :, 0:96], in_=pt[:, 0:384].rearrange("p (j c) -> p j c", j=4)),
                 X=[pbuf], W=[Vt.b])
            yield
            held = []
            for c0 in (288, 384):
                bk = rot()
                pt, pbuf = self.pb[bk]
                for kc in range(NKC):
                    S.op(S.PE, lambda h, kc=kc, pt=pt, c0=c0: h.matmul(pt[0:96, :], lhsT=Wv[:, kc, c0:c0 + 96], rhs=self.xT[:, kc, gsl],
                                                                      start=(kc == 0), stop=(kc == NKC - 1)), R=RR, X=[pbuf], sig=(kc == NKC - 1))
                held.append((pt, pbuf))
            S.op(S.ACT, lambda h: h.activation(out=QT[k][:], in_=QT[k][:], func=AF.Silu), R=[QT[k].b], W=[QT[k].b])
            S.op(S.ACT, lambda h: h.activation(out=KT[k][:], in_=KT[k][:], func=AF.Silu), R=[KT[k].b], W=[KT[k].b])
            (pto, pbo), (ptz, pbz) = held
            S.op(S.ACT, lambda h: h.activation(out=sgo[k][:], in_=pto[0:96, :], func=AF.Tanh, scale=0.5), X=[pbo], W=[sgo[k].b])
            S.op(S.ACT, lambda h: h.activation(out=szm[k][:], in_=ptz[0:96, :], func=AF.Silu), X=[pbz], W=[szm[k].b])
            yield

        N_A = 30

        def run_BC(i, chunks):
            hd, g = iters[i]
            k = i % 2
            gsl = slice(g * G, (g + 1) * G)
            Gr, cl, mu, we, ca = Grep[k], clamp[k], mu_chain[k], wexp[k], carry[k]
            Q_, K_, Vt, Vo = QT[k], KT[k], Vtok[k], Vones[k]
            sg, sz = sgo[k], szm[k]
            pS, pSb = self.pb[B_S]
            pT, pTb = self.pb[B_T]
            pN, pNb = self.pb[B_N]
            pD, pDb = self.pb[B_D]

            fstate = {"pt": 0, "em": 0}
            N_PTS = 13

            def fill(n=1):
                did = False
                fstate["pt"] += 1
                if chunks is not None:
                    want = (fstate["pt"] * N_A + N_PTS - 1) // N_PTS
                    while fstate["em"] < want:
                        if next(chunks, "end") != "end":
                            did = True
                        fstate["em"] += 1
                if not did:
                    self.filler(512, bank=3)

            if g == 0:
                S.op(S.POOL, lambda h: h.memset(St[:], 0.0), W=[St.b])
                S.op(S.POOL, lambda h: h.memset(Stb[:], 0.0), W=[Stb.b])
            for j in range(4):
                bs = slice(j * 128, (j + 1) * 128)
                self.mm(B_S, pS[:, bs], [(K_[0:96, bs], Q_[0:96, bs])], R=[K_.b, Q_.b])
            S.op(S.DVE, lambda h: h.scalar_tensor_tensor(
                out=argDT[:, :].rearrange("p (j t) -> p j t", j=4), in0=Gr[:, :].rearrange("p (j t) -> p j t", j=4), scalar=-1.0,
                in1=self.mnegf[:, :].unsqueeze(1).to_broadcast([128, 4, 128]), op0=ALU.mult, op1=ALU.add),
                R=[Gr.b, self.mnegf.b], W=[argDT.b] + DTb)
            for j in range(4):
                bs = slice(j * 128, (j + 1) * 128)
                S.op(S.ACT, lambda h, bs=bs, j=j: h.activation(out=argDT[:, bs], in_=argDT[:, bs], func=AF.Exp,
                                                              bias=gtokS_all[:, 4 * g + j, hd:hd + 1]), R=[argDT.b, gtokS_all.b], W=[DTb[j]])
            for j in range(4):
                bs = slice(j * 128, (j + 1) * 128)
                S.op(S.ACT, lambda h, bs=bs, j=j: h.activation(out=decQ[:, bs], in_=Gr[0:96, bs], func=AF.Exp, scale=-1.0,
                                                              bias=mu[0:96, j:j + 1]), R=[Gr.b, mu.b], W=[decb[j]])
            fill()
            S.op(S.DVE, lambda h: h.tensor_tensor(out=AT4[:], in0=pS[:, :], in1=argDT[:], op=ALU.mult), X=[pSb], R=[argDT.b] + DTb, W=[AT4.b])
            S.op(S.POOL, lambda h: h.tensor_tensor(out=decQb[:], in0=Q_[:], in1=decQ[:], op=ALU.mult), R=[Q_.b, decQ.b] + decb, W=[decQb.b])
            for j in range(4):
                bs = slice(j * 128, (j + 1) * 128)
                S.op(S.PE, lambda h, bs=bs, j=j: h.transpose(pT[:, j * 96:(j + 1) * 96], K_[0:96, bs], self.identf[0:96, 0:96]),
                     R=[K_.b, self.identf.b], X=[pTb], sig=(j == 3))
            S.op(S.DVE, lambda h: h.scalar_tensor_tensor(
                out=Khat4[:, :, :], in0=pT[:, 0:384].rearrange("p (j c) -> p j c", j=4), scalar=sK,
                in1=we[:, 0:4].unsqueeze(2).to_broadcast([128, 4, 96]), op0=ALU.mult, op1=ALU.mult),
                X=[pTb], R=[we.b], W=[Khat4.b])
            fill()
            for j in range(4):
                bs = slice(j * 128, (j + 1) * 128)
                self.mm(B_N, pN[0:96, bs], [(Vt[:, j, 0:96], AT4[:, bs])], R=[Vt.b, AT4.b], start=(j == 0), stop=False, sgc=True)
            for j in range(4):
                bs = slice(j * 128, (j + 1) * 128)
                self.mm(B_D, pD[0:96, bs], [(Vt[:, j, 96:192], AT4[:, bs])], R=[Vo, AT4.b], start=(j == 0), stop=False, sgc=True)
            for j in range(4):
                ub, ubuf, uo = (pT, pTb, B_T) if j < 2 else (pS, pSb, B_S)
                c0 = (j % 2) * 192
                self.mm(uo, ub[0:96, c0:c0 + 192], [(Khat4[:, j, :], Vt[:, j, :])], R=[Khat4.b, Vt.b, Vo])
            fill()
            for j in range(4):
                bs = slice(j * 128, (j + 1) * 128)
                ub, ubuf = (pT, pTb) if j < 2 else (pS, pSb)
                c0 = (j % 2) * 192
                self.mm(B_N, pN[0:96, bs], [(Stb[0:96, 0:96], decQb[:, bs])], R=[Stb.b, decQb.b], start=False, stop=True, sgc=True)
                self.mm(B_D, pD[0:96, bs], [(Stb[0:96, 96:192], decQb[:, bs])], R=[Stb.b, decQb.b], start=False, stop=True, sgc=True)
                S.op(S.DVE, lambda h, j=j, ub=ub, c0=c0: h.scalar_tensor_tensor(out=St[:], in0=St[:], scalar=ca[0:96, j:j + 1],
                                                                               in1=ub[0:96, c0:c0 + 192], op0=ALU.mult, op1=ALU.add),
                     X=[ubuf], R=[ca.b, St.b], W=[St.b])
                S.op(S.DVE, lambda h: h.tensor_copy(out=Stb[:], in_=St[:]), R=[St.b], W=[Stb.b])
                if j % 2 == 1:
                    fill()
            S.op(S.DVE, lambda h: h.tensor_tensor(out=tA[:], in0=pD[0:96, :], in1=cl[:], op=ALU.max), X=[pDb], R=[cl.b], W=[tA.b])
            S.op(S.DVE, lambda h: h.scalar_tensor_tensor(out=tA[:], in0=pD[0:96, :], scalar=-1.0, in1=tA[:], op0=ALU.mult, op1=ALU.max),
                 X=[pDb], R=[tA.b], W=[tA.b])
            fill()
            S.op(S.ACT, lambda h: h.activation(out=tA[:], in_=tA[:], func=AF.Ln), R=[tA.b], W=[tA.b])
            S.op(S.ACT, lambda h: h.activation(out=tA[:], in_=tA[:], func=AF.Exp, scale=-1.0, bias=lnhalf[0:96, 0:1]), R=[tA.b, lnhalf.b], W=[tA.b])
            fill()
            S.op(S.DVE, lambda h: h.tensor_tensor(out=tB[:], in0=pN[0:96, :], in1=tA[:], op=ALU.mult), X=[pNb], R=[tA.b], W=[tB.b])
            S.op(S.DVE, lambda h: h.scalar_tensor_tensor(out=tB[:], in0=sg[:], scalar=1.0, in1=tB[:], op0=ALU.add, op1=ALU.mult),
                 R=[tB.b, sg.b], W=[tB.b])
            fill()
            S.op(S.ACT, lambda h: h.activation(out=tBb[:], in_=tB[:], func=AF.Copy), R=[tB.b], W=[tBb.b])
            bk = rot()
            pt, pbuf = self.pb[bk]
            self.mm(bk, pt[0:96, :], [(avg96b[:, :], tBb[:, :])], R=[avg96b.b, tBb.b])
            fill()
            S.op(S.DVE, lambda h, pt=pt: h.tensor_tensor(out=tC[:], in0=tB[:], in1=pt[0:96, :], op=ALU.subtract), X=[pbuf], R=[tB.b], W=[tC.b])
            S.op(S.ACT, lambda h: h.activation(out=tAb[:], in_=tC[:], func=AF.Square), R=[tC.b], W=[tAb.b])
            fill()
            bk = rot()
            pt, pbuf = self.pb[bk]
            self.mm(bk, pt[0:96, :], [(avg96b[:, :], tAb[:, :])], R=[avg96b.b, tAb.b])
            fill()
            S.op(S.ACT, lambda h, pt=pt: h.activation(out=tB[:], in_=pt[0:96, :], func=AF.Ln, bias=epsb[0:96, 0:1]), X=[pbuf], R=[epsb.b], W=[tB.b])
            S.op(S.ACT, lambda h: h.activation(out=tB[:], in_=tB[:], func=AF.Exp, scale=-0.5), R=[tB.b], W=[tB.b])
            fill()
            S.op(S.DVE, lambda h: h.scalar_tensor_tensor(out=tC[:], in0=tC[:], scalar=ngt[:, hd:hd + 1], in1=tB[:], op0=ALU.mult, op1=ALU.mult),
                 R=[tC.b, ngt.b, tB.b], W=[tC.b])
            S.op(S.POOL, lambda h: h.tensor_tensor(out=self.ycat[0:96, 3 + hd, gsl], in0=tC[:], in1=sz[:], op=ALU.mult),
                 R=[tC.b, sz.b], W=[self.ycat_b[3 + hd][g]])
            if chunks is not None:
                for _ in chunks:
                    pass

        self.load_w(wml[0], self.wpk[l][:, int(offs[8]):int(offs[9])])
        for _ in gen_A(0):
            pass
        for i in range(len(iters)):
            hd, g = iters[i]
            if g == 0 and hd + 1 < 4:
                self.load_w(wml[(hd + 1) % 2], self.wpk[l][:, int(offs[9 + hd]):int(offs[10 + hd])])
            chunks = gen_A(i + 1) if i + 1 < len(iters) else None
            run_BC(i, chunks)

    def build_memT(self):
        S = self.S
        mt = [self.tile("memin", [128, D], dma=True) for _ in range(2)]
        rot = self.bankrot([4, 5, 6, 7])
        for t in range(2):
            S.op(S.SP, lambda h, t=t: h.dma_start(out=mt[t][:], in_=self.mem_in[t * 128:(t + 1) * 128, :]), W=[mt[t].b], stream=mt[t].st)
            for half in range(2):
                bk = rot()
                pt, pbuf = self.pb[bk]
                for c in range(4):
                    kc = half * 4 + c
                    S.op(S.PE, lambda h, kc=kc, c=c, pt=pt, t=t: h.transpose(
                        pt[:, c * 128:(c + 1) * 128], mt[t][:, kc * 128:(kc + 1) * 128], self.identf[:]),
                        R=[mt[t].b, self.identf.b], X=[pbuf], sig=(c == 3))
                dst = self.memT[:, half * 4:(half + 1) * 4, t * 128:(t + 1) * 128]
                srcp = pt[:, :].rearrange("p (c t) -> p c t", c=4)
                S.op(S.DVE, lambda h, dst=dst, srcp=srcp: h.tensor_copy(out=dst, in_=srcp), X=[pbuf], W=[self.memT.b])

    def phase_mem(self, l):
        S = self.S
        sizes, offs = win_offsets()
        wm = self.tile("wm", [128, NKC * 512], BF16, dma=True)
        wr = self.tile("wr", [128, NKC * 512], BF16, dma=True)
        self.load_w(wm, self.wmempk[l])
        self.load_w(wr, self.wpk[l][:, int(offs[12]):int(offs[13])])
        wmv = wm[:, :].rearrange("p (c n) -> p c n", c=NKC)
        wrv = wr[:, :].rearrange("p (c n) -> p c n", c=NKC)
        KmT = self.tile("KmT", [128, 2, MEM_LEN], BF16)
        Vm = self.tile("Vm", [128, 2, 4, 128], BF16)
        rot = self.bankrot([0, 1])
        rot_s = self.bankrot([2, 3, 4])
        rot_o = self.bankrot([5, 6])
        for p in range(2):
            bk = rot()
            pt, pbuf = self.pb[bk]
            self.mm(bk, pt[:, 0:MEM_LEN], [(wmv[:, kc, p * 128:(p + 1) * 128], self.memT[:, kc, :]) for kc in range(NKC)],
                    R=[wm.b, self.memT.b])
            S.op(S.DVE, lambda h, pt=pt, p=p: h.tensor_copy(out=KmT[:, p, :], in_=pt[:, 0:MEM_LEN]), X=[pbuf], W=[KmT.b])
        S.op(S.POOL, lambda h: h.memset(Vm[:], 1.0), W=[Vm.b])
        for mb in range(2):
            bk = rot()
            pt, pbuf = self.pb[bk]
            self.mm(bk, pt[:, 0:256], [(self.memT[:, kc, mb * 128:(mb + 1) * 128], wmv[:, kc, 256:512]) for kc in range(NKC)],
                    R=[wm.b, self.memT.b])
            for h4 in range(4):
                c0 = 0 if h4 % 2 == 0 else 64
                S.op(S.DVE, lambda h, pt=pt, mb=mb, h4=h4, c0=c0: h.tensor_copy(
                    out=Vm[:, mb, h4, c0:c0 + 64], in_=pt[:, h4 * 64:(h4 + 1) * 64]), X=[pbuf], W=[Vm.b])
        QTm = [self.tile("QTm", [128, G], BF16) for _ in range(2)]
        szp = [self.tile("szp", [128, G], BF16) for _ in range(2)]
        thm = self.tile("thm", [128, G])
        PT = [self.tile("PTm", [128, G], BF16) for _ in range(3)]
        rd = [self.tile("rdm", [128, G]) for _ in range(2)]
        tn = [self.tile("tnm", [128, G]) for _ in range(2)]
        it = 0
        ip = 0
        for g in range(NG):
            gsl = slice(g * G, (g + 1) * G)
            for p in range(2):
                q, z = QTm[it % 2], szp[it % 2]
                it += 1
                bk = rot()
                pt, pbuf = self.pb[bk]
                self.mm(bk, pt[:, :], [(wrv[:, kc, p * 128:(p + 1) * 128], self.xT[:, kc, gsl]) for kc in range(NKC)],
                        R=[wr.b, self.xT_b[g]])
                S.op(S.ACT, lambda h, pt=pt, q=q: h.activation(out=q[:], in_=pt[:, :], func=AF.Copy, scale=0.125), X=[pbuf], W=[q.b])
                bk = rot()
                pt, pbuf = self.pb[bk]
                self.mm(bk, pt[:, :], [(wrv[:, kc, 256 + p * 128:256 + (p + 1) * 128], self.xT[:, kc, gsl]) for kc in range(NKC)],
                        R=[wr.b, self.xT_b[g]])
                S.op(S.ACT, lambda h, pt=pt: h.activation(out=thm[:], in_=pt[:, :], func=AF.Tanh, scale=0.5), X=[pbuf], W=[thm.b])
                S.op(S.DVE, lambda h, pt=pt, z=z: h.scalar_tensor_tensor(out=z[:], in0=thm[:], scalar=1.0, in1=pt[:, :], op0=ALU.add, op1=ALU.mult),
                     X=[pbuf], R=[thm.b], W=[z.b])
                for hh in range(2):
                    h4 = 2 * p + hh
                    r0 = 64 * hh
                    bo = rot_o()
                    po, pobuf = self.pb[bo]
                    sts = []
                    for mb in range(2):
                        bs = rot_s()
                        ps_, psbuf = self.pb[bs]
                        self.mm(bs, ps_[:, :], [(KmT[r0:r0 + 64, p, mb * 128:(mb + 1) * 128], q[r0:r0 + 64, :])], R=[KmT.b, q.b])
                        sts.append((ps_, psbuf))
                    for mb in range(2):
                        ps_, psbuf = sts[mb]
                        pt_ = PT[ip % 3]
                        ip += 1
                        S.op(S.ACT, lambda h, ps_=ps_, pt_=pt_: h.activation(out=pt_[:], in_=ps_[:, :], func=AF.Exp), X=[psbuf], W=[pt_.b])
                        self.mm(bo, po[:, :], [(Vm[:, mb, h4, :], pt_[:])], R=[Vm.b, pt_.b], start=(mb == 0), stop=(mb == 1))
                    self.attn_epilogue(po, pobuf, hh == 1, z, rd[h4 % 2], tn[h4 % 2], 7 + p, g, half=True)

    def attn_epilogue(self, po, pobuf, odd, sz, rd, tn, chunk, g, half=False, act_recip=False):
        S = self.S
        gsl = slice(g * G, (g + 1) * G)
        nr = slice(64, 128) if odd else slice(0, 64)
        dr = slice(0, 64) if odd else slice(64, 128)
        if act_recip:
            S.op(S.ACT, lambda h: h.activation(out=rd[nr, :], in_=po[dr, :], func=AF.Ln), X=[pobuf], W=[rd.b])
            S.op(S.ACT, lambda h: h.activation(out=rd[nr, :], in_=rd[nr, :], func=AF.Exp, scale=-1.0), R=[rd.b], W=[rd.b])
        else:
            S.op(S.DVE, lambda h: h.reciprocal(out=rd[nr, :], in_=po[dr, :]), X=[pobuf], W=[rd.b])
        if half:
            S.op(S.DVE, lambda h: h.scalar_tensor_tensor(out=tn[nr, :], in0=po[nr, :], scalar=0.5, in1=rd[nr, :], op0=ALU.mult, op1=ALU.mult),
                 X=[pobuf], R=[rd.b], W=[tn.b])
        else:
            S.op(S.DVE, lambda h: h.tensor_tensor(out=tn[nr, :], in0=po[nr, :], in1=rd[nr, :], op=ALU.mult), X=[pobuf], R=[rd.b], W=[tn.b])
        if isinstance(sz, tuple):
            szt, sg_, szb = sz
            S.op(S.POOL, lambda h: h.tensor_tensor(out=self.ycat[nr, chunk, gsl], in0=tn[nr, :], in1=szt[nr, sg_, :], op=ALU.mult),
                 R=[tn.b, szb], W=[self.ycat_b[chunk][g]])
        else:
            S.op(S.POOL, lambda h: h.tensor_tensor(out=self.ycat[nr, chunk, gsl], in0=tn[nr, :], in1=sz[nr, :], op=ALU.mult),
                 R=[tn.b, sz.b], W=[self.ycat_b[chunk][g]])

    def phase_final(self, l, x_src, x_dst, make_xT):
        S = self.S
        nc = self.nc
        wout = self.tile("wout", [128, 9 * D], BF16, dma=True)
        self.load_w(wout, self.woutpk[l])
        woutv = wout[:, :].rearrange("p (c n) -> p c n", c=9)
        gam = self.tile("gam", [128, D], dma=True)
        bet = self.tile("bet", [128, D], dma=True)
        S.op(S.SP, lambda h: h.dma_start(out=gam[:], in_=self.lng[l].partition_broadcast(128)), W=[gam.b], stream=gam.st)
        S.op(S.SP, lambda h: h.dma_start(out=bet[:], in_=self.lnb[l].partition_broadcast(128)), W=[bet.b], stream=bet.st)
        xin = [self.tile("xin", [128, D], dma=True) for _ in range(3)]
        epsf = self.tile("epsf", [128, 1])
        S.op(S.POOL, lambda h: h.memset(epsf[:], float(LN_EPS)), W=[epsf.b])
        tt = [self.tile("tt", [128, D]) for _ in range(2)]
        xo = [self.tile("xo", [128, D]) for _ in range(2)]
        stats = [self.tile("stats", [128, 16]) for _ in range(2)]
        krows = [128, 128, 128, 96, 96, 96, 96, 128, 128]
        rot_y = self.bankrot([0, 1, 2, 3])
        rot_t = self.bankrot([4, 5, 6, 7])

        def load_x(t):
            xt = xin[t % 3]
            S.op(S.SP, lambda h: h.dma_start(out=xt[:], in_=x_src[t * 128:(t + 1) * 128, :]), W=[xt.b], stream=xt.st)

        nmr = [self.tile("nmr", [128, 2]) for _ in range(2)]
        junk = self.tile("junk", [128, D], BF16)

        held = {}

        def stage_M_pe(t):
            g = t // 4
            hb = []
            for half in range(2):
                bk = rot_y()
                pt, pbuf = self.pb[bk]
                pairs = [(self.ycat[0:krows[c], c, t * 128:(t + 1) * 128], woutv[0:krows[c], c, half * 512:(half + 1) * 512])
                         for c in range(9)]
                self.mm(bk, pt[:, :], pairs, R=[wout.b] + [self.ycat_b[c][g] for c in range(9)])
                hb.append((pt, pbuf))
            held[t] = hb

        def stage_M_dve(t):
            xt, tq, sq = xin[t % 3], tt[t % 2], stats[t % 2]
            for half in range(2):
                pt, pbuf = held[t][half]
                S.op(S.DVE, lambda h, pt=pt, half=half: h.scalar_tensor_tensor(
                    out=tq[:, half * 512:(half + 1) * 512], in0=xt[:, half * 512:(half + 1) * 512], scalar=float(ALPHA),
                    in1=pt[:, :], op0=ALU.mult, op1=ALU.add), R=[xt.b], X=[pbuf], W=[tq.b])
            for half in range(2):
                S.op(S.DVE, lambda h, half=half: h.bn_stats(out=sq[:, half * 6:(half + 1) * 6],
                                                            in_=tq[:, half * 512:(half + 1) * 512]), R=[tq.b], W=[sq.b])

        def stage_N(t):
            tq, xq, sq, nm = tt[t % 2], xo[t % 2], stats[t % 2], nmr[t % 2]
            S.op(S.DVE, lambda h: h.bn_aggr(out=sq[:, 12:14], in_=sq[:, 0:12]), R=[sq.b], W=[sq.b])
            S.op(S.ACT, lambda h: h.activation(out=sq[:, 14:15], in_=sq[:, 13:14], func=AF.Ln, bias=epsf[:, 0:1]), R=[sq.b, epsf.b], W=[sq.b])
            S.op(S.ACT, lambda h: h.activation(out=nm[:, 0:1], in_=sq[:, 14:15], func=AF.Exp, scale=-0.5), R=[sq.b], W=[nm.b])
            S.op(S.DVE, lambda h: h.tensor_scalar(out=nm[:, 1:2], in0=sq[:, 12:13], scalar1=nm[:, 0:1], scalar2=-1.0, op0=ALU.mult, op1=ALU.mult),
                 R=[sq.b, nm.b], W=[nm.b])
            S.op(S.ACT, lambda h: h.activation(out=tq[:], in_=tq[:], func=AF.Identity, scale=nm[:, 0:1], bias=nm[:, 1:2]), R=[tq.b, nm.b], W=[tq.b])

        def stage_N_b(t):
            tq, xq = tt[t % 2], xo[t % 2]
            S.op(S.POOL, lambda h: h.tensor_tensor(out=xq[:], in0=tq[:], in1=gam[:], op=ALU.mult), R=[tq.b, gam.b], W=[xq.b])
            S.op(S.POOL, lambda h: h.tensor_tensor(out=xq[:], in0=xq[:], in1=bet[:], op=ALU.add), R=[xq.b, bet.b], W=[xq.b])
            st = self.stq[t % 2]
            S.op(S.SP, lambda h: h.dma_start(out=x_dst[t * 128:(t + 1) * 128, :], in_=xq[:]), R=[xq.b], stream=st)

        load_x(0)
        load_x(1)
        stage_M_pe(0)
        stage_M_dve(0)
        for t in range(NT):
            if t + 2 < NT:
                load_x(t + 2)
            if t + 1 < NT:
                stage_M_pe(t + 1)
            stage_N(t)
            stage_N_b(t)
            if t + 1 < NT:
                stage_M_dve(t + 1)
            if make_xT and t >= 1:
                self.transpose_into_xT(xo[(t - 1) % 2], t - 1, rot_t)
        if make_xT:
            self.transpose_into_xT(xo[(NT - 1) % 2], NT - 1, rot_t)

    def _build(self):
        S = self.S
        L = self.L
        for l in range(L):
            x_src = self.x_in if l == 0 else self.xbuf[(l - 1) % 2]
            x_dst = self.out if l == L - 1 else self.xbuf[l % 2]
            if l == 0:
                self.run_phase(self.build_xT_from_dram, x_src)
            if l == 0:
                self.run_phase(self.build_memT)
            self.run_phase(self.phase_fox, l)
            if "noml" not in self.dbg:
                self.run_phase(self.phase_mlstm, l)
            self.run_phase(self.phase_mem, l)
            if "ycat" in self.dbg:
                S.barrier()
                d = self.dbg_dram("dbg_ycat", [128, 9 * S_LEN], BF16)
                S.op(S.SP, lambda h: h.dma_start(out=d, in_=self.ycat[:, :, :].rearrange("p c s -> p (c s)")),
                     R=[b for cb in self.ycat_b for b in cb], stream=self.stq[0])
            self.run_phase(self.phase_final, l, x_src, x_dst, make_xT=(l < L - 1))
        S.finish(S.SP)


def _pack_win(w):
    w3 = w.reshape(NKC, 128, IN_COLS).transpose(1, 0, 2)
    groups = []
    groups.append(np.arange(O_FF, O_FF + 6))
    for h in range(6):
        cols = [np.arange(o + 64 * h, o + 64 * (h + 1)) for o in (O_FQ, O_FK, O_FV)]
        if h % 2 == 0:
            cols.append(np.arange(O_FZ + 64 * h, O_FZ + 64 * (h + 2)))
        groups.append(np.concatenate(cols))
    groups.append(np.concatenate([np.arange(O_MI, O_MI + 4), np.arange(O_MF, O_MF + 4)]))
    for h in range(4):
        groups.append(np.concatenate([np.arange(o + 96 * h, o + 96 * (h + 1)) for o in (O_MQ, O_MK, O_MV, O_MO, O_MZ)]))
    groups.append(np.concatenate([np.arange(O_RQ, O_RQ + 256), np.arange(O_RZ, O_RZ + 256)]))
    parts = [np.ascontiguousarray(w3[:, :, gidx]).reshape(128, -1) for gidx in groups]
    return np.concatenate(parts, axis=1)


def win_offsets():
    sizes = [6] + [320, 192] * 3 + [8] + [480] * 4 + [512]
    offs = np.concatenate([[0], np.cumsum([NKC * s for s in sizes])])
    return sizes, offs


def _pack_wout(w):
    out = np.zeros((128, 9, D), np.float32)
    r = 0
    for c, k in enumerate([128, 128, 128, 96, 96, 96, 96, 128, 128]):
        out[0:k, c, :] = w[r:r + k, :]
        r += k
    return out.reshape(128, 9 * D)


def _pack_wmem(w):
    return np.ascontiguousarray(w.reshape(NKC, 128, 512).transpose(1, 0, 2)).reshape(128, NKC * 512)


def pack_layers(inp, layers):
    f = np.float32
    d = {}
    d["wpk"] = np.stack([_pack_win(np.asarray(inp["w_in"][l], f)) for l in layers])
    d["woutpk"] = np.stack([_pack_wout(np.asarray(inp["w_out"][l], f)) for l in layers])
    d["wmempk"] = np.stack([_pack_wmem(np.asarray(inp["w_mem_kv"][l], f)) for l in layers])
    d["foxb"] = np.stack([np.asarray(inp["fox_f_bias"][l], f).reshape(6, 1) for l in layers])
    d["mib"] = np.stack([np.asarray(inp["mlstm_i_bias"][l], f).reshape(4, 1) for l in layers])
    d["mfb"] = np.stack([np.asarray(inp["mlstm_f_bias"][l], f).reshape(4, 1) for l in layers])
    d["convw"] = np.stack([np.ascontiguousarray(np.asarray(inp["mlstm_conv_w"][l], f).reshape(4, 2, 4, 96).transpose(3, 1, 2, 0)).reshape(96, 32)
                           for l in layers])
    d["convb"] = np.stack([np.ascontiguousarray(np.asarray(inp["mlstm_conv_b"][l], f).reshape(2, 4, 96).transpose(2, 0, 1)).reshape(96, 8)
                           for l in layers])
    d["ng"] = np.stack([np.ascontiguousarray(np.asarray(inp["mlstm_norm_g"][l], f).reshape(4, 96).T) for l in layers])
    d["lng"] = np.stack([np.asarray(inp["ln_g"][l], f).reshape(1, D) for l in layers])
    d["lnb"] = np.stack([np.asarray(inp["ln_b"][l], f).reshape(1, D) for l in layers])
    return d


_PROG_CACHE = {}


def get_prog(L, **kw):
    key = (L, tuple(sorted(kw.items())))
    if key not in _PROG_CACHE:
        _PROG_CACHE[key] = Prog(L, **kw)
    return _PROG_CACHE[key]


def run_layers(x, mem, packed, n_cores=8):
    L = packed["wpk"].shape[0]
    prog = get_prog(L)
    B = x.shape[0]
    in_maps = []
    for c in range(n_cores):
        b = c % B
        m = {"x": np.ascontiguousarray(x[b]), "mem": np.ascontiguousarray(mem[b])}
        m.update(packed)
        in_maps.append(m)
    res = run_bass_kernel_spmd(prog.nc, in_maps, core_ids=list(range(n_cores)))
    return np.stack([np.asarray(res.results[b]["out"]) for b in range(B)])


FUSED = True


def kernel(x, mem, w_in, fox_f_bias, mlstm_conv_w, mlstm_conv_b, mlstm_i_bias, mlstm_f_bias,
           mlstm_norm_g, w_mem_kv, w_out, ln_g, ln_b):
    inp = dict(w_in=w_in, fox_f_bias=fox_f_bias, mlstm_conv_w=mlstm_conv_w, mlstm_conv_b=mlstm_conv_b,
               mlstm_i_bias=mlstm_i_bias, mlstm_f_bias=mlstm_f_bias, mlstm_norm_g=mlstm_norm_g,
               w_mem_kv=w_mem_kv, w_out=w_out, ln_g=ln_g, ln_b=ln_b)
    x = np.asarray(x, np.float32)
    mem = np.asarray(mem, np.float32)
    if FUSED:
        return run_layers(x, mem, pack_layers(inp, list(range(DEPTH))))
    for l in range(DEPTH):
        x = run_layers(x, mem, pack_layers(inp, [l]))
    return x
```

```python
import math
from contextlib import ExitStack
import numpy as np
import concourse.bass as bass
import concourse.mybir as mybir
from concourse.bass_utils import run_bass_kernel_spmd

F32 = mybir.dt.float32
BF16 = mybir.dt.bfloat16
AF = mybir.ActivationFunctionType
ALU = mybir.AluOpType

D = 1024
S_LEN = 4096
DEPTH = 4
NKC = 8
G = 512
NG = S_LEN // G
NT = S_LEN // 128
MEM_LEN = 256
LN_EPS = 1e-5
ALPHA = (2.0 * DEPTH) ** 0.25
IN_COLS = 3982
O_FQ, O_FK, O_FV, O_FF, O_FZ = 0, 384, 768, 1152, 1158
O_MQ, O_MK, O_MV, O_MI, O_MF, O_MO, O_MZ = 1542, 1926, 2310, 2694, 2698, 2702, 3086
O_RQ, O_RZ = 3470, 3726


class Stream:
    def __init__(self, name, sem, inc, q, dma):
        self.name, self.sem, self.inc, self.q, self.dma = name, sem, inc, q, dma
        self.n = 0


class Q:
    def __init__(self, name, h):
        self.name, self.h = name, h
        self.seen = {}
        self.stream = None


class Buf:
    __slots__ = ("name", "w", "r")

    def __init__(self, name):
        self.name = name
        self.w = None
        self.r = {}


class Sched:
    def __init__(self, nc):
        self.nc = nc
        self.PE = self._mkq("pe", nc.tensor)
        self.ACT = self._mkq("act", nc.scalar)
        self.DVE = self._mkq("dve", nc.vector)
        self.POOL = self._mkq("pool", nc.gpsimd)
        self.SP = Q("sp", nc.sync)
        self.queues = [self.PE, self.ACT, self.DVE, self.POOL, self.SP]
        self.streams = [q.stream for q in self.queues if q.stream is not None]
        self.nwaits = 0
        self.nops = 0

    def _mkq(self, name, h):
        q = Q(name, h)
        q.stream = Stream(name, self.nc.alloc_semaphore("s_" + name), 1, q, False)
        return q

    def dma_stream(self, name):
        s = Stream(name, self.nc.alloc_semaphore("d_" + name), 16, None, True)
        self.streams.append(s)
        return s

    def _wait(self, q, s, c):
        if q.seen.get(s, 0) >= c:
            return
        assert c <= s.n, f"wait on unissued signal {s.name} {c} > {s.n}"
        q.h.wait_ge(s.sem, c * s.inc)
        q.seen[s] = c
        self.nwaits += 1

    def op(self, q, emit, R=(), W=(), X=(), sig=True, stream=None):
        st = stream if stream is not None else q.stream
        deps = {}

        def add(d, same_ok):
            s, c = d
            if same_ok and (not s.dma) and s.q is q and stream is None:
                return
            if deps.get(s, 0) < c:
                deps[s] = c

        pe = q is self.PE
        for b in R:
            if b.w is not None:
                add(b.w, pe)
        for b in W:
            if b.w is not None:
                add(b.w, pe)
            for s, c in b.r.items():
                add((s, c), pe)
        for b in X:
            if b.w is not None:
                add(b.w, pe)
            for s, c in b.r.items():
                add((s, c), pe)
        for s, c in deps.items():
            if s.dma:
                c = s.n
            self._wait(q, s, c)
        ins = emit(q.h)
        cnt = st.n + 1
        if sig:
            ins.then_inc(st.sem, st.inc)
            st.n = cnt
        for b in R:
            if b.r.get(st, 0) < cnt:
                b.r[st] = cnt
        for b in W:
            b.w = (st, cnt)
            b.r = {}
        for b in X:
            b.w = (st, cnt)
            b.r = {}
        self.nops += 1
        return ins

    def barrier(self):
        for q in self.queues:
            for s in self.streams:
                if s.n > 0:
                    self._wait(q, s, s.n)

    def finish(self, q):
        for s in self.streams:
            if s.n > 0 and not (s.q is q):
                self._wait(q, s, s.n)


class Tile:
    def __init__(self, handle, name, stream=None):
        self.t = handle
        self.b = Buf(name)
        self.st = stream

    def __getitem__(self, k):
        return self.t[k]


class Prog:
    def __init__(self, L, stop_after=None, dbg=()):
        import os
        dbg = tuple(dbg) + tuple(x for x in os.environ.get("KDBG", "").split(",") if x)
        self.L = L
        self.stop_after = stop_after
        self.dbg = set(dbg)
        nc = bass.Bass("TRN2", target_bir_lowering=False)
        self.nc = nc
        self.S = Sched(nc)
        self.uid = 0
        self.stack = None
        self.stream_pool = []
        self.all_streams = []
        self.phase_streams = []
        self._decl_dram()
        self._alloc()
        self._consts()
        self._build()

    def tile(self, name, shape, dt=F32, dma=False):
        self.uid += 1
        nm = f"{name}_{self.uid}"
        if self.stack is not None:
            hnd = self.stack.enter_context(self.nc.sbuf_tensor(nm, list(shape), dt))
        else:
            hnd = self.nc.alloc_sbuf_tensor(nm, list(shape), dt)
        st = None
        if dma:
            if self.stream_pool:
                st = self.stream_pool.pop()
            else:
                st = self.S.dma_stream(f"ds{len(self.all_streams)}")
                self.all_streams.append(st)
            if self.stack is not None:
                self.phase_streams.append(st)
        return Tile(hnd, nm, st)

    def sub_arena(self):
        prog = self

        class _Sub:
            def __enter__(self_):
                self_.outer = prog.stack
                self_.es = ExitStack()
                self_.es.__enter__()
                prog.stack = self_.es
                return self_

            def __exit__(self_, *exc):
                prog.S.barrier()
                prog.stack = self_.outer
                return self_.es.__exit__(*exc)
        return _Sub()

    def run_phase(self, fn, *a, **kw):
        assert self.stack is None
        with ExitStack() as es:
            self.stack = es
            self.phase_streams = []
            fn(*a, **kw)
            self.S.barrier()
            self.stream_pool.extend(self.phase_streams)
            self.phase_streams = []
            self.stack = None

    def dram_in(self, name, shape, dt=F32):
        return self.nc.dram_tensor(name, list(shape), dt, kind="ExternalInput").ap()

    def _decl_dram(self):
        L = self.L
        nc = self.nc
        self.x_in = self.dram_in("x", [S_LEN, D])
        self.mem_in = self.dram_in("mem", [MEM_LEN, D])
        self.wpk = self.dram_in("wpk", [L, 128, NKC * IN_COLS])
        self.woutpk = self.dram_in("woutpk", [L, 128, 9 * D])
        self.wmempk = self.dram_in("wmempk", [L, 128, NKC * 512])
        self.foxb = self.dram_in("foxb", [L, 6, 1])
        self.mib = self.dram_in("mib", [L, 4, 1])
        self.mfb = self.dram_in("mfb", [L, 4, 1])
        self.convw = self.dram_in("convw", [L, 96, 32])
        self.convb = self.dram_in("convb", [L, 96, 8])
        self.ngd = self.dram_in("ng", [L, 96, 4])
        self.lng = self.dram_in("lng", [L, 1, D])
        self.lnb = self.dram_in("lnb", [L, 1, D])
        self.out = nc.dram_tensor("out", [S_LEN, D], F32, kind="ExternalOutput").ap()
        self.xbuf = [nc.dram_tensor(f"xbuf{i}", [S_LEN, D], F32).ap() for i in range(2)] if L > 1 else []
        self.dbg_out = {}
        self.gdram = nc.dram_tensor("gdram", [2, 4 * NG, G], F32).ap()
        self.gdram_b = [Buf(f"gdram_g{g}") for g in range(NG)]

    def dbg_dram(self, name, shape, dt=F32):
        ap = self.nc.dram_tensor(name, list(shape), dt, kind="ExternalOutput").ap()
        self.dbg_out[name] = ap
        return ap

    def _alloc(self):
        nc, S = self.nc, self.S
        self.pb = []
        for i in range(8):
            t = nc.alloc_psum_tensor(f"pb{i}", [128, 512], F32)
            self.pb.append((t, Buf(f"pb{i}")))
        self.xT = nc.alloc_sbuf_tensor("xT", [128, NKC, S_LEN], BF16)
        self.xT_b = [Buf(f"xT_g{g}") for g in range(NG)]
        self.ycat = nc.alloc_sbuf_tensor("ycat", [128, 9, S_LEN], BF16)
        self.ycat_b = [[Buf(f"ycat_{c}_{g}") for g in range(NG)] for c in range(9)]
        self.stq = [S.dma_stream("stq0"), S.dma_stream("stq1")]
        self.memT = self.tile("memT", [128, NKC, MEM_LEN], BF16)
        self.gtok_all = self.tile("gtok_all", [128, NT, 4])
        self.gtokS_all = self.tile("gtokS_all", [128, NT, 4])

    def _consts(self):
        S = self.S
        self.onesf = self.tile("onesf", [128, 128])
        self.zerf = self.tile("zerf", [128, 128])
        self.identf = self.tile("identf", [128, 128])
        self.identb = self.tile("identb", [128, 128], BF16)
        self.mnegf = self.tile("mnegf", [128, 128])
        self.mnegb = self.tile("mnegb", [128, 128], BF16)
        self.avg96 = self.tile("avg96", [128, 96])
        self.fillsrc = self.tile("fillsrc", [128, 512], BF16)
        S.op(S.POOL, lambda h: h.memset(self.fillsrc[:], 0.5), W=[self.fillsrc.b])
        o, z = self.onesf, self.zerf
        S.op(S.POOL, lambda h: h.memset(o[:], 1.0), W=[o.b])
        S.op(S.POOL, lambda h: h.memset(z[:], 0.0), W=[z.b])
        S.op(S.POOL, lambda h: h.memset(self.avg96[:], 1.0 / 96.0), W=[self.avg96.b])
        S.op(S.POOL, lambda h: h.affine_select(out=self.identf[:], in_=o[:], pattern=[[1, 128]],
                                               compare_op=ALU.is_equal, fill=0.0, base=0, channel_multiplier=-1),
             R=[o.b], W=[self.identf.b])
        S.op(S.DVE, lambda h: h.tensor_copy(out=self.identb[:], in_=self.identf[:]), R=[self.identf.b], W=[self.identb.b])
        S.op(S.POOL, lambda h: h.affine_select(out=self.mnegf[:], in_=z[:], pattern=[[1, 128]],
                                               compare_op=ALU.is_ge, fill=-30000.0, base=0, channel_multiplier=-1),
             R=[z.b], W=[self.mnegf.b])
        S.op(S.DVE, lambda h: h.tensor_copy(out=self.mnegb[:], in_=self.mnegf[:]), R=[self.mnegf.b], W=[self.mnegb.b])

    def mm(self, bank, out_ap, pairs, R, start=True, stop=True, sgc=False):
        S = self.S
        n = len(pairs)
        for i, (lhsT, rhs) in enumerate(pairs):
            S.op(S.PE, lambda h, lhsT=lhsT, rhs=rhs, i=i: h.matmul(
                out_ap, lhsT=lhsT, rhs=rhs, start=(start and i == 0), stop=(stop and i == n - 1), skip_group_check=sgc),
                R=R, X=[self.pb[bank][1]], sig=(i == n - 1))

    def filler(self, n=256, bank=7):
        S = self.S
        pt, pbuf = self.pb[bank]
        S.op(S.PE, lambda h: h.matmul(pt[:, 0:n], lhsT=self.identb[:], rhs=self.fillsrc[:, 0:n], start=True, stop=True),
             R=[self.identb.b, self.fillsrc.b], X=[pbuf], sig=False)

    def bankrot(self, banks):
        st = {"i": 0}

        def nxt():
            b = banks[st["i"] % len(banks)]
            st["i"] += 1
            return b
        return nxt

    def build_xT_from_dram(self, x_src):
        S = self.S
        xin = [self.tile("xin", [128, D], dma=True) for _ in range(2)]
        rot = self.bankrot([0, 1, 2, 3])
        for t in range(NT):
            xt = xin[t % 2]
            S.op(S.SP, lambda h, xt=xt, t=t: h.dma_start(out=xt[:], in_=x_src[t * 128:(t + 1) * 128, :]),
                 W=[xt.b], stream=xt.st)
            self.transpose_into_xT(xt, t, rot)

    def transpose_into_xT(self, src, t, rot, eng=None):
        S = self.S
        for half in range(2):
            bk = rot()
            pt, pbuf = self.pb[bk]
            for c in range(4):
                kc = half * 4 + c
                S.op(S.PE, lambda h, kc=kc, c=c, pt=pt: h.transpose(
                    pt[:, c * 128:(c + 1) * 128], src[:, kc * 128:(kc + 1) * 128], self.identf[:]),
                    R=[src.b, self.identf.b], X=[pbuf], sig=(c == 3))
            e = eng if eng is not None else (S.ACT if half == 0 else S.DVE)
            dst = self.xT[:, half * 4:(half + 1) * 4, t * 128:(t + 1) * 128]
            srcp = pt[:, :].rearrange("p (c t) -> p c t", c=4)
            if e is S.ACT:
                S.op(e, lambda h, dst=dst, srcp=srcp: h.activation(out=dst, in_=srcp, func=AF.Copy),
                     X=[pbuf], W=[self.xT_b[t // 4]])
            else:
                S.op(e, lambda h, dst=dst, srcp=srcp: h.tensor_copy(out=dst, in_=srcp),
                     X=[pbuf], W=[self.xT_b[t // 4]])

    def load_w(self, dst_tile, src_ap):
        S = self.S
        self.n_sw = getattr(self, "n_sw", 0) + 1
        st = S.dma_stream(f"sw{self.n_sw}")
        S.op(S.POOL, lambda h: h.dma_start(out=dst_tile[:], in_=src_ap), W=[dst_tile.b], stream=st)


    def phase_fox(self, l):
        S = self.S
        sizes, offs = win_offsets()
        rot = self.bankrot([0, 1])
        rot_s = self.bankrot([2, 3, 4])
        rot_o = self.bankrot([5, 6])
        shiftrows = self.tile("shiftrows", [6, S_LEN], BF16)
        negc_tok = self.tile("negc_tok", [128, NT, 6])
        negcref_rep = self.tile("negcref_rep", [128, 6, NG])
        with self.sub_arena():
            rotG = self.bankrot([2, 3, 4])
            gens = [self.fox_prep(l, offs, rot, shiftrows, negc_tok, negcref_rep),
                    self.mlstm_gates(l, offs, rotG, self.gtok_all, self.gtokS_all, self.gdram, self.gdram_b, float(96 ** -0.5))]
            alive = True
            while alive:
                alive = False
                for gn in gens:
                    if next(gn, "end") != "end":
                        alive = True
        wh = [self.tile("wh", [128, NKC * 320], BF16, dma=True), self.tile("wh", [128, NKC * 192], BF16, dma=True)]
        szp = self.tile("szp", [128, NG, G], BF16)
        szp_b = [Buf(f"szp_g{g}") for g in range(NG)]
        KaT = self.tile("KaT", [65, S_LEN], BF16)
        KaT_b = [Buf(f"KaT_g{g}") for g in range(NG)]
        S.op(S.POOL, lambda h: h.memset(KaT[64:65, :], 1.0), W=KaT_b)
        Vaug = self.tile("Vaug", [128, NT, 192], BF16)
        V_ones = Buf("V_ones")
        V_b = [[Buf(f"V_{par}_g{g}") for g in range(NG)] for par in range(2)]
        S.op(S.POOL, lambda h: h.memset(Vaug[:, :, 64:128], 1.0), W=[V_ones])
        QaT = [self.tile("QaT", [65, G], BF16, dma=True) for _ in range(2)]
        szf = [self.tile("szf", [128, G], BF16) for _ in range(2)]
        PT = [self.tile("PTf", [128, G], BF16) for _ in range(3)]
        rd = [self.tile("rdf", [128, G]) for _ in range(1)]
        tn = [self.tile("tnf", [128, G]) for _ in range(1)]
        thf = [self.tile("thf", [128, G]) for _ in range(2)]
        tabs = [self.tile("tab", [128, NT, NG]) for _ in range(2)]
        ipc = {"i": 0}

        def build_tab(hd):
            tab = tabs[hd % 2]
            for qg in range(NG):
                S.op(S.DVE, lambda h, qg=qg, tab=tab, hd=hd: h.tensor_scalar(
                    out=tab[:, :, qg], in0=negc_tok[:, :, hd], scalar1=negcref_rep[:, hd, qg:qg + 1], scalar2=None, op0=ALU.subtract),
                    R=[negc_tok.b, negcref_rep.b], W=[tab.b])

        def gen_proj(hd, g):
            odd = hd % 2
            W_ = wh[hd % 2]
            Wv = W_[:, :].rearrange("p (c n) -> p c n", c=NKC)
            gsl = slice(g * G, (g + 1) * G)
            qa = QaT[g % 2]
            vc0 = 128 if odd else 0
            RR = [W_.b, self.xT_b[g]]
            bk = rot()
            pt, pbuf = self.pb[bk]
            for kc in range(NKC):
                S.op(S.PE, lambda h, kc=kc, pt=pt: h.matmul(pt[:, :], lhsT=Wv[:, kc, 0:128], rhs=self.xT[:, kc, gsl],
                                                            start=(kc == 0), stop=(kc == NKC - 1)), R=RR, X=[pbuf], sig=(kc == NKC - 1))
                if kc % 2 == 1 and kc < NKC - 1:
                    yield
            S.op(S.DVE, lambda h, pt=pt: h.tensor_scalar(out=qa[0:64, :], in0=pt[0:64, :], scalar1=0.125, scalar2=None, op0=ALU.mult), X=[pbuf], W=[qa.b])
            S.op(S.DVE, lambda h, pt=pt: h.tensor_copy(out=KaT[0:64, gsl], in_=pt[64:128, :]), X=[pbuf], W=[KaT_b[g]])
            S.op(S.SP, lambda h: h.dma_start(out=qa[64:65, :], in_=shiftrows[hd:hd + 1, gsl]), R=[shiftrows.b], W=[qa.b], stream=qa.st)
            yield
            bk = rot()
            pt, pbuf = self.pb[bk]
            for j in range(4):
                blk = slice(g * G + j * 128, g * G + (j + 1) * 128)
                for kc in range(NKC):
                    S.op(S.PE, lambda h, kc=kc, pt=pt, j=j, blk=blk: h.matmul(
                        pt[:, j * 64:(j + 1) * 64], lhsT=self.xT[:, kc, blk], rhs=Wv[:, kc, 128:192],
                        start=(kc == 0), stop=(kc == NKC - 1)), R=RR, X=[pbuf], sig=(kc == NKC - 1))
                    if kc == 3:
                        yield
                yield
            S.op(S.DVE, lambda h, pt=pt: h.tensor_copy(
                out=Vaug[:, 4 * g:4 * g + 4, vc0:vc0 + 64], in_=pt[:, 0:256].rearrange("p (j c) -> p j c", j=4)),
                X=[pbuf], W=[V_b[odd][g]])
            yield
            if not odd:
                bk = rot()
                pt, pbuf = self.pb[bk]
                for kc in range(NKC):
                    S.op(S.PE, lambda h, kc=kc, pt=pt: h.matmul(pt[:, :], lhsT=Wv[:, kc, 192:320], rhs=self.xT[:, kc, gsl],
                                                                start=(kc == 0), stop=(kc == NKC - 1)), R=RR, X=[pbuf], sig=(kc == NKC - 1))
                    if kc % 2 == 1 and kc < NKC - 1:
                        yield
                th = thf[g % 2]
                S.op(S.ACT, lambda h, pt=pt: h.activation(out=th[:, :], in_=pt[:, :], func=AF.Tanh, scale=0.5), X=[pbuf], W=[th.b])
                S.op(S.DVE, lambda h, pt=pt: h.scalar_tensor_tensor(out=szp[:, g, :], in0=th[:, :], scalar=1.0, in1=pt[:, :], op0=ALU.add, op1=ALU.mult),
                     X=[pbuf], R=[th.b], W=[szp_b[g]])
            yield

        N_CHUNKS = 20

        def attention(hd, g, chunks):
            odd = hd % 2
            lc0 = 64 if odd else 0
            tab = tabs[hd % 2]
            qa = QaT[g % 2]
            nkb = 4 * g + 4
            emitted = {"n": 0}
            bo = rot_o()
            po, pobuf = self.pb[bo]

            def issue_st(kb):
                diag = kb >= 4 * g
                qoff = (kb - 4 * g) * 128 if diag else 0
                n = G - qoff
                bs = rot_s()
                ps_, psbuf = self.pb[bs]
                self.mm(bs, ps_[:, 0:n], [(KaT[0:65, kb * 128:(kb + 1) * 128], qa[0:65, qoff:G])], R=[KaT_b[kb // 4], qa.b],
                        start=True, stop=not diag)
                if diag:
                    self.mm(bs, ps_[:, 0:128], [(self.identb[:], self.mnegb[:])], R=[self.identb.b, self.mnegb.b], start=False, stop=True)
                return ps_, psbuf, qoff, n

            nxt = issue_st(0)
            for kb in range(nkb):
                cur = nxt
                if kb + 1 < nkb:
                    nxt = issue_st(kb + 1)
                ps_, psbuf, qoff, n = cur
                pt_ = PT[ipc["i"] % 3]
                ipc["i"] += 1
                S.op(S.ACT, lambda h, ps_=ps_, pt_=pt_, n=n, kb=kb: h.activation(
                    out=pt_[:, 0:n], in_=ps_[:, 0:n], func=AF.Exp, bias=tab[:, kb, g:g + 1]), X=[psbuf], R=[tab.b], W=[pt_.b])
                did = False
                if chunks is not None:
                    want = ((kb + 1) * N_CHUNKS + nkb - 1) // nkb
                    while emitted["n"] < want:
                        if next(chunks, "end") != "end":
                            did = True
                        emitted["n"] += 1
                if not did:
                    self.filler(256)
                self.mm(bo, po[:, qoff:G], [(Vaug[:, kb, lc0:lc0 + 128], pt_[:, 0:n])], R=[V_b[odd][kb // 4], V_ones, pt_.b],
                        start=(kb == 0), stop=(kb == nkb - 1))
            if chunks is not None:
                for _ in chunks:
                    pass
            self.attn_epilogue(po, pobuf, bool(odd), (szp, g, szp_b[g]), rd[0], tn[0], hd // 2, g, half=True)

        self.load_w(wh[0], self.wpk[l][:, int(offs[1]):int(offs[2])])
        build_tab(0)
        for _ in gen_proj(0, 0):
            pass
        for hd in range(6):
            if hd + 1 < 6:
                self.load_w(wh[(hd + 1) % 2], self.wpk[l][:, int(offs[2 + hd]):int(offs[3 + hd])])
                build_tab(hd + 1)
            for g in range(NG):
                if g + 1 < NG:
                    chunks = gen_proj(hd, g + 1)
                elif hd + 1 < 6:
                    chunks = gen_proj(hd + 1, 0)
                else:
                    chunks = None
                attention(hd, g, chunks)


    def fox_prep(self, l, offs, rot, shiftrows, negc_tok, negcref_rep):
        S = self.S
        wff = self.tile("wff", [128, NKC * 6], BF16, dma=True)
        self.load_w(wff, self.wpk[l][:, int(offs[0]):int(offs[1])])
        wffv = wff[:, :].rearrange("p (c n) -> p c n", c=NKC)
        fb = self.tile("fb", [6, 1], dma=True)
        S.op(S.SP, lambda h: h.dma_start(out=fb[:], in_=self.foxb[l]), W=[fb.b], stream=fb.st)
        nfb = self.tile("nfb", [6, 1])
        S.op(S.DVE, lambda h: h.tensor_scalar(out=nfb[:], in0=fb[:], scalar1=-1.0, scalar2=None, op0=ALU.mult), R=[fb.b], W=[nfb.b])
        negcref6 = self.tile("negcref6", [6, NG])
        oh6 = self.tile("oh6", [6, 6, 128])
        for hh in range(6):
            S.op(S.POOL, lambda h, hh=hh: h.affine_select(
                out=oh6[:, hh, :], in_=self.onesf[0:6, :], pattern=[[0, 128]], compare_op=ALU.is_equal,
                fill=0.0, base=-hh, channel_multiplier=1), R=[self.onesf.b], W=[oh6.b])
        e_t = [self.tile("e_t", [6, G]) for _ in range(2)]
        negc = [self.tile("negc", [6, G]) for _ in range(2)]
        for g in range(NG):
            gsl = slice(g * G, (g + 1) * G)
            bk = rot()
            pt, pbuf = self.pb[bk]
            self.mm(bk, pt[0:6, :], [(wffv[:, kc, :], self.xT[:, kc, gsl]) for kc in range(NKC)], R=[wff.b, self.xT_b[g]])
            e, nc_, ncp = e_t[g % 2], negc[g % 2], negc[(g - 1) % 2]
            lt = e
            S.op(S.ACT, lambda h, pt=pt, e=e: h.activation(out=e[:], in_=pt[0:6, :], func=AF.Exp, scale=-1.0, bias=nfb[:, 0:1]),
                 X=[pbuf], R=[nfb.b], W=[e.b])
            yield
            S.op(S.ACT, lambda h, e=e, lt=lt: h.activation(out=lt[:], in_=e[:], func=AF.Ln, bias=1.0), R=[e.b], W=[lt.b])
            yield
            init = 0.0 if g == 0 else ncp[:, G - 1:G]
            S.op(S.DVE, lambda h, lt=lt, nc_=nc_, init=init: h.tensor_tensor_scan(
                out=nc_[:], data0=self.onesf[0:6, 0:1].to_broadcast([6, G]), data1=lt[:], initial=init, op0=ALU.mult, op1=ALU.add),
                R=[lt.b, self.onesf.b] + ([ncp.b] if g > 0 else []), W=[nc_.b])
            S.op(S.DVE, lambda h, nc_=nc_, g=g: h.tensor_copy(out=negcref6[:, g:g + 1], in_=nc_[:, 0:1]), R=[nc_.b], W=[negcref6.b])
            S.op(S.DVE, lambda h, nc_=nc_, gsl=gsl: h.tensor_scalar(out=shiftrows[:, gsl], in0=nc_[:], scalar1=nc_[:, 0:1], scalar2=-1.0,
                                                                 op0=ALU.subtract, op1=ALU.mult), R=[nc_.b], W=[shiftrows.b])
            yield
            bk = rot()
            pt, pbuf = self.pb[bk]
            for j in range(4):
                S.op(S.PE, lambda h, pt=pt, j=j, nc_=nc_: h.transpose(pt[:, j * 6:(j + 1) * 6], nc_[0:6, j * 128:(j + 1) * 128], self.identf[0:6, 0:6]),
                     R=[nc_.b, self.identf.b], X=[pbuf], sig=(j == 3))
            S.op(S.DVE, lambda h, pt=pt, g=g: h.tensor_copy(out=negc_tok[:, 4 * g:4 * g + 4, :], in_=pt[:, 0:24].rearrange("p (j c) -> p j c", j=4)),
                 X=[pbuf], W=[negc_tok.b])
            yield
        bk = rot()
        pt, pbuf = self.pb[bk]
        for hh in range(6):
            self.mm(bk, pt[:, hh * NG:(hh + 1) * NG], [(oh6[:, hh, :], negcref6[:, :])], R=[oh6.b, negcref6.b])
        S.op(S.DVE, lambda h: h.tensor_copy(out=negcref_rep[:, :, :], in_=pt[:, 0:6 * NG].rearrange("p (a b) -> p a b", a=6)),
             X=[pbuf], W=[negcref_rep.b])


    def mlstm_gates(self, l, offs, rot, gtok_all, gtokS_all, gdram, gdram_b, sK):
        S = self.S
        wmif = self.tile("wmif", [128, NKC * 8], BF16, dma=True)
        self.load_w(wmif, self.wpk[l][:, int(offs[7]):int(offs[8])])
        wmifv = wmif[:, :].rearrange("p (c n) -> p c n", c=NKC)
        ib = self.tile("ib", [4, 1], dma=True)
        fbm = self.tile("fbm", [4, 1], dma=True)
        for (t_, src) in ((ib, self.mib[l]), (fbm, self.mfb[l])):
            S.op(S.SP, lambda h, t_=t_, src=src: h.dma_start(out=t_[:], in_=src), W=[t_.b], stream=t_.st)
        nfbm = self.tile("nfbm", [4, 1])
        S.op(S.DVE, lambda h: h.tensor_scalar(out=nfbm[:], in0=fbm[:], scalar1=-1.0, scalar2=None, op0=ALU.mult), R=[fbm.b], W=[nfbm.b])
        e_t = [self.tile("me", [4, G]) for _ in range(2)]
        negF = [self.tile("negF", [4, G]) for _ in range(2)]
        gg = [self.tile("gg", [4, G]) for _ in range(2)]
        Gc = [self.tile("Gc", [4, G], dma=True) for _ in range(2)]
        nM = [self.tile("nM", [4, G], dma=True) for _ in range(2)]
        onesb = self.onesf[0:4, 0:1].to_broadcast([4, G])
        gview = [gdram[k].rearrange("(h g) t -> h g t", g=NG) for k in range(2)]
        for g in range(NG):
            gsl = slice(g * G, (g + 1) * G)
            xb = self.xT_b[g]
            e_, nF, g_, G_, nM_ = e_t[g % 2], negF[g % 2], gg[g % 2], Gc[g % 2], nM[g % 2]
            nFp, Gp = negF[(g - 1) % 2], Gc[(g - 1) % 2]
            bI = rot()
            pI, pIb = self.pb[bI]
            self.mm(bI, pI[0:4, :], [(wmifv[:, kc, 0:4], self.xT[:, kc, gsl]) for kc in range(NKC)], R=[wmif.b, xb])
            bF = rot()
            pF, pFb = self.pb[bF]
            self.mm(bF, pF[0:4, :], [(wmifv[:, kc, 4:8], self.xT[:, kc, gsl]) for kc in range(NKC)], R=[wmif.b, xb])
            S.op(S.ACT, lambda h, pF=pF, e_=e_: h.activation(out=e_[:], in_=pF[0:4, :], func=AF.Exp, scale=-1.0, bias=nfbm[:, 0:1]),
                 X=[pFb], R=[nfbm.b], W=[e_.b])
            yield
            S.op(S.ACT, lambda h, e_=e_: h.activation(out=e_[:], in_=e_[:], func=AF.Ln, bias=1.0), R=[e_.b], W=[e_.b])
            yield
            initF = 0.0 if g == 0 else nFp[:, G - 1:G]
            S.op(S.DVE, lambda h, initF=initF, nF=nF, e_=e_: h.tensor_tensor_scan(out=nF[:], data0=onesb, data1=e_[:], initial=initF,
                                                                                op0=ALU.mult, op1=ALU.add),
                 R=[e_.b, self.onesf.b] + ([nFp.b] if g > 0 else []), W=[nF.b])
            S.op(S.DVE, lambda h, pI=pI, g_=g_, nF=nF: h.scalar_tensor_tensor(out=g_[:], in0=pI[0:4, :], scalar=ib[:, 0:1], in1=nF[:],
                                                                             op0=ALU.add, op1=ALU.add), X=[pIb], R=[ib.b, nF.b], W=[g_.b])
            yield
            initG = 0.0 if g == 0 else Gp[:, G - 1:G]
            S.op(S.DVE, lambda h, initG=initG, G_=G_, g_=g_: h.tensor_tensor_scan(out=G_[:], data0=onesb, data1=g_[:], initial=initG,
                                                                                op0=ALU.mult, op1=ALU.max),
                 R=[g_.b, self.onesf.b] + ([Gp.b] if g > 0 else []), W=[G_.b])
            S.op(S.DVE, lambda h, nM_=nM_, nF=nF, G_=G_: h.tensor_tensor(out=nM_[:], in0=nF[:], in1=G_[:], op=ALU.subtract),
                 R=[nF.b, G_.b], W=[nM_.b])
            S.op(S.SP, lambda h, G_=G_, g=g: h.dma_start(out=gview[0][:, g, :], in_=G_[:]), R=[G_.b], W=[gdram_b[g]], stream=G_.st)
            S.op(S.SP, lambda h, nM_=nM_, g=g: h.dma_start(out=gview[1][:, g, :], in_=nM_[:]), R=[nM_.b], W=[gdram_b[g]], stream=nM_.st)
            yield
            bk = rot()
            pt, pbuf = self.pb[bk]
            for j in range(4):
                S.op(S.PE, lambda h, pt=pt, j=j, g_=g_: h.transpose(pt[:, j * 4:(j + 1) * 4], g_[0:4, j * 128:(j + 1) * 128], self.identf[0:4, 0:4]),
                     R=[g_.b, self.identf.b], X=[pbuf], sig=(j == 3))
            S.op(S.DVE, lambda h, pt=pt, g=g: h.tensor_copy(out=gtok_all[:, 4 * g:4 * g + 4, :], in_=pt[:, 0:16].rearrange("p (j c) -> p j c", j=4)),
                 X=[pbuf], W=[gtok_all.b])
            yield
        S.op(S.DVE, lambda h: h.tensor_scalar(out=gtokS_all[:, :, :], in0=gtok_all[:, :, :], scalar1=float(math.log(sK)), scalar2=None, op0=ALU.add),
             R=[gtok_all.b], W=[gtokS_all.b])

    def phase_mlstm(self, l):
        S = self.S
        sizes, offs = win_offsets()
        sK = float(96 ** -0.5)
        rot = self.bankrot([0, 1, 2])
        B_S, B_T, B_N, B_D = 4, 5, 6, 7
        gtok_all, gtokS_all = self.gtok_all, self.gtokS_all
        gdram, gdram_b = self.gdram, self.gdram_b
        cw = self.tile("cw", [96, 32], dma=True)
        cb = self.tile("cb", [96, 8], dma=True)
        ngt = self.tile("ngt", [96, 4], dma=True)
        for (t_, src) in ((cw, self.convw[l]), (cb, self.convb[l]), (ngt, self.ngd[l])):
            S.op(S.SP, lambda h, t_=t_, src=src: h.dma_start(out=t_[:], in_=src), W=[t_.b], stream=t_.st)
        lnhalf = self.tile("lnhalf", [128, 1])
        epsb = self.tile("epsb", [128, 1])
        S.op(S.POOL, lambda h: h.memset(lnhalf[:], float(math.log(0.5))), W=[lnhalf.b])
        S.op(S.POOL, lambda h: h.memset(epsb[:], float(LN_EPS)), W=[epsb.b])
        wml = [self.tile("wml", [128, NKC * 480], BF16, dma=True) for _ in range(2)]
        Grep = [self.tile("Grep", [128, G], dma=True) for _ in range(2)]
        clamp = [self.tile("clamp", [96, G], dma=True) for _ in range(2)]
        mu_chain = [self.tile("mu_chain", [128, 8]) for _ in range(2)]
        wexp = [self.tile("wexp", [128, 4]) for _ in range(2)]
        carry = [self.tile("carry", [128, 4]) for _ in range(2)]
        warg = self.tile("warg", [128, 4])
        carg = self.tile("carg", [128, 4])
        qpre = self.tile("qpre", [96, G + 3])
        kpre = self.tile("kpre", [96, G + 3])
        QT = [self.tile("QT", [96, G]) for _ in range(2)]
        KT = [self.tile("KT", [96, G]) for _ in range(2)]
        Vtok = [self.tile("Vtok", [128, 4, 192], BF16) for _ in range(2)]
        Vones = [Buf("Vtok_ones0"), Buf("Vtok_ones1")]
        for k in range(2):
            S.op(S.POOL, lambda h, k=k: h.memset(Vtok[k][:, :, 96:192], 1.0), W=[Vones[k]])
        sgo = [self.tile("sgo", [96, G]) for _ in range(2)]
        szm = [self.tile("szm", [96, G], BF16) for _ in range(2)]
        argDT = self.tile("argDT", [128, G])
        DTb = [Buf(f"DTb{j}") for j in range(4)]
        decb = [Buf(f"decb{j}") for j in range(4)]
        AT4 = self.tile("AT4", [128, G], BF16)
        decQ = self.tile("decQ", [96, G])
        decQb = self.tile("decQb", [96, G], BF16)
        Stb = self.tile("Stb", [96, 192], BF16)
        tBb = self.tile("tBb", [96, G], BF16)
        tAb = self.tile("tAb", [96, G], BF16)
        avg96b = self.tile("avg96b", [96, 96], BF16)
        S.op(S.DVE, lambda h: h.tensor_copy(out=avg96b[:], in_=self.avg96[0:96, :]), R=[self.avg96.b], W=[avg96b.b])
        Khat4 = self.tile("Khat4", [128, 4, 96], BF16)
        St = self.tile("St", [96, 192])
        tA = self.tile("tA", [96, G])
        tB = self.tile("tB", [96, G])
        tC = self.tile("tC", [96, G])
        cnt = {"b": 0}
        iters = [(hd, g) for hd in range(4) for g in range(NG)]

        def gen_A(i):
            hd, g = iters[i]
            k = i % 2
            gsl = slice(g * G, (g + 1) * G)
            xb = self.xT_b[g]
            W_ = wml[hd % 2]
            Wv = W_[:, :].rearrange("p (c n) -> p c n", c=NKC)
            RR = [W_.b, xb]
            Gr, cl, mu, we, ca = Grep[k], clamp[k], mu_chain[k], wexp[k], carry[k]
            mup = mu_chain[1 - k]
            row = hd * NG + g
            S.op(S.SP, lambda h: h.dma_start(out=Gr[:], in_=gdram[0][row:row + 1, :].partition_broadcast(128)),
                 R=[gdram_b[g]], W=[Gr.b], stream=Gr.st)
            S.op(S.SP, lambda h: h.dma_start(out=cl[:], in_=gdram[1][row:row + 1, :].partition_broadcast(96)),
                 R=[gdram_b[g]], W=[cl.b], stream=cl.st)
            S.op(S.ACT, lambda h: h.activation(out=cl[:], in_=cl[:], func=AF.Exp), R=[cl.b], W=[cl.b])
            if g == 0:
                S.op(S.POOL, lambda h: h.memset(mu[:, 0:1], 0.0), W=[mu.b])
                S.op(S.POOL, lambda h: h.memset(qpre[:, 0:3], 0.0), W=[qpre.b])
                S.op(S.POOL, lambda h: h.memset(kpre[:, 0:3], 0.0), W=[kpre.b])
            else:
                S.op(S.DVE, lambda h: h.tensor_copy(out=mu[:, 0:1], in_=mup[:, 4:5]), R=[mup.b], W=[mu.b])
            S.op(S.DVE, lambda h: h.tensor_copy(out=mu[:, 1:5], in_=Gr[:, :].rearrange("p (j t) -> p j t", j=4)[:, :, 127]),
                 R=[Gr.b], W=[mu.b])
            S.op(S.DVE, lambda h: h.tensor_tensor(out=warg[:], in0=gtok_all[:, 4 * g:4 * g + 4, hd], in1=mu[:, 1:5], op=ALU.subtract),
                 R=[gtok_all.b, mu.b], W=[warg.b])
            S.op(S.ACT, lambda h: h.activation(out=we[:], in_=warg[:], func=AF.Exp), R=[warg.b], W=[we.b])
            S.op(S.DVE, lambda h: h.tensor_tensor(out=carg[:], in0=mu[:, 0:4], in1=mu[:, 1:5], op=ALU.subtract), R=[mu.b], W=[carg.b])
            S.op(S.ACT, lambda h: h.activation(out=ca[:], in_=carg[:], func=AF.Exp), R=[carg.b], W=[ca.b])
            yield
            for (pre, acc, c0, qk) in ((qpre, QT[k], 0, 0), (kpre, KT[k], 96, 1)):
                if g > 0:
                    S.op(S.DVE, lambda h, pre=pre: h.tensor_copy(out=pre[:, 0:3], in_=pre[:, G:G + 3]), R=[pre.b], W=[pre.b])
                bk = rot()
                pt, pbuf = self.pb[bk]
                for kc in range(NKC):
                    S.op(S.PE, lambda h, kc=kc, pt=pt, c0=c0: h.matmul(pt[0:96, :], lhsT=Wv[:, kc, c0:c0 + 96], rhs=self.xT[:, kc, gsl],
                                                                      start=(kc == 0), stop=(kc == NKC - 1)), R=RR, X=[pbuf], sig=(kc == NKC - 1))
                    if kc % 2 == 1 and kc < NKC - 1:
                        yield
                S.op(S.ACT, lambda h, pt=pt, pre=pre: h.activation(out=pre[:, 3:G + 3], in_=pt[0:96, :], func=AF.Copy), X=[pbuf], W=[pre.b])
                yield
                wi = lambda tap, qk=qk: cw[:, (qk * 4 + hd) * 4 + tap:(qk * 4 + hd) * 4 + tap + 1]
                bi = cb[:, qk * 4 + hd:qk * 4 + hd + 1]
                S.op(S.DVE, lambda h, pre=pre, acc=acc, wi=wi, bi=bi: h.tensor_scalar(
                    out=acc[:], in0=pre[:, 3:G + 3], scalar1=wi(3), scalar2=bi, op0=ALU.mult, op1=ALU.add),
                    R=[pre.b, cw.b, cb.b], W=[acc.b])
                for kk in (1, 2, 3):
                    S.op(S.DVE, lambda h, pre=pre, acc=acc, wi=wi, kk=kk: h.scalar_tensor_tensor(
                        out=acc[:], in0=pre[:, 3 - kk:G + 3 - kk], scalar=wi(3 - kk), in1=acc[:], op0=ALU.mult, op1=ALU.add),
                        R=[pre.b, cw.b, acc.b], W=[acc.b])
                    if kk == 2:
                        yield
                yield
            bk = rot()
            pt, pbuf = self.pb[bk]
            Vt = Vtok[k]
            for j in range(4):
                blk = slice(g * G + j * 128, g * G + (j + 1) * 128)
                for kc in range(NKC):
                    S.op(S.PE, lambda h, kc=kc, pt=pt, j=j, blk=blk: h.matmul(
                        pt[:, j * 96:(j + 1) * 96], lhsT=self.xT[:, kc, blk], rhs=Wv[:, kc, 192:288],
                        start=(kc == 0), stop=(kc == NKC - 1)), R=RR, X=[pbuf], sig=(kc == NKC - 1))
                yield
            S.op(S.DVE, lambda h, pt=pt, Vt=Vt: h.tensor_copy(out=Vt[:, :, 0:96], in_=pt[:, 0:384].rearrange("p (j c) -> p j c", j=4)),
                 X=[pbuf], W=[Vt.b])
            yield
            held = []
            for c0 in (288, 384):
                bk = rot()
                pt, pbuf = self.pb[bk]
                for kc in range(NKC):
                    S.op(S.PE, lambda h, kc=kc, pt=pt, c0=c0: h.matmul(pt[0:96, :], lhsT=Wv[:, kc, c0:c0 + 96], rhs=self.xT[:, kc, gsl],
                                                                      start=(kc == 0), stop=(kc == NKC - 1)), R=RR, X=[pbuf], sig=(kc == NKC - 1))
                held.append((pt, pbuf))
            S.op(S.ACT, lambda h: h.activation(out=QT[k][:], in_=QT[k][:], func=AF.Silu), R=[QT[k].b], W=[QT[k].b])
            S.op(S.ACT, lambda h: h.activation(out=KT[k][:], in_=KT[k][:], func=AF.Silu), R=[KT[k].b], W=[KT[k].b])
            (pto, pbo), (ptz, pbz) = held
            S.op(S.ACT, lambda h: h.activation(out=sgo[k][:], in_=pto[0:96, :], func=AF.Tanh, scale=0.5), X=[pbo], W=[sgo[k].b])
            S.op(S.ACT, lambda h: h.activation(out=szm[k][:], in_=ptz[0:96, :], func=AF.Silu), X=[pbz], W=[szm[k].b])
            yield

        N_A = 30

        def run_BC(i, chunks):
            hd, g = iters[i]
            k = i % 2
            gsl = slice(g * G, (g + 1) * G)
            Gr, cl, mu, we, ca = Grep[k], clamp[k], mu_chain[k], wexp[k], carry[k]
            Q_, K_, Vt, Vo = QT[k], KT[k], Vtok[k], Vones[k]
            sg, sz = sgo[k], szm[k]
            pS, pSb = self.pb[B_S]
            pT, pTb = self.pb[B_T]
            pN, pNb = self.pb[B_N]
            pD, pDb = self.pb[B_D]

            fstate = {"pt": 0, "em": 0}
            N_PTS = 13

            def fill(n=1):
                did = False
                fstate["pt"] += 1
                if chunks is not None:
                    want = (fstate["pt"] * N_A + N_PTS - 1) // N_PTS
                    while fstate["em"] < want:
                        if next(chunks, "end") != "end":
                            did = True
                        fstate["em"] += 1
                if not did:
                    self.filler(512, bank=3)

            if g == 0:
                S.op(S.POOL, lambda h: h.memset(St[:], 0.0), W=[St.b])
                S.op(S.POOL, lambda h: h.memset(Stb[:], 0.0), W=[Stb.b])
            for j in range(4):
                bs = slice(j * 128, (j + 1) * 128)
                self.mm(B_S, pS[:, bs], [(K_[0:96, bs], Q_[0:96, bs])], R=[K_.b, Q_.b])
            S.op(S.DVE, lambda h: h.scalar_tensor_tensor(
                out=argDT[:, :].rearrange("p (j t) -> p j t", j=4), in0=Gr[:, :].rearrange("p (j t) -> p j t", j=4), scalar=-1.0,
                in1=self.mnegf[:, :].unsqueeze(1).to_broadcast([128, 4, 128]), op0=ALU.mult, op1=ALU.add),
                R=[Gr.b, self.mnegf.b], W=[argDT.b] + DTb)
            for j in range(4):
                bs = slice(j * 128, (j + 1) * 128)
                S.op(S.ACT, lambda h, bs=bs, j=j: h.activation(out=argDT[:, bs], in_=argDT[:, bs], func=AF.Exp,
                                                              bias=gtokS_all[:, 4 * g + j, hd:hd + 1]), R=[argDT.b, gtokS_all.b], W=[DTb[j]])
            for j in range(4):
                bs = slice(j * 128, (j + 1) * 128)
                S.op(S.ACT, lambda h, bs=bs, j=j: h.activation(out=decQ[:, bs], in_=Gr[0:96, bs], func=AF.Exp, scale=-1.0,
                                                              bias=mu[0:96, j:j + 1]), R=[Gr.b, mu.b], W=[decb[j]])
            fill()
            S.op(S.DVE, lambda h: h.tensor_tensor(out=AT4[:], in0=pS[:, :], in1=argDT[:], op=ALU.mult), X=[pSb], R=[argDT.b] + DTb, W=[AT4.b])
            S.op(S.POOL, lambda h: h.tensor_tensor(out=decQb[:], in0=Q_[:], in1=decQ[:], op=ALU.mult), R=[Q_.b, decQ.b] + decb, W=[decQb.b])
            for j in range(4):
                bs = slice(j * 128, (j + 1) * 128)
                S.op(S.PE, lambda h, bs=bs, j=j: h.transpose(pT[:, j * 96:(j + 1) * 96], K_[0:96, bs], self.identf[0:96, 0:96]),
                     R=[K_.b, self.identf.b], X=[pTb], sig=(j == 3))
            S.op(S.DVE, lambda h: h.scalar_tensor_tensor(
                out=Khat4[:, :, :], in0=pT[:, 0:384].rearrange("p (j c) -> p j c", j=4), scalar=sK,
                in1=we[:, 0:4].unsqueeze(2).to_broadcast([128, 4, 96]), op0=ALU.mult, op1=ALU.mult),
                X=[pTb], R=[we.b], W=[Khat4.b])
            fill()
            for j in range(4):
                bs = slice(j * 128, (j + 1) * 128)
                self.mm(B_N, pN[0:96, bs], [(Vt[:, j, 0:96], AT4[:, bs])], R=[Vt.b, AT4.b], start=(j == 0), stop=False, sgc=True)
            for j in range(4):
                bs = slice(j * 128, (j + 1) * 128)
                self.mm(B_D, pD[0:96, bs], [(Vt[:, j, 96:192], AT4[:, bs])], R=[Vo, AT4.b], start=(j == 0), stop=False, sgc=True)
            for j in range(4):
                ub, ubuf, uo = (pT, pTb, B_T) if j < 2 else (pS, pSb, B_S)
                c0 = (j % 2) * 192
                self.mm(uo, ub[0:96, c0:c0 + 192], [(Khat4[:, j, :], Vt[:, j, :])], R=[Khat4.b, Vt.b, Vo])
            fill()
            for j in range(4):
                bs = slice(j * 128, (j + 1) * 128)
                ub, ubuf = (pT, pTb) if j < 2 else (pS, pSb)
                c0 = (j % 2) * 192
                self.mm(B_N, pN[0:96, bs], [(Stb[0:96, 0:96], decQb[:, bs])], R=[Stb.b, decQb.b], start=False, stop=True, sgc=True)
                self.mm(B_D, pD[0:96, bs], [(Stb[0:96, 96:192], decQb[:, bs])], R=[Stb.b, decQb.b], start=False, stop=True, sgc=True)
                S.op(S.DVE, lambda h, j=j, ub=ub, c0=c0: h.scalar_tensor_tensor(out=St[:], in0=St[:], scalar=ca[0:96, j:j + 1],
                                                                               in1=ub[0:96, c0:c0 + 192], op0=ALU.mult, op1=ALU.add),
                     X=[ubuf], R=[ca.b, St.b], W=[St.b])
                S.op(S.DVE, lambda h: h.tensor_copy(out=Stb[:], in_=St[:]), R=[St.b], W=[Stb.b])
                if j % 2 == 1:
                    fill()
            S.op(S.DVE, lambda h: h.tensor_tensor(out=tA[:], in0=pD[0:96, :], in1=cl[:], op=ALU.max), X=[pDb], R=[cl.b], W=[tA.b])
            S.op(S.DVE, lambda h: h.scalar_tensor_tensor(out=tA[:], in0=pD[0:96, :], scalar=-1.0, in1=tA[:], op0=ALU.mult, op1=ALU.max),
                 X=[pDb], R=[tA.b], W=[tA.b])
            fill()
            S.op(S.ACT, lambda h: h.activation(out=tA[:], in_=tA[:], func=AF.Ln), R=[tA.b], W=[tA.b])
            S.op(S.ACT, lambda h: h.activation(out=tA[:], in_=tA[:], func=AF.Exp, scale=-1.0, bias=lnhalf[0:96, 0:1]), R=[tA.b, lnhalf.b], W=[tA.b])
            fill()
            S.op(S.DVE, lambda h: h.tensor_tensor(out=tB[:], in0=pN[0:96, :], in1=tA[:], op=ALU.mult), X=[pNb], R=[tA.b], W=[tB.b])
            S.op(S.DVE, lambda h: h.scalar_tensor_tensor(out=tB[:], in0=sg[:], scalar=1.0, in1=tB[:], op0=ALU.add, op1=ALU.mult),
                 R=[tB.b, sg.b], W=[tB.b])
            fill()
            S.op(S.ACT, lambda h: h.activation(out=tBb[:], in_=tB[:], func=AF.Copy), R=[tB.b], W=[tBb.b])
            bk = rot()
            pt, pbuf = self.pb[bk]
            self.mm(bk, pt[0:96, :], [(avg96b[:, :], tBb[:, :])], R=[avg96b.b, tBb.b])
            fill()
            S.op(S.DVE, lambda h, pt=pt: h.tensor_tensor(out=tC[:], in0=tB[:], in1=pt[0:96, :], op=ALU.subtract), X=[pbuf], R=[tB.b], W=[tC.b])
            S.op(S.ACT, lambda h: h.activation(out=tAb[:], in_=tC[:], func=AF.Square), R=[tC.b], W=[tAb.b])
            fill()
            bk = rot()
            pt, pbuf = self.pb[bk]
            self.mm(bk, pt[0:96, :], [(avg96b[:, :], tAb[:, :])], R=[avg96b.b, tAb.b])
            fill()
            S.op(S.ACT, lambda h, pt=pt: h.activation(out=tB[:], in_=pt[0:96, :], func=AF.Ln, bias=epsb[0:96, 0:1]), X=[pbuf], R=[epsb.b], W=[tB.b])
            S.op(S.ACT, lambda h: h.activation(out=tB[:], in_=tB[:], func=AF.Exp, scale=-0.5), R=[tB.b], W=[tB.b])
            fill()
            S.op(S.DVE, lambda h: h.scalar_tensor_tensor(out=tC[:], in0=tC[:], scalar=ngt[:, hd:hd + 1], in1=tB[:], op0=ALU.mult, op1=ALU.mult),
                 R=[tC.b, ngt.b, tB.b], W=[tC.b])
            S.op(S.POOL, lambda h: h.tensor_tensor(out=self.ycat[0:96, 3 + hd, gsl], in0=tC[:], in1=sz[:], op=ALU.mult),
                 R=[tC.b, sz.b], W=[self.ycat_b[3 + hd][g]])
            if chunks is not None:
                for _ in chunks:
                    pass

        self.load_w(wml[0], self.wpk[l][:, int(offs[8]):int(offs[9])])
        for _ in gen_A(0):
            pass
        for i in range(len(iters)):
            hd, g = iters[i]
            if g == 0 and hd + 1 < 4:
                self.load_w(wml[(hd + 1) % 2], self.wpk[l][:, int(offs[9 + hd]):int(offs[10 + hd])])
            chunks = gen_A(i + 1) if i + 1 < len(iters) else None
            run_BC(i, chunks)

    def build_memT(self):
        S = self.S
        mt = [self.tile("memin", [128, D], dma=True) for _ in range(2)]
        rot = self.bankrot([4, 5, 6, 7])
        for t in range(2):
            S.op(S.SP, lambda h, t=t: h.dma_start(out=mt[t][:], in_=self.mem_in[t * 128:(t + 1) * 128, :]), W=[mt[t].b], stream=mt[t].st)
            for half in range(2):
                bk = rot()
                pt, pbuf = self.pb[bk]
                for c in range(4):
                    kc = half * 4 + c
                    S.op(S.PE, lambda h, kc=kc, c=c, pt=pt, t=t: h.transpose(
                        pt[:, c * 128:(c + 1) * 128], mt[t][:, kc * 128:(kc + 1) * 128], self.identf[:]),
                        R=[mt[t].b, self.identf.b], X=[pbuf], sig=(c == 3))
                dst = self.memT[:, half * 4:(half + 1) * 4, t * 128:(t + 1) * 128]
                srcp = pt[:, :].rearrange("p (c t) -> p c t", c=4)
                S.op(S.DVE, lambda h, dst=dst, srcp=srcp: h.tensor_copy(out=dst, in_=srcp), X=[pbuf], W=[self.memT.b])

    def phase_mem(self, l):
        S = self.S
        sizes, offs = win_offsets()
        wm = self.tile("wm", [128, NKC * 512], BF16, dma=True)
        wr = self.tile("wr", [128, NKC * 512], BF16, dma=True)
        self.load_w(wm, self.wmempk[l])
        self.load_w(wr, self.wpk[l][:, int(offs[12]):int(offs[13])])
        wmv = wm[:, :].rearrange("p (c n) -> p c n", c=NKC)
        wrv = wr[:, :].rearrange("p (c n) -> p c n", c=NKC)
        KmT = self.tile("KmT", [128, 2, MEM_LEN], BF16)
        Vm = self.tile("Vm", [128, 2, 4, 128], BF16)
        rot = self.bankrot([0, 1])
        rot_s = self.bankrot([2, 3, 4])
        rot_o = self.bankrot([5, 6])
        for p in range(2):
            bk = rot()
            pt, pbuf = self.pb[bk]
            self.mm(bk, pt[:, 0:MEM_LEN], [(wmv[:, kc, p * 128:(p + 1) * 128], self.memT[:, kc, :]) for kc in range(NKC)],
                    R=[wm.b, self.memT.b])
            S.op(S.DVE, lambda h, pt=pt, p=p: h.tensor_copy(out=KmT[:, p, :], in_=pt[:, 0:MEM_LEN]), X=[pbuf], W=[KmT.b])
        S.op(S.POOL, lambda h: h.memset(Vm[:], 1.0), W=[Vm.b])
        for mb in range(2):
            bk = rot()
            pt, pbuf = self.pb[bk]
            self.mm(bk, pt[:, 0:256], [(self.memT[:, kc, mb * 128:(mb + 1) * 128], wmv[:, kc, 256:512]) for kc in range(NKC)],
                    R=[wm.b, self.memT.b])
            for h4 in range(4):
                c0 = 0 if h4 % 2 == 0 else 64
                S.op(S.DVE, lambda h, pt=pt, mb=mb, h4=h4, c0=c0: h.tensor_copy(
                    out=Vm[:, mb, h4, c0:c0 + 64], in_=pt[:, h4 * 64:(h4 + 1) * 64]), X=[pbuf], W=[Vm.b])
        QTm = [self.tile("QTm", [128, G], BF16) for _ in range(2)]
        szp = [self.tile("szp", [128, G], BF16) for _ in range(2)]
        thm = self.tile("thm", [128, G])
        PT = [self.tile("PTm", [128, G], BF16) for _ in range(3)]
        rd = [self.tile("rdm", [128, G]) for _ in range(2)]
        tn = [self.tile("tnm", [128, G]) for _ in range(2)]
        it = 0
        ip = 0
        for g in range(NG):
            gsl = slice(g * G, (g + 1) * G)
            for p in range(2):
                q, z = QTm[it % 2], szp[it % 2]
                it += 1
                bk = rot()
                pt, pbuf = self.pb[bk]
                self.mm(bk, pt[:, :], [(wrv[:, kc, p * 128:(p + 1) * 128], self.xT[:, kc, gsl]) for kc in range(NKC)],
                        R=[wr.b, self.xT_b[g]])
                S.op(S.ACT, lambda h, pt=pt, q=q: h.activation(out=q[:], in_=pt[:, :], func=AF.Copy, scale=0.125), X=[pbuf], W=[q.b])
                bk = rot()
                pt, pbuf = self.pb[bk]
                self.mm(bk, pt[:, :], [(wrv[:, kc, 256 + p * 128:256 + (p + 1) * 128], self.xT[:, kc, gsl]) for kc in range(NKC)],
                        R=[wr.b, self.xT_b[g]])
                S.op(S.ACT, lambda h, pt=pt: h.activation(out=thm[:], in_=pt[:, :], func=AF.Tanh, scale=0.5), X=[pbuf], W=[thm.b])
                S.op(S.DVE, lambda h, pt=pt, z=z: h.scalar_tensor_tensor(out=z[:], in0=thm[:], scalar=1.0, in1=pt[:, :], op0=ALU.add, op1=ALU.mult),
                     X=[pbuf], R=[thm.b], W=[z.b])
                for hh in range(2):
                    h4 = 2 * p + hh
                    r0 = 64 * hh
                    bo = rot_o()
                    po, pobuf = self.pb[bo]
                    sts = []
                    for mb in range(2):
                        bs = rot_s()
                        ps_, psbuf = self.pb[bs]
                        self.mm(bs, ps_[:, :], [(KmT[r0:r0 + 64, p, mb * 128:(mb + 1) * 128], q[r0:r0 + 64, :])], R=[KmT.b, q.b])
                        sts.append((ps_, psbuf))
                    for mb in range(2):
                        ps_, psbuf = sts[mb]
                        pt_ = PT[ip % 3]
                        ip += 1
                        S.op(S.ACT, lambda h, ps_=ps_, pt_=pt_: h.activation(out=pt_[:], in_=ps_[:, :], func=AF.Exp), X=[psbuf], W=[pt_.b])
                        self.mm(bo, po[:, :], [(Vm[:, mb, h4, :], pt_[:])], R=[Vm.b, pt_.b], start=(mb == 0), stop=(mb == 1))
                    self.attn_epilogue(po, pobuf, hh == 1, z, rd[h4 % 2], tn[h4 % 2], 7 + p, g, half=True)

    def attn_epilogue(self, po, pobuf, odd, sz, rd, tn, chunk, g, half=False, act_recip=False):
        S = self.S
        gsl = slice(g * G, (g + 1) * G)
        nr = slice(64, 128) if odd else slice(0, 64)
        dr = slice(0, 64) if odd else slice(64, 128)
        if act_recip:
            S.op(S.ACT, lambda h: h.activation(out=rd[nr, :], in_=po[dr, :], func=AF.Ln), X=[pobuf], W=[rd.b])
            S.op(S.ACT, lambda h: h.activation(out=rd[nr, :], in_=rd[nr, :], func=AF.Exp, scale=-1.0), R=[rd.b], W=[rd.b])
        else:
            S.op(S.DVE, lambda h: h.reciprocal(out=rd[nr, :], in_=po[dr, :]), X=[pobuf], W=[rd.b])
        if half:
            S.op(S.DVE, lambda h: h.scalar_tensor_tensor(out=tn[nr, :], in0=po[nr, :], scalar=0.5, in1=rd[nr, :], op0=ALU.mult, op1=ALU.mult),
                 X=[pobuf], R=[rd.b], W=[tn.b])
        else:
            S.op(S.DVE, lambda h: h.tensor_tensor(out=tn[nr, :], in0=po[nr, :], in1=rd[nr, :], op=ALU.mult), X=[pobuf], R=[rd.b], W=[tn.b])
        if isinstance(sz, tuple):
            szt, sg_, szb = sz
            S.op(S.POOL, lambda h: h.tensor_tensor(out=self.ycat[nr, chunk, gsl], in0=tn[nr, :], in1=szt[nr, sg_, :], op=ALU.mult),
                 R=[tn.b, szb], W=[self.ycat_b[chunk][g]])
        else:
            S.op(S.POOL, lambda h: h.tensor_tensor(out=self.ycat[nr, chunk, gsl], in0=tn[nr, :], in1=sz[nr, :], op=ALU.mult),
                 R=[tn.b, sz.b], W=[self.ycat_b[chunk][g]])

    def phase_final(self, l, x_src, x_dst, make_xT):
        S = self.S
        nc = self.nc
        wout = self.tile("wout", [128, 9 * D], BF16, dma=True)
        self.load_w(wout, self.woutpk[l])
        woutv = wout[:, :].rearrange("p (c n) -> p c n", c=9)
        gam = self.tile("gam", [128, D], dma=True)
        bet = self.tile("bet", [128, D], dma=True)
        S.op(S.SP, lambda h: h.dma_start(out=gam[:], in_=self.lng[l].partition_broadcast(128)), W=[gam.b], stream=gam.st)
        S.op(S.SP, lambda h: h.dma_start(out=bet[:], in_=self.lnb[l].partition_broadcast(128)), W=[bet.b], stream=bet.st)
        xin = [self.tile("xin", [128, D], dma=True) for _ in range(3)]
        epsf = self.tile("epsf", [128, 1])
        S.op(S.POOL, lambda h: h.memset(epsf[:], float(LN_EPS)), W=[epsf.b])
        tt = [self.tile("tt", [128, D]) for _ in range(2)]
        xo = [self.tile("xo", [128, D]) for _ in range(2)]
        stats = [self.tile("stats", [128, 16]) for _ in range(2)]
        krows = [128, 128, 128, 96, 96, 96, 96, 128, 128]
        rot_y = self.bankrot([0, 1, 2, 3])
        rot_t = self.bankrot([4, 5, 6, 7])

        def load_x(t):
            xt = xin[t % 3]
            S.op(S.SP, lambda h: h.dma_start(out=xt[:], in_=x_src[t * 128:(t + 1) * 128, :]), W=[xt.b], stream=xt.st)

        nmr = [self.tile("nmr", [128, 2]) for _ in range(2)]
        junk = self.tile("junk", [128, D], BF16)

        held = {}

        def stage_M_pe(t):
            g = t // 4
            hb = []
            for half in range(2):
                bk = rot_y()
                pt, pbuf = self.pb[bk]
                pairs = [(self.ycat[0:krows[c], c, t * 128:(t + 1) * 128], woutv[0:krows[c], c, half * 512:(half + 1) * 512])
                         for c in range(9)]
                self.mm(bk, pt[:, :], pairs, R=[wout.b] + [self.ycat_b[c][g] for c in range(9)])
                hb.append((pt, pbuf))
            held[t] = hb

        def stage_M_dve(t):
            xt, tq, sq = xin[t % 3], tt[t % 2], stats[t % 2]
            for half in range(2):
                pt, pbuf = held[t][half]
                S.op(S.DVE, lambda h, pt=pt, half=half: h.scalar_tensor_tensor(
                    out=tq[:, half * 512:(half + 1) * 512], in0=xt[:, half * 512:(half + 1) * 512], scalar=float(ALPHA),
                    in1=pt[:, :], op0=ALU.mult, op1=ALU.add), R=[xt.b], X=[pbuf], W=[tq.b])
            for half in range(2):
                S.op(S.DVE, lambda h, half=half: h.bn_stats(out=sq[:, half * 6:(half + 1) * 6],
                                                            in_=tq[:, half * 512:(half + 1) * 512]), R=[tq.b], W=[sq.b])

        def stage_N(t):
            tq, xq, sq, nm = tt[t % 2], xo[t % 2], stats[t % 2], nmr[t % 2]
            S.op(S.DVE, lambda h: h.bn_aggr(out=sq[:, 12:14], in_=sq[:, 0:12]), R=[sq.b], W=[sq.b])
            S.op(S.ACT, lambda h: h.activation(out=sq[:, 14:15], in_=sq[:, 13:14], func=AF.Ln, bias=epsf[:, 0:1]), R=[sq.b, epsf.b], W=[sq.b])
            S.op(S.ACT, lambda h: h.activation(out=nm[:, 0:1], in_=sq[:, 14:15], func=AF.Exp, scale=-0.5), R=[sq.b], W=[nm.b])
            S.op(S.DVE, lambda h: h.tensor_scalar(out=nm[:, 1:2], in0=sq[:, 12:13], scalar1=nm[:, 0:1], scalar2=-1.0, op0=ALU.mult, op1=ALU.mult),
                 R=[sq.b, nm.b], W=[nm.b])
            S.op(S.ACT, lambda h: h.activation(out=tq[:], in_=tq[:], func=AF.Identity, scale=nm[:, 0:1], bias=nm[:, 1:2]), R=[tq.b, nm.b], W=[tq.b])

        def stage_N_b(t):
            tq, xq = tt[t % 2], xo[t % 2]
            S.op(S.POOL, lambda h: h.tensor_tensor(out=xq[:], in0=tq[:], in1=gam[:], op=ALU.mult), R=[tq.b, gam.b], W=[xq.b])
            S.op(S.POOL, lambda h: h.tensor_tensor(out=xq[:], in0=xq[:], in1=bet[:], op=ALU.add), R=[xq.b, bet.b], W=[xq.b])
            st = self.stq[t % 2]
            S.op(S.SP, lambda h: h.dma_start(out=x_dst[t * 128:(t + 1) * 128, :], in_=xq[:]), R=[xq.b], stream=st)

        load_x(0)
        load_x(1)
        stage_M_pe(0)
        stage_M_dve(0)
        for t in range(NT):
            if t + 2 < NT:
                load_x(t + 2)
            if t + 1 < NT:
                stage_M_pe(t + 1)
            stage_N(t)
            stage_N_b(t)
            if t + 1 < NT:
                stage_M_dve(t + 1)
            if make_xT and t >= 1:
                self.transpose_into_xT(xo[(t - 1) % 2], t - 1, rot_t)
        if make_xT:
            self.transpose_into_xT(xo[(NT - 1) % 2], NT - 1, rot_t)

    def _build(self):
        S = self.S
        L = self.L
        for l in range(L):
            x_src = self.x_in if l == 0 else self.xbuf[(l - 1) % 2]
            x_dst = self.out if l == L - 1 else self.xbuf[l % 2]
            if l == 0:
                self.run_phase(self.build_xT_from_dram, x_src)
            if l == 0:
                self.run_phase(self.build_memT)
            self.run_phase(self.phase_fox, l)
            if "noml" not in self.dbg:
                self.run_phase(self.phase_mlstm, l)
            self.run_phase(self.phase_mem, l)
            if "ycat" in self.dbg:
                S.barrier()
                d = self.dbg_dram("dbg_ycat", [128, 9 * S_LEN], BF16)
                S.op(S.SP, lambda h: h.dma_start(out=d, in_=self.ycat[:, :, :].rearrange("p c s -> p (c s)")),
                     R=[b for cb in self.ycat_b for b in cb], stream=self.stq[0])
            self.run_phase(self.phase_final, l, x_src, x_dst, make_xT=(l < L - 1))
        S.finish(S.SP)


def _pack_win(w):
    w3 = w.reshape(NKC, 128, IN_COLS).transpose(1, 0, 2)
    groups = []
    groups.append(np.arange(O_FF, O_FF + 6))
    for h in range(6):
        cols = [np.arange(o + 64 * h, o + 64 * (h + 1)) for o in (O_FQ, O_FK, O_FV)]
        if h % 2 == 0:
            cols.append(np.arange(O_FZ + 64 * h, O_FZ + 64 * (h + 2)))
        groups.append(np.concatenate(cols))
    groups.append(np.concatenate([np.arange(O_MI, O_MI + 4), np.arange(O_MF, O_MF + 4)]))
    for h in range(4):
        groups.append(np.concatenate([np.arange(o + 96 * h, o + 96 * (h + 1)) for o in (O_MQ, O_MK, O_MV, O_MO, O_MZ)]))
    groups.append(np.concatenate([np.arange(O_RQ, O_RQ + 256), np.arange(O_RZ, O_RZ + 256)]))
    parts = [np.ascontiguousarray(w3[:, :, gidx]).reshape(128, -1) for gidx in groups]
    return np.concatenate(parts, axis=1)


def win_offsets():
    sizes = [6] + [320, 192] * 3 + [8] + [480] * 4 + [512]
    offs = np.concatenate([[0], np.cumsum([NKC * s for s in sizes])])
    return sizes, offs


def _pack_wout(w):
    out = np.zeros((128, 9, D), np.float32)
    r = 0
    for c, k in enumerate([128, 128, 128, 96, 96, 96, 96, 128, 128]):
        out[0:k, c, :] = w[r:r + k, :]
        r += k
    return out.reshape(128, 9 * D)


def _pack_wmem(w):
    return np.ascontiguousarray(w.reshape(NKC, 128, 512).transpose(1, 0, 2)).reshape(128, NKC * 512)


def pack_layers(inp, layers):
    f = np.float32
    d = {}
    d["wpk"] = np.stack([_pack_win(np.asarray(inp["w_in"][l], f)) for l in layers])
    d["woutpk"] = np.stack([_pack_wout(np.asarray(inp["w_out"][l], f)) for l in layers])
    d["wmempk"] = np.stack([_pack_wmem(np.asarray(inp["w_mem_kv"][l], f)) for l in layers])
    d["foxb"] = np.stack([np.asarray(inp["fox_f_bias"][l], f).reshape(6, 1) for l in layers])
    d["mib"] = np.stack([np.asarray(inp["mlstm_i_bias"][l], f).reshape(4, 1) for l in layers])
    d["mfb"] = np.stack([np.asarray(inp["mlstm_f_bias"][l], f).reshape(4, 1) for l in layers])
    d["convw"] = np.stack([np.ascontiguousarray(np.asarray(inp["mlstm_conv_w"][l], f).reshape(4, 2, 4, 96).transpose(3, 1, 2, 0)).reshape(96, 32)
                           for l in layers])
    d["convb"] = np.stack([np.ascontiguousarray(np.asarray(inp["mlstm_conv_b"][l], f).reshape(2, 4, 96).transpose(2, 0, 1)).reshape(96, 8)
                           for l in layers])
    d["ng"] = np.stack([np.ascontiguousarray(np.asarray(inp["mlstm_norm_g"][l], f).reshape(4, 96).T) for l in layers])
    d["lng"] = np.stack([np.asarray(inp["ln_g"][l], f).reshape(1, D) for l in layers])
    d["lnb"] = np.stack([np.asarray(inp["ln_b"][l], f).reshape(1, D) for l in layers])
    return d


_PROG_CACHE = {}


def get_prog(L, **kw):
    key = (L, tuple(sorted(kw.items())))
    if key not in _PROG_CACHE:
        _PROG_CACHE[key] = Prog(L, **kw)
    return _PROG_CACHE[key]


def run_layers(x, mem, packed, n_cores=8):
    L = packed["wpk"].shape[0]
    prog = get_prog(L)
    B = x.shape[0]
    in_maps = []
    for c in range(n_cores):
        b = c % B
        m = {"x": np.ascontiguousarray(x[b]), "mem": np.ascontiguousarray(mem[b])}
        m.update(packed)
        in_maps.append(m)
    res = run_bass_kernel_spmd(prog.nc, in_maps, core_ids=list(range(n_cores)))
    return np.stack([np.asarray(res.results[b]["out"]) for b in range(B)])


FUSED = True


def kernel(x, mem, w_in, fox_f_bias, mlstm_conv_w, mlstm_conv_b, mlstm_i_bias, mlstm_f_bias,
           mlstm_norm_g, w_mem_kv, w_out, ln_g, ln_b):
    inp = dict(w_in=w_in, fox_f_bias=fox_f_bias, mlstm_conv_w=mlstm_conv_w, mlstm_conv_b=mlstm_conv_b,
               mlstm_i_bias=mlstm_i_bias, mlstm_f_bias=mlstm_f_bias, mlstm_norm_g=mlstm_norm_g,
               w_mem_kv=w_mem_kv, w_out=w_out, ln_g=ln_g, ln_b=ln_b)
    x = np.asarray(x, np.float32)
    mem = np.asarray(mem, np.float32)
    if FUSED:
        return run_layers(x, mem, pack_layers(inp, list(range(DEPTH))))
    for l in range(DEPTH):
        x = run_layers(x, mem, pack_layers(inp, [l]))
    return x
```
